# Optimizing a Trainium2 kernel written in Bass

```python
import jax, jax.numpy as jnp
from jax import lax
import numpy as np

D_MODEL = 2048
BATCH = 4
SEQ = 2048
DEPTH = 2
DEC_BATCH = 128
DEC_SEQ = 8
PAST_LEN = 16384
PAGE_SIZE = 128

N_HEADS = 4
D_QK = D_MODEL // 2
D_V = D_MODEL
DK_HEAD = D_QK // N_HEADS
DV_HEAD = D_V // N_HEADS
GATE_RANK = 16
GATE_NORMALIZER = 16
CHUNK = 16
D_POOL = D_MODEL // 2
POOL_GROUPS = 4
POOL_WINDOWS = (2, 4, 8, 16)
POOL_BUF = 15
POOL_IN_G = D_POOL // POOL_GROUPS
POOL_OUT_G = D_MODEL // POOL_GROUPS
N_IN = 2 * D_QK + 2 * D_V + D_POOL + 2 * D_MODEL + GATE_RANK
D_FF = ((8 * D_MODEL + 3 * 256 - 1) // (3 * 256)) * 256
EPS = 1e-6

kernel_name = "gla_pool_hybrid_adaln_decode_step"


def _rmsnorm(x, g):
    xf = x.astype(jnp.float32)
    y = xf * lax.rsqrt(jnp.mean(xf * xf, axis=-1, keepdims=True) + EPS)
    return (y * g.astype(jnp.float32)).astype(x.dtype)


def _gla(q, k, v, loga, s0):
    B, L, H, _ = q.shape
    n = -(-L // CHUNK)
    pad = n * CHUNK - L

    def prep(a):
        a = jnp.pad(a.astype(jnp.float32), ((0, 0), (0, pad), (0, 0), (0, 0)))
        return a.reshape(B, n, CHUNK, H, a.shape[-1]).transpose(1, 0, 3, 2, 4)

    q, k, v, loga = prep(q), prep(k), prep(v), prep(loga)
    b = jnp.cumsum(loga, axis=3)
    b_last = b[:, :, :, -1:, :]
    qe = q * jnp.exp(b)
    ke = k * jnp.exp(-b)
    kd = k * jnp.exp(b_last - b)
    decay = jnp.exp(b_last[:, :, :, 0, :])
    mask = jnp.tril(jnp.ones((CHUNK, CHUNK), dtype=bool))
    att = jnp.where(mask, jnp.einsum('nbhcd,nbhsd->nbhcs', qe, ke), 0.0)
    o_intra = jnp.einsum('nbhcs,nbhse->nbhce', att, v)

    def step(s, inp):
        qe_n, kd_n, v_n, dec_n = inp
        o = jnp.einsum('bhcd,bhde->bhce', qe_n, s)
        s = dec_n[..., None] * s + jnp.einsum('bhcd,bhce->bhde', kd_n, v_n)
        return s, o

    s_fin, o_inter = lax.scan(step, s0.astype(jnp.float32), (qe, kd, v, decay))
    o = (o_intra + o_inter).transpose(1, 0, 3, 2, 4).reshape(B, n * CHUNK, H, DV_HEAD)[:, :L]
    return o, s_fin


def _pool(u_ext, pos0, w_pool, pool_scale):
    B, T, _ = u_ext.shape
    L = T - POOL_BUF
    uf = u_ext.astype(jnp.float32).reshape(B, T, POOL_GROUPS, POOL_IN_G)
    cs = jnp.concatenate([jnp.zeros((B, 1, POOL_GROUPS, POOL_IN_G), jnp.float32),
                          jnp.cumsum(uf, axis=1)], axis=1)
    end = cs[:, POOL_BUF + 1:]
    u_new = uf[:, POOL_BUF:]
    pos = pos0 + jnp.arange(L)
    outs = []
    for g, w in enumerate(POOL_WINDOWS):
        start = cs[:, POOL_BUF + 1 - w: POOL_BUF + 1 - w + L, g]
        cnt = jnp.minimum(pos + 1, w).astype(jnp.float32)[None, :, None]
        outs.append((end[:, :, g] - start) / cnt - u_new[:, :, g])
    d = jnp.stack(outs, axis=2).astype(u_ext.dtype)
    y = jnp.einsum('blgi,gio->blgo', d, w_pool).reshape(B, L, D_MODEL)
    return y * pool_scale


def _layer(x, c, s0, buf0, pos0, w_ada, b_ada, norm1_g, w_in, w_a2, b_a, gla_norm_g,
           w_pool, pool_scale, w_o, norm2_g, w_gu, w_down):
    B, L, _ = x.shape
    mod = jax.nn.silu(c) @ w_ada + b_ada
    sh1, sc1, g1, sh2, sc2, g2 = [m[:, None, :] for m in jnp.split(mod, 6, axis=-1)]

    h = _rmsnorm(x, norm1_g) * (1 + sc1) + sh1
    p = h @ w_in
    sizes = [D_QK, D_QK, D_V, D_V, D_POOL, D_MODEL, D_MODEL]
    idx = [int(i) for i in np.cumsum(sizes)]
    q, k, v, og, u, ga, gb, alr = jnp.split(p, idx, axis=-1)
    gk = alr @ w_a2 + b_a
    loga = jax.nn.log_sigmoid(gk.astype(jnp.float32)) / GATE_NORMALIZER
    q = q.reshape(B, L, N_HEADS, DK_HEAD) * (DK_HEAD ** -0.5)
    k = k.reshape(B, L, N_HEADS, DK_HEAD)
    v = v.reshape(B, L, N_HEADS, DV_HEAD)
    o, s_new = _gla(q, k, v, loga.reshape(B, L, N_HEADS, DK_HEAD), s0)
    o = _rmsnorm(o, gla_norm_g).reshape(B, L, D_V).astype(x.dtype)
    y_a = o * jax.nn.silu(og)

    u_ext = jnp.concatenate([buf0.astype(u.dtype), u], axis=1)
    y_b = _pool(u_ext, pos0, w_pool, pool_scale)
    buf_new = u_ext[:, -POOL_BUF:]

    merged = jax.nn.sigmoid(ga) * y_a + jax.nn.sigmoid(gb) * y_b
    x = x + g1 * (merged @ w_o)

    h2 = _rmsnorm(x, norm2_g) * (1 + sc2) + sh2
    gate, up = jnp.split(h2 @ w_gu, 2, axis=-1)
    x = x + g2 * ((jax.nn.silu(gate) * up) @ w_down)
    return x, s_new, buf_new


def setup_inputs(seed: int = 0) -> dict:
    key = jax.random.key(seed)
    ks = jax.random.split(key, 24)
    f32 = jnp.float32
    nrm = lambda k, s, sc: jax.random.normal(k, s, f32) * sc
    return {
        "x_prompt": nrm(ks[0], (BATCH, SEQ, D_MODEL), 1.0),
        "x_sample": nrm(ks[1], (DEC_BATCH, DEC_SEQ, D_MODEL), 1.0),
        "state_gla": nrm(ks[2], (DEPTH, DEC_BATCH, N_HEADS, DK_HEAD, DV_HEAD), 1.0),
        "state_pool": nrm(ks[3], (DEPTH, DEC_BATCH, POOL_BUF, D_POOL), 1.0),
        "c_prompt": nrm(ks[4], (BATCH, D_MODEL), 1.0),
        "c_sample": nrm(ks[5], (DEC_BATCH, D_MODEL), 1.0),
        "w_ada": nrm(ks[6], (DEPTH, D_MODEL, 6 * D_MODEL), 0.5 * D_MODEL ** -0.5),
        "b_ada": nrm(ks[7], (DEPTH, 6 * D_MODEL), 0.02),
        "norm1_g": 1.0 + nrm(ks[8], (DEPTH, D_MODEL), 0.1),
        "w_in": nrm(ks[9], (DEPTH, D_MODEL, N_IN), D_MODEL ** -0.5),
        "w_a2": nrm(ks[10], (DEPTH, GATE_RANK, D_QK), GATE_RANK ** -0.5),
        "b_a": nrm(ks[11], (DEPTH, D_QK), 0.1),
        "gla_norm_g": 1.0 + nrm(ks[12], (DEPTH, DV_HEAD), 0.1),
        "w_pool": nrm(ks[13], (DEPTH, POOL_GROUPS, POOL_IN_G, POOL_OUT_G), POOL_IN_G ** -0.5),
        "pool_scale": 1.0 + nrm(ks[14], (DEPTH, D_MODEL), 0.1),
        "w_o": nrm(ks[15], (DEPTH, D_MODEL, D_MODEL), D_MODEL ** -0.5),
        "norm2_g": 1.0 + nrm(ks[16], (DEPTH, D_MODEL), 0.1),
        "w_gu": nrm(ks[17], (DEPTH, D_MODEL, 2 * D_FF), D_MODEL ** -0.5),
        "w_down": nrm(ks[18], (DEPTH, D_FF, D_MODEL), D_FF ** -0.5),
        "final_norm_g": 1.0 + nrm(ks[19], (D_MODEL,), 0.1),
    }


def reference(x_prompt, x_sample, state_gla, state_pool, c_prompt, c_sample,
              w_ada, b_ada, norm1_g, w_in, w_a2, b_a, gla_norm_g, w_pool, pool_scale,
              w_o, norm2_g, w_gu, w_down, final_norm_g):
    xp, xs = x_prompt, x_sample
    gla_p, pool_p, gla_s, pool_s = [], [], [], []
    for l in range(DEPTH):
        lw = (w_ada[l], b_ada[l], norm1_g[l], w_in[l], w_a2[l], b_a[l], gla_norm_g[l],
              w_pool[l], pool_scale[l], w_o[l], norm2_g[l], w_gu[l], w_down[l])
        s0_p = jnp.zeros((BATCH, N_HEADS, DK_HEAD, DV_HEAD), jnp.float32)
        buf0_p = jnp.zeros((BATCH, POOL_BUF, D_POOL), xp.dtype)
        xp, s_p, b_p = _layer(xp, c_prompt, s0_p, buf0_p, 0, *lw)
        xs, s_s, b_s = _layer(xs, c_sample, state_gla[l], state_pool[l], PAST_LEN, *lw)
        gla_p.append(s_p)
        pool_p.append(b_p)
        gla_s.append(s_s)
        pool_s.append(b_s)
    y_prompt = _rmsnorm(xp, final_norm_g)
    y_sample = _rmsnorm(xs, final_norm_g)
    return (y_prompt, y_sample, jnp.stack(gla_p), jnp.stack(pool_p), jnp.stack(gla_s), jnp.stack(pool_s))
```

```python
import contextlib
import numpy as np
import concourse.bass as bass
import concourse.mybir as mybir
from concourse.bass_utils import run_bass_kernel_spmd

F32 = mybir.dt.float32
BF16 = mybir.dt.bfloat16
ALU = mybir.AluOpType
AF = mybir.ActivationFunctionType

D = 2048
SEQ = 1024
NCORE = 8
SPC = 16
T = SEQ + SPC * 8
NT = T // 128
NPT = SEQ // 128
NMAT = 5 + 28
NIN = 11280
DFF = 5632
DEPTH = 2
OQ, OK_, OV, OOG, OU, OGA, OGB, OALR = 0, 1024, 2048, 4096, 6144, 7168, 9216, 11264
WINS = (2, 4, 8, 16)
EPS = 1e-6


class Tracker:
    SEM_CAP = 30000

    def __init__(self, nc, stack, n_sems):
        self.nc = nc
        self.engs = {"pe": nc.tensor, "act": nc.scalar, "dve": nc.vector, "pool": nc.gpsimd, "sp": nc.sync}
        self.free_sems = [stack.enter_context(nc.semaphore(f"s{i}")) for i in range(n_sems)]
        self.cur = {}
        self.seen = {e: {} for e in self.engs}
        self.lastw = {}
        self.readers = {}
        self.dsem = {}
        self._pend = []

    def wait(self, en, tok):
        if tok is None:
            return
        sem, val = tok
        sid = id(sem)
        if self.seen[en].get(sid, 0) >= val:
            return
        self.engs[en].wait_ge(sem, val)
        self.seen[en][sid] = val

    def _deps(self, en, reads, writes):
        for k in reads:
            self.wait(en, self.lastw.get(k))
        for k in writes:
            self.wait(en, self.lastw.get(k))
            for r in self.readers.get(k, ()):
                self.wait(en, r)

    def _commit(self, tok, reads, writes):
        for k in reads:
            self.readers.setdefault(k, []).append(tok)
        for k in writes:
            self.lastw[k] = tok
            self.readers[k] = []

    def op(self, en, fn, reads=(), writes=()):
        self._deps(en, reads, writes)
        ins = fn(self.engs[en])
        c = self.cur.get(en)
        if c is None or c[1] >= self.SEM_CAP:
            c = [self.free_sems.pop(), 0]
            self.cur[en] = c
        c[1] += 1
        ins.then_inc(c[0], 1)
        tok = (c[0], c[1])
        self._commit(tok, reads, writes)
        return tok

    def pe(self, fn, reads=(), writes=(), mark=True):
        if not mark:
            self._deps("pe", reads, writes)
            fn(self.engs["pe"])
            self._pend.append((tuple(reads), tuple(writes)))
            return None
        tok = self.op("pe", fn, reads, writes)
        for r, w in self._pend:
            self._commit(tok, r, w)
        self._pend = []
        return tok

    def collective(self, fn, reads=(), writes=()):
        self._deps("pool", reads, writes)
        ins = fn(self.engs["pool"])
        sem = self.free_sems.pop()
        ins.then_inc(sem)
        tok = (sem, 1)
        self._commit(tok, reads, writes)
        return tok

    def dma(self, q, semkey, out, in_, reads=(), writes=()):
        d = self.dsem.get(semkey)
        if d is None:
            d = [self.free_sems.pop(), 0]
            self.dsem[semkey] = d
        if d[1] > 0:
            self.wait(q, (d[0], d[1]))
        self._deps(q, reads, writes)
        ins = self.engs[q].dma_start(out=out, in_=in_)
        d[1] += 16
        ins.then_inc(d[0], 16)
        tok = (d[0], d[1])
        self._commit(tok, reads, writes)
        return tok


def build_program():
    nc = bass.Bass("TRN2", target_bir_lowering=False)
    din = lambda n, s: nc.dram_tensor(n, s, F32, kind="ExternalInput").ap()
    dout = lambda n, s: nc.dram_tensor(n, s, F32, kind="ExternalOutput").ap()
    xin = din("xin", [T, D]); crow = din("crow", [256, D])
    sgla = din("sgla", [DEPTH, SPC, 4, 256, 512]); spool = din("spool", [DEPTH, SPC * 15, 1024])
    w_ada = din("w_ada", [DEPTH, D, 6 * D]); b_ada = din("b_ada", [DEPTH, 6 * D]); n1g = din("norm1_g", [DEPTH, D])
    w_in = din("w_in", [DEPTH, D, NIN]); w_a2 = din("w_a2", [DEPTH, 16, 1024]); b_a = din("b_a", [DEPTH, 1024])
    ggla = din("gla_norm_g", [DEPTH, 512]); w_pool = din("w_pool", [DEPTH, 1024, 512]); pscale = din("pool_scale", [DEPTH, D])
    w_o = din("w_o", [DEPTH, D, D]); n2g = din("norm2_g", [DEPTH, D]); w_gu = din("w_gu", [DEPTH, D, 2 * DFF])
    w_down = din("w_down", [DEPTH, DFF, D]); fng = din("final_norm_g", [1, D])
    cmats = din("cmats", [NMAT, 128, 128]); flag_d = din("flag", [128, 1]); cmk = din("cmk", [128, 16 * 128]); rmk = din("rmk", [128, 16])
    y = dout("y", [T, D]); gla_p = dout("gla_p", [DEPTH, 4, 256, 512]); pool_p = dout("pool_p", [DEPTH, 15, 1024])
    gla_s = dout("gla_s", [DEPTH, SPC, 4, 256, 512]); pool_s = dout("pool_s", [DEPTH, SPC, 15, 1024])
    scr = lambda n, s: nc.dram_tensor(n, s, F32).ap()
    p_scr = scr("p_scr", [T, NIN]); gu_scr = scr("gu_scr", [T, 2 * DFF]); o_scr = scr("o_scr", [T, D])
    qg_scr = nc.dram_tensor("qg_scr", [4, NPT, 128, 256], BF16).ap()
    ex_in_t = [nc.dram_tensor(f"ex_in{l}", [512, 1024], F32) for l in range(DEPTH)]
    ex_out_t = [nc.dram_tensor(f"ex_out{l}", [1024, 1024], F32) for l in range(DEPTH)]
    exu_in_t = [nc.dram_tensor(f"exu_in{l}", [128, 1024], F32) for l in range(DEPTH)]
    exu_out_t = [nc.dram_tensor(f"exu_out{l}", [256, 1024], F32) for l in range(DEPTH)]
    ex_in = [t_.ap() for t_ in ex_in_t]; ex_out = [t_.ap() for t_ in ex_out_t]
    exu_in = [t_.ap() for t_ in exu_in_t]; exu_out = [t_.ap() for t_ in exu_out_t]
    xa = scr("xa", [T, D]); xb = scr("xb", [T, D]); xc = scr("xc", [T, D]); mod_scr = scr("mod_scr", [DEPTH, 256, 6 * D])

    with contextlib.ExitStack() as st:
        TR = Tracker(nc, st, 100)
        cnt = [0]

        def sb(shape, dt, name=None):
            cnt[0] += 1
            return st.enter_context(nc.sbuf_tensor(name or f"t{cnt[0]}", shape, dt))

        actT = sb([128, 16, T], BF16, "actT")
        NW = 2
        wbuf = [sb([128, 16, 512], BF16, f"wb{i}") for i in range(NW)]
        psf = [st.enter_context(nc.psum_tensor(f"psf{i}", [128, 512], F32)) for i in range(6)]
        psb = [st.enter_context(nc.psum_tensor(f"psb{i}", [128, 1024], BF16)) for i in range(2)]
        rot = {"f": 0, "b": 0, "w": 0}

        held = set()

        def fbank():
            while True:
                i = rot["f"] % 6; rot["f"] += 1
                if i not in held:
                    return psf[i], f"psf{i}"

        def bbank():
            i = rot["b"] % 2; rot["b"] += 1
            return psb[i], f"psb{i}"

        pools = {}

        def stage(tag, shape, dt, n=2):
            if tag not in pools:
                pools[tag] = [[sb(shape, dt, f"{tag}{i}") for i in range(n)], 0]
            p = pools[tag]
            i = p[1] % n; p[1] += 1
            return p[0][i], f"{tag}{i}"

        st.enter_context(nc.Block())

        Bbuf = sb([128, 8, 1024], F32, "Bbuf")
        BK = [f"B{i}" for i in range(8)]
        cm_f = Bbuf[:, 0:2, :].rearrange("p a c -> p (a c)")
        hb = sb([128, D], BF16, "hb")
        mb = hb
        idb = sb([128, 128], BF16, "idb")
        cmat_b = sb([128, NMAT, 128], BF16, "cmat_b")
        flag = sb([128, 1], F32, "flag_sb")
        Gst = sb([128, 2], F32, "Gst"); Gtot = sb([128, 4, 2], F32, "Gtot")
        cm_b = sb([128, 16, 128], BF16, "cm_b")
        rm_b = sb([128, 16], BF16, "rm_b")
        ones_b = sb([128, 128], BF16, "ones_b")
        rs = sb([128, 8], F32, "rs")
        junk = hb
        TR.dma("sp", "cl", flag[:], flag_d[:, :], writes=["flag"])
        for i in range(NMAT):
            tmpc, kc_ = stage("cld", [128, 128], F32)
            TR.dma("sp", "cl", tmpc[:], cmats[i], writes=[kc_])
            TR.op("dve", lambda e, i=i, tmpc=tmpc: e.tensor_copy(out=cmat_b[:, i, :], in_=tmpc[:]), reads=[kc_], writes=["cmat"])
        TR.op("dve", lambda e: e.tensor_copy(out=idb[:], in_=cmat_b[:, 0, :]), reads=["cmat"], writes=["idb"])
        TR.dma("sp", "cl", cm_f, cmk[:, :], writes=["B0", "B1"])
        TR.op("dve", lambda e: e.tensor_copy(out=cm_b[:].rearrange("p j c -> p (j c)"), in_=cm_f), reads=["B0", "B1"], writes=["cm_b"])
        TR.dma("sp", "cl", cm_f[:, 0:16], rmk[:, :], reads=[], writes=["B0", "B1"])
        TR.op("dve", lambda e: e.tensor_copy(out=rm_b[:], in_=cm_f[:, 0:16]), reads=["B0", "B1"], writes=["rm_b"])
        TR.op("dve", lambda e: e.memset(ones_b[:], 1.0), writes=["ones_b"])
        MINCL = {0: 1, 1: 3}
        MREV = {0: 2, 1: 4}

        def to_feat(src_b, src_key, kc, t, dst=None, dst_key="actT"):
            dst = actT if dst is None else dst
            for k0 in range(0, kc, 8):
                kn = min(8, kc - k0)
                bk, bkey = bbank()
                for k in range(kn):
                    TR.pe(lambda e, k=k: e.transpose(bk[:, k * 128:(k + 1) * 128], src_b[:, (k0 + k) * 128:(k0 + k + 1) * 128], idb[:]),
                          reads=[src_key, "idb"], writes=[bkey], mark=(k == kn - 1))
                TR.op("act", lambda e: e.copy(out=dst[:, k0:k0 + kn, t * 128:(t + 1) * 128],
                                              in_=bk[:, 0:kn * 128].rearrange("p (k c) -> p k c", c=128)),
                      reads=[bkey], writes=[(dst_key, t)])

        def linear(*a, **k):
            for _ in linear_gen(*a, **k):
                pass

        fill_state = {"gen": None}

        def fillf(n=1):
            g = fill_state["gen"]
            for _ in range(n):
                if g is None:
                    return
                try:
                    next(g)
                except StopIteration:
                    fill_state["gen"] = None
                    return

        bg_state = {"gen": None, "on": False}

        def bgf(n=1):
            g = bg_state["gen"]
            for _ in range(n):
                if g is None:
                    return
                try:
                    next(g)
                except StopIteration:
                    bg_state["gen"] = None
                    return

        def linear_gen(kc, tiles, W, r0, c0, ncols, epi, pw=512, lhs=None, lhs_key="actT", own_w=None, bg=False):
            groups = [(t, c, min(pw, c0 + ncols - c)) for c in range(c0, c0 + ncols, pw) for t in tiles]
            pre = getattr(epi, "pre", None)
            if pre:
                pre(*groups[0])
            wb = wkey = None
            lhs = actT if lhs is None else lhs
            for gi, (t, c, pc) in enumerate(groups):
                if bg:
                    bgf(1)
                if t == tiles[0]:
                    if own_w is not None:
                        wb, wkey, wi = own_w, "wown", "own"
                    else:
                        wi = rot["w"] % NW; rot["w"] += 1
                        wb, wkey = wbuf[wi], f"wb{wi}"
                    TR.dma("pool", f"w{wi}", wb[:, 0:kc, 0:pc], W[r0:r0 + kc * 128, c:c + pc].rearrange("(k p) n -> p k n", p=128), writes=[wkey])
                ps, pkey = fbank()
                for k in range(kc):
                    TR.pe(lambda e, k=k: e.matmul(ps[:, 0:pc], lhsT=lhs[:, k, t * 128:(t + 1) * 128], rhs=wb[:, k, 0:pc],
                                                  start=(k == 0), stop=(k == kc - 1)),
                          reads=[(lhs_key, t), wkey], writes=[pkey], mark=(k == kc - 1))
                if pre and gi + 1 < len(groups):
                    pre(*groups[gi + 1])
                epi(t, c, pc, ps, pkey)
                yield

        def store_epi(dst, dkey):
            def epi(t, c, pc, ps, pkey):
                sg, skey = stage("sto", [128, 512], F32, 2)
                TR.op("act", lambda e: e.copy(out=sg[:, 0:pc], in_=ps[:, 0:pc]), reads=[pkey], writes=[skey])
                TR.dma("sp", "st" + skey, dst[t * 128:(t + 1) * 128, c:c + pc], sg[:, 0:pc], reads=[skey], writes=[(dkey, t)])
            return epi

        def rstd_of(src, skey, n, col):
            TR.op("dve", lambda e: e.memset(rs[:, col + 1:col + 2], 0.0), writes=["rs"])
            TR.op("act", lambda e: e.activation(out=junk[:, 0:n], in_=src, func=AF.Square, accum_out=rs[:, col + 1:col + 2]),
                  reads=[skey, "rs"], writes=["hb", "rs"])
            TR.op("act", lambda e: e.activation(out=rs[:, col + 1:col + 2], in_=rs[:, col + 1:col + 2], func=AF.Ln, scale=1.0 / n, bias=EPS),
                  reads=["rs"], writes=["rs"])
            TR.op("act", lambda e: e.activation(out=rs[:, col:col + 1], in_=rs[:, col + 1:col + 2], func=AF.Exp, scale=-0.5),
                  reads=["rs"], writes=["rs"])

        xt = Bbuf[:, 0:2, :].rearrange("p a c -> p (a c)")
        ht = Bbuf[:, 2:4, :].rearrange("p a c -> p (a c)")
        rowA = Bbuf[:, 4:6, :].rearrange("p a c -> p (a c)")
        rowB = Bbuf[:, 6:8, :].rearrange("p a c -> p (a c)")
        XT, HT, RA, RB = ["B0", "B1"], ["B2", "B3"], ["B4", "B5"], ["B6", "B7"]
        gT = sb([128, 16], F32, "gT"); scT = sb([128, 16], F32, "scT"); shT = sb([128, 16], F32, "shT")

        def norm_stage(l, xsrc, xkey, gain, gl, c_sh, c_sc):
            with nc.allow_non_contiguous_dma(reason="tiny per-feature vectors"):
                TR.dma("sp", "ld0", gT[:], gain[gl, :].rearrange("(k p) -> p k", p=128), writes=["gT"])
                TR.dma("sp", "ld1", scT[:], mod_scr[l, 0, c_sc:c_sc + D].rearrange("(k p) -> p k", p=128), reads=[("mod", l, c_sc // (3 * D))], writes=["scT"])
                TR.dma("sp", "ld2", shT[:], mod_scr[l, 0, c_sh:c_sh + D].rearrange("(k p) -> p k", p=128), reads=[("mod", l, c_sh // (3 * D))], writes=["shT"])
            TR.op("dve", lambda e: e.scalar_tensor_tensor(out=scT[:], in0=scT[:], scalar=1.0, in1=gT[:], op0=ALU.add, op1=ALU.mult),
                  reads=["scT", "gT"], writes=["scT"])
            for t in range(NT):
                TR.dma("sp", "ldx", xt, xsrc[t * 128:(t + 1) * 128, :], reads=[(xkey, t)], writes=XT)
                rstd_of(xt, "B0", D, 0)
                if t < NT - 1:
                    TR.op("dve", lambda e: e.tensor_scalar(out=hb[:], in0=xt, scalar1=rs[:, 0:1], scalar2=None, op0=ALU.mult),
                          reads=XT + ["rs"], writes=["hb"])
                    for k0 in range(0, 16, 8):
                        bk, bkey = bbank()
                        for k in range(8):
                            TR.pe(lambda e, k=k: e.transpose(bk[:, k * 128:(k + 1) * 128], hb[:, (k0 + k) * 128:(k0 + k + 1) * 128], idb[:]),
                                  reads=["hb", "idb"], writes=[bkey], mark=(k == 7))
                        for k in range(8):
                            TR.op("act", lambda e, k=k: e.activation(out=actT[:, k0 + k, t * 128:(t + 1) * 128], in_=bk[:, k * 128:(k + 1) * 128],
                                                                     func=AF.Identity, scale=scT[:, k0 + k:k0 + k + 1], bias=shT[:, k0 + k:k0 + k + 1]),
                                  reads=[bkey, "scT", "shT"], writes=[("actT", t)])
                else:
                    TR.dma("sp", "ld0", ht, gain[gl:gl + 1, :].partition_broadcast(128), writes=HT)
                    TR.dma("sp", "ld1", rowA, mod_scr[l, 128:256, c_sc:c_sc + D], reads=[("mod", l, c_sc // (3 * D))], writes=RA)
                    TR.dma("sp", "ld2", rowB, mod_scr[l, 128:256, c_sh:c_sh + D], reads=[("mod", l, c_sh // (3 * D))], writes=RB)
                    TR.op("dve", lambda e: e.scalar_tensor_tensor(out=rowA, in0=rowA, scalar=1.0, in1=ht, op0=ALU.add, op1=ALU.mult),
                          reads=RA + HT, writes=RA)
                    TR.op("dve", lambda e: e.scalar_tensor_tensor(out=ht, in0=xt, scalar=rs[:, 0:1], in1=rowA, op0=ALU.mult, op1=ALU.mult),
                          reads=XT + ["rs"] + RA, writes=HT)
                    TR.op("dve", lambda e: e.tensor_tensor(out=hb[:], in0=ht, in1=rowB, op=ALU.add), reads=HT + RB, writes=["hb"])
                    to_feat(hb, "hb", 16, t)

        brow = sb([128, 512], F32, "brow")
        csT = sb([128, 16, 256], BF16, "csT")
        wada = sb([128, 16, 64], BF16, "wada")
        for t in range(2):
            TR.dma("sp", "ldx", xt, crow[t * 128:(t + 1) * 128, :], writes=XT)
            TR.op("act", lambda e: e.activation(out=ht, in_=xt, func=AF.Sigmoid), reads=XT, writes=HT)
            TR.op("dve", lambda e: e.tensor_tensor(out=hb[:], in0=ht, in1=xt, op=ALU.mult), reads=HT + XT, writes=["hb"])
            to_feat(hb, "hb", 16, t, dst=csT, dst_key="csT")

        def epi_mod(l):
            def epi(t, c, pc, ps, pkey):
                if t == 0:
                    TR.dma("sp", "ldb", brow[:, 0:pc], b_ada[l:l + 1, c:c + pc].partition_broadcast(128), writes=["brow"])
                sg, skey = stage("sto", [128, 512], F32, 2)
                TR.op("dve", lambda e: e.tensor_tensor(out=sg[:, 0:pc], in0=ps[:, 0:pc], in1=brow[:, 0:pc], op=ALU.add),
                      reads=[pkey, "brow"], writes=[skey])
                TR.dma("sp", "st" + skey, mod_scr[l, t * 128:(t + 1) * 128, c:c + pc], sg[:, 0:pc], reads=[skey], writes=[("mod", l, c // (3 * D))])
            return epi

        linear(16, [0, 1], w_ada[0], 0, 0, 3 * D, epi_mod(0), lhs=csT, lhs_key="csT")

        def ada_bg(l, c0, ncols):
            return linear_gen(16, [0, 1], w_ada[l], 0, c0, ncols, epi_mod(l), pw=64, lhs=csT, lhs_key="csT", own_w=wada)

        qkvs = [sb([128, 1024], F32, f"qkv{i}") for i in range(2)]
        alrs = [sb([128, 16], F32, f"alr_f{i}") for i in range(2)]
        qkv = qkvs[0]
        qb = sb([128, 256], BF16, "qb"); kb = sb([128, 256], BF16, "kb"); vb = sb([128, 512], BF16, "vb"); ab = sb([128, 16], BF16, "ab")
        alrT = sb([17, 128], BF16, "alrT")
        wa2f = Bbuf[0:17, 0, :]; wa2b = sb([17, 1024], BF16, "wa2b")
        TR.op("dve", lambda e: e.memset(alrT[:], 1.0), writes=["alrT"])
        ef = sb([128, 256], F32, "ef"); lgb = sb([128, 256], BF16, "lgb")
        E1 = sb([128, 256], F32, "E1"); E2 = sb([128, 256], F32, "E2"); E3 = sb([128, 256], F32, "E3")
        qe = sb([128, 2, 128], BF16, "qe"); ke = sb([128, 2, 128], BF16, "ke"); kd = sb([128, 256], BF16, "kd")
        attm = sb([128, 128], BF16, "attm")
        Sf = sb([128, 2, 512], F32, "Sf"); Sb = sb([128, 2, 512], BF16, "Sb")

        onf = sb([128, 512], F32, "onf")
        ggrow = sb([128, 512], F32, "ggrow")

        def gla_stage(l):
            TR.dma("sp", "ld0", Bbuf[0:16, 0, :], w_a2[l], writes=["B0"])
            TR.dma("sp", "ld1", Bbuf[16:17, 0, :], b_a[l:l + 1, :], writes=["B0"])
            TR.op("dve", lambda e: e.tensor_copy(out=wa2b[:], in_=wa2f), reads=["B0"], writes=["wa2b"])
            TR.dma("sp", "ld2", ggrow[:], ggla[l:l + 1, :].partition_broadcast(128), writes=["ggrow"])
            order = [(h_, t_) for h_ in range(4) for t_ in range(NT)]

            def gla_loads(i):
                h_, t_ = order[i]
                r_ = slice(t_ * 128, (t_ + 1) * 128)
                q_, a_, sfx = qkvs[i % 2], alrs[i % 2], str(i % 2)
                TR.dma("sp", "lq0" + sfx, q_[:, 0:256], p_scr[r_, OQ + h_ * 256:OQ + (h_ + 1) * 256], reads=[("p", t_)], writes=["qkv" + sfx])
                TR.dma("sp", "lq1" + sfx, q_[:, 256:512], p_scr[r_, OK_ + h_ * 256:OK_ + (h_ + 1) * 256], reads=[("p", t_)], writes=["qkv" + sfx])
                TR.dma("sp", "lq2" + sfx, q_[:, 512:1024], p_scr[r_, OV + h_ * 512:OV + (h_ + 1) * 512], reads=[("p", t_)], writes=["qkv" + sfx])
                TR.dma("sp", "lq3" + sfx, a_[:], p_scr[r_, OALR:OALR + 16], reads=[("p", t_)], writes=["alr_f" + sfx])

            gla_loads(0)
            for h in range(4):
                TR.op("dve", lambda e: e.memset(Sf[:], 0.0), writes=["Sf"])
                TR.op("dve", lambda e: e.memset(Sb[:], 0.0), writes=["Sb"])
                TR.op("dve", lambda e: e.memset(Gst[:], 1.0), writes=["Gst"])
                for t in range(NT):
                    ty = 1 if t == NT - 1 else 0
                    r = slice(t * 128, (t + 1) * 128)
                    gi_ = h * NT + t
                    bgf(3)
                    if gi_ + 1 < len(order):
                        gla_loads(gi_ + 1)
                    qkv_, alr_, sfx = qkvs[gi_ % 2], alrs[gi_ % 2], str(gi_ % 2)
                    TR.op("dve", lambda e: e.tensor_scalar(out=qb[:], in0=qkv_[:, 0:256], scalar1=0.0625, scalar2=None, op0=ALU.mult),
                          reads=["qkv" + sfx], writes=["qb"])
                    TR.op("dve", lambda e: e.tensor_copy(out=kb[:], in_=qkv_[:, 256:512]), reads=["qkv" + sfx], writes=["kb"])
                    TR.op("dve", lambda e: e.tensor_copy(out=vb[:], in_=qkv_[:, 512:1024]), reads=["qkv" + sfx], writes=["vb"])
                    TR.op("dve", lambda e: e.tensor_copy(out=ab[:], in_=alr_[:]), reads=["alr_f" + sfx], writes=["ab"])
                    bk, bkey = bbank()
                    for ch in range(2):
                        TR.pe(lambda e, ch=ch: e.transpose(bk[:, ch * 128:(ch + 1) * 128], qb[:, ch * 128:(ch + 1) * 128], idb[:]),
                              reads=["qb", "idb"], writes=[bkey], mark=False)
                        TR.pe(lambda e, ch=ch: e.transpose(bk[:, 256 + ch * 128:256 + (ch + 1) * 128], kb[:, ch * 128:(ch + 1) * 128], idb[:]),
                              reads=["kb", "idb"], writes=[bkey], mark=False)
                    TR.pe(lambda e: e.transpose(bk[0:16, 512:640], ab[:, 0:16], idb[:]), reads=["ab", "idb"], writes=[bkey])
                    TR.op("act", lambda e: e.copy(out=alrT[0:16, :], in_=bk[0:16, 512:640]), reads=[bkey], writes=["alrT"])
                    fillf()
                    g, gkey = fbank()
                    TR.pe(lambda e: e.matmul(g[:, 0:256], lhsT=alrT[0:17, :], rhs=wa2b[0:17, h * 256:(h + 1) * 256], start=True, stop=True),
                          reads=["alrT", "wa2b"], writes=[gkey])
                    TR.op("act", lambda e: e.activation(out=ef[:], in_=g[:, 0:256], func=AF.Exp, scale=-1.0), reads=[gkey], writes=["ef"])
                    TR.op("act", lambda e: e.activation(out=ef[:], in_=ef[:], func=AF.Ln, bias=1.0), reads=["ef"], writes=["ef"])
                    TR.op("dve", lambda e: e.tensor_scalar(out=lgb[:], in0=ef[:], scalar1=-1.0 / 16, scalar2=None, op0=ALU.mult),
                          reads=["ef"], writes=["lgb"])
                    fillf()
                    B, Bkey = fbank()
                    for ch in range(2):
                        TR.pe(lambda e, ch=ch: e.matmul(B[:, ch * 128:(ch + 1) * 128], lhsT=lgb[:, ch * 128:(ch + 1) * 128],
                                                        rhs=cmat_b[:, MINCL[ty], :], start=True, stop=True),
                              reads=["lgb", "cmat"], writes=[Bkey], mark=False)
                    TR.pe(lambda e: e.matmul(B[:, 256:512], lhsT=cmat_b[:, MREV[ty], :], rhs=lgb[:], start=True, stop=True),
                          reads=["lgb", "cmat"], writes=[Bkey])
                    TR.op("act", lambda e: e.activation(out=E1[:], in_=B[:, 0:256], func=AF.Exp), reads=[Bkey], writes=["E1"])
                    TR.op("act", lambda e: e.activation(out=E2[:], in_=B[:, 0:256], func=AF.Exp, scale=-1.0), reads=[Bkey], writes=["E2"])
                    TR.op("act", lambda e: e.activation(out=E3[:], in_=B[:, 256:512], func=AF.Exp), reads=[Bkey], writes=["E3"])
                    TR.op("dve", lambda e: e.tensor_tensor(out=qe[:].rearrange("p c k -> p (c k)"), in0=bk[:, 0:256], in1=E1[:], op=ALU.mult),
                          reads=[bkey, "E1"], writes=["qe"])
                    TR.op("dve", lambda e: e.tensor_tensor(out=ke[:].rearrange("p c k -> p (c k)"), in0=bk[:, 256:512], in1=E2[:], op=ALU.mult),
                          reads=[bkey, "E2"], writes=["ke"])
                    TR.op("dve", lambda e: e.tensor_tensor(out=kd[:], in0=kb[:], in1=E3[:], op=ALU.mult), reads=["kb", "E3"], writes=["kd"])
                    if ty == 0:
                        qg, qgkey = stage("qg", [128, 2, 128], BF16, 2)
                        for ch in range(2):
                            TR.op("dve", lambda e, ch=ch: e.tensor_scalar(out=qg[:, ch, :], in0=qe[:, ch, :], scalar1=Gst[:, ch:ch + 1], scalar2=None, op0=ALU.mult),
                                  reads=["qe", "Gst"], writes=[qgkey])
                        TR.dma("sp", "s" + qgkey, qg_scr[h, t], qg[:].rearrange("p c k -> p (c k)"), reads=[qgkey], writes=[("qg", h, t)])
                    fillf()
                    A, Akey = fbank()
                    for ch in range(2):
                        TR.pe(lambda e, ch=ch: e.matmul(A[:, 0:128], lhsT=ke[:, ch, :], rhs=qe[:, ch, :], start=(ch == 0), stop=(ch == 1)),
                              reads=["ke", "qe"], writes=[Akey], mark=(ch == 1))
                    TR.op("dve", lambda e: e.tensor_tensor(out=attm[:], in0=A[:, 0:128], in1=cmat_b[:, MINCL[ty], :], op=ALU.mult),
                          reads=[Akey, "cmat"], writes=["attm"])
                    fillf()
                    O, Okey = fbank()
                    if ty == 0:
                        TR.pe(lambda e: e.matmul(O[:, :], lhsT=attm[:], rhs=vb[:], start=True, stop=False), reads=["attm", "vb"], writes=[Okey], mark=False)
                        for ch in range(2):
                            TR.pe(lambda e, ch=ch: e.matmul(O[:, :], lhsT=qe[:, ch, :], rhs=Sb[:, ch, :], start=False, stop=(ch == 1)),
                                  reads=["qe", "Sb"], writes=[Okey], mark=(ch == 1))
                        for ch in range(2):
                            U, Ukey = fbank()
                            TR.pe(lambda e, ch=ch: e.matmul(U[:, :], lhsT=kd[:, ch * 128:(ch + 1) * 128], rhs=vb[:], start=True, stop=True),
                                  reads=["kd", "vb"], writes=[Ukey])
                            TR.op("dve", lambda e, ch=ch: e.scalar_tensor_tensor(out=Sf[:, ch, :], in0=Sf[:, ch, :], scalar=E1[:, ch * 128 + 127:ch * 128 + 128],
                                                                                 in1=U[:, :], op0=ALU.mult, op1=ALU.add),
                                  reads=["Sf", "E1", Ukey], writes=["Sf"])
                        TR.op("act", lambda e: e.copy(out=Sb[:], in_=Sf[:]), reads=["Sf"], writes=["Sb"])
                        TR.op("dve", lambda e: e.tensor_tensor(out=Gst[:], in0=Gst[:], in1=E1[:].rearrange("p (c k) -> p c k", k=128)[:, :, 127], op=ALU.mult),
                              reads=["Gst", "E1"], writes=["Gst"])
                        if t == NPT - 1:
                            TR.dma("sp", "stS", exS(ex_in[l], h), Sf[:], reads=["Sf"], writes=[("exin", l)])
                            TR.op("dve", lambda e: e.tensor_copy(out=Gtot[:, h, :], in_=Gst[:]), reads=["Gst"], writes=["Gtot"])
                    else:
                        held.add(int(Okey[3:]))
                        TR.pe(lambda e: e.matmul(O[:, :], lhsT=attm[:], rhs=vb[:], start=True, stop=False), reads=["attm", "vb"], writes=[Okey], mark=False)
                        def s0_load(j_):
                            s0_, k_ = stage("s0f", [128, 2, 512], F32, 3)
                            for ch_ in range(2):
                                TR.dma("sp", f"l{k_}{ch_}", s0_[:, ch_, :], sgla[l, j_, h, ch_ * 128:(ch_ + 1) * 128, :], writes=[k_ + str(ch_)])
                            return s0_, k_

                        nxt_s0 = s0_load(0)
                        for j in range(SPC):
                            s0, s0key = nxt_s0
                            if j + 1 < SPC:
                                nxt_s0 = s0_load(j + 1)
                            s0b, s0bkey = stage("s0b", [128, 2, 512], BF16, 2)
                            qm, qmkey = stage("qm", [128, 2, 128], BF16, 2)
                            km, kmkey = stage("km", [128, 256], BF16, 2)
                            TR.op("act", lambda e: e.copy(out=s0b[:], in_=s0[:]), reads=[s0key + "0", s0key + "1"], writes=[s0bkey])
                            TR.op("dve", lambda e, j=j: e.tensor_tensor(out=qm[:], in0=qe[:], in1=cm_b[:, j, :].unsqueeze(1).to_broadcast([128, 2, 128]),
                                                                        op=ALU.mult), reads=["qe", "cm_b"], writes=[qmkey])
                            TR.op("dve", lambda e, j=j: e.tensor_scalar(out=km[:], in0=kd[:], scalar1=rm_b[:, j:j + 1], scalar2=None, op0=ALU.mult),
                                  reads=["kd", "rm_b"], writes=[kmkey])
                            for ch in range(2):
                                last = (j == SPC - 1 and ch == 1)
                                TR.pe(lambda e, ch=ch: e.matmul(O[:, :], lhsT=qm[:, ch, :], rhs=s0b[:, ch, :], start=False, stop=last),
                                      reads=[qmkey, s0bkey], writes=[Okey], mark=True)
                            for ch in range(2):
                                U, Ukey = fbank()
                                TR.pe(lambda e, ch=ch: e.matmul(U[:, :], lhsT=km[:, ch * 128:(ch + 1) * 128], rhs=vb[:], start=True, stop=True),
                                      reads=[kmkey, "vb"], writes=[Ukey])
                                cix = ch * 128 + 8 * j + 7
                                TR.op("dve", lambda e, ch=ch, cix=cix: e.scalar_tensor_tensor(out=s0[:, ch, :], in0=s0[:, ch, :], scalar=E1[:, cix:cix + 1],
                                                                                              in1=U[:, :], op0=ALU.mult, op1=ALU.add),
                                      reads=[s0key + str(ch), "E1", Ukey], writes=[s0key + str(ch)])
                            TR.dma("sp", "s" + s0key, gla_s[l, j, h].rearrange("(c p) e -> p c e", p=128), s0[:], reads=[s0key + "0", s0key + "1"],
                                   writes=[("glas", l, j, h)])
                    held.clear()
                    if ty == 0:
                        TR.op("act", lambda e: e.copy(out=onf[:], in_=O[:, :]), reads=[Okey], writes=["onf"])
                    else:
                        rstd_of(O[:, :], Okey, 512, 2)
                        TR.op("dve", lambda e: e.scalar_tensor_tensor(out=onf[:], in0=O[:, :], scalar=rs[:, 2:3], in1=ggrow[:], op0=ALU.mult, op1=ALU.mult),
                              reads=[Okey, "rs", "ggrow"], writes=["onf"])
                    TR.dma("sp", "sto_o", o_scr[r, h * 512:(h + 1) * 512], onf[:], reads=["onf"], writes=[("o", t)])

        def exS(buf, h, blk=0):
            return buf[blk * 512 + h * 128:blk * 512 + (h + 1) * 128, :].rearrange("r (two e) -> (r two) e", two=2).rearrange("(c p) e -> p c e", p=128)

        def exchange_stage(l):
            TR.dma("sp", "exu", exu_in[l][:, :], p_scr[SEQ - 128:SEQ, OU:OU + 1024], reads=[("p2", NPT - 1)], writes=[("exuin", l)])
            pairs = [[0, 1], [2, 3], [4, 5], [6, 7]]
            TR.collective(lambda g: g.collective_compute("AllGather", ALU.bypass, replica_groups=pairs,
                                                          ins=[exu_in_t[l].ap().opt()], outs=[exu_out_t[l].ap().opt()]),
                          reads=[("exuin", l)], writes=[("exuout", l)])
            TR.collective(lambda g: g.collective_compute("AllGather", ALU.bypass, replica_groups=pairs,
                                                          ins=[ex_in_t[l].ap().opt()], outs=[ex_out_t[l].ap().opt()]),
                          reads=[("exin", l)], writes=[("exout", l)])

        def correct_stage(l):
            for h in range(4):
                sp_, spkey = stage("s0f", [128, 2, 512], F32, 3)
                spb, spbkey = stage("s0b", [128, 2, 512], BF16, 2)
                sl, slkey = stage("s0f", [128, 2, 512], F32, 3)
                TR.dma("sp", "l" + spkey, sp_[:], exS(ex_out[l], h, 0), reads=[("exout", l)], writes=[spkey + "0", spkey + "1"])
                TR.op("dve", lambda e: e.tensor_scalar(out=sp_[:], in0=sp_[:], scalar1=flag[:, 0:1], scalar2=None, op0=ALU.mult),
                      reads=[spkey + "0", spkey + "1", "flag"], writes=[spkey + "0", spkey + "1"])
                TR.op("act", lambda e: e.copy(out=spb[:], in_=sp_[:]), reads=[spkey + "0", spkey + "1"], writes=[spbkey])
                TR.dma("sp", "l" + slkey, sl[:], exS(ex_in[l], h, 0), reads=[("exin", l)], writes=[slkey + "0", slkey + "1"])
                for ch in range(2):
                    TR.op("dve", lambda e, ch=ch: e.scalar_tensor_tensor(out=sl[:, ch, :], in0=sp_[:, ch, :], scalar=Gtot[:, h, ch:ch + 1], in1=sl[:, ch, :],
                                                                         op0=ALU.mult, op1=ALU.add), reads=[spkey + "0", spkey + "1", slkey + "0", slkey + "1", "Gtot"], writes=[slkey + "0", slkey + "1"])
                TR.dma("sp", "s" + slkey, gla_p[l, h].rearrange("(c p) e -> p c e", p=128), sl[:], reads=[slkey + "0", slkey + "1"], writes=[("glap", l, h)])
                crot = [0]

                def corr_loads(t_):
                    cq, cqk = stage("qm", [128, 2, 128], BF16, 2)
                    crot[0] += 1
                    co, cok = qkvs[crot[0] % 2][:, 0:512], "qkv" + str(crot[0] % 2)
                    TR.dma("sp", "l" + cqk, cq[:].rearrange("p c k -> p (c k)"), qg_scr[h, t_], reads=[("qg", h, t_)], writes=[cqk])
                    TR.dma("sp", "l" + cok, co, o_scr[t_ * 128:(t_ + 1) * 128, h * 512:(h + 1) * 512], reads=[("o", t_)], writes=[cok])
                    return cq, cqk, co, cok

                nxt_ld = corr_loads(0)
                for t in range(NPT):
                    r = slice(t * 128, (t + 1) * 128)
                    cq, cqk, co, cok = nxt_ld
                    if t + 1 < NPT:
                        nxt_ld = corr_loads(t + 1)
                    C, Ckey = fbank()
                    for ch in range(2):
                        TR.pe(lambda e, ch=ch: e.matmul(C[:, :], lhsT=cq[:, ch, :], rhs=spb[:, ch, :], start=(ch == 0), stop=(ch == 1)),
                              reads=[cqk, spbkey], writes=[Ckey], mark=(ch == 1))
                    TR.op("dve", lambda e: e.tensor_tensor(out=co, in0=C[:, :], in1=co, op=ALU.add),
                          reads=[Ckey, cok], writes=[cok])
                    rstd_of(co, cok, 512, 2)
                    TR.op("dve", lambda e: e.scalar_tensor_tensor(out=onf[:], in0=co, scalar=rs[:, 2:3], in1=ggrow[:], op0=ALU.mult, op1=ALU.mult),
                          reads=[cok, "rs", "ggrow"], writes=["onf"])
                    TR.dma("sp", "sto_o", o_scr[r, h * 512:(h + 1) * 512], onf[:], reads=["onf"], writes=[("o", t)])

        ogt = Bbuf[:, 0, :]; gat = Bbuf[:, 1, :]; gbt = Bbuf[:, 2, :]; ot = Bbuf[:, 3, :]; sgt = Bbuf[:, 4, :]; ut = Bbuf[:, 5, :]
        psrow = Bbuf[:, 6:8, :].rearrange("p a c -> p (a c)")
        ub = [sb([128, 1024], BF16, f"ub{i}") for i in range(2)]
        hist_b = [sb([120, 1024], BF16, f"hist_b{i}") for i in range(2)]
        dTb = sb([128, 8, 128], BF16, "dTb")
        wpool_b = sb([128, 8, 512], BF16, "wpool_b")

        def merge_stage(l):
            TR.dma("pool", "wp", wpool_b[:], w_pool[l].rearrange("(k p) o -> p k o", p=128), writes=["wpool"])
            TR.dma("sp", "ld0", psrow, pscale[l:l + 1, :].partition_broadcast(128), writes=["B6", "B7"])
            for hf in range(2):
                TR.dma("sp", "ld1", Bbuf[0:120, 5, :], spool[l, hf * 120:(hf + 1) * 120, :], writes=["B5"])
                TR.op("dve", lambda e, hf=hf: e.tensor_copy(out=hist_b[hf][:], in_=Bbuf[0:120, 5, :]), reads=["B5"], writes=[f"hist_b{hf}"])
            TR.dma("sp", "lm4", ut, exu_out[l][0:128, :], reads=[("exuout", l)], writes=["B5"])
            TR.op("dve", lambda e: e.tensor_copy(out=ub[1][:], in_=ut), reads=["B5"], writes=["ub1"])
            for t in range(NT):
                ty = 1 if t == NT - 1 else 0
                r = slice(t * 128, (t + 1) * 128)
                TR.dma("sp", "lm4", ut, p_scr[r, OU:OU + 1024], reads=[("p2", t)], writes=["B5"])
                cur, ckey = ub[t % 2], f"ub{t % 2}"
                prv, pvkey = ub[(t + 1) % 2], f"ub{(t + 1) % 2}"
                TR.op("dve", lambda e: e.tensor_copy(out=cur[:], in_=ut), reads=["B5"], writes=[ckey])
                dbanks = [fbank(), fbank()]
                for g in range(4):
                    Dk, Dkey = dbanks[g // 2]
                    base = 5 + g * 7
                    for ic in range(2):
                        col = ((g % 2) * 2 + ic) * 128
                        cs = slice(g * 256 + ic * 128, g * 256 + (ic + 1) * 128)
                        lastg = (g % 2 == 1 and ic == 1)
                        if ty == 0:
                            mc, mp = (base + 2, base + 6) if t == 0 else (base + 0, base + 1)
                            TR.pe(lambda e: e.matmul(Dk[:, col:col + 128], lhsT=cur[:, cs], rhs=cmat_b[:, mc, :], start=True, stop=False),
                                  reads=[ckey, "cmat"], writes=[Dkey], mark=False)
                            TR.pe(lambda e: e.matmul(Dk[:, col:col + 128], lhsT=prv[:, cs], rhs=cmat_b[:, mp, :], start=False, stop=True),
                                  reads=[pvkey, "cmat"], writes=[Dkey], mark=lastg)
                        else:
                            TR.pe(lambda e: e.matmul(Dk[:, col:col + 128], lhsT=cur[:, cs], rhs=cmat_b[:, base + 3, :], start=True, stop=False),
                                  reads=[ckey, "cmat"], writes=[Dkey], mark=False)
                            for hf in range(2):
                                TR.pe(lambda e, hf=hf: e.matmul(Dk[:, col:col + 128], lhsT=hist_b[hf][0:120, cs], rhs=cmat_b[0:120, base + 4 + hf, :],
                                                                start=False, stop=(hf == 1)),
                                      reads=[f"hist_b{hf}", "cmat"], writes=[Dkey], mark=(lastg and hf == 1))
                for i2 in range(2):
                    Dk, Dkey = dbanks[i2]
                    TR.op("act", lambda e, i2=i2, Dk=Dk: e.copy(out=dTb[:, i2 * 4:(i2 + 1) * 4, :], in_=Dk[:, :].rearrange("p (k c) -> p k c", c=128)),
                          reads=[Dkey], writes=["dTb"])
                for hfc in range(2):
                    c0 = hfc * 1024
                    TR.dma("sp", "lm0", ogt, p_scr[r, OOG + c0:OOG + c0 + 1024], reads=[("p2", t)], writes=["B0"])
                    TR.dma("sp", "lm1", gat, p_scr[r, OGA + c0:OGA + c0 + 1024], reads=[("p2", t)], writes=["B1"])
                    TR.dma("sp", "lm2", gbt, p_scr[r, OGB + c0:OGB + c0 + 1024], reads=[("p2", t)], writes=["B2"])
                    TR.dma("sp", "lm3", ot, o_scr[r, c0:c0 + 1024], reads=[("o", t)], writes=["B3"])
                    TR.op("act", lambda e: e.activation(out=sgt, in_=ogt, func=AF.Sigmoid), reads=["B0"], writes=["B4"])
                    TR.op("dve", lambda e: e.tensor_tensor(out=ogt, in0=ogt, in1=sgt, op=ALU.mult), reads=["B0", "B4"], writes=["B0"])
                    TR.op("dve", lambda e: e.tensor_tensor(out=ogt, in0=ogt, in1=ot, op=ALU.mult), reads=["B0", "B3"], writes=["B0"])
                    TR.op("act", lambda e: e.activation(out=sgt, in_=gat, func=AF.Sigmoid), reads=["B1"], writes=["B4"])
                    TR.op("dve", lambda e: e.tensor_tensor(out=ogt, in0=ogt, in1=sgt, op=ALU.mult), reads=["B0", "B4"], writes=["B0"])
                    TR.op("act", lambda e: e.activation(out=sgt, in_=gbt, func=AF.Sigmoid), reads=["B2"], writes=["B4"])
                    TR.op("dve", lambda e: e.tensor_tensor(out=sgt, in0=sgt, in1=psrow[:, c0:c0 + 1024], op=ALU.mult), reads=["B4", "B6", "B7"], writes=["B4"])
                    for g2 in range(2):
                        g = hfc * 2 + g2
                        Y, Ykey = fbank()
                        for ic in range(2):
                            TR.pe(lambda e, ic=ic: e.matmul(Y[:, :], lhsT=dTb[:, g * 2 + ic, :], rhs=wpool_b[:, g * 2 + ic, :], start=(ic == 0), stop=(ic == 1)),
                                  reads=["dTb", "wpool"], writes=[Ykey], mark=(ic == 1))
                        gs = slice(g2 * 512, (g2 + 1) * 512)
                        TR.op("dve", lambda e, gs=gs: e.tensor_tensor(out=gbt[:, gs], in0=Y[:, :], in1=sgt[:, gs], op=ALU.mult),
                              reads=[Ykey, "B4", "B2"], writes=["B2"])
                    TR.op("dve", lambda e: e.tensor_tensor(out=mb[:, c0:c0 + 1024], in0=gbt, in1=ogt, op=ALU.add), reads=["B2", "B0"], writes=["hb"])
                to_feat(mb, "hb", 16, t)
            TR.dma("sp", "po0", pool_p[l], p_scr[SEQ - 15:SEQ, OU:OU + 1024], reads=[("p2", NPT - 1)], writes=[("poolp", l)])
            TR.dma("sp", "po1", pool_s[l, :, 0:7, :], spool[l].rearrange("(j r) c -> j r c", r=15)[:, 8:15, :], writes=[("pools0", l)])
            TR.dma("sp", "po2", pool_s[l, :, 7:15, :], p_scr[SEQ:T, OU:OU + 1024].rearrange("(j r) c -> j r c", r=8),
                   reads=[("p2", NT - 1)], writes=[("pools1", l)])

        g1row = sb([128, 2, 512], F32, "g1row")

        def resid_epi(l, gcol, xprev, pkey_prev, xnext, nkey):
            pend = {}

            def pre(t, c, pc):
                xs, xskey = stage("xs", [128, 512], F32, 2)
                TR.dma("sp", "l" + xskey, xs[:, 0:pc], xprev[t * 128:(t + 1) * 128, c:c + pc], reads=[(pkey_prev, t)], writes=[xskey])
                pend[(t, c)] = (xs, xskey)

            def epi(t, c, pc, ps, pkey):
                ty = 1 if t == NT - 1 else 0
                if t == 0:
                    for ty2 in range(2):
                        TR.dma("sp", "ldg", g1row[:, ty2, 0:pc], mod_scr[l, ty2 * 128:(ty2 + 1) * 128, gcol + c:gcol + c + pc],
                               reads=[("mod", l, gcol // (3 * D))], writes=["g1row"])
                xs, xskey = pend.pop((t, c))
                sg, skey = stage("sto", [128, 512], F32, 2)
                TR.op("dve", lambda e: e.tensor_tensor(out=sg[:, 0:pc], in0=ps[:, 0:pc], in1=g1row[:, ty, 0:pc], op=ALU.mult),
                      reads=[pkey, "g1row"], writes=[skey])
                TR.op("dve", lambda e: e.tensor_tensor(out=sg[:, 0:pc], in0=sg[:, 0:pc], in1=xs[:, 0:pc], op=ALU.add),
                      reads=[skey, xskey], writes=[skey])
                TR.dma("sp", "st" + skey, xnext[t * 128:(t + 1) * 128, c:c + pc], sg[:, 0:pc], reads=[skey], writes=[(nkey, t)])
            epi.pre = pre
            return epi

        xcur, xkey = xin, "xin"
        tiles = list(range(NT))
        for l in range(DEPTH):
            if l == 0:
                bg_state["gen"] = ada_bg(0, 3 * D, 3 * D)
            norm_stage(l, xcur, xkey, n1g, l, 0, D)
            linear(16, tiles, w_in[l], 0, OQ, OOG - OQ, store_epi(p_scr, "p"))
            linear(16, tiles, w_in[l], 0, OALR, 16, store_epi(p_scr, "p"))
            fill_state["gen"] = linear_gen(16, tiles, w_in[l], 0, OOG, OALR - OOG, store_epi(p_scr, "p2"))
            gla_stage(l)
            fillf(10 ** 6)
            bgf(10 ** 6)
            if l == 0:
                bg_state["gen"] = ada_bg(1, 0, 6 * D)
            exchange_stage(l)
            correct_stage(l)
            merge_stage(l)
            x1, x1key = xa, "xa"
            linear(16, tiles, w_o[l], 0, 0, D, resid_epi(l, 2 * D, xcur, xkey, x1, x1key), bg=True)
            norm_stage(l, x1, x1key, n2g, l, 3 * D, 4 * D)
            linear(16, tiles, w_gu[l], 0, 0, 2 * DFF, store_epi(gu_scr, "gu"), bg=True)
            prev, prevkey = x1, x1key
            for gi, (k0, kc) in enumerate(((0, 16), (16, 16), (32, 12))):
                n = kc * 128
                for t in tiles:
                    r = slice(t * 128, (t + 1) * 128)
                    for c0 in range(0, n, 1024):
                        w = min(1024, n - c0)
                        gt_ = Bbuf[:, 0, 0:w]; upt = Bbuf[:, 1, 0:w]; sg2 = Bbuf[:, 2, 0:w]
                        TR.dma("sp", "lf0", gt_, gu_scr[r, k0 * 128 + c0:k0 * 128 + c0 + w], reads=[("gu", t)], writes=["B0"])
                        TR.dma("sp", "lf1", upt, gu_scr[r, DFF + k0 * 128 + c0:DFF + k0 * 128 + c0 + w], reads=[("gu", t)], writes=["B1"])
                        TR.op("act", lambda e: e.activation(out=sg2, in_=gt_, func=AF.Sigmoid), reads=["B0"], writes=["B2"])
                        TR.op("dve", lambda e: e.tensor_tensor(out=gt_, in0=gt_, in1=sg2, op=ALU.mult), reads=["B0", "B2"], writes=["B0"])
                        TR.op("dve", lambda e: e.tensor_tensor(out=mb[:, c0:c0 + w], in0=gt_, in1=upt, op=ALU.mult), reads=["B0", "B1"], writes=["hb"])
                    to_feat(mb, "hb", kc, t)
                nxt, nkey = ((xb, "xb"), (xc, "xc"), (xb, "xb"))[gi]
                linear(kc, tiles, w_down[l], k0 * 128, 0, D, resid_epi(l, 5 * D, prev, prevkey, nxt, nkey), bg=True)
                if gi > 0:
                    pass
                prev, prevkey = nxt, nkey
            xcur, xkey = prev, prevkey
            bgf(10 ** 6)

        TR.dma("sp", "ld0", rowA, fng[0:1, :].partition_broadcast(128), writes=RA)
        for t in tiles:
            TR.dma("sp", "ldx", xt, xcur[t * 128:(t + 1) * 128, :], reads=[(xkey, t)], writes=XT)
            rstd_of(xt, "B0", D, 0)
            TR.op("dve", lambda e: e.scalar_tensor_tensor(out=ht, in0=xt, scalar=rs[:, 0:1], in1=rowA, op0=ALU.mult, op1=ALU.mult),
                  reads=XT + ["rs"] + RA, writes=HT)
            TR.dma("sp", "sty", y[t * 128:(t + 1) * 128, :], ht, reads=HT, writes=[("y", t)])
        for d in TR.dsem.values():
            TR.wait("sp", (d[0], d[1]))
    return nc


def _consts(first_half):
    s = np.arange(128)[:, None]; c = np.arange(128)[None, :]
    m = np.zeros((NMAT, 128, 128), np.float32)
    m[0] = np.eye(128)
    m[1] = (s <= c)
    m[2] = (s > c)
    same = (s // 8) == (c // 8)
    m[3] = (s <= c) & same
    m[4] = (s > c) & same
    for g, w in enumerate(WINS):
        b = 5 + g * 7
        m[b + 0] = ((s <= c) & (s > c - w)) / w - (s == c)
        m[b + 1] = ((s - 128) > (c - w)) / w
        cntc = np.minimum(c + 1, w)
        if first_half:
            m[b + 2] = ((s <= c) & (s > c - w)) / cntc - (s == c)
            m[b + 6] = 0.0
        else:
            m[b + 2] = m[b + 0]
            m[b + 6] = m[b + 1]
        m[b + 3] = ((s <= c) & (s > c - w) & same) / w - (s == c)
        for hf in range(2):
            hr = np.arange(128)[:, None]
            jj = hr // 15 + hf * 8; rr = hr % 15
            pos_h = rr - 15
            ci = c % 8; cj = c // 8
            m[b + 4 + hf] = ((hr < 120) & (jj == cj) & (pos_h > ci - w)) / w
    cm = np.zeros((128, 16, 128), np.float32)
    for j in range(16):
        cm[:, j, 8 * j:8 * j + 8] = 1.0
    rm = np.zeros((128, 16), np.float32)
    for j in range(16):
        rm[8 * j:8 * j + 8, j] = 1.0
    return m, cm.reshape(128, 2048), rm


_NC = None


def kernel(x_prompt, x_sample, state_gla, state_pool, c_prompt, c_sample, w_ada, b_ada, norm1_g, w_in, w_a2, b_a,
           gla_norm_g, w_pool, pool_scale, w_o, norm2_g, w_gu, w_down, final_norm_g):
    global _NC
    f = lambda a: np.ascontiguousarray(np.asarray(a, dtype=np.float32))
    x_prompt, x_sample, state_gla, state_pool, c_prompt, c_sample = map(f, (x_prompt, x_sample, state_gla, state_pool, c_prompt, c_sample))
    shared = {
        "w_ada": f(w_ada), "b_ada": f(b_ada), "norm1_g": f(norm1_g), "w_in": f(w_in), "w_a2": f(w_a2), "b_a": f(b_a),
        "gla_norm_g": f(gla_norm_g), "w_pool": f(w_pool).reshape(DEPTH, 1024, 512), "pool_scale": f(pool_scale), "w_o": f(w_o),
        "norm2_g": f(norm2_g), "w_gu": f(w_gu), "w_down": f(w_down), "final_norm_g": f(final_norm_g).reshape(1, D),
    }
    cst = [_consts(True), _consts(False)]
    in_maps = []
    for c in range(NCORE):
        b, half = c // 2, c % 2
        sl = slice(SPC * c, SPC * (c + 1))
        m = dict(shared)
        m["cmats"], m["cmk"], m["rmk"] = cst[half][0], cst[half][1], cst[half][2]
        m["flag"] = np.full((128, 1), float(half), np.float32)
        m["xin"] = np.concatenate([x_prompt[b, half * SEQ:(half + 1) * SEQ], x_sample[sl].reshape(SPC * 8, D)], axis=0)
        m["crow"] = np.concatenate([np.repeat(c_prompt[b:b + 1], 128, axis=0), np.repeat(c_sample[sl], 8, axis=0)], axis=0)
        m["sgla"] = np.ascontiguousarray(state_gla[:, sl])
        m["spool"] = np.ascontiguousarray(state_pool[:, sl]).reshape(DEPTH, SPC * 15, 1024)
        in_maps.append(m)
    if _NC is None:
        _NC = build_program()
    res = run_bass_kernel_spmd(_NC, in_maps, core_ids=list(range(NCORE)))
    R = res.results
    y_prompt = np.stack([np.concatenate([R[2 * b]["y"][:SEQ], R[2 * b + 1]["y"][:SEQ]], axis=0) for b in range(4)])
    y_sample = np.concatenate([R[c]["y"][SEQ:].reshape(SPC, 8, D) for c in range(NCORE)], axis=0)
    gla_pp = np.stack([R[2 * b + 1]["gla_p"] for b in range(4)], axis=1)
    pool_pp = np.stack([R[2 * b + 1]["pool_p"] for b in range(4)], axis=1)
    gla_ss = np.concatenate([R[c]["gla_s"] for c in range(NCORE)], axis=1)
    pool_ss = np.concatenate([R[c]["pool_s"] for c in range(NCORE)], axis=1)
    return (y_prompt.astype(np.float32), y_sample.astype(np.float32), gla_pp.astype(np.float32), pool_pp.astype(np.float32),
            gla_ss.astype(np.float32), pool_ss.astype(np.float32))
```

```python
import contextlib
import numpy as np
import concourse.bass as bass
import concourse.mybir as mybir
from concourse.bass_utils import run_bass_kernel_spmd

F32 = mybir.dt.float32
BF16 = mybir.dt.bfloat16
ALU = mybir.AluOpType
AF = mybir.ActivationFunctionType

D = 2048
SEQ = 1024
NCORE = 8
SPC = 16
T = SEQ + SPC * 8
NT = T // 128
NPT = SEQ // 128
NMAT = 5 + 28
NIN = 11280
DFF = 5632
DEPTH = 2
OQ, OK_, OV, OOG, OU, OGA, OGB, OALR = 0, 1024, 2048, 4096, 6144, 7168, 9216, 11264
WINS = (2, 4, 8, 16)
EPS = 1e-6


class Tracker:
    SEM_CAP = 30000

    def __init__(self, nc, stack, n_sems):
        self.nc = nc
        self.engs = {"pe": nc.tensor, "act": nc.scalar, "dve": nc.vector, "pool": nc.gpsimd, "sp": nc.sync}
        self.free_sems = [stack.enter_context(nc.semaphore(f"s{i}")) for i in range(n_sems)]
        self.cur = {}
        self.seen = {e: {} for e in self.engs}
        self.lastw = {}
        self.readers = {}
        self.dsem = {}
        self._pend = []

    def wait(self, en, tok):
        if tok is None:
            return
        sem, val = tok
        sid = id(sem)
        if self.seen[en].get(sid, 0) >= val:
            return
        self.engs[en].wait_ge(sem, val)
        self.seen[en][sid] = val

    def _deps(self, en, reads, writes):
        for k in reads:
            self.wait(en, self.lastw.get(k))
        for k in writes:
            self.wait(en, self.lastw.get(k))
            for r in self.readers.get(k, ()):
                self.wait(en, r)

    def _commit(self, tok, reads, writes):
        for k in reads:
            self.readers.setdefault(k, []).append(tok)
        for k in writes:
            self.lastw[k] = tok
            self.readers[k] = []

    def op(self, en, fn, reads=(), writes=()):
        self._deps(en, reads, writes)
        ins = fn(self.engs[en])
        c = self.cur.get(en)
        if c is None or c[1] >= self.SEM_CAP:
            c = [self.free_sems.pop(), 0]
            self.cur[en] = c
        c[1] += 1
        ins.then_inc(c[0], 1)
        tok = (c[0], c[1])
        self._commit(tok, reads, writes)
        return tok

    def pe(self, fn, reads=(), writes=(), mark=True):
        if not mark:
            self._deps("pe", reads, writes)
            fn(self.engs["pe"])
            self._pend.append((tuple(reads), tuple(writes)))
            return None
        tok = self.op("pe", fn, reads, writes)
        for r, w in self._pend:
            self._commit(tok, r, w)
        self._pend = []
        return tok

    def collective(self, fn, reads=(), writes=()):
        self._deps("pool", reads, writes)
        ins = fn(self.engs["pool"])
        sem = self.free_sems.pop()
        ins.then_inc(sem)
        tok = (sem, 1)
        self._commit(tok, reads, writes)
        return tok

    def dma(self, q, semkey, out, in_, reads=(), writes=()):
        d = self.dsem.get(semkey)
        if d is None:
            d = [self.free_sems.pop(), 0]
            self.dsem[semkey] = d
        if d[1] > 0:
            self.wait(q, (d[0], d[1]))
        self._deps(q, reads, writes)
        ins = self.engs[q].dma_start(out=out, in_=in_)
        d[1] += 16
        ins.then_inc(d[0], 16)
        tok = (d[0], d[1])
        self._commit(tok, reads, writes)
        return tok


def build_program():
    nc = bass.Bass("TRN2", target_bir_lowering=False)
    din = lambda n, s: nc.dram_tensor(n, s, F32, kind="ExternalInput").ap()
    dout = lambda n, s: nc.dram_tensor(n, s, F32, kind="ExternalOutput").ap()
    xin = din("xin", [T, D]); crow = din("crow", [256, D])
    sgla = din("sgla", [DEPTH, SPC, 4, 256, 512]); spool = din("spool", [DEPTH, SPC * 15, 1024])
    w_ada = din("w_ada", [DEPTH, D, 6 * D]); b_ada = din("b_ada", [DEPTH, 6 * D]); n1g = din("norm1_g", [DEPTH, D])
    w_in = din("w_in", [DEPTH, D, NIN]); w_a2 = din("w_a2", [DEPTH, 16, 1024]); b_a = din("b_a", [DEPTH, 1024])
    ggla = din("gla_norm_g", [DEPTH, 512]); w_pool = din("w_pool", [DEPTH, 1024, 512]); pscale = din("pool_scale", [DEPTH, D])
    w_o = din("w_o", [DEPTH, D, D]); n2g = din("norm2_g", [DEPTH, D]); w_gu = din("w_gu", [DEPTH, D, 2 * DFF])
    w_down = din("w_down", [DEPTH, DFF, D]); fng = din("final_norm_g", [1, D])
    cmats = din("cmats", [NMAT, 128, 128]); flag_d = din("flag", [128, 1]); cmk = din("cmk", [128, 16 * 128]); rmk = din("rmk", [128, 16])
    y = dout("y", [T, D]); gla_p = dout("gla_p", [DEPTH, 4, 256, 512]); pool_p = dout("pool_p", [DEPTH, 15, 1024])
    gla_s = dout("gla_s", [DEPTH, SPC, 4, 256, 512]); pool_s = dout("pool_s", [DEPTH, SPC, 15, 1024])
    scr = lambda n, s: nc.dram_tensor(n, s, F32).ap()
    p_scr = scr("p_scr", [T, NIN]); gu_scr = scr("gu_scr", [T, 2 * DFF]); o_scr = scr("o_scr", [T, D])
    qg_scr = nc.dram_tensor("qg_scr", [4, NPT, 128, 256], BF16).ap()
    ex_in_t = [nc.dram_tensor(f"ex_in{l}", [512, 1024], F32) for l in range(DEPTH)]
    ex_out_t = [nc.dram_tensor(f"ex_out{l}", [1024, 1024], F32) for l in range(DEPTH)]
    exu_in_t = [nc.dram_tensor(f"exu_in{l}", [128, 1024], F32) for l in range(DEPTH)]
    exu_out_t = [nc.dram_tensor(f"exu_out{l}", [256, 1024], F32) for l in range(DEPTH)]
    ex_in = [t_.ap() for t_ in ex_in_t]; ex_out = [t_.ap() for t_ in ex_out_t]
    exu_in = [t_.ap() for t_ in exu_in_t]; exu_out = [t_.ap() for t_ in exu_out_t]
    xa = scr("xa", [T, D]); xb = scr("xb", [T, D]); xc = scr("xc", [T, D]); mod_scr = scr("mod_scr", [DEPTH, 256, 6 * D])

    with contextlib.ExitStack() as st:
        TR = Tracker(nc, st, 100)
        cnt = [0]

        def sb(shape, dt, name=None):
            cnt[0] += 1
            return st.enter_context(nc.sbuf_tensor(name or f"t{cnt[0]}", shape, dt))

        actT = sb([128, 16, T], BF16, "actT")
        NW = 2
        wbuf = [sb([128, 16, 512], BF16, f"wb{i}") for i in range(NW)]
        psf = [st.enter_context(nc.psum_tensor(f"psf{i}", [128, 512], F32)) for i in range(6)]
        psb = [st.enter_context(nc.psum_tensor(f"psb{i}", [128, 1024], BF16)) for i in range(2)]
        rot = {"f": 0, "b": 0, "w": 0}

        held = set()

        def fbank():
            while True:
                i = rot["f"] % 6; rot["f"] += 1
                if i not in held:
                    return psf[i], f"psf{i}"

        def bbank():
            i = rot["b"] % 2; rot["b"] += 1
            return psb[i], f"psb{i}"

        pools = {}

        def stage(tag, shape, dt, n=2):
            if tag not in pools:
                pools[tag] = [[sb(shape, dt, f"{tag}{i}") for i in range(n)], 0]
            p = pools[tag]
            i = p[1] % n; p[1] += 1
            return p[0][i], f"{tag}{i}"

        st.enter_context(nc.Block())

        Bbuf = sb([128, 8, 1024], F32, "Bbuf")
        BK = [f"B{i}" for i in range(8)]
        cm_f = Bbuf[:, 0:2, :].rearrange("p a c -> p (a c)")
        hb = sb([128, D], BF16, "hb")
        mb = hb
        idb = sb([128, 128], BF16, "idb")
        cmat_b = sb([128, NMAT, 128], BF16, "cmat_b")
        flag = sb([128, 1], F32, "flag_sb")
        Gst = sb([128, 2], F32, "Gst"); Gtot = sb([128, 4, 2], F32, "Gtot")
        cm_b = sb([128, 16, 128], BF16, "cm_b")
        rm_b = sb([128, 16], BF16, "rm_b")
        ones_b = sb([128, 128], BF16, "ones_b")
        rs = sb([128, 8], F32, "rs")
        junk = hb
        TR.dma("sp", "cl", flag[:], flag_d[:, :], writes=["flag"])
        for i in range(NMAT):
            tmpc, kc_ = stage("cld", [128, 128], F32)
            TR.dma("sp", "cl", tmpc[:], cmats[i], writes=[kc_])
            TR.op("dve", lambda e, i=i, tmpc=tmpc: e.tensor_copy(out=cmat_b[:, i, :], in_=tmpc[:]), reads=[kc_], writes=["cmat"])
        TR.op("dve", lambda e: e.tensor_copy(out=idb[:], in_=cmat_b[:, 0, :]), reads=["cmat"], writes=["idb"])
        TR.dma("sp", "cl", cm_f, cmk[:, :], writes=["B0", "B1"])
        TR.op("dve", lambda e: e.tensor_copy(out=cm_b[:].rearrange("p j c -> p (j c)"), in_=cm_f), reads=["B0", "B1"], writes=["cm_b"])
        TR.dma("sp", "cl", cm_f[:, 0:16], rmk[:, :], reads=[], writes=["B0", "B1"])
        TR.op("dve", lambda e: e.tensor_copy(out=rm_b[:], in_=cm_f[:, 0:16]), reads=["B0", "B1"], writes=["rm_b"])
        TR.op("dve", lambda e: e.memset(ones_b[:], 1.0), writes=["ones_b"])
        MINCL = {0: 1, 1: 3}
        MREV = {0: 2, 1: 4}

        def to_feat(src_b, src_key, kc, t, dst=None, dst_key="actT"):
            dst = actT if dst is None else dst
            for k0 in range(0, kc, 8):
                kn = min(8, kc - k0)
                bk, bkey = bbank()
                for k in range(kn):
                    TR.pe(lambda e, k=k: e.transpose(bk[:, k * 128:(k + 1) * 128], src_b[:, (k0 + k) * 128:(k0 + k + 1) * 128], idb[:]),
                          reads=[src_key, "idb"], writes=[bkey], mark=(k == kn - 1))
                TR.op("act", lambda e: e.copy(out=dst[:, k0:k0 + kn, t * 128:(t + 1) * 128],
                                              in_=bk[:, 0:kn * 128].rearrange("p (k c) -> p k c", c=128)),
                      reads=[bkey], writes=[(dst_key, t)])

        def linear(*a, **k):
            for _ in linear_gen(*a, **k):
                pass

        fill_state = {"gen": None}

        def fillf(n=1):
            g = fill_state["gen"]
            for _ in range(n):
                if g is None:
                    return
                try:
                    next(g)
                except StopIteration:
                    fill_state["gen"] = None
                    return

        bg_state = {"gen": None, "on": False}

        def chain_gens(*gens):
            for g_ in gens:
                yield from g_

        def bgf(n=1):
            g = bg_state["gen"]
            for _ in range(n):
                if g is None:
                    return
                try:
                    next(g)
                except StopIteration:
                    bg_state["gen"] = None
                    return

        def linear_gen(kc, tiles, W, r0, c0, ncols, epi, pw=512, lhs=None, lhs_key="actT", own_w=None, bg=False):
            groups = [(t, c, min(pw, c0 + ncols - c)) for c in range(c0, c0 + ncols, pw) for t in tiles]
            pre = getattr(epi, "pre", None)
            if pre:
                pre(*groups[0])
            wb = wkey = None
            lhs = actT if lhs is None else lhs
            for gi, (t, c, pc) in enumerate(groups):
                if bg:
                    bgf(1)
                if t == tiles[0]:
                    if own_w is not None:
                        wb, wkey, wi = own_w, "wown", "own"
                    else:
                        wi = rot["w"] % NW; rot["w"] += 1
                        wb, wkey = wbuf[wi], f"wb{wi}"
                    TR.dma("pool", f"w{wi}", wb[:, 0:kc, 0:pc], W[r0:r0 + kc * 128, c:c + pc].rearrange("(k p) n -> p k n", p=128), writes=[wkey])
                ps, pkey = fbank()
                for k in range(kc):
                    TR.pe(lambda e, k=k: e.matmul(ps[:, 0:pc], lhsT=lhs[:, k, t * 128:(t + 1) * 128], rhs=wb[:, k, 0:pc],
                                                  start=(k == 0), stop=(k == kc - 1)),
                          reads=[(lhs_key, t), wkey], writes=[pkey], mark=(k == kc - 1))
                if pre and gi + 1 < len(groups):
                    pre(*groups[gi + 1])
                epi(t, c, pc, ps, pkey)
                yield

        def store_epi(dst, dkey):
            def epi(t, c, pc, ps, pkey):
                sg, skey = stage("sto", [128, 512], F32, 2)
                TR.op("act", lambda e: e.copy(out=sg[:, 0:pc], in_=ps[:, 0:pc]), reads=[pkey], writes=[skey])
                TR.dma("sp", "st" + skey, dst[t * 128:(t + 1) * 128, c:c + pc], sg[:, 0:pc], reads=[skey], writes=[(dkey, t)])
            return epi

        def rstd_of(src, skey, n, col):
            TR.op("dve", lambda e: e.memset(rs[:, col + 1:col + 2], 0.0), writes=["rs"])
            TR.op("act", lambda e: e.activation(out=junk[:, 0:n], in_=src, func=AF.Square, accum_out=rs[:, col + 1:col + 2]),
                  reads=[skey, "rs"], writes=["hb", "rs"])
            TR.op("act", lambda e: e.activation(out=rs[:, col + 1:col + 2], in_=rs[:, col + 1:col + 2], func=AF.Ln, scale=1.0 / n, bias=EPS),
                  reads=["rs"], writes=["rs"])
            TR.op("act", lambda e: e.activation(out=rs[:, col:col + 1], in_=rs[:, col + 1:col + 2], func=AF.Exp, scale=-0.5),
                  reads=["rs"], writes=["rs"])

        xt = Bbuf[:, 0:2, :].rearrange("p a c -> p (a c)")
        ht = Bbuf[:, 2:4, :].rearrange("p a c -> p (a c)")
        rowA = Bbuf[:, 4:6, :].rearrange("p a c -> p (a c)")
        rowB = Bbuf[:, 6:8, :].rearrange("p a c -> p (a c)")
        XT, HT, RA, RB = ["B0", "B1"], ["B2", "B3"], ["B4", "B5"], ["B6", "B7"]
        gT = sb([128, 16], F32, "gT"); scT = sb([128, 16], F32, "scT"); shT = sb([128, 16], F32, "shT")

        def norm_stage(l, xsrc, xkey, gain, gl, c_sh, c_sc):
            with nc.allow_non_contiguous_dma(reason="tiny per-feature vectors"):
                TR.dma("sp", "ld0", gT[:], gain[gl, :].rearrange("(k p) -> p k", p=128), writes=["gT"])
                TR.dma("sp", "ld1", scT[:], mod_scr[l, 0, c_sc:c_sc + D].rearrange("(k p) -> p k", p=128), reads=[("mod", l, c_sc // (3 * D))], writes=["scT"])
                TR.dma("sp", "ld2", shT[:], mod_scr[l, 0, c_sh:c_sh + D].rearrange("(k p) -> p k", p=128), reads=[("mod", l, c_sh // (3 * D))], writes=["shT"])
            TR.op("dve", lambda e: e.scalar_tensor_tensor(out=scT[:], in0=scT[:], scalar=1.0, in1=gT[:], op0=ALU.add, op1=ALU.mult),
                  reads=["scT", "gT"], writes=["scT"])
            for t in range(NT):
                TR.dma("sp", "ldx", xt, xsrc[t * 128:(t + 1) * 128, :], reads=[(xkey, t)], writes=XT)
                rstd_of(xt, "B0", D, 0)
                if t < NT - 1:
                    TR.op("dve", lambda e: e.tensor_scalar(out=hb[:], in0=xt, scalar1=rs[:, 0:1], scalar2=None, op0=ALU.mult),
                          reads=XT + ["rs"], writes=["hb"])
                    for k0 in range(0, 16, 8):
                        bk, bkey = bbank()
                        for k in range(8):
                            TR.pe(lambda e, k=k: e.transpose(bk[:, k * 128:(k + 1) * 128], hb[:, (k0 + k) * 128:(k0 + k + 1) * 128], idb[:]),
                                  reads=["hb", "idb"], writes=[bkey], mark=(k == 7))
                        for k in range(8):
                            TR.op("act", lambda e, k=k: e.activation(out=actT[:, k0 + k, t * 128:(t + 1) * 128], in_=bk[:, k * 128:(k + 1) * 128],
                                                                     func=AF.Identity, scale=scT[:, k0 + k:k0 + k + 1], bias=shT[:, k0 + k:k0 + k + 1]),
                                  reads=[bkey, "scT", "shT"], writes=[("actT", t)])
                else:
                    TR.dma("sp", "ld0", ht, gain[gl:gl + 1, :].partition_broadcast(128), writes=HT)
                    TR.dma("sp", "ld1", rowA, mod_scr[l, 128:256, c_sc:c_sc + D], reads=[("mod", l, c_sc // (3 * D))], writes=RA)
                    TR.dma("sp", "ld2", rowB, mod_scr[l, 128:256, c_sh:c_sh + D], reads=[("mod", l, c_sh // (3 * D))], writes=RB)
                    TR.op("dve", lambda e: e.scalar_tensor_tensor(out=rowA, in0=rowA, scalar=1.0, in1=ht, op0=ALU.add, op1=ALU.mult),
                          reads=RA + HT, writes=RA)
                    TR.op("dve", lambda e: e.scalar_tensor_tensor(out=ht, in0=xt, scalar=rs[:, 0:1], in1=rowA, op0=ALU.mult, op1=ALU.mult),
                          reads=XT + ["rs"] + RA, writes=HT)
                    TR.op("dve", lambda e: e.tensor_tensor(out=hb[:], in0=ht, in1=rowB, op=ALU.add), reads=HT + RB, writes=["hb"])
                    to_feat(hb, "hb", 16, t)

        brow = sb([128, 512], F32, "brow")
        csT = sb([128, 16, 256], BF16, "csT")
        for t in range(2):
            TR.dma("sp", "ldx", xt, crow[t * 128:(t + 1) * 128, :], writes=XT)
            TR.op("act", lambda e: e.activation(out=ht, in_=xt, func=AF.Sigmoid), reads=XT, writes=HT)
            TR.op("dve", lambda e: e.tensor_tensor(out=hb[:], in0=ht, in1=xt, op=ALU.mult), reads=HT + XT, writes=["hb"])
            to_feat(hb, "hb", 16, t, dst=csT, dst_key="csT")

        def epi_mod(l):
            def epi(t, c, pc, ps, pkey):
                if t == 0:
                    TR.dma("sp", "ldb", brow[:, 0:pc], b_ada[l:l + 1, c:c + pc].partition_broadcast(128), writes=["brow"])
                sg, skey = stage("sto", [128, 512], F32, 2)
                TR.op("dve", lambda e: e.tensor_tensor(out=sg[:, 0:pc], in0=ps[:, 0:pc], in1=brow[:, 0:pc], op=ALU.add),
                      reads=[pkey, "brow"], writes=[skey])
                TR.dma("sp", "st" + skey, mod_scr[l, t * 128:(t + 1) * 128, c:c + pc], sg[:, 0:pc], reads=[skey], writes=[("mod", l, c // (3 * D))])
            return epi

        linear(16, [0, 1], w_ada[0], 0, 0, 3 * D, epi_mod(0), lhs=csT, lhs_key="csT")

        def ada_bg(l, c0, ncols):
            return linear_gen(16, [0, 1], w_ada[l], 0, c0, ncols, epi_mod(l), lhs=csT, lhs_key="csT")

        qkvs = [sb([128, 1024], F32, f"qkv{i}") for i in range(2)]
        alrs = [sb([128, 16], F32, f"alr_f{i}") for i in range(2)]
        qkv = qkvs[0]
        qb = sb([128, 256], BF16, "qb"); kb = sb([128, 256], BF16, "kb"); vb = sb([128, 512], BF16, "vb"); ab = sb([128, 16], BF16, "ab")
        alrT = sb([17, 128], BF16, "alrT")
        wa2f = Bbuf[0:17, 0, :]; wa2b = sb([17, 1024], BF16, "wa2b")
        TR.op("dve", lambda e: e.memset(alrT[:], 1.0), writes=["alrT"])
        ef = sb([128, 256], F32, "ef"); lgb = sb([128, 256], BF16, "lgb")
        E1 = sb([128, 256], F32, "E1"); E2 = sb([128, 256], F32, "E2"); E3 = sb([128, 256], F32, "E3")
        qe = sb([128, 2, 128], BF16, "qe"); ke = sb([128, 2, 128], BF16, "ke"); kd = sb([128, 256], BF16, "kd")
        attm = sb([128, 128], BF16, "attm")
        Sf = sb([128, 2, 512], F32, "Sf"); Sb = sb([128, 2, 512], BF16, "Sb")

        onf = sb([128, 512], F32, "onf")
        ggrow = sb([128, 512], F32, "ggrow")

        def gla_stage(l):
            TR.dma("sp", "ld0", Bbuf[0:16, 0, :], w_a2[l], writes=["B0"])
            TR.dma("sp", "ld1", Bbuf[16:17, 0, :], b_a[l:l + 1, :], writes=["B0"])
            TR.op("dve", lambda e: e.tensor_copy(out=wa2b[:], in_=wa2f), reads=["B0"], writes=["wa2b"])
            TR.dma("sp", "ld2", ggrow[:], ggla[l:l + 1, :].partition_broadcast(128), writes=["ggrow"])
            order = [(h_, t_) for h_ in range(4) for t_ in range(NT)]

            def gla_loads(i):
                h_, t_ = order[i]
                r_ = slice(t_ * 128, (t_ + 1) * 128)
                q_, a_, sfx = qkvs[i % 2], alrs[i % 2], str(i % 2)
                TR.dma("sp", "lq0" + sfx, q_[:, 0:256], p_scr[r_, OQ + h_ * 256:OQ + (h_ + 1) * 256], reads=[("p", t_)], writes=["qkv" + sfx])
                TR.dma("sp", "lq1" + sfx, q_[:, 256:512], p_scr[r_, OK_ + h_ * 256:OK_ + (h_ + 1) * 256], reads=[("p", t_)], writes=["qkv" + sfx])
                TR.dma("sp", "lq2" + sfx, q_[:, 512:1024], p_scr[r_, OV + h_ * 512:OV + (h_ + 1) * 512], reads=[("p", t_)], writes=["qkv" + sfx])
                TR.dma("sp", "lq3" + sfx, a_[:], p_scr[r_, OALR:OALR + 16], reads=[("p", t_)], writes=["alr_f" + sfx])

            gla_loads(0)
            for h in range(4):
                TR.op("dve", lambda e: e.memset(Sf[:], 0.0), writes=["Sf"])
                TR.op("dve", lambda e: e.memset(Sb[:], 0.0), writes=["Sb"])
                TR.op("dve", lambda e: e.memset(Gst[:], 1.0), writes=["Gst"])
                for t in range(NT):
                    ty = 1 if t == NT - 1 else 0
                    r = slice(t * 128, (t + 1) * 128)
                    gi_ = h * NT + t
                    if gi_ + 1 < len(order):
                        gla_loads(gi_ + 1)
                    qkv_, alr_, sfx = qkvs[gi_ % 2], alrs[gi_ % 2], str(gi_ % 2)
                    TR.op("dve", lambda e: e.tensor_scalar(out=qb[:], in0=qkv_[:, 0:256], scalar1=0.0625, scalar2=None, op0=ALU.mult),
                          reads=["qkv" + sfx], writes=["qb"])
                    TR.op("dve", lambda e: e.tensor_copy(out=kb[:], in_=qkv_[:, 256:512]), reads=["qkv" + sfx], writes=["kb"])
                    TR.op("dve", lambda e: e.tensor_copy(out=vb[:], in_=qkv_[:, 512:1024]), reads=["qkv" + sfx], writes=["vb"])
                    TR.op("dve", lambda e: e.tensor_copy(out=ab[:], in_=alr_[:]), reads=["alr_f" + sfx], writes=["ab"])
                    bk, bkey = bbank()
                    for ch in range(2):
                        TR.pe(lambda e, ch=ch: e.transpose(bk[:, ch * 128:(ch + 1) * 128], qb[:, ch * 128:(ch + 1) * 128], idb[:]),
                              reads=["qb", "idb"], writes=[bkey], mark=False)
                        TR.pe(lambda e, ch=ch: e.transpose(bk[:, 256 + ch * 128:256 + (ch + 1) * 128], kb[:, ch * 128:(ch + 1) * 128], idb[:]),
                              reads=["kb", "idb"], writes=[bkey], mark=False)
                    TR.pe(lambda e: e.transpose(bk[0:16, 512:640], ab[:, 0:16], idb[:]), reads=["ab", "idb"], writes=[bkey])
                    TR.op("act", lambda e: e.copy(out=alrT[0:16, :], in_=bk[0:16, 512:640]), reads=[bkey], writes=["alrT"])
                    fillf()
                    g, gkey = fbank()
                    TR.pe(lambda e: e.matmul(g[:, 0:256], lhsT=alrT[0:17, :], rhs=wa2b[0:17, h * 256:(h + 1) * 256], start=True, stop=True),
                          reads=["alrT", "wa2b"], writes=[gkey])
                    TR.op("act", lambda e: e.activation(out=ef[:], in_=g[:, 0:256], func=AF.Exp, scale=-1.0), reads=[gkey], writes=["ef"])
                    TR.op("act", lambda e: e.activation(out=ef[:], in_=ef[:], func=AF.Ln, bias=1.0), reads=["ef"], writes=["ef"])
                    TR.op("dve", lambda e: e.tensor_scalar(out=lgb[:], in0=ef[:], scalar1=-1.0 / 16, scalar2=None, op0=ALU.mult),
                          reads=["ef"], writes=["lgb"])
                    fillf()
                    B, Bkey = fbank()
                    for ch in range(2):
                        TR.pe(lambda e, ch=ch: e.matmul(B[:, ch * 128:(ch + 1) * 128], lhsT=lgb[:, ch * 128:(ch + 1) * 128],
                                                        rhs=cmat_b[:, MINCL[ty], :], start=True, stop=True),
                              reads=["lgb", "cmat"], writes=[Bkey], mark=False)
                    TR.pe(lambda e: e.matmul(B[:, 256:512], lhsT=cmat_b[:, MREV[ty], :], rhs=lgb[:], start=True, stop=True),
                          reads=["lgb", "cmat"], writes=[Bkey])
                    TR.op("act", lambda e: e.activation(out=E1[:], in_=B[:, 0:256], func=AF.Exp), reads=[Bkey], writes=["E1"])
                    TR.op("act", lambda e: e.activation(out=E2[:], in_=B[:, 0:256], func=AF.Exp, scale=-1.0), reads=[Bkey], writes=["E2"])
                    TR.op("act", lambda e: e.activation(out=E3[:], in_=B[:, 256:512], func=AF.Exp), reads=[Bkey], writes=["E3"])
                    TR.op("dve", lambda e: e.tensor_tensor(out=qe[:].rearrange("p c k -> p (c k)"), in0=bk[:, 0:256], in1=E1[:], op=ALU.mult),
                          reads=[bkey, "E1"], writes=["qe"])
                    TR.op("dve", lambda e: e.tensor_tensor(out=ke[:].rearrange("p c k -> p (c k)"), in0=bk[:, 256:512], in1=E2[:], op=ALU.mult),
                          reads=[bkey, "E2"], writes=["ke"])
                    TR.op("dve", lambda e: e.tensor_tensor(out=kd[:], in0=kb[:], in1=E3[:], op=ALU.mult), reads=["kb", "E3"], writes=["kd"])
                    if ty == 0:
                        qg, qgkey = stage("qg", [128, 2, 128], BF16, 2)
                        for ch in range(2):
                            TR.op("dve", lambda e, ch=ch: e.tensor_scalar(out=qg[:, ch, :], in0=qe[:, ch, :], scalar1=Gst[:, ch:ch + 1], scalar2=None, op0=ALU.mult),
                                  reads=["qe", "Gst"], writes=[qgkey])
                        TR.dma("sp", "s" + qgkey, qg_scr[h, t], qg[:].rearrange("p c k -> p (c k)"), reads=[qgkey], writes=[("qg", h, t)])
                    fillf()
                    A, Akey = fbank()
                    for ch in range(2):
                        TR.pe(lambda e, ch=ch: e.matmul(A[:, 0:128], lhsT=ke[:, ch, :], rhs=qe[:, ch, :], start=(ch == 0), stop=(ch == 1)),
                              reads=["ke", "qe"], writes=[Akey], mark=(ch == 1))
                    TR.op("dve", lambda e: e.tensor_tensor(out=attm[:], in0=A[:, 0:128], in1=cmat_b[:, MINCL[ty], :], op=ALU.mult),
                          reads=[Akey, "cmat"], writes=["attm"])
                    fillf()
                    O, Okey = fbank()
                    if ty == 0:
                        TR.pe(lambda e: e.matmul(O[:, :], lhsT=attm[:], rhs=vb[:], start=True, stop=False), reads=["attm", "vb"], writes=[Okey], mark=False)
                        for ch in range(2):
                            TR.pe(lambda e, ch=ch: e.matmul(O[:, :], lhsT=qe[:, ch, :], rhs=Sb[:, ch, :], start=False, stop=(ch == 1)),
                                  reads=["qe", "Sb"], writes=[Okey], mark=(ch == 1))
                        for ch in range(2):
                            U, Ukey = fbank()
                            TR.pe(lambda e, ch=ch: e.matmul(U[:, :], lhsT=kd[:, ch * 128:(ch + 1) * 128], rhs=vb[:], start=True, stop=True),
                                  reads=["kd", "vb"], writes=[Ukey])
                            TR.op("dve", lambda e, ch=ch: e.scalar_tensor_tensor(out=Sf[:, ch, :], in0=Sf[:, ch, :], scalar=E1[:, ch * 128 + 127:ch * 128 + 128],
                                                                                 in1=U[:, :], op0=ALU.mult, op1=ALU.add),
                                  reads=["Sf", "E1", Ukey], writes=["Sf"])
                        TR.op("act", lambda e: e.copy(out=Sb[:], in_=Sf[:]), reads=["Sf"], writes=["Sb"])
                        TR.op("dve", lambda e: e.tensor_tensor(out=Gst[:], in0=Gst[:], in1=E1[:].rearrange("p (c k) -> p c k", k=128)[:, :, 127], op=ALU.mult),
                              reads=["Gst", "E1"], writes=["Gst"])
                        if t == NPT - 1:
                            TR.dma("sp", "stS", exS(ex_in[l], h), Sf[:], reads=["Sf"], writes=[("exin", l)])
                            TR.op("dve", lambda e: e.tensor_copy(out=Gtot[:, h, :], in_=Gst[:]), reads=["Gst"], writes=["Gtot"])
                    else:
                        held.add(int(Okey[3:]))
                        TR.pe(lambda e: e.matmul(O[:, :], lhsT=attm[:], rhs=vb[:], start=True, stop=False), reads=["attm", "vb"], writes=[Okey], mark=False)
                        def s0_load(j_):
                            s0_, k_ = stage("s0f", [128, 2, 512], F32, 3)
                            for ch_ in range(2):
                                TR.dma("sp", f"l{k_}{ch_}", s0_[:, ch_, :], sgla[l, j_, h, ch_ * 128:(ch_ + 1) * 128, :], writes=[k_ + str(ch_)])
                            return s0_, k_

                        nxt_s0 = s0_load(0)
                        for j in range(SPC):
                            s0, s0key = nxt_s0
                            if j + 1 < SPC:
                                nxt_s0 = s0_load(j + 1)
                            s0b, s0bkey = stage("s0b", [128, 2, 512], BF16, 2)
                            qm, qmkey = stage("qm", [128, 2, 128], BF16, 2)
                            km, kmkey = stage("km", [128, 256], BF16, 2)
                            TR.op("act", lambda e: e.copy(out=s0b[:], in_=s0[:]), reads=[s0key + "0", s0key + "1"], writes=[s0bkey])
                            TR.op("dve", lambda e, j=j: e.tensor_tensor(out=qm[:], in0=qe[:], in1=cm_b[:, j, :].unsqueeze(1).to_broadcast([128, 2, 128]),
                                                                        op=ALU.mult), reads=["qe", "cm_b"], writes=[qmkey])
                            TR.op("dve", lambda e, j=j: e.tensor_scalar(out=km[:], in0=kd[:], scalar1=rm_b[:, j:j + 1], scalar2=None, op0=ALU.mult),
                                  reads=["kd", "rm_b"], writes=[kmkey])
                            for ch in range(2):
                                last = (j == SPC - 1 and ch == 1)
                                TR.pe(lambda e, ch=ch: e.matmul(O[:, :], lhsT=qm[:, ch, :], rhs=s0b[:, ch, :], start=False, stop=last),
                                      reads=[qmkey, s0bkey], writes=[Okey], mark=True)
                            for ch in range(2):
                                U, Ukey = fbank()
                                TR.pe(lambda e, ch=ch: e.matmul(U[:, :], lhsT=km[:, ch * 128:(ch + 1) * 128], rhs=vb[:], start=True, stop=True),
                                      reads=[kmkey, "vb"], writes=[Ukey])
                                cix = ch * 128 + 8 * j + 7
                                TR.op("dve", lambda e, ch=ch, cix=cix: e.scalar_tensor_tensor(out=s0[:, ch, :], in0=s0[:, ch, :], scalar=E1[:, cix:cix + 1],
                                                                                              in1=U[:, :], op0=ALU.mult, op1=ALU.add),
                                      reads=[s0key + str(ch), "E1", Ukey], writes=[s0key + str(ch)])
                            TR.dma("sp", "s" + s0key, gla_s[l, j, h].rearrange("(c p) e -> p c e", p=128), s0[:], reads=[s0key + "0", s0key + "1"],
                                   writes=[("glas", l, j, h)])
                    held.clear()
                    if ty == 0:
                        TR.op("act", lambda e: e.copy(out=onf[:], in_=O[:, :]), reads=[Okey], writes=["onf"])
                    else:
                        rstd_of(O[:, :], Okey, 512, 2)
                        TR.op("dve", lambda e: e.scalar_tensor_tensor(out=onf[:], in0=O[:, :], scalar=rs[:, 2:3], in1=ggrow[:], op0=ALU.mult, op1=ALU.mult),
                              reads=[Okey, "rs", "ggrow"], writes=["onf"])
                    TR.dma("sp", "sto_o", o_scr[r, h * 512:(h + 1) * 512], onf[:], reads=["onf"], writes=[("o", t)])

        def exS(buf, h, blk=0):
            return buf[blk * 512 + h * 128:blk * 512 + (h + 1) * 128, :].rearrange("r (two e) -> (r two) e", two=2).rearrange("(c p) e -> p c e", p=128)

        def exchange_stage(l):
            TR.dma("sp", "exu", exu_in[l][:, :], p_scr[SEQ - 128:SEQ, OU:OU + 1024], reads=[("p2", NPT - 1)], writes=[("exuin", l)])
            pairs = [[0, 1], [2, 3], [4, 5], [6, 7]]
            TR.collective(lambda g: g.collective_compute("AllGather", ALU.bypass, replica_groups=pairs,
                                                          ins=[exu_in_t[l].ap().opt()], outs=[exu_out_t[l].ap().opt()]),
                          reads=[("exuin", l)], writes=[("exuout", l)])
            TR.collective(lambda g: g.collective_compute("AllGather", ALU.bypass, replica_groups=pairs,
                                                          ins=[ex_in_t[l].ap().opt()], outs=[ex_out_t[l].ap().opt()]),
                          reads=[("exin", l)], writes=[("exout", l)])

        def correct_stage(l):
            for h in range(4):
                sp_, spkey = stage("s0f", [128, 2, 512], F32, 3)
                spb, spbkey = stage("s0b", [128, 2, 512], BF16, 2)
                sl, slkey = stage("s0f", [128, 2, 512], F32, 3)
                TR.dma("sp", "l" + spkey, sp_[:], exS(ex_out[l], h, 0), reads=[("exout", l)], writes=[spkey + "0", spkey + "1"])
                TR.op("dve", lambda e: e.tensor_scalar(out=sp_[:], in0=sp_[:], scalar1=flag[:, 0:1], scalar2=None, op0=ALU.mult),
                      reads=[spkey + "0", spkey + "1", "flag"], writes=[spkey + "0", spkey + "1"])
                TR.op("act", lambda e: e.copy(out=spb[:], in_=sp_[:]), reads=[spkey + "0", spkey + "1"], writes=[spbkey])
                TR.dma("sp", "l" + slkey, sl[:], exS(ex_in[l], h, 0), reads=[("exin", l)], writes=[slkey + "0", slkey + "1"])
                for ch in range(2):
                    TR.op("dve", lambda e, ch=ch: e.scalar_tensor_tensor(out=sl[:, ch, :], in0=sp_[:, ch, :], scalar=Gtot[:, h, ch:ch + 1], in1=sl[:, ch, :],
                                                                         op0=ALU.mult, op1=ALU.add), reads=[spkey + "0", spkey + "1", slkey + "0", slkey + "1", "Gtot"], writes=[slkey + "0", slkey + "1"])
                TR.dma("sp", "s" + slkey, gla_p[l, h].rearrange("(c p) e -> p c e", p=128), sl[:], reads=[slkey + "0", slkey + "1"], writes=[("glap", l, h)])
                crot = [0]

                def corr_loads(t_):
                    cq, cqk = stage("qm", [128, 2, 128], BF16, 2)
                    crot[0] += 1
                    co, cok = qkvs[crot[0] % 2][:, 0:512], "qkv" + str(crot[0] % 2)
                    TR.dma("sp", "l" + cqk, cq[:].rearrange("p c k -> p (c k)"), qg_scr[h, t_], reads=[("qg", h, t_)], writes=[cqk])
                    TR.dma("sp", "l" + cok, co, o_scr[t_ * 128:(t_ + 1) * 128, h * 512:(h + 1) * 512], reads=[("o", t_)], writes=[cok])
                    return cq, cqk, co, cok

                nxt_ld = corr_loads(0)
                for t in range(NPT):
                    r = slice(t * 128, (t + 1) * 128)
                    cq, cqk, co, cok = nxt_ld
                    if t + 1 < NPT:
                        nxt_ld = corr_loads(t + 1)
                    C, Ckey = fbank()
                    for ch in range(2):
                        TR.pe(lambda e, ch=ch: e.matmul(C[:, :], lhsT=cq[:, ch, :], rhs=spb[:, ch, :], start=(ch == 0), stop=(ch == 1)),
                              reads=[cqk, spbkey], writes=[Ckey], mark=(ch == 1))
                    TR.op("dve", lambda e: e.tensor_tensor(out=co, in0=C[:, :], in1=co, op=ALU.add),
                          reads=[Ckey, cok], writes=[cok])
                    rstd_of(co, cok, 512, 2)
                    TR.op("dve", lambda e: e.scalar_tensor_tensor(out=onf[:], in0=co, scalar=rs[:, 2:3], in1=ggrow[:], op0=ALU.mult, op1=ALU.mult),
                          reads=[cok, "rs", "ggrow"], writes=["onf"])
                    TR.dma("sp", "sto_o", o_scr[r, h * 512:(h + 1) * 512], onf[:], reads=["onf"], writes=[("o", t)])

        ogt = Bbuf[:, 0, :]; gat = Bbuf[:, 1, :]; gbt = Bbuf[:, 2, :]; ot = Bbuf[:, 3, :]; sgt = Bbuf[:, 4, :]; ut = Bbuf[:, 5, :]
        psrow = Bbuf[:, 6:8, :].rearrange("p a c -> p (a c)")
        ub = [sb([128, 1024], BF16, f"ub{i}") for i in range(2)]
        hist_b = [sb([120, 1024], BF16, f"hist_b{i}") for i in range(2)]
        dTb = sb([128, 8, 128], BF16, "dTb")
        wpool_b = sb([128, 8, 512], BF16, "wpool_b")

        def merge_stage(l):
            TR.dma("pool", "wp", wpool_b[:], w_pool[l].rearrange("(k p) o -> p k o", p=128), writes=["wpool"])
            TR.dma("sp", "ld0", psrow, pscale[l:l + 1, :].partition_broadcast(128), writes=["B6", "B7"])
            for hf in range(2):
                TR.dma("sp", "ld1", Bbuf[0:120, 5, :], spool[l, hf * 120:(hf + 1) * 120, :], writes=["B5"])
                TR.op("dve", lambda e, hf=hf: e.tensor_copy(out=hist_b[hf][:], in_=Bbuf[0:120, 5, :]), reads=["B5"], writes=[f"hist_b{hf}"])
            TR.dma("sp", "lm4", ut, exu_out[l][0:128, :], reads=[("exuout", l)], writes=["B5"])
            TR.op("dve", lambda e: e.tensor_copy(out=ub[1][:], in_=ut), reads=["B5"], writes=["ub1"])
            for t in range(NT):
                ty = 1 if t == NT - 1 else 0
                r = slice(t * 128, (t + 1) * 128)
                TR.dma("sp", "lm4", ut, p_scr[r, OU:OU + 1024], reads=[("p2", t)], writes=["B5"])
                cur, ckey = ub[t % 2], f"ub{t % 2}"
                prv, pvkey = ub[(t + 1) % 2], f"ub{(t + 1) % 2}"
                TR.op("dve", lambda e: e.tensor_copy(out=cur[:], in_=ut), reads=["B5"], writes=[ckey])
                dbanks = [fbank(), fbank()]
                for g in range(4):
                    Dk, Dkey = dbanks[g // 2]
                    base = 5 + g * 7
                    for ic in range(2):
                        col = ((g % 2) * 2 + ic) * 128
                        cs = slice(g * 256 + ic * 128, g * 256 + (ic + 1) * 128)
                        lastg = (g % 2 == 1 and ic == 1)
                        if ty == 0:
                            mc, mp = (base + 2, base + 6) if t == 0 else (base + 0, base + 1)
                            TR.pe(lambda e: e.matmul(Dk[:, col:col + 128], lhsT=cur[:, cs], rhs=cmat_b[:, mc, :], start=True, stop=False),
                                  reads=[ckey, "cmat"], writes=[Dkey], mark=False)
                            TR.pe(lambda e: e.matmul(Dk[:, col:col + 128], lhsT=prv[:, cs], rhs=cmat_b[:, mp, :], start=False, stop=True),
                                  reads=[pvkey, "cmat"], writes=[Dkey], mark=lastg)
                        else:
                            TR.pe(lambda e: e.matmul(Dk[:, col:col + 128], lhsT=cur[:, cs], rhs=cmat_b[:, base + 3, :], start=True, stop=False),
                                  reads=[ckey, "cmat"], writes=[Dkey], mark=False)
                            for hf in range(2):
                                TR.pe(lambda e, hf=hf: e.matmul(Dk[:, col:col + 128], lhsT=hist_b[hf][0:120, cs], rhs=cmat_b[0:120, base + 4 + hf, :],
                                                                start=False, stop=(hf == 1)),
                                      reads=[f"hist_b{hf}", "cmat"], writes=[Dkey], mark=(lastg and hf == 1))
                for i2 in range(2):
                    Dk, Dkey = dbanks[i2]
                    TR.op("act", lambda e, i2=i2, Dk=Dk: e.copy(out=dTb[:, i2 * 4:(i2 + 1) * 4, :], in_=Dk[:, :].rearrange("p (k c) -> p k c", c=128)),
                          reads=[Dkey], writes=["dTb"])
                for hfc in range(2):
                    bgf(4)
                    c0 = hfc * 1024
                    TR.dma("sp", "lm0", ogt, p_scr[r, OOG + c0:OOG + c0 + 1024], reads=[("p2", t)], writes=["B0"])
                    TR.dma("sp", "lm1", gat, p_scr[r, OGA + c0:OGA + c0 + 1024], reads=[("p2", t)], writes=["B1"])
                    TR.dma("sp", "lm2", gbt, p_scr[r, OGB + c0:OGB + c0 + 1024], reads=[("p2", t)], writes=["B2"])
                    TR.dma("sp", "lm3", ot, o_scr[r, c0:c0 + 1024], reads=[("o", t)], writes=["B3"])
                    TR.op("act", lambda e: e.activation(out=sgt, in_=ogt, func=AF.Sigmoid), reads=["B0"], writes=["B4"])
                    TR.op("dve", lambda e: e.tensor_tensor(out=ogt, in0=ogt, in1=sgt, op=ALU.mult), reads=["B0", "B4"], writes=["B0"])
                    TR.op("dve", lambda e: e.tensor_tensor(out=ogt, in0=ogt, in1=ot, op=ALU.mult), reads=["B0", "B3"], writes=["B0"])
                    TR.op("act", lambda e: e.activation(out=sgt, in_=gat, func=AF.Sigmoid), reads=["B1"], writes=["B4"])
                    TR.op("dve", lambda e: e.tensor_tensor(out=ogt, in0=ogt, in1=sgt, op=ALU.mult), reads=["B0", "B4"], writes=["B0"])
                    TR.op("act", lambda e: e.activation(out=sgt, in_=gbt, func=AF.Sigmoid), reads=["B2"], writes=["B4"])
                    TR.op("dve", lambda e: e.tensor_tensor(out=sgt, in0=sgt, in1=psrow[:, c0:c0 + 1024], op=ALU.mult), reads=["B4", "B6", "B7"], writes=["B4"])
                    for g2 in range(2):
                        g = hfc * 2 + g2
                        Y, Ykey = fbank()
                        for ic in range(2):
                            TR.pe(lambda e, ic=ic: e.matmul(Y[:, :], lhsT=dTb[:, g * 2 + ic, :], rhs=wpool_b[:, g * 2 + ic, :], start=(ic == 0), stop=(ic == 1)),
                                  reads=["dTb", "wpool"], writes=[Ykey], mark=(ic == 1))
                        gs = slice(g2 * 512, (g2 + 1) * 512)
                        TR.op("dve", lambda e, gs=gs: e.tensor_tensor(out=gbt[:, gs], in0=Y[:, :], in1=sgt[:, gs], op=ALU.mult),
                              reads=[Ykey, "B4", "B2"], writes=["B2"])
                    TR.op("dve", lambda e: e.tensor_tensor(out=mb[:, c0:c0 + 1024], in0=gbt, in1=ogt, op=ALU.add), reads=["B2", "B0"], writes=["hb"])
                to_feat(mb, "hb", 16, t)
            TR.dma("sp", "po0", pool_p[l], p_scr[SEQ - 15:SEQ, OU:OU + 1024], reads=[("p2", NPT - 1)], writes=[("poolp", l)])
            TR.dma("sp", "po1", pool_s[l, :, 0:7, :], spool[l].rearrange("(j r) c -> j r c", r=15)[:, 8:15, :], writes=[("pools0", l)])
            TR.dma("sp", "po2", pool_s[l, :, 7:15, :], p_scr[SEQ:T, OU:OU + 1024].rearrange("(j r) c -> j r c", r=8),
                   reads=[("p2", NT - 1)], writes=[("pools1", l)])

        g1row = sb([128, 2, 512], F32, "g1row")

        def resid_epi(l, gcol, xprev, pkey_prev, xnext, nkey):
            pend = {}

            def pre(t, c, pc):
                xs, xskey = stage("xs", [128, 512], F32, 2)
                TR.dma("sp", "l" + xskey, xs[:, 0:pc], xprev[t * 128:(t + 1) * 128, c:c + pc], reads=[(pkey_prev, t)], writes=[xskey])
                pend[(t, c)] = (xs, xskey)

            def epi(t, c, pc, ps, pkey):
                ty = 1 if t == NT - 1 else 0
                if t == 0:
                    for ty2 in range(2):
                        TR.dma("sp", "ldg", g1row[:, ty2, 0:pc], mod_scr[l, ty2 * 128:(ty2 + 1) * 128, gcol + c:gcol + c + pc],
                               reads=[("mod", l, gcol // (3 * D))], writes=["g1row"])
                xs, xskey = pend.pop((t, c))
                sg, skey = stage("sto", [128, 512], F32, 2)
                TR.op("dve", lambda e: e.tensor_tensor(out=sg[:, 0:pc], in0=ps[:, 0:pc], in1=g1row[:, ty, 0:pc], op=ALU.mult),
                      reads=[pkey, "g1row"], writes=[skey])
                TR.op("dve", lambda e: e.tensor_tensor(out=sg[:, 0:pc], in0=sg[:, 0:pc], in1=xs[:, 0:pc], op=ALU.add),
                      reads=[skey, xskey], writes=[skey])
                TR.dma("sp", "st" + skey, xnext[t * 128:(t + 1) * 128, c:c + pc], sg[:, 0:pc], reads=[skey], writes=[(nkey, t)])
            epi.pre = pre
            return epi

        xcur, xkey = xin, "xin"
        tiles = list(range(NT))
        for l in range(DEPTH):
            norm_stage(l, xcur, xkey, n1g, l, 0, D)
            linear(16, tiles, w_in[l], 0, OQ, OOG - OQ, store_epi(p_scr, "p"))
            linear(16, tiles, w_in[l], 0, OALR, 16, store_epi(p_scr, "p"))
            fill_state["gen"] = linear_gen(16, tiles, w_in[l], 0, OOG, OALR - OOG, store_epi(p_scr, "p2"))
            gla_stage(l)
            fillf(10 ** 6)
            if l == 0:
                bg_state["gen"] = chain_gens(ada_bg(0, 3 * D, 3 * D), ada_bg(1, 0, 6 * D))
            exchange_stage(l)
            correct_stage(l)
            merge_stage(l)
            bgf(10 ** 6)
            x1, x1key = xa, "xa"
            linear(16, tiles, w_o[l], 0, 0, D, resid_epi(l, 2 * D, xcur, xkey, x1, x1key))
            norm_stage(l, x1, x1key, n2g, l, 3 * D, 4 * D)
            linear(16, tiles, w_gu[l], 0, 0, 2 * DFF, store_epi(gu_scr, "gu"))
            prev, prevkey = x1, x1key
            for gi, (k0, kc) in enumerate(((0, 16), (16, 16), (32, 12))):
                n = kc * 128
                for t in tiles:
                    r = slice(t * 128, (t + 1) * 128)
                    for c0 in range(0, n, 1024):
                        w = min(1024, n - c0)
                        gt_ = Bbuf[:, 0, 0:w]; upt = Bbuf[:, 1, 0:w]; sg2 = Bbuf[:, 2, 0:w]
                        TR.dma("sp", "lf0", gt_, gu_scr[r, k0 * 128 + c0:k0 * 128 + c0 + w], reads=[("gu", t)], writes=["B0"])
                        TR.dma("sp", "lf1", upt, gu_scr[r, DFF + k0 * 128 + c0:DFF + k0 * 128 + c0 + w], reads=[("gu", t)], writes=["B1"])
                        TR.op("act", lambda e: e.activation(out=sg2, in_=gt_, func=AF.Sigmoid), reads=["B0"], writes=["B2"])
                        TR.op("dve", lambda e: e.tensor_tensor(out=gt_, in0=gt_, in1=sg2, op=ALU.mult), reads=["B0", "B2"], writes=["B0"])
                        TR.op("dve", lambda e: e.tensor_tensor(out=mb[:, c0:c0 + w], in0=gt_, in1=upt, op=ALU.mult), reads=["B0", "B1"], writes=["hb"])
                    to_feat(mb, "hb", kc, t)
                nxt, nkey = ((xb, "xb"), (xc, "xc"), (xb, "xb"))[gi]
                linear(kc, tiles, w_down[l], k0 * 128, 0, D, resid_epi(l, 5 * D, prev, prevkey, nxt, nkey))
                if gi > 0:
                    pass
                prev, prevkey = nxt, nkey
            xcur, xkey = prev, prevkey

        TR.dma("sp", "ld0", rowA, fng[0:1, :].partition_broadcast(128), writes=RA)
        for t in tiles:
            TR.dma("sp", "ldx", xt, xcur[t * 128:(t + 1) * 128, :], reads=[(xkey, t)], writes=XT)
            rstd_of(xt, "B0", D, 0)
            TR.op("dve", lambda e: e.scalar_tensor_tensor(out=ht, in0=xt, scalar=rs[:, 0:1], in1=rowA, op0=ALU.mult, op1=ALU.mult),
                  reads=XT + ["rs"] + RA, writes=HT)
            TR.dma("sp", "sty", y[t * 128:(t + 1) * 128, :], ht, reads=HT, writes=[("y", t)])
        for d in TR.dsem.values():
            TR.wait("sp", (d[0], d[1]))
    return nc


def _consts(first_half):
    s = np.arange(128)[:, None]; c = np.arange(128)[None, :]
    m = np.zeros((NMAT, 128, 128), np.float32)
    m[0] = np.eye(128)
    m[1] = (s <= c)
    m[2] = (s > c)
    same = (s // 8) == (c // 8)
    m[3] = (s <= c) & same
    m[4] = (s > c) & same
    for g, w in enumerate(WINS):
        b = 5 + g * 7
        m[b + 0] = ((s <= c) & (s > c - w)) / w - (s == c)
        m[b + 1] = ((s - 128) > (c - w)) / w
        cntc = np.minimum(c + 1, w)
        if first_half:
            m[b + 2] = ((s <= c) & (s > c - w)) / cntc - (s == c)
            m[b + 6] = 0.0
        else:
            m[b + 2] = m[b + 0]
            m[b + 6] = m[b + 1]
        m[b + 3] = ((s <= c) & (s > c - w) & same) / w - (s == c)
        for hf in range(2):
            hr = np.arange(128)[:, None]
            jj = hr // 15 + hf * 8; rr = hr % 15
            pos_h = rr - 15
            ci = c % 8; cj = c // 8
            m[b + 4 + hf] = ((hr < 120) & (jj == cj) & (pos_h > ci - w)) / w
    cm = np.zeros((128, 16, 128), np.float32)
    for j in range(16):
        cm[:, j, 8 * j:8 * j + 8] = 1.0
    rm = np.zeros((128, 16), np.float32)
    for j in range(16):
        rm[8 * j:8 * j + 8, j] = 1.0
    return m, cm.reshape(128, 2048), rm


_NC = None


def kernel(x_prompt, x_sample, state_gla, state_pool, c_prompt, c_sample, w_ada, b_ada, norm1_g, w_in, w_a2, b_a,
           gla_norm_g, w_pool, pool_scale, w_o, norm2_g, w_gu, w_down, final_norm_g):
    global _NC
    f = lambda a: np.ascontiguousarray(np.asarray(a, dtype=np.float32))
    x_prompt, x_sample, state_gla, state_pool, c_prompt, c_sample = map(f, (x_prompt, x_sample, state_gla, state_pool, c_prompt, c_sample))
    shared = {
        "w_ada": f(w_ada), "b_ada": f(b_ada), "norm1_g": f(norm1_g), "w_in": f(w_in), "w_a2": f(w_a2), "b_a": f(b_a),
        "gla_norm_g": f(gla_norm_g), "w_pool": f(w_pool).reshape(DEPTH, 1024, 512), "pool_scale": f(pool_scale), "w_o": f(w_o),
        "norm2_g": f(norm2_g), "w_gu": f(w_gu), "w_down": f(w_down), "final_norm_g": f(final_norm_g).reshape(1, D),
    }
    cst = [_consts(True), _consts(False)]
    in_maps = []
    for c in range(NCORE):
        b, half = c // 2, c % 2
        sl = slice(SPC * c, SPC * (c + 1))
        m = dict(shared)
        m["cmats"], m["cmk"], m["rmk"] = cst[half][0], cst[half][1], cst[half][2]
        m["flag"] = np.full((128, 1), float(half), np.float32)
        m["xin"] = np.concatenate([x_prompt[b, half * SEQ:(half + 1) * SEQ], x_sample[sl].reshape(SPC * 8, D)], axis=0)
        m["crow"] = np.concatenate([np.repeat(c_prompt[b:b + 1], 128, axis=0), np.repeat(c_sample[sl], 8, axis=0)], axis=0)
        m["sgla"] = np.ascontiguousarray(state_gla[:, sl])
        m["spool"] = np.ascontiguousarray(state_pool[:, sl]).reshape(DEPTH, SPC * 15, 1024)
        in_maps.append(m)
    if _NC is None:
        _NC = build_program()
    res = run_bass_kernel_spmd(_NC, in_maps, core_ids=list(range(NCORE)))
    R = res.results
    y_prompt = np.stack([np.concatenate([R[2 * b]["y"][:SEQ], R[2 * b + 1]["y"][:SEQ]], axis=0) for b in range(4)])
    y_sample = np.concatenate([R[c]["y"][SEQ:].reshape(SPC, 8, D) for c in range(NCORE)], axis=0)
    gla_pp = np.stack([R[2 * b + 1]["gla_p"] for b in range(4)], axis=1)
    pool_pp = np.stack([R[2 * b + 1]["pool_p"] for b in range(4)], axis=1)
    gla_ss = np.concatenate([R[c]["gla_s"] for c in range(NCORE)], axis=1)
    pool_ss = np.concatenate([R[c]["pool_s"] for c in range(NCORE)], axis=1)
    return (y_prompt.astype(np.float32), y_sample.astype(np.float32), gla_pp.astype(np.float32), pool_pp.astype(np.float32),
            gla_ss.astype(np.float32), pool_ss.astype(np.float32))
```

```python
import contextlib
import numpy as np
import concourse.bass as bass
import concourse.mybir as mybir
from concourse.bass_utils import run_bass_kernel_spmd

F32 = mybir.dt.float32
BF16 = mybir.dt.bfloat16
ALU = mybir.AluOpType
AF = mybir.ActivationFunctionType

D = 2048
SEQ = 1024
NCORE = 8
SPC = 16
T = SEQ + SPC * 8
NT = T // 128
NPT = SEQ // 128
NMAT = 5 + 28
NIN = 11280
DFF = 5632
DEPTH = 2
OQ, OK_, OV, OOG, OU, OGA, OGB, OALR = 0, 1024, 2048, 4096, 6144, 7168, 9216, 11264
WINS = (2, 4, 8, 16)
EPS = 1e-6


class Tracker:
    SEM_CAP = 30000

    def __init__(self, nc, stack, n_sems):
        self.nc = nc
        self.engs = {"pe": nc.tensor, "act": nc.scalar, "dve": nc.vector, "pool": nc.gpsimd, "sp": nc.sync}
        self.free_sems = [stack.enter_context(nc.semaphore(f"s{i}")) for i in range(n_sems)]
        self.cur = {}
        self.seen = {e: {} for e in self.engs}
        self.lastw = {}
        self.readers = {}
        self.dsem = {}
        self._pend = []

    def wait(self, en, tok):
        if tok is None:
            return
        sem, val = tok
        sid = id(sem)
        if self.seen[en].get(sid, 0) >= val:
            return
        self.engs[en].wait_ge(sem, val)
        self.seen[en][sid] = val

    def _deps(self, en, reads, writes):
        for k in reads:
            self.wait(en, self.lastw.get(k))
        for k in writes:
            self.wait(en, self.lastw.get(k))
            for r in self.readers.get(k, ()):
                self.wait(en, r)

    def _commit(self, tok, reads, writes):
        for k in reads:
            self.readers.setdefault(k, []).append(tok)
        for k in writes:
            self.lastw[k] = tok
            self.readers[k] = []

    def op(self, en, fn, reads=(), writes=()):
        self._deps(en, reads, writes)
        ins = fn(self.engs[en])
        c = self.cur.get(en)
        if c is None or c[1] >= self.SEM_CAP:
            c = [self.free_sems.pop(), 0]
            self.cur[en] = c
        c[1] += 1
        ins.then_inc(c[0], 1)
        tok = (c[0], c[1])
        self._commit(tok, reads, writes)
        return tok

    def pe(self, fn, reads=(), writes=(), mark=True):
        if not mark:
            self._deps("pe", reads, writes)
            fn(self.engs["pe"])
            self._pend.append((tuple(reads), tuple(writes)))
            return None
        tok = self.op("pe", fn, reads, writes)
        for r, w in self._pend:
            self._commit(tok, r, w)
        self._pend = []
        return tok

    def collective(self, fn, reads=(), writes=()):
        self._deps("pool", reads, writes)
        ins = fn(self.engs["pool"])
        sem = self.free_sems.pop()
        ins.then_inc(sem)
        tok = (sem, 1)
        self._commit(tok, reads, writes)
        return tok

    def dma(self, q, semkey, out, in_, reads=(), writes=()):
        d = self.dsem.get(semkey)
        if d is None:
            d = [self.free_sems.pop(), 0]
            self.dsem[semkey] = d
        if d[1] > 0:
            self.wait(q, (d[0], d[1]))
        self._deps(q, reads, writes)
        ins = self.engs[q].dma_start(out=out, in_=in_)
        d[1] += 16
        ins.then_inc(d[0], 16)
        tok = (d[0], d[1])
        self._commit(tok, reads, writes)
        return tok


def build_program():
    nc = bass.Bass("TRN2", target_bir_lowering=False)
    din = lambda n, s: nc.dram_tensor(n, s, F32, kind="ExternalInput").ap()
    dout = lambda n, s: nc.dram_tensor(n, s, F32, kind="ExternalOutput").ap()
    xin = din("xin", [T, D]); crow = din("crow", [256, D])
    sgla = din("sgla", [DEPTH, SPC, 4, 256, 512]); spool = din("spool", [DEPTH, SPC * 15, 1024])
    w_ada = din("w_ada", [DEPTH, D, 6 * D]); b_ada = din("b_ada", [DEPTH, 6 * D]); n1g = din("norm1_g", [DEPTH, D])
    w_in = din("w_in", [DEPTH, D, NIN]); w_a2 = din("w_a2", [DEPTH, 16, 1024]); b_a = din("b_a", [DEPTH, 1024])
    ggla = din("gla_norm_g", [DEPTH, 512]); w_pool = din("w_pool", [DEPTH, 1024, 512]); pscale = din("pool_scale", [DEPTH, D])
    w_o = din("w_o", [DEPTH, D, D]); n2g = din("norm2_g", [DEPTH, D]); w_gu = din("w_gu", [DEPTH, D, 2 * DFF])
    w_down = din("w_down", [DEPTH, DFF, D]); fng = din("final_norm_g", [1, D])
    cmats = din("cmats", [NMAT, 128, 128]); flag_d = din("flag", [128, 1]); cmk = din("cmk", [128, 16 * 128]); rmk = din("rmk", [128, 16])
    y = dout("y", [T, D]); gla_p = dout("gla_p", [DEPTH, 4, 256, 512]); pool_p = dout("pool_p", [DEPTH, 15, 1024])
    gla_s = dout("gla_s", [DEPTH, SPC, 4, 256, 512]); pool_s = dout("pool_s", [DEPTH, SPC, 15, 1024])
    scr = lambda n, s: nc.dram_tensor(n, s, F32).ap()
    p_scr = scr("p_scr", [T, NIN]); gu_scr = scr("gu_scr", [T, 2 * DFF]); o_scr = scr("o_scr", [T, D])
    qg_scr = nc.dram_tensor("qg_scr", [4, NPT, 128, 256], BF16).ap()
    act_scr = nc.dram_tensor("act_scr", [T, DFF], BF16).ap()
    ex_in_t = [nc.dram_tensor(f"ex_in{l}", [512, 1024], F32) for l in range(DEPTH)]
    ex_out_t = [nc.dram_tensor(f"ex_out{l}", [1024, 1024], F32) for l in range(DEPTH)]
    exu_in_t = [nc.dram_tensor(f"exu_in{l}", [128, 1024], F32) for l in range(DEPTH)]
    exu_out_t = [nc.dram_tensor(f"exu_out{l}", [256, 1024], F32) for l in range(DEPTH)]
    ex_in = [t_.ap() for t_ in ex_in_t]; ex_out = [t_.ap() for t_ in ex_out_t]
    exu_in = [t_.ap() for t_ in exu_in_t]; exu_out = [t_.ap() for t_ in exu_out_t]
    xa = scr("xa", [T, D]); xb = scr("xb", [T, D]); xc = scr("xc", [T, D]); mod_scr = scr("mod_scr", [DEPTH, 256, 6 * D])

    with contextlib.ExitStack() as st:
        TR = Tracker(nc, st, 100)
        cnt = [0]

        def sb(shape, dt, name=None):
            cnt[0] += 1
            return st.enter_context(nc.sbuf_tensor(name or f"t{cnt[0]}", shape, dt))

        actT = sb([128, 16, T], BF16, "actT")
        NW = 2
        wbuf = [sb([128, 16, 512], BF16, f"wb{i}") for i in range(NW)]
        psf = [st.enter_context(nc.psum_tensor(f"psf{i}", [128, 512], F32)) for i in range(6)]
        psb = [st.enter_context(nc.psum_tensor(f"psb{i}", [128, 1024], BF16)) for i in range(2)]
        rot = {"f": 0, "b": 0, "w": 0}

        held = set()

        def fbank():
            while True:
                i = rot["f"] % 6; rot["f"] += 1
                if i not in held:
                    return psf[i], f"psf{i}"

        def bbank():
            i = rot["b"] % 2; rot["b"] += 1
            return psb[i], f"psb{i}"

        pools = {}

        def stage(tag, shape, dt, n=2):
            if tag not in pools:
                pools[tag] = [[sb(shape, dt, f"{tag}{i}") for i in range(n)], 0]
            p = pools[tag]
            i = p[1] % n; p[1] += 1
            return p[0][i], f"{tag}{i}"

        st.enter_context(nc.Block())

        Bbuf = sb([128, 8, 1024], F32, "Bbuf")
        BK = [f"B{i}" for i in range(8)]
        cm_f = Bbuf[:, 0:2, :].rearrange("p a c -> p (a c)")
        hb = sb([128, D], BF16, "hb")
        mb = hb
        idb = sb([128, 128], BF16, "idb")
        cmat_b = sb([128, NMAT, 128], BF16, "cmat_b")
        flag = sb([128, 1], F32, "flag_sb")
        Gst = sb([128, 2], F32, "Gst"); Gtot = sb([128, 4, 2], F32, "Gtot")
        cm_b = sb([128, 16, 128], BF16, "cm_b")
        rm_b = sb([128, 16], BF16, "rm_b")
        ones_b = sb([128, 128], BF16, "ones_b")
        rs = sb([128, 8], F32, "rs")
        junk = hb
        TR.dma("sp", "cl", flag[:], flag_d[:, :], writes=["flag"])
        for i in range(NMAT):
            tmpc, kc_ = stage("cld", [128, 128], F32)
            TR.dma("sp", "cl", tmpc[:], cmats[i], writes=[kc_])
            TR.op("dve", lambda e, i=i, tmpc=tmpc: e.tensor_copy(out=cmat_b[:, i, :], in_=tmpc[:]), reads=[kc_], writes=["cmat"])
        TR.op("dve", lambda e: e.tensor_copy(out=idb[:], in_=cmat_b[:, 0, :]), reads=["cmat"], writes=["idb"])
        TR.dma("sp", "cl", cm_f, cmk[:, :], writes=["B0", "B1"])
        TR.op("dve", lambda e: e.tensor_copy(out=cm_b[:].rearrange("p j c -> p (j c)"), in_=cm_f), reads=["B0", "B1"], writes=["cm_b"])
        TR.dma("sp", "cl", cm_f[:, 0:16], rmk[:, :], reads=[], writes=["B0", "B1"])
        TR.op("dve", lambda e: e.tensor_copy(out=rm_b[:], in_=cm_f[:, 0:16]), reads=["B0", "B1"], writes=["rm_b"])
        TR.op("dve", lambda e: e.memset(ones_b[:], 1.0), writes=["ones_b"])
        MINCL = {0: 1, 1: 3}
        MREV = {0: 2, 1: 4}

        def to_feat(src_b, src_key, kc, t, dst=None, dst_key="actT"):
            dst = actT if dst is None else dst
            for k0 in range(0, kc, 8):
                kn = min(8, kc - k0)
                bk, bkey = bbank()
                for k in range(kn):
                    TR.pe(lambda e, k=k: e.transpose(bk[:, k * 128:(k + 1) * 128], src_b[:, (k0 + k) * 128:(k0 + k + 1) * 128], idb[:]),
                          reads=[src_key, "idb"], writes=[bkey], mark=(k == kn - 1))
                TR.op("act", lambda e: e.copy(out=dst[:, k0:k0 + kn, t * 128:(t + 1) * 128],
                                              in_=bk[:, 0:kn * 128].rearrange("p (k c) -> p k c", c=128)),
                      reads=[bkey], writes=[(dst_key, t)])

        def linear(*a, **k):
            for _ in linear_gen(*a, **k):
                pass

        fill_state = {"gen": None}

        def fillf(n=1):
            g = fill_state["gen"]
            for _ in range(n):
                if g is None:
                    return
                try:
                    next(g)
                except StopIteration:
                    fill_state["gen"] = None
                    return

        bg_state = {"gen": None, "on": False}

        def chain_gens(*gens):
            for g_ in gens:
                yield from g_

        def bgf(n=1):
            g = bg_state["gen"]
            for _ in range(n):
                if g is None:
                    return
                try:
                    next(g)
                except StopIteration:
                    bg_state["gen"] = None
                    return

        def linear_gen(kc, tiles, W, r0, c0, ncols, epi, pw=512, lhs=None, lhs_key="actT", own_w=None, bg=False):
            groups = [(t, c, min(pw, c0 + ncols - c)) for c in range(c0, c0 + ncols, pw) for t in tiles]
            pre = getattr(epi, "pre", None)
            if pre:
                pre(*groups[0])
            wb = wkey = None
            lhs = actT if lhs is None else lhs
            for gi, (t, c, pc) in enumerate(groups):
                if bg:
                    bgf(1)
                if t == tiles[0]:
                    if own_w is not None:
                        wb, wkey, wi = own_w, "wown", "own"
                    else:
                        wi = rot["w"] % NW; rot["w"] += 1
                        wb, wkey = wbuf[wi], f"wb{wi}"
                    TR.dma("pool", f"w{wi}", wb[:, 0:kc, 0:pc], W[r0:r0 + kc * 128, c:c + pc].rearrange("(k p) n -> p k n", p=128), writes=[wkey])
                ps, pkey = fbank()
                for k in range(kc):
                    TR.pe(lambda e, k=k: e.matmul(ps[:, 0:pc], lhsT=lhs[:, k, t * 128:(t + 1) * 128], rhs=wb[:, k, 0:pc],
                                                  start=(k == 0), stop=(k == kc - 1)),
                          reads=[(lhs_key, t), wkey], writes=[pkey], mark=(k == kc - 1))
                if pre and gi + 1 < len(groups):
                    pre(*groups[gi + 1])
                epi(t, c, pc, ps, pkey)
                yield

        def store_epi(dst, dkey):
            def epi(t, c, pc, ps, pkey):
                sg, skey = stage("sto", [128, 512], F32, 2)
                TR.op("act", lambda e: e.copy(out=sg[:, 0:pc], in_=ps[:, 0:pc]), reads=[pkey], writes=[skey])
                TR.dma("sp", "st" + skey, dst[t * 128:(t + 1) * 128, c:c + pc], sg[:, 0:pc], reads=[skey], writes=[(dkey, t)])
            return epi

        def rstd_of(src, skey, n, col):
            TR.op("dve", lambda e: e.memset(rs[:, col + 1:col + 2], 0.0), writes=["rs"])
            TR.op("act", lambda e: e.activation(out=junk[:, 0:n], in_=src, func=AF.Square, accum_out=rs[:, col + 1:col + 2]),
                  reads=[skey, "rs"], writes=["hb", "rs"])
            TR.op("act", lambda e: e.activation(out=rs[:, col + 1:col + 2], in_=rs[:, col + 1:col + 2], func=AF.Ln, scale=1.0 / n, bias=EPS),
                  reads=["rs"], writes=["rs"])
            TR.op("act", lambda e: e.activation(out=rs[:, col:col + 1], in_=rs[:, col + 1:col + 2], func=AF.Exp, scale=-0.5),
                  reads=["rs"], writes=["rs"])

        xt = Bbuf[:, 0:2, :].rearrange("p a c -> p (a c)")
        ht = Bbuf[:, 2:4, :].rearrange("p a c -> p (a c)")
        rowA = Bbuf[:, 4:6, :].rearrange("p a c -> p (a c)")
        rowB = Bbuf[:, 6:8, :].rearrange("p a c -> p (a c)")
        XT, HT, RA, RB = ["B0", "B1"], ["B2", "B3"], ["B4", "B5"], ["B6", "B7"]
        gT = sb([128, 16], F32, "gT"); scT = sb([128, 16], F32, "scT"); shT = sb([128, 16], F32, "shT")

        def norm_stage(l, xsrc, xkey, gain, gl, c_sh, c_sc):
            with nc.allow_non_contiguous_dma(reason="tiny per-feature vectors"):
                TR.dma("sp", "ld0", gT[:], gain[gl, :].rearrange("(k p) -> p k", p=128), writes=["gT"])
                TR.dma("sp", "ld1", scT[:], mod_scr[l, 0, c_sc:c_sc + D].rearrange("(k p) -> p k", p=128), reads=[("mod", l, c_sc // (3 * D))], writes=["scT"])
                TR.dma("sp", "ld2", shT[:], mod_scr[l, 0, c_sh:c_sh + D].rearrange("(k p) -> p k", p=128), reads=[("mod", l, c_sh // (3 * D))], writes=["shT"])
            TR.op("dve", lambda e: e.scalar_tensor_tensor(out=scT[:], in0=scT[:], scalar=1.0, in1=gT[:], op0=ALU.add, op1=ALU.mult),
                  reads=["scT", "gT"], writes=["scT"])
            for t in range(NT):
                TR.dma("sp", "ldx", xt, xsrc[t * 128:(t + 1) * 128, :], reads=[(xkey, t)], writes=XT)
                rstd_of(xt, "B0", D, 0)
                if t < NT - 1:
                    TR.op("dve", lambda e: e.tensor_scalar(out=hb[:], in0=xt, scalar1=rs[:, 0:1], scalar2=None, op0=ALU.mult),
                          reads=XT + ["rs"], writes=["hb"])
                    for k0 in range(0, 16, 8):
                        bk, bkey = bbank()
                        for k in range(8):
                            TR.pe(lambda e, k=k: e.transpose(bk[:, k * 128:(k + 1) * 128], hb[:, (k0 + k) * 128:(k0 + k + 1) * 128], idb[:]),
                                  reads=["hb", "idb"], writes=[bkey], mark=(k == 7))
                        for k in range(8):
                            TR.op("act", lambda e, k=k: e.activation(out=actT[:, k0 + k, t * 128:(t + 1) * 128], in_=bk[:, k * 128:(k + 1) * 128],
                                                                     func=AF.Identity, scale=scT[:, k0 + k:k0 + k + 1], bias=shT[:, k0 + k:k0 + k + 1]),
                                  reads=[bkey, "scT", "shT"], writes=[("actT", t)])
                else:
                    TR.dma("sp", "ld0", ht, gain[gl:gl + 1, :].partition_broadcast(128), writes=HT)
                    TR.dma("sp", "ld1", rowA, mod_scr[l, 128:256, c_sc:c_sc + D], reads=[("mod", l, c_sc // (3 * D))], writes=RA)
                    TR.dma("sp", "ld2", rowB, mod_scr[l, 128:256, c_sh:c_sh + D], reads=[("mod", l, c_sh // (3 * D))], writes=RB)
                    TR.op("dve", lambda e: e.scalar_tensor_tensor(out=rowA, in0=rowA, scalar=1.0, in1=ht, op0=ALU.add, op1=ALU.mult),
                          reads=RA + HT, writes=RA)
                    TR.op("dve", lambda e: e.scalar_tensor_tensor(out=ht, in0=xt, scalar=rs[:, 0:1], in1=rowA, op0=ALU.mult, op1=ALU.mult),
                          reads=XT + ["rs"] + RA, writes=HT)
                    TR.op("dve", lambda e: e.tensor_tensor(out=hb[:], in0=ht, in1=rowB, op=ALU.add), reads=HT + RB, writes=["hb"])
                    to_feat(hb, "hb", 16, t)

        brow = sb([128, 512], F32, "brow")
        csT = sb([128, 16, 256], BF16, "csT")
        for t in range(2):
            TR.dma("sp", "ldx", xt, crow[t * 128:(t + 1) * 128, :], writes=XT)
            TR.op("act", lambda e: e.activation(out=ht, in_=xt, func=AF.Sigmoid), reads=XT, writes=HT)
            TR.op("dve", lambda e: e.tensor_tensor(out=hb[:], in0=ht, in1=xt, op=ALU.mult), reads=HT + XT, writes=["hb"])
            to_feat(hb, "hb", 16, t, dst=csT, dst_key="csT")

        def epi_mod(l):
            def epi(t, c, pc, ps, pkey):
                if t == 0:
                    TR.dma("sp", "ldb", brow[:, 0:pc], b_ada[l:l + 1, c:c + pc].partition_broadcast(128), writes=["brow"])
                sg, skey = stage("sto", [128, 512], F32, 2)
                TR.op("dve", lambda e: e.tensor_tensor(out=sg[:, 0:pc], in0=ps[:, 0:pc], in1=brow[:, 0:pc], op=ALU.add),
                      reads=[pkey, "brow"], writes=[skey])
                TR.dma("sp", "st" + skey, mod_scr[l, t * 128:(t + 1) * 128, c:c + pc], sg[:, 0:pc], reads=[skey], writes=[("mod", l, c // (3 * D))])
            return epi

        linear(16, [0, 1], w_ada[0], 0, 0, 3 * D, epi_mod(0), lhs=csT, lhs_key="csT")

        def ada_bg(l, c0, ncols):
            return linear_gen(16, [0, 1], w_ada[l], 0, c0, ncols, epi_mod(l), lhs=csT, lhs_key="csT")

        qkvs = [sb([128, 1024], F32, f"qkv{i}") for i in range(2)]
        alrs = [sb([128, 16], F32, f"alr_f{i}") for i in range(2)]
        qkv = qkvs[0]
        qb = sb([128, 256], BF16, "qb"); kb = sb([128, 256], BF16, "kb"); vb = sb([128, 512], BF16, "vb"); ab = sb([128, 16], BF16, "ab")
        alrT = sb([17, 128], BF16, "alrT")
        wa2f = Bbuf[0:17, 0, :]; wa2b = sb([17, 1024], BF16, "wa2b")
        TR.op("dve", lambda e: e.memset(alrT[:], 1.0), writes=["alrT"])
        ef = sb([128, 256], F32, "ef"); lgb = sb([128, 256], BF16, "lgb")
        E1 = sb([128, 256], F32, "E1"); E2 = sb([128, 256], F32, "E2"); E3 = sb([128, 256], F32, "E3")
        qe = sb([128, 2, 128], BF16, "qe"); ke = sb([128, 2, 128], BF16, "ke"); kd = sb([128, 256], BF16, "kd")
        attm = sb([128, 128], BF16, "attm")
        Sf = sb([128, 2, 512], F32, "Sf"); Sb = sb([128, 2, 512], BF16, "Sb")

        onf = sb([128, 512], F32, "onf")
        ggrow = sb([128, 512], F32, "ggrow")

        def gla_stage(l):
            TR.dma("sp", "ld0", Bbuf[0:16, 0, :], w_a2[l], writes=["B0"])
            TR.dma("sp", "ld1", Bbuf[16:17, 0, :], b_a[l:l + 1, :], writes=["B0"])
            TR.op("dve", lambda e: e.tensor_copy(out=wa2b[:], in_=wa2f), reads=["B0"], writes=["wa2b"])
            TR.dma("sp", "ld2", ggrow[:], ggla[l:l + 1, :].partition_broadcast(128), writes=["ggrow"])
            order = [(h_, t_) for h_ in range(4) for t_ in range(NT)]

            def gla_loads(i):
                h_, t_ = order[i]
                r_ = slice(t_ * 128, (t_ + 1) * 128)
                q_, a_, sfx = qkvs[i % 2], alrs[i % 2], str(i % 2)
                TR.dma("sp", "lq0" + sfx, q_[:, 0:256], p_scr[r_, OQ + h_ * 256:OQ + (h_ + 1) * 256], reads=[("p", t_)], writes=["qkv" + sfx])
                TR.dma("sp", "lq1" + sfx, q_[:, 256:512], p_scr[r_, OK_ + h_ * 256:OK_ + (h_ + 1) * 256], reads=[("p", t_)], writes=["qkv" + sfx])
                TR.dma("sp", "lq2" + sfx, q_[:, 512:1024], p_scr[r_, OV + h_ * 512:OV + (h_ + 1) * 512], reads=[("p", t_)], writes=["qkv" + sfx])
                TR.dma("sp", "lq3" + sfx, a_[:], p_scr[r_, OALR:OALR + 16], reads=[("p", t_)], writes=["alr_f" + sfx])

            gla_loads(0)
            for h in range(4):
                TR.op("dve", lambda e: e.memset(Sf[:], 0.0), writes=["Sf"])
                TR.op("dve", lambda e: e.memset(Sb[:], 0.0), writes=["Sb"])
                TR.op("dve", lambda e: e.memset(Gst[:], 1.0), writes=["Gst"])
                for t in range(NT):
                    ty = 1 if t == NT - 1 else 0
                    r = slice(t * 128, (t + 1) * 128)
                    gi_ = h * NT + t
                    if gi_ + 1 < len(order):
                        gla_loads(gi_ + 1)
                    qkv_, alr_, sfx = qkvs[gi_ % 2], alrs[gi_ % 2], str(gi_ % 2)
                    TR.op("dve", lambda e: e.tensor_scalar(out=qb[:], in0=qkv_[:, 0:256], scalar1=0.0625, scalar2=None, op0=ALU.mult),
                          reads=["qkv" + sfx], writes=["qb"])
                    TR.op("dve", lambda e: e.tensor_copy(out=kb[:], in_=qkv_[:, 256:512]), reads=["qkv" + sfx], writes=["kb"])
                    TR.op("dve", lambda e: e.tensor_copy(out=vb[:], in_=qkv_[:, 512:1024]), reads=["qkv" + sfx], writes=["vb"])
                    TR.op("dve", lambda e: e.tensor_copy(out=ab[:], in_=alr_[:]), reads=["alr_f" + sfx], writes=["ab"])
                    bk, bkey = bbank()
                    for ch in range(2):
                        TR.pe(lambda e, ch=ch: e.transpose(bk[:, ch * 128:(ch + 1) * 128], qb[:, ch * 128:(ch + 1) * 128], idb[:]),
                              reads=["qb", "idb"], writes=[bkey], mark=False)
                        TR.pe(lambda e, ch=ch: e.transpose(bk[:, 256 + ch * 128:256 + (ch + 1) * 128], kb[:, ch * 128:(ch + 1) * 128], idb[:]),
                              reads=["kb", "idb"], writes=[bkey], mark=False)
                    TR.pe(lambda e: e.transpose(bk[0:16, 512:640], ab[:, 0:16], idb[:]), reads=["ab", "idb"], writes=[bkey])
                    TR.op("act", lambda e: e.copy(out=alrT[0:16, :], in_=bk[0:16, 512:640]), reads=[bkey], writes=["alrT"])
                    fillf()
                    g, gkey = fbank()
                    TR.pe(lambda e: e.matmul(g[:, 0:256], lhsT=alrT[0:17, :], rhs=wa2b[0:17, h * 256:(h + 1) * 256], start=True, stop=True),
                          reads=["alrT", "wa2b"], writes=[gkey])
                    TR.op("act", lambda e: e.activation(out=ef[:], in_=g[:, 0:256], func=AF.Exp, scale=-1.0), reads=[gkey], writes=["ef"])
                    TR.op("act", lambda e: e.activation(out=ef[:], in_=ef[:], func=AF.Ln, bias=1.0), reads=["ef"], writes=["ef"])
                    TR.op("dve", lambda e: e.tensor_scalar(out=lgb[:], in0=ef[:], scalar1=-1.0 / 16, scalar2=None, op0=ALU.mult),
                          reads=["ef"], writes=["lgb"])
                    fillf()
                    B, Bkey = fbank()
                    for ch in range(2):
                        TR.pe(lambda e, ch=ch: e.matmul(B[:, ch * 128:(ch + 1) * 128], lhsT=lgb[:, ch * 128:(ch + 1) * 128],
                                                        rhs=cmat_b[:, MINCL[ty], :], start=True, stop=True),
                              reads=["lgb", "cmat"], writes=[Bkey], mark=False)
                    TR.pe(lambda e: e.matmul(B[:, 256:512], lhsT=cmat_b[:, MREV[ty], :], rhs=lgb[:], start=True, stop=True),
                          reads=["lgb", "cmat"], writes=[Bkey])
                    TR.op("act", lambda e: e.activation(out=E1[:], in_=B[:, 0:256], func=AF.Exp), reads=[Bkey], writes=["E1"])
                    TR.op("act", lambda e: e.activation(out=E2[:], in_=B[:, 0:256], func=AF.Exp, scale=-1.0), reads=[Bkey], writes=["E2"])
                    TR.op("act", lambda e: e.activation(out=E3[:], in_=B[:, 256:512], func=AF.Exp), reads=[Bkey], writes=["E3"])
                    TR.op("dve", lambda e: e.tensor_tensor(out=qe[:].rearrange("p c k -> p (c k)"), in0=bk[:, 0:256], in1=E1[:], op=ALU.mult),
                          reads=[bkey, "E1"], writes=["qe"])
                    TR.op("dve", lambda e: e.tensor_tensor(out=ke[:].rearrange("p c k -> p (c k)"), in0=bk[:, 256:512], in1=E2[:], op=ALU.mult),
                          reads=[bkey, "E2"], writes=["ke"])
                    TR.op("dve", lambda e: e.tensor_tensor(out=kd[:], in0=kb[:], in1=E3[:], op=ALU.mult), reads=["kb", "E3"], writes=["kd"])
                    if ty == 0:
                        qg, qgkey = stage("qg", [128, 2, 128], BF16, 2)
                        for ch in range(2):
                            TR.op("dve", lambda e, ch=ch: e.tensor_scalar(out=qg[:, ch, :], in0=qe[:, ch, :], scalar1=Gst[:, ch:ch + 1], scalar2=None, op0=ALU.mult),
                                  reads=["qe", "Gst"], writes=[qgkey])
                        TR.dma("sp", "s" + qgkey, qg_scr[h, t], qg[:].rearrange("p c k -> p (c k)"), reads=[qgkey], writes=[("qg", h, t)])
                    fillf()
                    A, Akey = fbank()
                    for ch in range(2):
                        TR.pe(lambda e, ch=ch: e.matmul(A[:, 0:128], lhsT=ke[:, ch, :], rhs=qe[:, ch, :], start=(ch == 0), stop=(ch == 1)),
                              reads=["ke", "qe"], writes=[Akey], mark=(ch == 1))
                    TR.op("dve", lambda e: e.tensor_tensor(out=attm[:], in0=A[:, 0:128], in1=cmat_b[:, MINCL[ty], :], op=ALU.mult),
                          reads=[Akey, "cmat"], writes=["attm"])
                    fillf()
                    O, Okey = fbank()
                    if ty == 0:
                        TR.pe(lambda e: e.matmul(O[:, :], lhsT=attm[:], rhs=vb[:], start=True, stop=False), reads=["attm", "vb"], writes=[Okey], mark=False)
                        for ch in range(2):
                            TR.pe(lambda e, ch=ch: e.matmul(O[:, :], lhsT=qe[:, ch, :], rhs=Sb[:, ch, :], start=False, stop=(ch == 1)),
                                  reads=["qe", "Sb"], writes=[Okey], mark=(ch == 1))
                        for ch in range(2):
                            U, Ukey = fbank()
                            TR.pe(lambda e, ch=ch: e.matmul(U[:, :], lhsT=kd[:, ch * 128:(ch + 1) * 128], rhs=vb[:], start=True, stop=True),
                                  reads=["kd", "vb"], writes=[Ukey])
                            TR.op("dve", lambda e, ch=ch: e.scalar_tensor_tensor(out=Sf[:, ch, :], in0=Sf[:, ch, :], scalar=E1[:, ch * 128 + 127:ch * 128 + 128],
                                                                                 in1=U[:, :], op0=ALU.mult, op1=ALU.add),
                                  reads=["Sf", "E1", Ukey], writes=["Sf"])
                        TR.op("act", lambda e: e.copy(out=Sb[:], in_=Sf[:]), reads=["Sf"], writes=["Sb"])
                        TR.op("dve", lambda e: e.tensor_tensor(out=Gst[:], in0=Gst[:], in1=E1[:].rearrange("p (c k) -> p c k", k=128)[:, :, 127], op=ALU.mult),
                              reads=["Gst", "E1"], writes=["Gst"])
                        if t == NPT - 1:
                            TR.dma("sp", "stS", exS(ex_in[l], h), Sf[:], reads=["Sf"], writes=[("exin", l)])
                            TR.op("dve", lambda e: e.tensor_copy(out=Gtot[:, h, :], in_=Gst[:]), reads=["Gst"], writes=["Gtot"])
                    else:
                        held.add(int(Okey[3:]))
                        TR.pe(lambda e: e.matmul(O[:, :], lhsT=attm[:], rhs=vb[:], start=True, stop=False), reads=["attm", "vb"], writes=[Okey], mark=False)
                        def s0_load(j_):
                            s0_, k_ = stage("s0f", [128, 2, 512], F32, 3)
                            for ch_ in range(2):
                                TR.dma("sp", f"l{k_}{ch_}", s0_[:, ch_, :], sgla[l, j_, h, ch_ * 128:(ch_ + 1) * 128, :], writes=[k_ + str(ch_)])
                            return s0_, k_

                        nxt_s0 = s0_load(0)
                        for j in range(SPC):
                            s0, s0key = nxt_s0
                            if j + 1 < SPC:
                                nxt_s0 = s0_load(j + 1)
                            s0b, s0bkey = stage("s0b", [128, 2, 512], BF16, 2)
                            qm, qmkey = stage("qm", [128, 2, 128], BF16, 2)
                            km, kmkey = stage("km", [128, 256], BF16, 2)
                            TR.op("act", lambda e: e.copy(out=s0b[:], in_=s0[:]), reads=[s0key + "0", s0key + "1"], writes=[s0bkey])
                            TR.op("dve", lambda e, j=j: e.tensor_tensor(out=qm[:], in0=qe[:], in1=cm_b[:, j, :].unsqueeze(1).to_broadcast([128, 2, 128]),
                                                                        op=ALU.mult), reads=["qe", "cm_b"], writes=[qmkey])
                            TR.op("dve", lambda e, j=j: e.tensor_scalar(out=km[:], in0=kd[:], scalar1=rm_b[:, j:j + 1], scalar2=None, op0=ALU.mult),
                                  reads=["kd", "rm_b"], writes=[kmkey])
                            for ch in range(2):
                                last = (j == SPC - 1 and ch == 1)
                                TR.pe(lambda e, ch=ch: e.matmul(O[:, :], lhsT=qm[:, ch, :], rhs=s0b[:, ch, :], start=False, stop=last),
                                      reads=[qmkey, s0bkey], writes=[Okey], mark=True)
                            for ch in range(2):
                                U, Ukey = fbank()
                                TR.pe(lambda e, ch=ch: e.matmul(U[:, :], lhsT=km[:, ch * 128:(ch + 1) * 128], rhs=vb[:], start=True, stop=True),
                                      reads=[kmkey, "vb"], writes=[Ukey])
                                cix = ch * 128 + 8 * j + 7
                                TR.op("dve", lambda e, ch=ch, cix=cix: e.scalar_tensor_tensor(out=s0[:, ch, :], in0=s0[:, ch, :], scalar=E1[:, cix:cix + 1],
                                                                                              in1=U[:, :], op0=ALU.mult, op1=ALU.add),
                                      reads=[s0key + str(ch), "E1", Ukey], writes=[s0key + str(ch)])
                            TR.dma("sp", "s" + s0key, gla_s[l, j, h].rearrange("(c p) e -> p c e", p=128), s0[:], reads=[s0key + "0", s0key + "1"],
                                   writes=[("glas", l, j, h)])
                    held.clear()
                    if ty == 0:
                        TR.op("act", lambda e: e.copy(out=onf[:], in_=O[:, :]), reads=[Okey], writes=["onf"])
                    else:
                        rstd_of(O[:, :], Okey, 512, 2)
                        TR.op("dve", lambda e: e.scalar_tensor_tensor(out=onf[:], in0=O[:, :], scalar=rs[:, 2:3], in1=ggrow[:], op0=ALU.mult, op1=ALU.mult),
                              reads=[Okey, "rs", "ggrow"], writes=["onf"])
                    TR.dma("sp", "sto_o", o_scr[r, h * 512:(h + 1) * 512], onf[:], reads=["onf"], writes=[("o", t)])

        def exS(buf, h, blk=0):
            return buf[blk * 512 + h * 128:blk * 512 + (h + 1) * 128, :].rearrange("r (two e) -> (r two) e", two=2).rearrange("(c p) e -> p c e", p=128)

        def exchange_stage(l):
            TR.dma("sp", "exu", exu_in[l][:, :], p_scr[SEQ - 128:SEQ, OU:OU + 1024], reads=[("p2", NPT - 1)], writes=[("exuin", l)])
            pairs = [[0, 1], [2, 3], [4, 5], [6, 7]]
            TR.collective(lambda g: g.collective_compute("AllGather", ALU.bypass, replica_groups=pairs,
                                                          ins=[exu_in_t[l].ap().opt()], outs=[exu_out_t[l].ap().opt()]),
                          reads=[("exuin", l)], writes=[("exuout", l)])
            TR.collective(lambda g: g.collective_compute("AllGather", ALU.bypass, replica_groups=pairs,
                                                          ins=[ex_in_t[l].ap().opt()], outs=[ex_out_t[l].ap().opt()]),
                          reads=[("exin", l)], writes=[("exout", l)])

        def correct_stage(l):
            for h in range(4):
                sp_, spkey = stage("s0f", [128, 2, 512], F32, 3)
                spb, spbkey = stage("s0b", [128, 2, 512], BF16, 2)
                sl, slkey = stage("s0f", [128, 2, 512], F32, 3)
                TR.dma("sp", "l" + spkey, sp_[:], exS(ex_out[l], h, 0), reads=[("exout", l)], writes=[spkey + "0", spkey + "1"])
                TR.op("dve", lambda e: e.tensor_scalar(out=sp_[:], in0=sp_[:], scalar1=flag[:, 0:1], scalar2=None, op0=ALU.mult),
                      reads=[spkey + "0", spkey + "1", "flag"], writes=[spkey + "0", spkey + "1"])
                TR.op("act", lambda e: e.copy(out=spb[:], in_=sp_[:]), reads=[spkey + "0", spkey + "1"], writes=[spbkey])
                TR.dma("sp", "l" + slkey, sl[:], exS(ex_in[l], h, 0), reads=[("exin", l)], writes=[slkey + "0", slkey + "1"])
                for ch in range(2):
                    TR.op("dve", lambda e, ch=ch: e.scalar_tensor_tensor(out=sl[:, ch, :], in0=sp_[:, ch, :], scalar=Gtot[:, h, ch:ch + 1], in1=sl[:, ch, :],
                                                                         op0=ALU.mult, op1=ALU.add), reads=[spkey + "0", spkey + "1", slkey + "0", slkey + "1", "Gtot"], writes=[slkey + "0", slkey + "1"])
                TR.dma("sp", "s" + slkey, gla_p[l, h].rearrange("(c p) e -> p c e", p=128), sl[:], reads=[slkey + "0", slkey + "1"], writes=[("glap", l, h)])
                crot = [0]

                def corr_loads(t_):
                    cq, cqk = stage("qm", [128, 2, 128], BF16, 2)
                    crot[0] += 1
                    co, cok = qkvs[crot[0] % 2][:, 0:512], "qkv" + str(crot[0] % 2)
                    TR.dma("sp", "l" + cqk, cq[:].rearrange("p c k -> p (c k)"), qg_scr[h, t_], reads=[("qg", h, t_)], writes=[cqk])
                    TR.dma("sp", "l" + cok, co, o_scr[t_ * 128:(t_ + 1) * 128, h * 512:(h + 1) * 512], reads=[("o", t_)], writes=[cok])
                    return cq, cqk, co, cok

                nxt_ld = corr_loads(0)
                for t in range(NPT):
                    r = slice(t * 128, (t + 1) * 128)
                    cq, cqk, co, cok = nxt_ld
                    if t + 1 < NPT:
                        nxt_ld = corr_loads(t + 1)
                    C, Ckey = fbank()
                    for ch in range(2):
                        TR.pe(lambda e, ch=ch: e.matmul(C[:, :], lhsT=cq[:, ch, :], rhs=spb[:, ch, :], start=(ch == 0), stop=(ch == 1)),
                              reads=[cqk, spbkey], writes=[Ckey], mark=(ch == 1))
                    TR.op("dve", lambda e: e.tensor_tensor(out=co, in0=C[:, :], in1=co, op=ALU.add),
                          reads=[Ckey, cok], writes=[cok])
                    rstd_of(co, cok, 512, 2)
                    TR.op("dve", lambda e: e.scalar_tensor_tensor(out=onf[:], in0=co, scalar=rs[:, 2:3], in1=ggrow[:], op0=ALU.mult, op1=ALU.mult),
                          reads=[cok, "rs", "ggrow"], writes=["onf"])
                    TR.dma("sp", "sto_o", o_scr[r, h * 512:(h + 1) * 512], onf[:], reads=["onf"], writes=[("o", t)])

        ogt = Bbuf[:, 0, :]; gat = Bbuf[:, 1, :]; gbt = Bbuf[:, 2, :]; ot = Bbuf[:, 3, :]; sgt = Bbuf[:, 4, :]; ut = Bbuf[:, 5, :]
        psrow = Bbuf[:, 6:8, :].rearrange("p a c -> p (a c)")
        ub = [sb([128, 1024], BF16, f"ub{i}") for i in range(2)]
        hist_b = [sb([120, 1024], BF16, f"hist_b{i}") for i in range(2)]
        dTb = sb([128, 8, 128], BF16, "dTb")
        wpool_b = sb([128, 8, 512], BF16, "wpool_b")

        def merge_stage(l):
            TR.dma("pool", "wp", wpool_b[:], w_pool[l].rearrange("(k p) o -> p k o", p=128), writes=["wpool"])
            TR.dma("sp", "ld0", psrow, pscale[l:l + 1, :].partition_broadcast(128), writes=["B6", "B7"])
            for hf in range(2):
                TR.dma("sp", "ld1", Bbuf[0:120, 5, :], spool[l, hf * 120:(hf + 1) * 120, :], writes=["B5"])
                TR.op("dve", lambda e, hf=hf: e.tensor_copy(out=hist_b[hf][:], in_=Bbuf[0:120, 5, :]), reads=["B5"], writes=[f"hist_b{hf}"])
            TR.dma("sp", "lm4", ut, exu_out[l][0:128, :], reads=[("exuout", l)], writes=["B5"])
            TR.op("dve", lambda e: e.tensor_copy(out=ub[1][:], in_=ut), reads=["B5"], writes=["ub1"])
            for t in range(NT):
                ty = 1 if t == NT - 1 else 0
                r = slice(t * 128, (t + 1) * 128)
                TR.dma("sp", "lm4", ut, p_scr[r, OU:OU + 1024], reads=[("p2", t)], writes=["B5"])
                cur, ckey = ub[t % 2], f"ub{t % 2}"
                prv, pvkey = ub[(t + 1) % 2], f"ub{(t + 1) % 2}"
                TR.op("dve", lambda e: e.tensor_copy(out=cur[:], in_=ut), reads=["B5"], writes=[ckey])
                dbanks = [fbank(), fbank()]
                for g in range(4):
                    Dk, Dkey = dbanks[g // 2]
                    base = 5 + g * 7
                    for ic in range(2):
                        col = ((g % 2) * 2 + ic) * 128
                        cs = slice(g * 256 + ic * 128, g * 256 + (ic + 1) * 128)
                        lastg = (g % 2 == 1 and ic == 1)
                        if ty == 0:
                            mc, mp = (base + 2, base + 6) if t == 0 else (base + 0, base + 1)
                            TR.pe(lambda e: e.matmul(Dk[:, col:col + 128], lhsT=cur[:, cs], rhs=cmat_b[:, mc, :], start=True, stop=False),
                                  reads=[ckey, "cmat"], writes=[Dkey], mark=False)
                            TR.pe(lambda e: e.matmul(Dk[:, col:col + 128], lhsT=prv[:, cs], rhs=cmat_b[:, mp, :], start=False, stop=True),
                                  reads=[pvkey, "cmat"], writes=[Dkey], mark=lastg)
                        else:
                            TR.pe(lambda e: e.matmul(Dk[:, col:col + 128], lhsT=cur[:, cs], rhs=cmat_b[:, base + 3, :], start=True, stop=False),
                                  reads=[ckey, "cmat"], writes=[Dkey], mark=False)
                            for hf in range(2):
                                TR.pe(lambda e, hf=hf: e.matmul(Dk[:, col:col + 128], lhsT=hist_b[hf][0:120, cs], rhs=cmat_b[0:120, base + 4 + hf, :],
                                                                start=False, stop=(hf == 1)),
                                      reads=[f"hist_b{hf}", "cmat"], writes=[Dkey], mark=(lastg and hf == 1))
                for i2 in range(2):
                    Dk, Dkey = dbanks[i2]
                    TR.op("act", lambda e, i2=i2, Dk=Dk: e.copy(out=dTb[:, i2 * 4:(i2 + 1) * 4, :], in_=Dk[:, :].rearrange("p (k c) -> p k c", c=128)),
                          reads=[Dkey], writes=["dTb"])
                for hfc in range(2):
                    bgf(4)
                    c0 = hfc * 1024
                    TR.dma("sp", "lm0", ogt, p_scr[r, OOG + c0:OOG + c0 + 1024], reads=[("p2", t)], writes=["B0"])
                    TR.dma("sp", "lm1", gat, p_scr[r, OGA + c0:OGA + c0 + 1024], reads=[("p2", t)], writes=["B1"])
                    TR.dma("sp", "lm2", gbt, p_scr[r, OGB + c0:OGB + c0 + 1024], reads=[("p2", t)], writes=["B2"])
                    TR.dma("sp", "lm3", ot, o_scr[r, c0:c0 + 1024], reads=[("o", t)], writes=["B3"])
                    TR.op("act", lambda e: e.activation(out=sgt, in_=ogt, func=AF.Sigmoid), reads=["B0"], writes=["B4"])
                    TR.op("dve", lambda e: e.tensor_tensor(out=ogt, in0=ogt, in1=sgt, op=ALU.mult), reads=["B0", "B4"], writes=["B0"])
                    TR.op("dve", lambda e: e.tensor_tensor(out=ogt, in0=ogt, in1=ot, op=ALU.mult), reads=["B0", "B3"], writes=["B0"])
                    TR.op("act", lambda e: e.activation(out=sgt, in_=gat, func=AF.Sigmoid), reads=["B1"], writes=["B4"])
                    TR.op("dve", lambda e: e.tensor_tensor(out=ogt, in0=ogt, in1=sgt, op=ALU.mult), reads=["B0", "B4"], writes=["B0"])
                    TR.op("act", lambda e: e.activation(out=sgt, in_=gbt, func=AF.Sigmoid), reads=["B2"], writes=["B4"])
                    TR.op("dve", lambda e: e.tensor_tensor(out=sgt, in0=sgt, in1=psrow[:, c0:c0 + 1024], op=ALU.mult), reads=["B4", "B6", "B7"], writes=["B4"])
                    for g2 in range(2):
                        g = hfc * 2 + g2
                        Y, Ykey = fbank()
                        for ic in range(2):
                            TR.pe(lambda e, ic=ic: e.matmul(Y[:, :], lhsT=dTb[:, g * 2 + ic, :], rhs=wpool_b[:, g * 2 + ic, :], start=(ic == 0), stop=(ic == 1)),
                                  reads=["dTb", "wpool"], writes=[Ykey], mark=(ic == 1))
                        gs = slice(g2 * 512, (g2 + 1) * 512)
                        TR.op("dve", lambda e, gs=gs: e.tensor_tensor(out=gbt[:, gs], in0=Y[:, :], in1=sgt[:, gs], op=ALU.mult),
                              reads=[Ykey, "B4", "B2"], writes=["B2"])
                    TR.op("dve", lambda e: e.tensor_tensor(out=mb[:, c0:c0 + 1024], in0=gbt, in1=ogt, op=ALU.add), reads=["B2", "B0"], writes=["hb"])
                to_feat(mb, "hb", 16, t)
            TR.dma("sp", "po0", pool_p[l], p_scr[SEQ - 15:SEQ, OU:OU + 1024], reads=[("p2", NPT - 1)], writes=[("poolp", l)])
            TR.dma("sp", "po1", pool_s[l, :, 0:7, :], spool[l].rearrange("(j r) c -> j r c", r=15)[:, 8:15, :], writes=[("pools0", l)])
            TR.dma("sp", "po2", pool_s[l, :, 7:15, :], p_scr[SEQ:T, OU:OU + 1024].rearrange("(j r) c -> j r c", r=8),
                   reads=[("p2", NT - 1)], writes=[("pools1", l)])

        g1row = sb([128, 2, 512], F32, "g1row")

        def resid_epi(l, gcol, xprev, pkey_prev, xnext, nkey):
            pend = {}

            def pre(t, c, pc):
                xs, xskey = stage("xs", [128, 512], F32, 2)
                TR.dma("sp", "l" + xskey, xs[:, 0:pc], xprev[t * 128:(t + 1) * 128, c:c + pc], reads=[(pkey_prev, t)], writes=[xskey])
                pend[(t, c)] = (xs, xskey)

            def epi(t, c, pc, ps, pkey):
                ty = 1 if t == NT - 1 else 0
                if t == 0:
                    for ty2 in range(2):
                        TR.dma("sp", "ldg", g1row[:, ty2, 0:pc], mod_scr[l, ty2 * 128:(ty2 + 1) * 128, gcol + c:gcol + c + pc],
                               reads=[("mod", l, gcol // (3 * D))], writes=["g1row"])
                xs, xskey = pend.pop((t, c))
                sg, skey = stage("sto", [128, 512], F32, 2)
                TR.op("dve", lambda e: e.tensor_tensor(out=sg[:, 0:pc], in0=ps[:, 0:pc], in1=g1row[:, ty, 0:pc], op=ALU.mult),
                      reads=[pkey, "g1row"], writes=[skey])
                TR.op("dve", lambda e: e.tensor_tensor(out=sg[:, 0:pc], in0=sg[:, 0:pc], in1=xs[:, 0:pc], op=ALU.add),
                      reads=[skey, xskey], writes=[skey])
                TR.dma("sp", "st" + skey, xnext[t * 128:(t + 1) * 128, c:c + pc], sg[:, 0:pc], reads=[skey], writes=[(nkey, t)])
            epi.pre = pre
            return epi

        def gu_stage(l):
            stg = [(Bbuf[:, 6, :].bitcast(BF16)[:, 0:512], "B6"), (Bbuf[:, 7, :].bitcast(BF16)[:, 0:512], "B7")]
            n = 0
            for p in range(DFF // 512):
                for wi, c in ((0, p * 512), (1, DFF + p * 512)):
                    TR.dma("pool", f"w{wi}", wbuf[wi][:, :, :], w_gu[l][0:D, c:c + 512].rearrange("(k p) n -> p k n", p=128), writes=[f"wb{wi}"])
                for t in tiles:
                    G, Gk = fbank()
                    U, Uk = fbank()
                    for ps_, pk_, wi in ((G, Gk, 0), (U, Uk, 1)):
                        for k in range(16):
                            TR.pe(lambda e, k=k: e.matmul(ps_[:, :], lhsT=actT[:, k, t * 128:(t + 1) * 128], rhs=wbuf[wi][:, k, :],
                                                          start=(k == 0), stop=(k == 15)),
                                  reads=[("actT", t), f"wb{wi}"], writes=[pk_], mark=(k == 15))
                    sg, skey = stage("sto", [128, 512], F32, 2)
                    ab, abk = stg[n % 2]; n += 1
                    TR.op("act", lambda e: e.activation(out=sg[:], in_=G[:, :], func=AF.Sigmoid), reads=[Gk], writes=[skey])
                    TR.op("dve", lambda e: e.tensor_tensor(out=sg[:], in0=G[:, :], in1=sg[:], op=ALU.mult), reads=[Gk, skey], writes=[skey])
                    TR.op("dve", lambda e: e.tensor_tensor(out=ab, in0=U[:, :], in1=sg[:], op=ALU.mult), reads=[Uk, skey], writes=[abk])
                    TR.dma("sp", "st" + abk, act_scr[t * 128:(t + 1) * 128, p * 512:(p + 1) * 512], ab, reads=[abk], writes=[("act", t)])

        xcur, xkey = xin, "xin"
        tiles = list(range(NT))
        for l in range(DEPTH):
            norm_stage(l, xcur, xkey, n1g, l, 0, D)
            linear(16, tiles, w_in[l], 0, OQ, OOG - OQ, store_epi(p_scr, "p"))
            linear(16, tiles, w_in[l], 0, OALR, 16, store_epi(p_scr, "p"))
            fill_state["gen"] = linear_gen(16, tiles, w_in[l], 0, OOG, OALR - OOG, store_epi(p_scr, "p2"))
            gla_stage(l)
            fillf(10 ** 6)
            if l == 0:
                bg_state["gen"] = chain_gens(ada_bg(0, 3 * D, 3 * D), ada_bg(1, 0, 6 * D))
            exchange_stage(l)
            correct_stage(l)
            merge_stage(l)
            bgf(10 ** 6)
            x1, x1key = xa, "xa"
            linear(16, tiles, w_o[l], 0, 0, D, resid_epi(l, 2 * D, xcur, xkey, x1, x1key))
            norm_stage(l, x1, x1key, n2g, l, 3 * D, 4 * D)
            gu_stage(l)
            prev, prevkey = x1, x1key
            jobs = [(k0, kc, t) for (k0, kc) in ((0, 16), (16, 16), (32, 12)) for t in tiles]
            slots = [(Bbuf[:, i, :].bitcast(BF16), f"B{i}") for i in range(4)]

            def act_load(ji):
                k0_, kc_, t_ = jobs[ji]
                buf, bkey_ = slots[ji % 4]
                TR.dma("sp", "la" + bkey_, buf[:, 0:kc_ * 128], act_scr[t_ * 128:(t_ + 1) * 128, k0_ * 128:(k0_ + kc_) * 128],
                       reads=[("act", t_)], writes=[bkey_])

            act_load(0)
            ji = 0
            for gi, (k0, kc) in enumerate(((0, 16), (16, 16), (32, 12))):
                for t in tiles:
                    if ji + 1 < len(jobs):
                        act_load(ji + 1)
                    buf, bkey_ = slots[ji % 4]
                    to_feat(buf, bkey_, kc, t)
                    ji += 1
                nxt, nkey = ((xb, "xb"), (xc, "xc"), (xb, "xb"))[gi]
                linear(kc, tiles, w_down[l], k0 * 128, 0, D, resid_epi(l, 5 * D, prev, prevkey, nxt, nkey))
                if gi > 0:
                    pass
                prev, prevkey = nxt, nkey
            xcur, xkey = prev, prevkey

        TR.dma("sp", "ld0", rowA, fng[0:1, :].partition_broadcast(128), writes=RA)
        for t in tiles:
            TR.dma("sp", "ldx", xt, xcur[t * 128:(t + 1) * 128, :], reads=[(xkey, t)], writes=XT)
            rstd_of(xt, "B0", D, 0)
            TR.op("dve", lambda e: e.scalar_tensor_tensor(out=ht, in0=xt, scalar=rs[:, 0:1], in1=rowA, op0=ALU.mult, op1=ALU.mult),
                  reads=XT + ["rs"] + RA, writes=HT)
            TR.dma("sp", "sty", y[t * 128:(t + 1) * 128, :], ht, reads=HT, writes=[("y", t)])
        for d in TR.dsem.values():
            TR.wait("sp", (d[0], d[1]))
    return nc


def _consts(first_half):
    s = np.arange(128)[:, None]; c = np.arange(128)[None, :]
    m = np.zeros((NMAT, 128, 128), np.float32)
    m[0] = np.eye(128)
    m[1] = (s <= c)
    m[2] = (s > c)
    same = (s // 8) == (c // 8)
    m[3] = (s <= c) & same
    m[4] = (s > c) & same
    for g, w in enumerate(WINS):
        b = 5 + g * 7
        m[b + 0] = ((s <= c) & (s > c - w)) / w - (s == c)
        m[b + 1] = ((s - 128) > (c - w)) / w
        cntc = np.minimum(c + 1, w)
        if first_half:
            m[b + 2] = ((s <= c) & (s > c - w)) / cntc - (s == c)
            m[b + 6] = 0.0
        else:
            m[b + 2] = m[b + 0]
            m[b + 6] = m[b + 1]
        m[b + 3] = ((s <= c) & (s > c - w) & same) / w - (s == c)
        for hf in range(2):
            hr = np.arange(128)[:, None]
            jj = hr // 15 + hf * 8; rr = hr % 15
            pos_h = rr - 15
            ci = c % 8; cj = c // 8
            m[b + 4 + hf] = ((hr < 120) & (jj == cj) & (pos_h > ci - w)) / w
    cm = np.zeros((128, 16, 128), np.float32)
    for j in range(16):
        cm[:, j, 8 * j:8 * j + 8] = 1.0
    rm = np.zeros((128, 16), np.float32)
    for j in range(16):
        rm[8 * j:8 * j + 8, j] = 1.0
    return m, cm.reshape(128, 2048), rm


_NC = None


def kernel(x_prompt, x_sample, state_gla, state_pool, c_prompt, c_sample, w_ada, b_ada, norm1_g, w_in, w_a2, b_a,
           gla_norm_g, w_pool, pool_scale, w_o, norm2_g, w_gu, w_down, final_norm_g):
    global _NC
    f = lambda a: np.ascontiguousarray(np.asarray(a, dtype=np.float32))
    x_prompt, x_sample, state_gla, state_pool, c_prompt, c_sample = map(f, (x_prompt, x_sample, state_gla, state_pool, c_prompt, c_sample))
    shared = {
        "w_ada": f(w_ada), "b_ada": f(b_ada), "norm1_g": f(norm1_g), "w_in": f(w_in), "w_a2": f(w_a2), "b_a": f(b_a),
        "gla_norm_g": f(gla_norm_g), "w_pool": f(w_pool).reshape(DEPTH, 1024, 512), "pool_scale": f(pool_scale), "w_o": f(w_o),
        "norm2_g": f(norm2_g), "w_gu": f(w_gu), "w_down": f(w_down), "final_norm_g": f(final_norm_g).reshape(1, D),
    }
    cst = [_consts(True), _consts(False)]
    in_maps = []
    for c in range(NCORE):
        b, half = c // 2, c % 2
        sl = slice(SPC * c, SPC * (c + 1))
        m = dict(shared)
        m["cmats"], m["cmk"], m["rmk"] = cst[half][0], cst[half][1], cst[half][2]
        m["flag"] = np.full((128, 1), float(half), np.float32)
        m["xin"] = np.concatenate([x_prompt[b, half * SEQ:(half + 1) * SEQ], x_sample[sl].reshape(SPC * 8, D)], axis=0)
        m["crow"] = np.concatenate([np.repeat(c_prompt[b:b + 1], 128, axis=0), np.repeat(c_sample[sl], 8, axis=0)], axis=0)
        m["sgla"] = np.ascontiguousarray(state_gla[:, sl])
        m["spool"] = np.ascontiguousarray(state_pool[:, sl]).reshape(DEPTH, SPC * 15, 1024)
        in_maps.append(m)
    if _NC is None:
        _NC = build_program()
    res = run_bass_kernel_spmd(_NC, in_maps, core_ids=list(range(NCORE)))
    R = res.results
    y_prompt = np.stack([np.concatenate([R[2 * b]["y"][:SEQ], R[2 * b + 1]["y"][:SEQ]], axis=0) for b in range(4)])
    y_sample = np.concatenate([R[c]["y"][SEQ:].reshape(SPC, 8, D) for c in range(NCORE)], axis=0)
    gla_pp = np.stack([R[2 * b + 1]["gla_p"] for b in range(4)], axis=1)
    pool_pp = np.stack([R[2 * b + 1]["pool_p"] for b in range(4)], axis=1)
    gla_ss = np.concatenate([R[c]["gla_s"] for c in range(NCORE)], axis=1)
    pool_ss = np.concatenate([R[c]["pool_s"] for c in range(NCORE)], axis=1)
    return (y_prompt.astype(np.float32), y_sample.astype(np.float32), gla_pp.astype(np.float32), pool_pp.astype(np.float32),
            gla_ss.astype(np.float32), pool_ss.astype(np.float32))
```

```python
import contextlib
import numpy as np
import concourse.bass as bass
import concourse.mybir as mybir
from concourse.bass_utils import run_bass_kernel_spmd

F32 = mybir.dt.float32
BF16 = mybir.dt.bfloat16
ALU = mybir.AluOpType
AF = mybir.ActivationFunctionType

D = 2048
SEQ = 1024
NCORE = 8
SPC = 16
T = SEQ + SPC * 8
NT = T // 128
NPT = SEQ // 128
NMAT = 5 + 28
NIN = 11280
DFF = 5632
DEPTH = 2
OQ, OK_, OV, OOG, OU, OGA, OGB, OALR = 0, 1024, 2048, 4096, 6144, 7168, 9216, 11264
WINS = (2, 4, 8, 16)
EPS = 1e-6


class Tracker:
    SEM_CAP = 30000

    def __init__(self, nc, stack, n_sems):
        self.nc = nc
        self.engs = {"pe": nc.tensor, "act": nc.scalar, "dve": nc.vector, "pool": nc.gpsimd, "sp": nc.sync}
        self.free_sems = [stack.enter_context(nc.semaphore(f"s{i}")) for i in range(n_sems)]
        self.cur = {}
        self.seen = {e: {} for e in self.engs}
        self.lastw = {}
        self.readers = {}
        self.dsem = {}
        self._pend = []

    def wait(self, en, tok):
        if tok is None:
            return
        sem, val = tok
        sid = id(sem)
        if self.seen[en].get(sid, 0) >= val:
            return
        self.engs[en].wait_ge(sem, val)
        self.seen[en][sid] = val

    def _deps(self, en, reads, writes):
        for k in reads:
            self.wait(en, self.lastw.get(k))
        for k in writes:
            self.wait(en, self.lastw.get(k))
            for r in self.readers.get(k, ()):
                self.wait(en, r)

    def _commit(self, tok, reads, writes):
        for k in reads:
            self.readers.setdefault(k, []).append(tok)
        for k in writes:
            self.lastw[k] = tok
            self.readers[k] = []

    def op(self, en, fn, reads=(), writes=()):
        self._deps(en, reads, writes)
        ins = fn(self.engs[en])
        c = self.cur.get(en)
        if c is None or c[1] >= self.SEM_CAP:
            c = [self.free_sems.pop(), 0]
            self.cur[en] = c
        c[1] += 1
        ins.then_inc(c[0], 1)
        tok = (c[0], c[1])
        self._commit(tok, reads, writes)
        return tok

    def pe(self, fn, reads=(), writes=(), mark=True):
        if not mark:
            self._deps("pe", reads, writes)
            fn(self.engs["pe"])
            self._pend.append((tuple(reads), tuple(writes)))
            return None
        tok = self.op("pe", fn, reads, writes)
        for r, w in self._pend:
            self._commit(tok, r, w)
        self._pend = []
        return tok

    def collective(self, fn, reads=(), writes=()):
        self._deps("pool", reads, writes)
        ins = fn(self.engs["pool"])
        sem = self.free_sems.pop()
        ins.then_inc(sem)
        tok = (sem, 1)
        self._commit(tok, reads, writes)
        return tok

    def dma(self, q, semkey, out, in_, reads=(), writes=()):
        d = self.dsem.get(semkey)
        if d is None:
            d = [self.free_sems.pop(), 0]
            self.dsem[semkey] = d
        if d[1] > 0:
            self.wait(q, (d[0], d[1]))
        self._deps(q, reads, writes)
        ins = self.engs[q].dma_start(out=out, in_=in_)
        d[1] += 16
        ins.then_inc(d[0], 16)
        tok = (d[0], d[1])
        self._commit(tok, reads, writes)
        return tok


def build_program():
    nc = bass.Bass("TRN2", target_bir_lowering=False)
    din = lambda n, s: nc.dram_tensor(n, s, F32, kind="ExternalInput").ap()
    dout = lambda n, s: nc.dram_tensor(n, s, F32, kind="ExternalOutput").ap()
    xin = din("xin", [T, D]); crow = din("crow", [256, D])
    sgla = din("sgla", [DEPTH, SPC, 4, 256, 512]); spool = din("spool", [DEPTH, SPC * 15, 1024])
    w_ada = din("w_ada", [DEPTH, D, 6 * D]); b_ada = din("b_ada", [DEPTH, 6 * D]); n1g = din("norm1_g", [DEPTH, D])
    w_in = din("w_in", [DEPTH, D, NIN]); w_a2 = din("w_a2", [DEPTH, 16, 1024]); b_a = din("b_a", [DEPTH, 1024])
    ggla = din("gla_norm_g", [DEPTH, 512]); w_pool = din("w_pool", [DEPTH, 1024, 512]); pscale = din("pool_scale", [DEPTH, D])
    w_o = din("w_o", [DEPTH, D, D]); n2g = din("norm2_g", [DEPTH, D]); w_gu = din("w_gu", [DEPTH, D, 2 * DFF])
    w_down = din("w_down", [DEPTH, DFF, D]); fng = din("final_norm_g", [1, D])
    cmats = din("cmats", [NMAT, 128, 128]); flag_d = din("flag", [128, 1]); cmk = din("cmk", [128, 16 * 128]); rmk = din("rmk", [128, 16])
    y = dout("y", [T, D]); gla_p = dout("gla_p", [DEPTH, 4, 256, 512]); pool_p = dout("pool_p", [DEPTH, 15, 1024])
    gla_s = dout("gla_s", [DEPTH, SPC, 4, 256, 512]); pool_s = dout("pool_s", [DEPTH, SPC, 15, 1024])
    scr = lambda n, s: nc.dram_tensor(n, s, F32).ap()
    p_scr = scr("p_scr", [T, NIN]); gu_scr = scr("gu_scr", [T, 2 * DFF]); o_scr = scr("o_scr", [T, D])
    qg_scr = nc.dram_tensor("qg_scr", [4, NPT, 128, 256], BF16).ap()
    ex_in_t = [nc.dram_tensor(f"ex_in{l}", [512, 1024], F32) for l in range(DEPTH)]
    ex_out_t = [nc.dram_tensor(f"ex_out{l}", [1024, 1024], F32) for l in range(DEPTH)]
    exu_in_t = [nc.dram_tensor(f"exu_in{l}", [128, 1024], F32) for l in range(DEPTH)]
    exu_out_t = [nc.dram_tensor(f"exu_out{l}", [256, 1024], F32) for l in range(DEPTH)]
    ex_in = [t_.ap() for t_ in ex_in_t]; ex_out = [t_.ap() for t_ in ex_out_t]
    exu_in = [t_.ap() for t_ in exu_in_t]; exu_out = [t_.ap() for t_ in exu_out_t]
    xa = scr("xa", [T, D]); xb = scr("xb", [T, D]); xc = scr("xc", [T, D]); mod_scr = scr("mod_scr", [DEPTH, 256, 6 * D])

    with contextlib.ExitStack() as st:
        TR = Tracker(nc, st, 100)
        cnt = [0]

        def sb(shape, dt, name=None):
            cnt[0] += 1
            return st.enter_context(nc.sbuf_tensor(name or f"t{cnt[0]}", shape, dt))

        actT = sb([128, 16, T], BF16, "actT")
        NW = 2
        wbuf = [sb([128, 16, 512], BF16, f"wb{i}") for i in range(NW)]
        psf = [st.enter_context(nc.psum_tensor(f"psf{i}", [128, 512], F32)) for i in range(6)]
        psb = [st.enter_context(nc.psum_tensor(f"psb{i}", [128, 1024], BF16)) for i in range(2)]
        rot = {"f": 0, "b": 0, "w": 0}

        held = set()

        def fbank():
            while True:
                i = rot["f"] % 6; rot["f"] += 1
                if i not in held:
                    return psf[i], f"psf{i}"

        def bbank():
            i = rot["b"] % 2; rot["b"] += 1
            return psb[i], f"psb{i}"

        pools = {}

        def stage(tag, shape, dt, n=2):
            if tag not in pools:
                pools[tag] = [[sb(shape, dt, f"{tag}{i}") for i in range(n)], 0]
            p = pools[tag]
            i = p[1] % n; p[1] += 1
            return p[0][i], f"{tag}{i}"

        st.enter_context(nc.Block())

        Bbuf = sb([128, 8, 1024], F32, "Bbuf")
        BK = [f"B{i}" for i in range(8)]
        cm_f = Bbuf[:, 0:2, :].rearrange("p a c -> p (a c)")
        hb = sb([128, D], BF16, "hb")
        mb = hb
        idb = sb([128, 128], BF16, "idb")
        cmat_b = sb([128, NMAT, 128], BF16, "cmat_b")
        flag = sb([128, 1], F32, "flag_sb")
        Gst = sb([128, 2], F32, "Gst"); Gtot = sb([128, 4, 2], F32, "Gtot")
        cm_b = sb([128, 16, 128], BF16, "cm_b")
        rm_b = sb([128, 16], BF16, "rm_b")
        ones_b = sb([128, 128], BF16, "ones_b")
        rs = sb([128, 8], F32, "rs")
        junk = hb
        TR.dma("sp", "cl", flag[:], flag_d[:, :], writes=["flag"])
        for i in range(NMAT):
            tmpc, kc_ = stage("cld", [128, 128], F32)
            TR.dma("sp", "cl", tmpc[:], cmats[i], writes=[kc_])
            TR.op("dve", lambda e, i=i, tmpc=tmpc: e.tensor_copy(out=cmat_b[:, i, :], in_=tmpc[:]), reads=[kc_], writes=["cmat"])
        TR.op("dve", lambda e: e.tensor_copy(out=idb[:], in_=cmat_b[:, 0, :]), reads=["cmat"], writes=["idb"])
        TR.dma("sp", "cl", cm_f, cmk[:, :], writes=["B0", "B1"])
        TR.op("dve", lambda e: e.tensor_copy(out=cm_b[:].rearrange("p j c -> p (j c)"), in_=cm_f), reads=["B0", "B1"], writes=["cm_b"])
        TR.dma("sp", "cl", cm_f[:, 0:16], rmk[:, :], reads=[], writes=["B0", "B1"])
        TR.op("dve", lambda e: e.tensor_copy(out=rm_b[:], in_=cm_f[:, 0:16]), reads=["B0", "B1"], writes=["rm_b"])
        TR.op("dve", lambda e: e.memset(ones_b[:], 1.0), writes=["ones_b"])
        MINCL = {0: 1, 1: 3}
        MREV = {0: 2, 1: 4}

        def to_feat(src_b, src_key, kc, t, dst=None, dst_key="actT"):
            dst = actT if dst is None else dst
            for k0 in range(0, kc, 8):
                kn = min(8, kc - k0)
                bk, bkey = bbank()
                for k in range(kn):
                    TR.pe(lambda e, k=k: e.transpose(bk[:, k * 128:(k + 1) * 128], src_b[:, (k0 + k) * 128:(k0 + k + 1) * 128], idb[:]),
                          reads=[src_key, "idb"], writes=[bkey], mark=(k == kn - 1))
                TR.op("act", lambda e: e.copy(out=dst[:, k0:k0 + kn, t * 128:(t + 1) * 128],
                                              in_=bk[:, 0:kn * 128].rearrange("p (k c) -> p k c", c=128)),
                      reads=[bkey], writes=[(dst_key, t)])

        def linear(*a, **k):
            for _ in linear_gen(*a, **k):
                pass

        fill_state = {"gen": None}

        def fillf(n=1):
            g = fill_state["gen"]
            for _ in range(n):
                if g is None:
                    return
                try:
                    next(g)
                except StopIteration:
                    fill_state["gen"] = None
                    return

        bg_state = {"gen": None, "on": False}

        def chain_gens(*gens):
            for g_ in gens:
                yield from g_

        def bgf(n=1):
            g = bg_state["gen"]
            for _ in range(n):
                if g is None:
                    return
                try:
                    next(g)
                except StopIteration:
                    bg_state["gen"] = None
                    return

        def linear_gen(kc, tiles, W, r0, c0, ncols, epi, pw=512, lhs=None, lhs_key="actT", own_w=None, bg=False):
            groups = [(t, c, min(pw, c0 + ncols - c)) for c in range(c0, c0 + ncols, pw) for t in tiles]
            pre = getattr(epi, "pre", None)
            if pre:
                pre(*groups[0])
            wb = wkey = None
            lhs = actT if lhs is None else lhs
            for gi, (t, c, pc) in enumerate(groups):
                if bg:
                    bgf(1)
                if t == tiles[0]:
                    if own_w is not None:
                        wb, wkey, wi = own_w, "wown", "own"
                    else:
                        wi = rot["w"] % NW; rot["w"] += 1
                        wb, wkey = wbuf[wi], f"wb{wi}"
                    TR.dma("pool", f"w{wi}", wb[:, 0:kc, 0:pc], W[r0:r0 + kc * 128, c:c + pc].rearrange("(k p) n -> p k n", p=128), writes=[wkey])
                ps, pkey = fbank()
                for k in range(kc):
                    TR.pe(lambda e, k=k: e.matmul(ps[:, 0:pc], lhsT=lhs[:, k, t * 128:(t + 1) * 128], rhs=wb[:, k, 0:pc],
                                                  start=(k == 0), stop=(k == kc - 1)),
                          reads=[(lhs_key, t), wkey], writes=[pkey], mark=(k == kc - 1))
                if pre and gi + 1 < len(groups):
                    pre(*groups[gi + 1])
                epi(t, c, pc, ps, pkey)
                yield

        def store_epi(dst, dkey):
            def epi(t, c, pc, ps, pkey):
                sg, skey = stage("sto", [128, 512], F32, 2)
                TR.op("act", lambda e: e.copy(out=sg[:, 0:pc], in_=ps[:, 0:pc]), reads=[pkey], writes=[skey])
                TR.dma("sp", "st" + skey, dst[t * 128:(t + 1) * 128, c:c + pc], sg[:, 0:pc], reads=[skey], writes=[(dkey, t)])
            return epi

        def rstd_of(src, skey, n, col):
            TR.op("dve", lambda e: e.memset(rs[:, col + 1:col + 2], 0.0), writes=["rs"])
            TR.op("act", lambda e: e.activation(out=junk[:, 0:n], in_=src, func=AF.Square, accum_out=rs[:, col + 1:col + 2]),
                  reads=[skey, "rs"], writes=["hb", "rs"])
            TR.op("act", lambda e: e.activation(out=rs[:, col + 1:col + 2], in_=rs[:, col + 1:col + 2], func=AF.Ln, scale=1.0 / n, bias=EPS),
                  reads=["rs"], writes=["rs"])
            TR.op("act", lambda e: e.activation(out=rs[:, col:col + 1], in_=rs[:, col + 1:col + 2], func=AF.Exp, scale=-0.5),
                  reads=["rs"], writes=["rs"])

        xt = Bbuf[:, 0:2, :].rearrange("p a c -> p (a c)")
        ht = Bbuf[:, 2:4, :].rearrange("p a c -> p (a c)")
        rowA = Bbuf[:, 4:6, :].rearrange("p a c -> p (a c)")
        rowB = Bbuf[:, 6:8, :].rearrange("p a c -> p (a c)")
        XT, HT, RA, RB = ["B0", "B1"], ["B2", "B3"], ["B4", "B5"], ["B6", "B7"]
        gT = sb([128, 16], F32, "gT"); scT = sb([128, 16], F32, "scT"); shT = sb([128, 16], F32, "shT")

        def norm_stage(l, xsrc, xkey, gain, gl, c_sh, c_sc):
            with nc.allow_non_contiguous_dma(reason="tiny per-feature vectors"):
                TR.dma("sp", "ld0", gT[:], gain[gl, :].rearrange("(k p) -> p k", p=128), writes=["gT"])
                TR.dma("sp", "ld1", scT[:], mod_scr[l, 0, c_sc:c_sc + D].rearrange("(k p) -> p k", p=128), reads=[("mod", l, c_sc // (3 * D))], writes=["scT"])
                TR.dma("sp", "ld2", shT[:], mod_scr[l, 0, c_sh:c_sh + D].rearrange("(k p) -> p k", p=128), reads=[("mod", l, c_sh // (3 * D))], writes=["shT"])
            TR.op("dve", lambda e: e.scalar_tensor_tensor(out=scT[:], in0=scT[:], scalar=1.0, in1=gT[:], op0=ALU.add, op1=ALU.mult),
                  reads=["scT", "gT"], writes=["scT"])
            xalt = Bbuf[:, 4:6, :].rearrange("p a c -> p (a c)")

            def xbuf(t_):
                return (xt, XT) if (t_ % 2 == 0 or t_ == NT - 1) else (xalt, RA)

            def xload(t_):
                xb_, xk_ = xbuf(t_)
                TR.dma("sp", "ldx" + str(t_ % 2), xb_, xsrc[t_ * 128:(t_ + 1) * 128, :], reads=[(xkey, t_)], writes=xk_)

            xload(0)
            for t in range(NT):
                if t + 1 < NT:
                    xload(t + 1)
                xt_, XT_ = xbuf(t)
                rstd_of(xt_, XT_[0], D, 0)
                if t < NT - 1:
                    TR.op("dve", lambda e: e.tensor_scalar(out=hb[:], in0=xt_, scalar1=rs[:, 0:1], scalar2=None, op0=ALU.mult),
                          reads=XT_ + ["rs"], writes=["hb"])
                    for k0 in range(0, 16, 8):
                        bk, bkey = bbank()
                        for k in range(8):
                            TR.pe(lambda e, k=k: e.transpose(bk[:, k * 128:(k + 1) * 128], hb[:, (k0 + k) * 128:(k0 + k + 1) * 128], idb[:]),
                                  reads=["hb", "idb"], writes=[bkey], mark=(k == 7))
                        for k in range(8):
                            TR.op("act", lambda e, k=k: e.activation(out=actT[:, k0 + k, t * 128:(t + 1) * 128], in_=bk[:, k * 128:(k + 1) * 128],
                                                                     func=AF.Identity, scale=scT[:, k0 + k:k0 + k + 1], bias=shT[:, k0 + k:k0 + k + 1]),
                                  reads=[bkey, "scT", "shT"], writes=[("actT", t)])
                else:
                    TR.dma("sp", "ld0", ht, gain[gl:gl + 1, :].partition_broadcast(128), writes=HT)
                    TR.dma("sp", "ld1", rowA, mod_scr[l, 128:256, c_sc:c_sc + D], reads=[("mod", l, c_sc // (3 * D))], writes=RA)
                    TR.dma("sp", "ld2", rowB, mod_scr[l, 128:256, c_sh:c_sh + D], reads=[("mod", l, c_sh // (3 * D))], writes=RB)
                    TR.op("dve", lambda e: e.scalar_tensor_tensor(out=rowA, in0=rowA, scalar=1.0, in1=ht, op0=ALU.add, op1=ALU.mult),
                          reads=RA + HT, writes=RA)
                    TR.op("dve", lambda e: e.scalar_tensor_tensor(out=ht, in0=xt, scalar=rs[:, 0:1], in1=rowA, op0=ALU.mult, op1=ALU.mult),
                          reads=XT + ["rs"] + RA, writes=HT)
                    TR.op("dve", lambda e: e.tensor_tensor(out=hb[:], in0=ht, in1=rowB, op=ALU.add), reads=HT + RB, writes=["hb"])
                    to_feat(hb, "hb", 16, t)

        brow = sb([128, 512], F32, "brow")
        csT = sb([128, 16, 256], BF16, "csT")
        for t in range(2):
            TR.dma("sp", "ldx", xt, crow[t * 128:(t + 1) * 128, :], writes=XT)
            TR.op("act", lambda e: e.activation(out=ht, in_=xt, func=AF.Sigmoid), reads=XT, writes=HT)
            TR.op("dve", lambda e: e.tensor_tensor(out=hb[:], in0=ht, in1=xt, op=ALU.mult), reads=HT + XT, writes=["hb"])
            to_feat(hb, "hb", 16, t, dst=csT, dst_key="csT")

        def epi_mod(l):
            def epi(t, c, pc, ps, pkey):
                if t == 0:
                    TR.dma("sp", "ldb", brow[:, 0:pc], b_ada[l:l + 1, c:c + pc].partition_broadcast(128), writes=["brow"])
                sg, skey = stage("sto", [128, 512], F32, 2)
                TR.op("dve", lambda e: e.tensor_tensor(out=sg[:, 0:pc], in0=ps[:, 0:pc], in1=brow[:, 0:pc], op=ALU.add),
                      reads=[pkey, "brow"], writes=[skey])
                TR.dma("sp", "st" + skey, mod_scr[l, t * 128:(t + 1) * 128, c:c + pc], sg[:, 0:pc], reads=[skey], writes=[("mod", l, c // (3 * D))])
            return epi

        linear(16, [0, 1], w_ada[0], 0, 0, 3 * D, epi_mod(0), lhs=csT, lhs_key="csT")

        def ada_bg(l, c0, ncols):
            return linear_gen(16, [0, 1], w_ada[l], 0, c0, ncols, epi_mod(l), lhs=csT, lhs_key="csT")

        qkvs = [sb([128, 1024], F32, f"qkv{i}") for i in range(2)]
        alrs = [sb([128, 16], F32, f"alr_f{i}") for i in range(2)]
        qkv = qkvs[0]
        qb = sb([128, 256], BF16, "qb"); kb = sb([128, 256], BF16, "kb"); vb = sb([128, 512], BF16, "vb"); ab = sb([128, 16], BF16, "ab")
        alrT = sb([17, 128], BF16, "alrT")
        wa2f = Bbuf[0:17, 0, :]; wa2b = sb([17, 1024], BF16, "wa2b")
        TR.op("dve", lambda e: e.memset(alrT[:], 1.0), writes=["alrT"])
        ef = sb([128, 256], F32, "ef"); lgb = sb([128, 256], BF16, "lgb")
        E1 = sb([128, 256], F32, "E1"); E2 = sb([128, 256], F32, "E2"); E3 = sb([128, 256], F32, "E3")
        qe = sb([128, 2, 128], BF16, "qe"); ke = sb([128, 2, 128], BF16, "ke"); kd = sb([128, 256], BF16, "kd")
        attm = sb([128, 128], BF16, "attm")
        Sf = sb([128, 2, 512], F32, "Sf"); Sb = sb([128, 2, 512], BF16, "Sb")

        onf = sb([128, 512], F32, "onf")
        ggrow = sb([128, 512], F32, "ggrow")

        def gla_stage(l):
            TR.dma("sp", "ld0", Bbuf[0:16, 0, :], w_a2[l], writes=["B0"])
            TR.dma("sp", "ld1", Bbuf[16:17, 0, :], b_a[l:l + 1, :], writes=["B0"])
            TR.op("dve", lambda e: e.tensor_copy(out=wa2b[:], in_=wa2f), reads=["B0"], writes=["wa2b"])
            TR.dma("sp", "ld2", ggrow[:], ggla[l:l + 1, :].partition_broadcast(128), writes=["ggrow"])
            order = [(h_, t_) for h_ in range(4) for t_ in range(NT)]

            def gla_loads(i):
                h_, t_ = order[i]
                r_ = slice(t_ * 128, (t_ + 1) * 128)
                q_, a_, sfx = qkvs[i % 2], alrs[i % 2], str(i % 2)
                TR.dma("sp", "lq0" + sfx, q_[:, 0:256], p_scr[r_, OQ + h_ * 256:OQ + (h_ + 1) * 256], reads=[("p", t_)], writes=["qkv" + sfx])
                TR.dma("sp", "lq1" + sfx, q_[:, 256:512], p_scr[r_, OK_ + h_ * 256:OK_ + (h_ + 1) * 256], reads=[("p", t_)], writes=["qkv" + sfx])
                TR.dma("sp", "lq2" + sfx, q_[:, 512:1024], p_scr[r_, OV + h_ * 512:OV + (h_ + 1) * 512], reads=[("p", t_)], writes=["qkv" + sfx])
                TR.dma("sp", "lq3" + sfx, a_[:], p_scr[r_, OALR:OALR + 16], reads=[("p", t_)], writes=["alr_f" + sfx])

            gla_loads(0)
            for h in range(4):
                TR.op("dve", lambda e: e.memset(Sf[:], 0.0), writes=["Sf"])
                TR.op("dve", lambda e: e.memset(Sb[:], 0.0), writes=["Sb"])
                TR.op("dve", lambda e: e.memset(Gst[:], 1.0), writes=["Gst"])
                for t in range(NT):
                    ty = 1 if t == NT - 1 else 0
                    r = slice(t * 128, (t + 1) * 128)
                    gi_ = h * NT + t
                    if gi_ + 1 < len(order):
                        gla_loads(gi_ + 1)
                    qkv_, alr_, sfx = qkvs[gi_ % 2], alrs[gi_ % 2], str(gi_ % 2)
                    TR.op("dve", lambda e: e.tensor_scalar(out=qb[:], in0=qkv_[:, 0:256], scalar1=0.0625, scalar2=None, op0=ALU.mult),
                          reads=["qkv" + sfx], writes=["qb"])
                    TR.op("dve", lambda e: e.tensor_copy(out=kb[:], in_=qkv_[:, 256:512]), reads=["qkv" + sfx], writes=["kb"])
                    TR.op("dve", lambda e: e.tensor_copy(out=vb[:], in_=qkv_[:, 512:1024]), reads=["qkv" + sfx], writes=["vb"])
                    TR.op("dve", lambda e: e.tensor_copy(out=ab[:], in_=alr_[:]), reads=["alr_f" + sfx], writes=["ab"])
                    bk, bkey = bbank()
                    for ch in range(2):
                        TR.pe(lambda e, ch=ch: e.transpose(bk[:, ch * 128:(ch + 1) * 128], qb[:, ch * 128:(ch + 1) * 128], idb[:]),
                              reads=["qb", "idb"], writes=[bkey], mark=False)
                        TR.pe(lambda e, ch=ch: e.transpose(bk[:, 256 + ch * 128:256 + (ch + 1) * 128], kb[:, ch * 128:(ch + 1) * 128], idb[:]),
                              reads=["kb", "idb"], writes=[bkey], mark=False)
                    TR.pe(lambda e: e.transpose(bk[0:16, 512:640], ab[:, 0:16], idb[:]), reads=["ab", "idb"], writes=[bkey])
                    TR.op("act", lambda e: e.copy(out=alrT[0:16, :], in_=bk[0:16, 512:640]), reads=[bkey], writes=["alrT"])
                    fillf()
                    g, gkey = fbank()
                    TR.pe(lambda e: e.matmul(g[:, 0:256], lhsT=alrT[0:17, :], rhs=wa2b[0:17, h * 256:(h + 1) * 256], start=True, stop=True),
                          reads=["alrT", "wa2b"], writes=[gkey])
                    TR.op("act", lambda e: e.activation(out=ef[:], in_=g[:, 0:256], func=AF.Exp, scale=-1.0), reads=[gkey], writes=["ef"])
                    TR.op("act", lambda e: e.activation(out=ef[:], in_=ef[:], func=AF.Ln, bias=1.0), reads=["ef"], writes=["ef"])
                    TR.op("dve", lambda e: e.tensor_scalar(out=lgb[:], in0=ef[:], scalar1=-1.0 / 16, scalar2=None, op0=ALU.mult),
                          reads=["ef"], writes=["lgb"])
                    fillf()
                    B, Bkey = fbank()
                    for ch in range(2):
                        TR.pe(lambda e, ch=ch: e.matmul(B[:, ch * 128:(ch + 1) * 128], lhsT=lgb[:, ch * 128:(ch + 1) * 128],
                                                        rhs=cmat_b[:, MINCL[ty], :], start=True, stop=True),
                              reads=["lgb", "cmat"], writes=[Bkey], mark=False)
                    TR.pe(lambda e: e.matmul(B[:, 256:512], lhsT=cmat_b[:, MREV[ty], :], rhs=lgb[:], start=True, stop=True),
                          reads=["lgb", "cmat"], writes=[Bkey])
                    TR.op("act", lambda e: e.activation(out=E1[:], in_=B[:, 0:256], func=AF.Exp), reads=[Bkey], writes=["E1"])
                    TR.op("act", lambda e: e.activation(out=E2[:], in_=B[:, 0:256], func=AF.Exp, scale=-1.0), reads=[Bkey], writes=["E2"])
                    TR.op("act", lambda e: e.activation(out=E3[:], in_=B[:, 256:512], func=AF.Exp), reads=[Bkey], writes=["E3"])
                    TR.op("dve", lambda e: e.tensor_tensor(out=qe[:].rearrange("p c k -> p (c k)"), in0=bk[:, 0:256], in1=E1[:], op=ALU.mult),
                          reads=[bkey, "E1"], writes=["qe"])
                    TR.op("dve", lambda e: e.tensor_tensor(out=ke[:].rearrange("p c k -> p (c k)"), in0=bk[:, 256:512], in1=E2[:], op=ALU.mult),
                          reads=[bkey, "E2"], writes=["ke"])
                    TR.op("dve", lambda e: e.tensor_tensor(out=kd[:], in0=kb[:], in1=E3[:], op=ALU.mult), reads=["kb", "E3"], writes=["kd"])
                    if ty == 0:
                        qg, qgkey = stage("qg", [128, 2, 128], BF16, 2)
                        for ch in range(2):
                            TR.op("dve", lambda e, ch=ch: e.tensor_scalar(out=qg[:, ch, :], in0=qe[:, ch, :], scalar1=Gst[:, ch:ch + 1], scalar2=None, op0=ALU.mult),
                                  reads=["qe", "Gst"], writes=[qgkey])
                        TR.dma("sp", "s" + qgkey, qg_scr[h, t], qg[:].rearrange("p c k -> p (c k)"), reads=[qgkey], writes=[("qg", h, t)])
                    fillf()
                    A, Akey = fbank()
                    for ch in range(2):
                        TR.pe(lambda e, ch=ch: e.matmul(A[:, 0:128], lhsT=ke[:, ch, :], rhs=qe[:, ch, :], start=(ch == 0), stop=(ch == 1)),
                              reads=["ke", "qe"], writes=[Akey], mark=(ch == 1))
                    TR.op("dve", lambda e: e.tensor_tensor(out=attm[:], in0=A[:, 0:128], in1=cmat_b[:, MINCL[ty], :], op=ALU.mult),
                          reads=[Akey, "cmat"], writes=["attm"])
                    fillf()
                    O, Okey = fbank()
                    if ty == 0:
                        TR.pe(lambda e: e.matmul(O[:, :], lhsT=attm[:], rhs=vb[:], start=True, stop=False), reads=["attm", "vb"], writes=[Okey], mark=False)
                        for ch in range(2):
                            TR.pe(lambda e, ch=ch: e.matmul(O[:, :], lhsT=qe[:, ch, :], rhs=Sb[:, ch, :], start=False, stop=(ch == 1)),
                                  reads=["qe", "Sb"], writes=[Okey], mark=(ch == 1))
                        for ch in range(2):
                            U, Ukey = fbank()
                            TR.pe(lambda e, ch=ch: e.matmul(U[:, :], lhsT=kd[:, ch * 128:(ch + 1) * 128], rhs=vb[:], start=True, stop=True),
                                  reads=["kd", "vb"], writes=[Ukey])
                            TR.op("dve", lambda e, ch=ch: e.scalar_tensor_tensor(out=Sf[:, ch, :], in0=Sf[:, ch, :], scalar=E1[:, ch * 128 + 127:ch * 128 + 128],
                                                                                 in1=U[:, :], op0=ALU.mult, op1=ALU.add),
                                  reads=["Sf", "E1", Ukey], writes=["Sf"])
                        TR.op("act", lambda e: e.copy(out=Sb[:], in_=Sf[:]), reads=["Sf"], writes=["Sb"])
                        TR.op("dve", lambda e: e.tensor_tensor(out=Gst[:], in0=Gst[:], in1=E1[:].rearrange("p (c k) -> p c k", k=128)[:, :, 127], op=ALU.mult),
                              reads=["Gst", "E1"], writes=["Gst"])
                        if t == NPT - 1:
                            TR.dma("sp", "stS", exS(ex_in[l], h), Sf[:], reads=["Sf"], writes=[("exin", l)])
                            TR.op("dve", lambda e: e.tensor_copy(out=Gtot[:, h, :], in_=Gst[:]), reads=["Gst"], writes=["Gtot"])
                    else:
                        held.add(int(Okey[3:]))
                        TR.pe(lambda e: e.matmul(O[:, :], lhsT=attm[:], rhs=vb[:], start=True, stop=False), reads=["attm", "vb"], writes=[Okey], mark=False)
                        def s0_load(j_):
                            s0_, k_ = stage("s0f", [128, 2, 512], F32, 3)
                            for ch_ in range(2):
                                TR.dma("sp", f"l{k_}{ch_}", s0_[:, ch_, :], sgla[l, j_, h, ch_ * 128:(ch_ + 1) * 128, :], writes=[k_ + str(ch_)])
                            return s0_, k_

                        nxt_s0 = s0_load(0)
                        for j in range(SPC):
                            s0, s0key = nxt_s0
                            if j + 1 < SPC:
                                nxt_s0 = s0_load(j + 1)
                            s0b, s0bkey = stage("s0b", [128, 2, 512], BF16, 2)
                            qm, qmkey = stage("qm", [128, 2, 128], BF16, 2)
                            km, kmkey = stage("km", [128, 256], BF16, 2)
                            TR.op("act", lambda e: e.copy(out=s0b[:], in_=s0[:]), reads=[s0key + "0", s0key + "1"], writes=[s0bkey])
                            TR.op("dve", lambda e, j=j: e.tensor_tensor(out=qm[:], in0=qe[:], in1=cm_b[:, j, :].unsqueeze(1).to_broadcast([128, 2, 128]),
                                                                        op=ALU.mult), reads=["qe", "cm_b"], writes=[qmkey])
                            TR.op("dve", lambda e, j=j: e.tensor_scalar(out=km[:], in0=kd[:], scalar1=rm_b[:, j:j + 1], scalar2=None, op0=ALU.mult),
                                  reads=["kd", "rm_b"], writes=[kmkey])
                            for ch in range(2):
                                last = (j == SPC - 1 and ch == 1)
                                TR.pe(lambda e, ch=ch: e.matmul(O[:, :], lhsT=qm[:, ch, :], rhs=s0b[:, ch, :], start=False, stop=last),
                                      reads=[qmkey, s0bkey], writes=[Okey], mark=True)
                            for ch in range(2):
                                U, Ukey = fbank()
                                TR.pe(lambda e, ch=ch: e.matmul(U[:, :], lhsT=km[:, ch * 128:(ch + 1) * 128], rhs=vb[:], start=True, stop=True),
                                      reads=[kmkey, "vb"], writes=[Ukey])
                                cix = ch * 128 + 8 * j + 7
                                TR.op("dve", lambda e, ch=ch, cix=cix: e.scalar_tensor_tensor(out=s0[:, ch, :], in0=s0[:, ch, :], scalar=E1[:, cix:cix + 1],
                                                                                              in1=U[:, :], op0=ALU.mult, op1=ALU.add),
                                      reads=[s0key + str(ch), "E1", Ukey], writes=[s0key + str(ch)])
                            TR.dma("sp", "s" + s0key, gla_s[l, j, h].rearrange("(c p) e -> p c e", p=128), s0[:], reads=[s0key + "0", s0key + "1"],
                                   writes=[("glas", l, j, h)])
                    held.clear()
                    if ty == 0:
                        TR.op("act", lambda e: e.copy(out=onf[:], in_=O[:, :]), reads=[Okey], writes=["onf"])
                    else:
                        rstd_of(O[:, :], Okey, 512, 2)
                        TR.op("dve", lambda e: e.scalar_tensor_tensor(out=onf[:], in0=O[:, :], scalar=rs[:, 2:3], in1=ggrow[:], op0=ALU.mult, op1=ALU.mult),
                              reads=[Okey, "rs", "ggrow"], writes=["onf"])
                    TR.dma("sp", "sto_o", o_scr[r, h * 512:(h + 1) * 512], onf[:], reads=["onf"], writes=[("o", t)])

        def exS(buf, h, blk=0):
            return buf[blk * 512 + h * 128:blk * 512 + (h + 1) * 128, :].rearrange("r (two e) -> (r two) e", two=2).rearrange("(c p) e -> p c e", p=128)

        def exchange_stage(l):
            TR.dma("sp", "exu", exu_in[l][:, :], p_scr[SEQ - 128:SEQ, OU:OU + 1024], reads=[("p2", NPT - 1)], writes=[("exuin", l)])
            pairs = [[0, 1], [2, 3], [4, 5], [6, 7]]
            TR.collective(lambda g: g.collective_compute("AllGather", ALU.bypass, replica_groups=pairs,
                                                          ins=[exu_in_t[l].ap().opt()], outs=[exu_out_t[l].ap().opt()]),
                          reads=[("exuin", l)], writes=[("exuout", l)])
            TR.collective(lambda g: g.collective_compute("AllGather", ALU.bypass, replica_groups=pairs,
                                                          ins=[ex_in_t[l].ap().opt()], outs=[ex_out_t[l].ap().opt()]),
                          reads=[("exin", l)], writes=[("exout", l)])

        def correct_stage(l):
            for h in range(4):
                sp_, spkey = stage("s0f", [128, 2, 512], F32, 3)
                spb, spbkey = stage("s0b", [128, 2, 512], BF16, 2)
                sl, slkey = stage("s0f", [128, 2, 512], F32, 3)
                TR.dma("sp", "l" + spkey, sp_[:], exS(ex_out[l], h, 0), reads=[("exout", l)], writes=[spkey + "0", spkey + "1"])
                TR.op("dve", lambda e: e.tensor_scalar(out=sp_[:], in0=sp_[:], scalar1=flag[:, 0:1], scalar2=None, op0=ALU.mult),
                      reads=[spkey + "0", spkey + "1", "flag"], writes=[spkey + "0", spkey + "1"])
                TR.op("act", lambda e: e.copy(out=spb[:], in_=sp_[:]), reads=[spkey + "0", spkey + "1"], writes=[spbkey])
                TR.dma("sp", "l" + slkey, sl[:], exS(ex_in[l], h, 0), reads=[("exin", l)], writes=[slkey + "0", slkey + "1"])
                for ch in range(2):
                    TR.op("dve", lambda e, ch=ch: e.scalar_tensor_tensor(out=sl[:, ch, :], in0=sp_[:, ch, :], scalar=Gtot[:, h, ch:ch + 1], in1=sl[:, ch, :],
                                                                         op0=ALU.mult, op1=ALU.add), reads=[spkey + "0", spkey + "1", slkey + "0", slkey + "1", "Gtot"], writes=[slkey + "0", slkey + "1"])
                TR.dma("sp", "s" + slkey, gla_p[l, h].rearrange("(c p) e -> p c e", p=128), sl[:], reads=[slkey + "0", slkey + "1"], writes=[("glap", l, h)])
                crot = [0]

                def corr_loads(t_):
                    cq, cqk = stage("qm", [128, 2, 128], BF16, 2)
                    crot[0] += 1
                    co, cok = qkvs[crot[0] % 2][:, 0:512], "qkv" + str(crot[0] % 2)
                    TR.dma("sp", "l" + cqk, cq[:].rearrange("p c k -> p (c k)"), qg_scr[h, t_], reads=[("qg", h, t_)], writes=[cqk])
                    TR.dma("sp", "l" + cok, co, o_scr[t_ * 128:(t_ + 1) * 128, h * 512:(h + 1) * 512], reads=[("o", t_)], writes=[cok])
                    return cq, cqk, co, cok

                nxt_ld = corr_loads(0)
                for t in range(NPT):
                    r = slice(t * 128, (t + 1) * 128)
                    cq, cqk, co, cok = nxt_ld
                    if t + 1 < NPT:
                        nxt_ld = corr_loads(t + 1)
                    C, Ckey = fbank()
                    for ch in range(2):
                        TR.pe(lambda e, ch=ch: e.matmul(C[:, :], lhsT=cq[:, ch, :], rhs=spb[:, ch, :], start=(ch == 0), stop=(ch == 1)),
                              reads=[cqk, spbkey], writes=[Ckey], mark=(ch == 1))
                    TR.op("dve", lambda e: e.tensor_tensor(out=co, in0=C[:, :], in1=co, op=ALU.add),
                          reads=[Ckey, cok], writes=[cok])
                    rstd_of(co, cok, 512, 2)
                    TR.op("dve", lambda e: e.scalar_tensor_tensor(out=onf[:], in0=co, scalar=rs[:, 2:3], in1=ggrow[:], op0=ALU.mult, op1=ALU.mult),
                          reads=[cok, "rs", "ggrow"], writes=["onf"])
                    TR.dma("sp", "sto_o", o_scr[r, h * 512:(h + 1) * 512], onf[:], reads=["onf"], writes=[("o", t)])

        ogt = Bbuf[:, 0, :]; gat = Bbuf[:, 1, :]; gbt = Bbuf[:, 2, :]; ot = Bbuf[:, 3, :]; sgt = Bbuf[:, 4, :]; ut = Bbuf[:, 5, :]
        psrow = Bbuf[:, 6:8, :].rearrange("p a c -> p (a c)")
        ub = [sb([128, 1024], BF16, f"ub{i}") for i in range(2)]
        hist_b = [sb([120, 1024], BF16, f"hist_b{i}") for i in range(2)]
        dTb = sb([128, 8, 128], BF16, "dTb")
        wpool_b = sb([128, 8, 512], BF16, "wpool_b")

        def merge_stage(l):
            TR.dma("pool", "wp", wpool_b[:], w_pool[l].rearrange("(k p) o -> p k o", p=128), writes=["wpool"])
            TR.dma("sp", "ld0", psrow, pscale[l:l + 1, :].partition_broadcast(128), writes=["B6", "B7"])
            for hf in range(2):
                TR.dma("sp", "ld1", Bbuf[0:120, 5, :], spool[l, hf * 120:(hf + 1) * 120, :], writes=["B5"])
                TR.op("dve", lambda e, hf=hf: e.tensor_copy(out=hist_b[hf][:], in_=Bbuf[0:120, 5, :]), reads=["B5"], writes=[f"hist_b{hf}"])
            TR.dma("sp", "lm4", ut, exu_out[l][0:128, :], reads=[("exuout", l)], writes=["B5"])
            TR.op("dve", lambda e: e.tensor_copy(out=ub[1][:], in_=ut), reads=["B5"], writes=["ub1"])
            for t in range(NT):
                ty = 1 if t == NT - 1 else 0
                r = slice(t * 128, (t + 1) * 128)
                TR.dma("sp", "lm4", ut, p_scr[r, OU:OU + 1024], reads=[("p2", t)], writes=["B5"])
                cur, ckey = ub[t % 2], f"ub{t % 2}"
                prv, pvkey = ub[(t + 1) % 2], f"ub{(t + 1) % 2}"
                TR.op("dve", lambda e: e.tensor_copy(out=cur[:], in_=ut), reads=["B5"], writes=[ckey])
                dbanks = [fbank(), fbank()]
                for g in range(4):
                    Dk, Dkey = dbanks[g // 2]
                    base = 5 + g * 7
                    for ic in range(2):
                        col = ((g % 2) * 2 + ic) * 128
                        cs = slice(g * 256 + ic * 128, g * 256 + (ic + 1) * 128)
                        lastg = (g % 2 == 1 and ic == 1)
                        if ty == 0:
                            mc, mp = (base + 2, base + 6) if t == 0 else (base + 0, base + 1)
                            TR.pe(lambda e: e.matmul(Dk[:, col:col + 128], lhsT=cur[:, cs], rhs=cmat_b[:, mc, :], start=True, stop=False),
                                  reads=[ckey, "cmat"], writes=[Dkey], mark=False)
                            TR.pe(lambda e: e.matmul(Dk[:, col:col + 128], lhsT=prv[:, cs], rhs=cmat_b[:, mp, :], start=False, stop=True),
                                  reads=[pvkey, "cmat"], writes=[Dkey], mark=lastg)
                        else:
                            TR.pe(lambda e: e.matmul(Dk[:, col:col + 128], lhsT=cur[:, cs], rhs=cmat_b[:, base + 3, :], start=True, stop=False),
                                  reads=[ckey, "cmat"], writes=[Dkey], mark=False)
                            for hf in range(2):
                                TR.pe(lambda e, hf=hf: e.matmul(Dk[:, col:col + 128], lhsT=hist_b[hf][0:120, cs], rhs=cmat_b[0:120, base + 4 + hf, :],
                                                                start=False, stop=(hf == 1)),
                                      reads=[f"hist_b{hf}", "cmat"], writes=[Dkey], mark=(lastg and hf == 1))
                for i2 in range(2):
                    Dk, Dkey = dbanks[i2]
                    TR.op("act", lambda e, i2=i2, Dk=Dk: e.copy(out=dTb[:, i2 * 4:(i2 + 1) * 4, :], in_=Dk[:, :].rearrange("p (k c) -> p k c", c=128)),
                          reads=[Dkey], writes=["dTb"])
                for hfc in range(2):
                    bgf(4)
                    c0 = hfc * 1024
                    TR.dma("sp", "lm0", ogt, p_scr[r, OOG + c0:OOG + c0 + 1024], reads=[("p2", t)], writes=["B0"])
                    TR.dma("sp", "lm1", gat, p_scr[r, OGA + c0:OGA + c0 + 1024], reads=[("p2", t)], writes=["B1"])
                    TR.dma("sp", "lm2", gbt, p_scr[r, OGB + c0:OGB + c0 + 1024], reads=[("p2", t)], writes=["B2"])
                    TR.dma("sp", "lm3", ot, o_scr[r, c0:c0 + 1024], reads=[("o", t)], writes=["B3"])
                    TR.op("act", lambda e: e.activation(out=sgt, in_=ogt, func=AF.Sigmoid), reads=["B0"], writes=["B4"])
                    TR.op("dve", lambda e: e.tensor_tensor(out=ogt, in0=ogt, in1=sgt, op=ALU.mult), reads=["B0", "B4"], writes=["B0"])
                    TR.op("dve", lambda e: e.tensor_tensor(out=ogt, in0=ogt, in1=ot, op=ALU.mult), reads=["B0", "B3"], writes=["B0"])
                    TR.op("act", lambda e: e.activation(out=sgt, in_=gat, func=AF.Sigmoid), reads=["B1"], writes=["B4"])
                    TR.op("dve", lambda e: e.tensor_tensor(out=ogt, in0=ogt, in1=sgt, op=ALU.mult), reads=["B0", "B4"], writes=["B0"])
                    TR.op("act", lambda e: e.activation(out=sgt, in_=gbt, func=AF.Sigmoid), reads=["B2"], writes=["B4"])
                    TR.op("dve", lambda e: e.tensor_tensor(out=sgt, in0=sgt, in1=psrow[:, c0:c0 + 1024], op=ALU.mult), reads=["B4", "B6", "B7"], writes=["B4"])
                    for g2 in range(2):
                        g = hfc * 2 + g2
                        Y, Ykey = fbank()
                        for ic in range(2):
                            TR.pe(lambda e, ic=ic: e.matmul(Y[:, :], lhsT=dTb[:, g * 2 + ic, :], rhs=wpool_b[:, g * 2 + ic, :], start=(ic == 0), stop=(ic == 1)),
                                  reads=["dTb", "wpool"], writes=[Ykey], mark=(ic == 1))
                        gs = slice(g2 * 512, (g2 + 1) * 512)
                        TR.op("dve", lambda e, gs=gs: e.tensor_tensor(out=gbt[:, gs], in0=Y[:, :], in1=sgt[:, gs], op=ALU.mult),
                              reads=[Ykey, "B4", "B2"], writes=["B2"])
                    TR.op("dve", lambda e: e.tensor_tensor(out=mb[:, c0:c0 + 1024], in0=gbt, in1=ogt, op=ALU.add), reads=["B2", "B0"], writes=["hb"])
                to_feat(mb, "hb", 16, t)
            TR.dma("sp", "po0", pool_p[l], p_scr[SEQ - 15:SEQ, OU:OU + 1024], reads=[("p2", NPT - 1)], writes=[("poolp", l)])
            TR.dma("sp", "po1", pool_s[l, :, 0:7, :], spool[l].rearrange("(j r) c -> j r c", r=15)[:, 8:15, :], writes=[("pools0", l)])
            TR.dma("sp", "po2", pool_s[l, :, 7:15, :], p_scr[SEQ:T, OU:OU + 1024].rearrange("(j r) c -> j r c", r=8),
                   reads=[("p2", NT - 1)], writes=[("pools1", l)])

        g1row = sb([128, 2, 512], F32, "g1row")

        def resid_epi(l, gcol, xprev, pkey_prev, xnext, nkey):
            pend = {}

            def pre(t, c, pc):
                xs, xskey = stage("xs", [128, 512], F32, 2)
                TR.dma("sp", "l" + xskey, xs[:, 0:pc], xprev[t * 128:(t + 1) * 128, c:c + pc], reads=[(pkey_prev, t)], writes=[xskey])
                pend[(t, c)] = (xs, xskey)

            def epi(t, c, pc, ps, pkey):
                ty = 1 if t == NT - 1 else 0
                if t == 0:
                    for ty2 in range(2):
                        TR.dma("sp", "ldg", g1row[:, ty2, 0:pc], mod_scr[l, ty2 * 128:(ty2 + 1) * 128, gcol + c:gcol + c + pc],
                               reads=[("mod", l, gcol // (3 * D))], writes=["g1row"])
                xs, xskey = pend.pop((t, c))
                sg, skey = stage("sto", [128, 512], F32, 2)
                TR.op("dve", lambda e: e.tensor_tensor(out=sg[:, 0:pc], in0=ps[:, 0:pc], in1=g1row[:, ty, 0:pc], op=ALU.mult),
                      reads=[pkey, "g1row"], writes=[skey])
                TR.op("dve", lambda e: e.tensor_tensor(out=sg[:, 0:pc], in0=sg[:, 0:pc], in1=xs[:, 0:pc], op=ALU.add),
                      reads=[skey, xskey], writes=[skey])
                TR.dma("sp", "st" + skey, xnext[t * 128:(t + 1) * 128, c:c + pc], sg[:, 0:pc], reads=[skey], writes=[(nkey, t)])
            epi.pre = pre
            return epi

        xcur, xkey = xin, "xin"
        tiles = list(range(NT))
        for l in range(DEPTH):
            norm_stage(l, xcur, xkey, n1g, l, 0, D)
            linear(16, tiles, w_in[l], 0, OQ, OOG - OQ, store_epi(p_scr, "p"))
            linear(16, tiles, w_in[l], 0, OALR, 16, store_epi(p_scr, "p"))
            fill_state["gen"] = linear_gen(16, tiles, w_in[l], 0, OOG, OALR - OOG, store_epi(p_scr, "p2"))
            gla_stage(l)
            fillf(10 ** 6)
            if l == 0:
                bg_state["gen"] = chain_gens(ada_bg(0, 3 * D, 3 * D), ada_bg(1, 0, 6 * D))
            exchange_stage(l)
            correct_stage(l)
            merge_stage(l)
            bgf(10 ** 6)
            x1, x1key = xa, "xa"
            linear(16, tiles, w_o[l], 0, 0, D, resid_epi(l, 2 * D, xcur, xkey, x1, x1key))
            norm_stage(l, x1, x1key, n2g, l, 3 * D, 4 * D)
            linear(16, tiles, w_gu[l], 0, 0, 2 * DFF, store_epi(gu_scr, "gu"))
            prev, prevkey = x1, x1key
            steps = [(k0, kc, t, c0, min(1024, kc * 128 - c0)) for (k0, kc) in ((0, 16), (16, 16), (32, 12)) for t in tiles
                     for c0 in range(0, kc * 128, 1024)]
            sets = [(0, 1, 2), (3, 4, 5)]

            def ffn_load(si):
                k0_, kc_, t_, c0_, w_ = steps[si]
                a_, b_, _ = sets[si % 2]
                r_ = slice(t_ * 128, (t_ + 1) * 128)
                TR.dma("sp", f"lf0{si % 2}", Bbuf[:, a_, 0:w_], gu_scr[r_, k0_ * 128 + c0_:k0_ * 128 + c0_ + w_], reads=[("gu", t_)], writes=[f"B{a_}"])
                TR.dma("sp", f"lf1{si % 2}", Bbuf[:, b_, 0:w_], gu_scr[r_, DFF + k0_ * 128 + c0_:DFF + k0_ * 128 + c0_ + w_], reads=[("gu", t_)], writes=[f"B{b_}"])

            ffn_load(0)
            si = 0
            for gi, (k0, kc) in enumerate(((0, 16), (16, 16), (32, 12))):
                n = kc * 128
                for t in tiles:
                    for c0 in range(0, n, 1024):
                        w = min(1024, n - c0)
                        if si + 1 < len(steps):
                            ffn_load(si + 1)
                        a_, b_, c_ = sets[si % 2]
                        si += 1
                        gt_ = Bbuf[:, a_, 0:w]; upt = Bbuf[:, b_, 0:w]; sg2 = Bbuf[:, c_, 0:w]
                        TR.op("act", lambda e: e.activation(out=sg2, in_=gt_, func=AF.Sigmoid), reads=[f"B{a_}"], writes=[f"B{c_}"])
                        TR.op("dve", lambda e: e.tensor_tensor(out=gt_, in0=gt_, in1=sg2, op=ALU.mult), reads=[f"B{a_}", f"B{c_}"], writes=[f"B{a_}"])
                        TR.op("dve", lambda e: e.tensor_tensor(out=mb[:, c0:c0 + w], in0=gt_, in1=upt, op=ALU.mult), reads=[f"B{a_}", f"B{b_}"], writes=["hb"])
                    to_feat(mb, "hb", kc, t)
                nxt, nkey = ((xb, "xb"), (xc, "xc"), (xb, "xb"))[gi]
                linear(kc, tiles, w_down[l], k0 * 128, 0, D, resid_epi(l, 5 * D, prev, prevkey, nxt, nkey))
                if gi > 0:
                    pass
                prev, prevkey = nxt, nkey
            xcur, xkey = prev, prevkey

        TR.dma("sp", "ld0", rowA, fng[0:1, :].partition_broadcast(128), writes=RA)
        for t in tiles:
            TR.dma("sp", "ldx", xt, xcur[t * 128:(t + 1) * 128, :], reads=[(xkey, t)], writes=XT)
            rstd_of(xt, "B0", D, 0)
            TR.op("dve", lambda e: e.scalar_tensor_tensor(out=ht, in0=xt, scalar=rs[:, 0:1], in1=rowA, op0=ALU.mult, op1=ALU.mult),
                  reads=XT + ["rs"] + RA, writes=HT)
            TR.dma("sp", "sty", y[t * 128:(t + 1) * 128, :], ht, reads=HT, writes=[("y", t)])
        for d in TR.dsem.values():
            TR.wait("sp", (d[0], d[1]))
    return nc


def _consts(first_half):
    s = np.arange(128)[:, None]; c = np.arange(128)[None, :]
    m = np.zeros((NMAT, 128, 128), np.float32)
    m[0] = np.eye(128)
    m[1] = (s <= c)
    m[2] = (s > c)
    same = (s // 8) == (c // 8)
    m[3] = (s <= c) & same
    m[4] = (s > c) & same
    for g, w in enumerate(WINS):
        b = 5 + g * 7
        m[b + 0] = ((s <= c) & (s > c - w)) / w - (s == c)
        m[b + 1] = ((s - 128) > (c - w)) / w
        cntc = np.minimum(c + 1, w)
        if first_half:
            m[b + 2] = ((s <= c) & (s > c - w)) / cntc - (s == c)
            m[b + 6] = 0.0
        else:
            m[b + 2] = m[b + 0]
            m[b + 6] = m[b + 1]
        m[b + 3] = ((s <= c) & (s > c - w) & same) / w - (s == c)
        for hf in range(2):
            hr = np.arange(128)[:, None]
            jj = hr // 15 + hf * 8; rr = hr % 15
            pos_h = rr - 15
            ci = c % 8; cj = c // 8
            m[b + 4 + hf] = ((hr < 120) & (jj == cj) & (pos_h > ci - w)) / w
    cm = np.zeros((128, 16, 128), np.float32)
    for j in range(16):
        cm[:, j, 8 * j:8 * j + 8] = 1.0
    rm = np.zeros((128, 16), np.float32)
    for j in range(16):
        rm[8 * j:8 * j + 8, j] = 1.0
    return m, cm.reshape(128, 2048), rm


_NC = None


def kernel(x_prompt, x_sample, state_gla, state_pool, c_prompt, c_sample, w_ada, b_ada, norm1_g, w_in, w_a2, b_a,
           gla_norm_g, w_pool, pool_scale, w_o, norm2_g, w_gu, w_down, final_norm_g):
    global _NC
    f = lambda a: np.ascontiguousarray(np.asarray(a, dtype=np.float32))
    x_prompt, x_sample, state_gla, state_pool, c_prompt, c_sample = map(f, (x_prompt, x_sample, state_gla, state_pool, c_prompt, c_sample))
    shared = {
        "w_ada": f(w_ada), "b_ada": f(b_ada), "norm1_g": f(norm1_g), "w_in": f(w_in), "w_a2": f(w_a2), "b_a": f(b_a),
        "gla_norm_g": f(gla_norm_g), "w_pool": f(w_pool).reshape(DEPTH, 1024, 512), "pool_scale": f(pool_scale), "w_o": f(w_o),
        "norm2_g": f(norm2_g), "w_gu": f(w_gu), "w_down": f(w_down), "final_norm_g": f(final_norm_g).reshape(1, D),
    }
    cst = [_consts(True), _consts(False)]
    in_maps = []
    for c in range(NCORE):
        b, half = c // 2, c % 2
        sl = slice(SPC * c, SPC * (c + 1))
        m = dict(shared)
        m["cmats"], m["cmk"], m["rmk"] = cst[half][0], cst[half][1], cst[half][2]
        m["flag"] = np.full((128, 1), float(half), np.float32)
        m["xin"] = np.concatenate([x_prompt[b, half * SEQ:(half + 1) * SEQ], x_sample[sl].reshape(SPC * 8, D)], axis=0)
        m["crow"] = np.concatenate([np.repeat(c_prompt[b:b + 1], 128, axis=0), np.repeat(c_sample[sl], 8, axis=0)], axis=0)
        m["sgla"] = np.ascontiguousarray(state_gla[:, sl])
        m["spool"] = np.ascontiguousarray(state_pool[:, sl]).reshape(DEPTH, SPC * 15, 1024)
        in_maps.append(m)
    if _NC is None:
        _NC = build_program()
    res = run_bass_kernel_spmd(_NC, in_maps, core_ids=list(range(NCORE)))
    R = res.results
    y_prompt = np.stack([np.concatenate([R[2 * b]["y"][:SEQ], R[2 * b + 1]["y"][:SEQ]], axis=0) for b in range(4)])
    y_sample = np.concatenate([R[c]["y"][SEQ:].reshape(SPC, 8, D) for c in range(NCORE)], axis=0)
    gla_pp = np.stack([R[2 * b + 1]["gla_p"] for b in range(4)], axis=1)
    pool_pp = np.stack([R[2 * b + 1]["pool_p"] for b in range(4)], axis=1)
    gla_ss = np.concatenate([R[c]["gla_s"] for c in range(NCORE)], axis=1)
    pool_ss = np.concatenate([R[c]["pool_s"] for c in range(NCORE)], axis=1)
    return (y_prompt.astype(np.float32), y_sample.astype(np.float32), gla_pp.astype(np.float32), pool_pp.astype(np.float32),
            gla_ss.astype(np.float32), pool_ss.astype(np.float32))
```

```python
import contextlib
import numpy as np
import concourse.bass as bass
import concourse.mybir as mybir
from concourse.bass_utils import run_bass_kernel_spmd

F32 = mybir.dt.float32
BF16 = mybir.dt.bfloat16
ALU = mybir.AluOpType
AF = mybir.ActivationFunctionType

D = 2048
SEQ = 1024
NCORE = 8
SPC = 16
T = SEQ + SPC * 8
NT = T // 128
NPT = SEQ // 128
NMAT = 5 + 28
NIN = 11280
DFF = 5632
DEPTH = 2
OQ, OK_, OV, OOG, OU, OGA, OGB, OALR = 0, 1024, 2048, 4096, 6144, 7168, 9216, 11264
WINS = (2, 4, 8, 16)
EPS = 1e-6


class Tracker:
    SEM_CAP = 30000

    def __init__(self, nc, stack, n_sems):
        self.nc = nc
        self.engs = {"pe": nc.tensor, "act": nc.scalar, "dve": nc.vector, "pool": nc.gpsimd, "sp": nc.sync}
        self.free_sems = [stack.enter_context(nc.semaphore(f"s{i}")) for i in range(n_sems)]
        self.cur = {}
        self.seen = {e: {} for e in self.engs}
        self.lastw = {}
        self.readers = {}
        self.dsem = {}
        self._pend = []

    def wait(self, en, tok):
        if tok is None:
            return
        sem, val = tok
        sid = id(sem)
        if self.seen[en].get(sid, 0) >= val:
            return
        self.engs[en].wait_ge(sem, val)
        self.seen[en][sid] = val

    def _deps(self, en, reads, writes):
        for k in reads:
            self.wait(en, self.lastw.get(k))
        for k in writes:
            self.wait(en, self.lastw.get(k))
            for r in self.readers.get(k, ()):
                self.wait(en, r)

    def _commit(self, tok, reads, writes):
        for k in reads:
            self.readers.setdefault(k, []).append(tok)
        for k in writes:
            self.lastw[k] = tok
            self.readers[k] = []

    def op(self, en, fn, reads=(), writes=()):
        self._deps(en, reads, writes)
        ins = fn(self.engs[en])
        c = self.cur.get(en)
        if c is None or c[1] >= self.SEM_CAP:
            c = [self.free_sems.pop(), 0]
            self.cur[en] = c
        c[1] += 1
        ins.then_inc(c[0], 1)
        tok = (c[0], c[1])
        self._commit(tok, reads, writes)
        return tok

    def pe(self, fn, reads=(), writes=(), mark=True):
        if not mark:
            self._deps("pe", reads, writes)
            fn(self.engs["pe"])
            self._pend.append((tuple(reads), tuple(writes)))
            return None
        tok = self.op("pe", fn, reads, writes)
        for r, w in self._pend:
            self._commit(tok, r, w)
        self._pend = []
        return tok

    def collective(self, fn, reads=(), writes=()):
        self._deps("pool", reads, writes)
        ins = fn(self.engs["pool"])
        sem = self.free_sems.pop()
        ins.then_inc(sem)
        tok = (sem, 1)
        self._commit(tok, reads, writes)
        return tok

    def dma(self, q, semkey, out, in_, reads=(), writes=()):
        d = self.dsem.get(semkey)
        if d is None:
            d = [self.free_sems.pop(), 0]
            self.dsem[semkey] = d
        if d[1] > 0:
            self.wait(q, (d[0], d[1]))
        self._deps(q, reads, writes)
        ins = self.engs[q].dma_start(out=out, in_=in_)
        d[1] += 16
        ins.then_inc(d[0], 16)
        tok = (d[0], d[1])
        self._commit(tok, reads, writes)
        return tok


def build_program():
    nc = bass.Bass("TRN2", target_bir_lowering=False)
    din = lambda n, s: nc.dram_tensor(n, s, F32, kind="ExternalInput").ap()
    dout = lambda n, s: nc.dram_tensor(n, s, F32, kind="ExternalOutput").ap()
    xin = din("xin", [T, D]); crow = din("crow", [256, D])
    sgla = din("sgla", [DEPTH, SPC, 4, 256, 512]); spool = din("spool", [DEPTH, SPC * 15, 1024])
    w_ada = din("w_ada", [DEPTH, D, 6 * D]); b_ada = din("b_ada", [DEPTH, 6 * D]); n1g = din("norm1_g", [DEPTH, D])
    w_in = din("w_in", [DEPTH, D, NIN]); w_a2 = din("w_a2", [DEPTH, 16, 1024]); b_a = din("b_a", [DEPTH, 1024])
    ggla = din("gla_norm_g", [DEPTH, 512]); w_pool = din("w_pool", [DEPTH, 1024, 512]); pscale = din("pool_scale", [DEPTH, D])
    w_o = din("w_o", [DEPTH, D, D]); n2g = din("norm2_g", [DEPTH, D]); w_gu = din("w_gu", [DEPTH, D, 2 * DFF])
    w_down = din("w_down", [DEPTH, DFF, D]); fng = din("final_norm_g", [1, D])
    cmats = din("cmats", [NMAT, 128, 128]); flag_d = din("flag", [128, 1]); cmk = din("cmk", [128, 16 * 128]); rmk = din("rmk", [128, 16])
    y = dout("y", [T, D]); gla_p = dout("gla_p", [DEPTH, 4, 256, 512]); pool_p = dout("pool_p", [DEPTH, 15, 1024])
    gla_s = dout("gla_s", [DEPTH, SPC, 4, 256, 512]); pool_s = dout("pool_s", [DEPTH, SPC, 15, 1024])
    scr = lambda n, s: nc.dram_tensor(n, s, F32).ap()
    p_scr = scr("p_scr", [T, NIN]); gu_scr = scr("gu_scr", [T, 2 * DFF]); o_scr = scr("o_scr", [T, D])
    qg_scr = nc.dram_tensor("qg_scr", [4, NPT, 128, 256], BF16).ap()
    ex_in_t = [nc.dram_tensor(f"ex_in{l}", [512, 1024], F32) for l in range(DEPTH)]
    ex_out_t = [nc.dram_tensor(f"ex_out{l}", [1024, 1024], F32) for l in range(DEPTH)]
    exu_in_t = [nc.dram_tensor(f"exu_in{l}", [128, 1024], F32) for l in range(DEPTH)]
    exu_out_t = [nc.dram_tensor(f"exu_out{l}", [256, 1024], F32) for l in range(DEPTH)]
    ex_in = [t_.ap() for t_ in ex_in_t]; ex_out = [t_.ap() for t_ in ex_out_t]
    exu_in = [t_.ap() for t_ in exu_in_t]; exu_out = [t_.ap() for t_ in exu_out_t]
    xa = scr("xa", [T, D]); xb = scr("xb", [T, D]); xc = scr("xc", [T, D]); mod_scr = scr("mod_scr", [DEPTH, 256, 6 * D])

    with contextlib.ExitStack() as st:
        TR = Tracker(nc, st, 100)
        cnt = [0]

        def sb(shape, dt, name=None):
            cnt[0] += 1
            return st.enter_context(nc.sbuf_tensor(name or f"t{cnt[0]}", shape, dt))

        actT = sb([128, 16, T], BF16, "actT")
        NW = 2
        wbuf = [sb([128, 16, 512], BF16, f"wb{i}") for i in range(NW)]
        psf = [st.enter_context(nc.psum_tensor(f"psf{i}", [128, 512], F32)) for i in range(6)]
        psb = [st.enter_context(nc.psum_tensor(f"psb{i}", [128, 1024], BF16)) for i in range(2)]
        rot = {"f": 0, "b": 0, "w": 0}

        held = set()

        def fbank():
            while True:
                i = rot["f"] % 6; rot["f"] += 1
                if i not in held:
                    return psf[i], f"psf{i}"

        def bbank():
            i = rot["b"] % 2; rot["b"] += 1
            return psb[i], f"psb{i}"

        pools = {}

        def stage(tag, shape, dt, n=2):
            if tag not in pools:
                pools[tag] = [[sb(shape, dt, f"{tag}{i}") for i in range(n)], 0]
            p = pools[tag]
            i = p[1] % n; p[1] += 1
            return p[0][i], f"{tag}{i}"

        st.enter_context(nc.Block())

        Bbuf = sb([128, 8, 1024], F32, "Bbuf")
        BK = [f"B{i}" for i in range(8)]
        cm_f = Bbuf[:, 0:2, :].rearrange("p a c -> p (a c)")
        hb = sb([128, D], BF16, "hb")
        mb = hb
        idb = sb([128, 128], BF16, "idb")
        cmat_b = sb([128, NMAT, 128], BF16, "cmat_b")
        flag = sb([128, 1], F32, "flag_sb")
        Gst = sb([128, 2], F32, "Gst"); Gtot = sb([128, 4, 2], F32, "Gtot")
        cm_b = sb([128, 16, 128], BF16, "cm_b")
        rm_b = sb([128, 16], BF16, "rm_b")
        ones_b = sb([128, 128], BF16, "ones_b")
        rs = sb([128, 8], F32, "rs")
        junk = hb
        TR.dma("sp", "cl", flag[:], flag_d[:, :], writes=["flag"])
        cst_f = Bbuf[:, 0:5, :].rearrange("p a c -> p (a c)")[:, 0:NMAT * 128].rearrange("p (n c) -> p n c", c=128)
        TR.dma("sp", "cl", cst_f, cmats.rearrange("n p c -> p n c"), writes=["B0", "B1", "B2", "B3", "B4"])
        TR.op("dve", lambda e: e.tensor_copy(out=cmat_b[:], in_=cst_f), reads=["B0", "B1", "B2", "B3", "B4"], writes=["cmat"])
        TR.op("dve", lambda e: e.tensor_copy(out=idb[:], in_=cmat_b[:, 0, :]), reads=["cmat"], writes=["idb"])
        TR.dma("sp", "cl", cm_f, cmk[:, :], writes=["B0", "B1"])
        TR.op("dve", lambda e: e.tensor_copy(out=cm_b[:].rearrange("p j c -> p (j c)"), in_=cm_f), reads=["B0", "B1"], writes=["cm_b"])
        TR.dma("sp", "cl", cm_f[:, 0:16], rmk[:, :], reads=[], writes=["B0", "B1"])
        TR.op("dve", lambda e: e.tensor_copy(out=rm_b[:], in_=cm_f[:, 0:16]), reads=["B0", "B1"], writes=["rm_b"])
        TR.op("dve", lambda e: e.memset(ones_b[:], 1.0), writes=["ones_b"])
        MINCL = {0: 1, 1: 3}
        MREV = {0: 2, 1: 4}

        def to_feat(src_b, src_key, kc, t, dst=None, dst_key="actT"):
            dst = actT if dst is None else dst
            for k0 in range(0, kc, 8):
                kn = min(8, kc - k0)
                bk, bkey = bbank()
                for k in range(kn):
                    TR.pe(lambda e, k=k: e.transpose(bk[:, k * 128:(k + 1) * 128], src_b[:, (k0 + k) * 128:(k0 + k + 1) * 128], idb[:]),
                          reads=[src_key, "idb"], writes=[bkey], mark=(k == kn - 1))
                TR.op("act", lambda e: e.copy(out=dst[:, k0:k0 + kn, t * 128:(t + 1) * 128],
                                              in_=bk[:, 0:kn * 128].rearrange("p (k c) -> p k c", c=128)),
                      reads=[bkey], writes=[(dst_key, t)])

        def linear(*a, **k):
            for _ in linear_gen(*a, **k):
                pass

        fill_state = {"gen": None}

        def fillf(n=1):
            g = fill_state["gen"]
            for _ in range(n):
                if g is None:
                    return
                try:
                    next(g)
                except StopIteration:
                    fill_state["gen"] = None
                    return

        bg_state = {"gen": None, "on": False}

        def chain_gens(*gens):
            for g_ in gens:
                yield from g_

        def bgf(n=1):
            g = bg_state["gen"]
            for _ in range(n):
                if g is None:
                    return
                try:
                    next(g)
                except StopIteration:
                    bg_state["gen"] = None
                    return

        def linear_gen(kc, tiles, W, r0, c0, ncols, epi, pw=512, lhs=None, lhs_key="actT", own_w=None, bg=False):
            groups = [(t, c, min(pw, c0 + ncols - c)) for c in range(c0, c0 + ncols, pw) for t in tiles]
            pre = getattr(epi, "pre", None)
            if pre:
                pre(*groups[0])
            wb = wkey = None
            lhs = actT if lhs is None else lhs
            for gi, (t, c, pc) in enumerate(groups):
                if bg:
                    bgf(1)
                if t == tiles[0]:
                    if own_w is not None:
                        wb, wkey, wi = own_w, "wown", "own"
                    else:
                        wi = rot["w"] % NW; rot["w"] += 1
                        wb, wkey = wbuf[wi], f"wb{wi}"
                    TR.dma("pool", f"w{wi}", wb[:, 0:kc, 0:pc], W[r0:r0 + kc * 128, c:c + pc].rearrange("(k p) n -> p k n", p=128), writes=[wkey])
                ps, pkey = fbank()
                for k in range(kc):
                    TR.pe(lambda e, k=k: e.matmul(ps[:, 0:pc], lhsT=lhs[:, k, t * 128:(t + 1) * 128], rhs=wb[:, k, 0:pc],
                                                  start=(k == 0), stop=(k == kc - 1)),
                          reads=[(lhs_key, t), wkey], writes=[pkey], mark=(k == kc - 1))
                if pre and gi + 1 < len(groups):
                    pre(*groups[gi + 1])
                epi(t, c, pc, ps, pkey)
                yield

        def store_epi(dst, dkey):
            def epi(t, c, pc, ps, pkey):
                sg, skey = stage("sto", [128, 512], F32, 2)
                TR.op("act", lambda e: e.copy(out=sg[:, 0:pc], in_=ps[:, 0:pc]), reads=[pkey], writes=[skey])
                TR.dma("sp", "st" + skey, dst[t * 128:(t + 1) * 128, c:c + pc], sg[:, 0:pc], reads=[skey], writes=[(dkey, t)])
            return epi

        def rstd_of(src, skey, n, col):
            TR.op("dve", lambda e: e.memset(rs[:, col + 1:col + 2], 0.0), writes=["rs"])
            TR.op("act", lambda e: e.activation(out=junk[:, 0:n], in_=src, func=AF.Square, accum_out=rs[:, col + 1:col + 2]),
                  reads=[skey, "rs"], writes=["hb", "rs"])
            TR.op("act", lambda e: e.activation(out=rs[:, col + 1:col + 2], in_=rs[:, col + 1:col + 2], func=AF.Ln, scale=1.0 / n, bias=EPS),
                  reads=["rs"], writes=["rs"])
            TR.op("act", lambda e: e.activation(out=rs[:, col:col + 1], in_=rs[:, col + 1:col + 2], func=AF.Exp, scale=-0.5),
                  reads=["rs"], writes=["rs"])

        xt = Bbuf[:, 0:2, :].rearrange("p a c -> p (a c)")
        ht = Bbuf[:, 2:4, :].rearrange("p a c -> p (a c)")
        rowA = Bbuf[:, 4:6, :].rearrange("p a c -> p (a c)")
        rowB = Bbuf[:, 6:8, :].rearrange("p a c -> p (a c)")
        XT, HT, RA, RB = ["B0", "B1"], ["B2", "B3"], ["B4", "B5"], ["B6", "B7"]
        gT = sb([128, 16], F32, "gT"); scT = sb([128, 16], F32, "scT"); shT = sb([128, 16], F32, "shT")

        def norm_stage(l, xsrc, xkey, gain, gl, c_sh, c_sc):
            with nc.allow_non_contiguous_dma(reason="tiny per-feature vectors"):
                TR.dma("sp", "ld0", gT[:], gain[gl, :].rearrange("(k p) -> p k", p=128), writes=["gT"])
                TR.dma("sp", "ld1", scT[:], mod_scr[l, 0, c_sc:c_sc + D].rearrange("(k p) -> p k", p=128), reads=[("mod", l, c_sc // (3 * D))], writes=["scT"])
                TR.dma("sp", "ld2", shT[:], mod_scr[l, 0, c_sh:c_sh + D].rearrange("(k p) -> p k", p=128), reads=[("mod", l, c_sh // (3 * D))], writes=["shT"])
            TR.op("dve", lambda e: e.scalar_tensor_tensor(out=scT[:], in0=scT[:], scalar=1.0, in1=gT[:], op0=ALU.add, op1=ALU.mult),
                  reads=["scT", "gT"], writes=["scT"])
            xalt = Bbuf[:, 4:6, :].rearrange("p a c -> p (a c)")

            def xbuf(t_):
                return (xt, XT) if (t_ % 2 == 0 or t_ == NT - 1) else (xalt, RA)

            def xload(t_):
                xb_, xk_ = xbuf(t_)
                TR.dma("sp", "ldx" + str(t_ % 2), xb_, xsrc[t_ * 128:(t_ + 1) * 128, :], reads=[(xkey, t_)], writes=xk_)

            xload(0)
            for t in range(NT):
                if t + 1 < NT:
                    xload(t + 1)
                xt_, XT_ = xbuf(t)
                rstd_of(xt_, XT_[0], D, 0)
                if t < NT - 1:
                    TR.op("dve", lambda e: e.tensor_scalar(out=hb[:], in0=xt_, scalar1=rs[:, 0:1], scalar2=None, op0=ALU.mult),
                          reads=XT_ + ["rs"], writes=["hb"])
                    for k0 in range(0, 16, 8):
                        bk, bkey = bbank()
                        for k in range(8):
                            TR.pe(lambda e, k=k: e.transpose(bk[:, k * 128:(k + 1) * 128], hb[:, (k0 + k) * 128:(k0 + k + 1) * 128], idb[:]),
                                  reads=["hb", "idb"], writes=[bkey], mark=(k == 7))
                        for k in range(8):
                            TR.op("act", lambda e, k=k: e.activation(out=actT[:, k0 + k, t * 128:(t + 1) * 128], in_=bk[:, k * 128:(k + 1) * 128],
                                                                     func=AF.Identity, scale=scT[:, k0 + k:k0 + k + 1], bias=shT[:, k0 + k:k0 + k + 1]),
                                  reads=[bkey, "scT", "shT"], writes=[("actT", t)])
                else:
                    TR.dma("sp", "ld0", ht, gain[gl:gl + 1, :].partition_broadcast(128), writes=HT)
                    TR.dma("sp", "ld1", rowA, mod_scr[l, 128:256, c_sc:c_sc + D], reads=[("mod", l, c_sc // (3 * D))], writes=RA)
                    TR.dma("sp", "ld2", rowB, mod_scr[l, 128:256, c_sh:c_sh + D], reads=[("mod", l, c_sh // (3 * D))], writes=RB)
                    TR.op("dve", lambda e: e.scalar_tensor_tensor(out=rowA, in0=rowA, scalar=1.0, in1=ht, op0=ALU.add, op1=ALU.mult),
                          reads=RA + HT, writes=RA)
                    TR.op("dve", lambda e: e.scalar_tensor_tensor(out=ht, in0=xt, scalar=rs[:, 0:1], in1=rowA, op0=ALU.mult, op1=ALU.mult),
                          reads=XT + ["rs"] + RA, writes=HT)
                    TR.op("dve", lambda e: e.tensor_tensor(out=hb[:], in0=ht, in1=rowB, op=ALU.add), reads=HT + RB, writes=["hb"])
                    to_feat(hb, "hb", 16, t)

        brow = sb([128, 512], F32, "brow")
        csT = sb([128, 16, 256], BF16, "csT")
        for t in range(2):
            TR.dma("sp", "ldx", xt, crow[t * 128:(t + 1) * 128, :], writes=XT)
            TR.op("act", lambda e: e.activation(out=ht, in_=xt, func=AF.Sigmoid), reads=XT, writes=HT)
            TR.op("dve", lambda e: e.tensor_tensor(out=hb[:], in0=ht, in1=xt, op=ALU.mult), reads=HT + XT, writes=["hb"])
            to_feat(hb, "hb", 16, t, dst=csT, dst_key="csT")

        def epi_mod(l):
            def epi(t, c, pc, ps, pkey):
                if t == 0:
                    TR.dma("sp", "ldb", brow[:, 0:pc], b_ada[l:l + 1, c:c + pc].partition_broadcast(128), writes=["brow"])
                sg, skey = stage("sto", [128, 512], F32, 2)
                TR.op("dve", lambda e: e.tensor_tensor(out=sg[:, 0:pc], in0=ps[:, 0:pc], in1=brow[:, 0:pc], op=ALU.add),
                      reads=[pkey, "brow"], writes=[skey])
                TR.dma("sp", "st" + skey, mod_scr[l, t * 128:(t + 1) * 128, c:c + pc], sg[:, 0:pc], reads=[skey], writes=[("mod", l, c // (3 * D))])
            return epi

        linear(16, [0, 1], w_ada[0], 0, 0, 3 * D, epi_mod(0), lhs=csT, lhs_key="csT")

        def ada_bg(l, c0, ncols):
            return linear_gen(16, [0, 1], w_ada[l], 0, c0, ncols, epi_mod(l), lhs=csT, lhs_key="csT")

        qkvs = [sb([128, 1024], F32, f"qkv{i}") for i in range(2)]
        alrs = [sb([128, 16], F32, f"alr_f{i}") for i in range(2)]
        qkv = qkvs[0]
        qb = sb([128, 256], BF16, "qb"); kb = sb([128, 256], BF16, "kb"); vb = sb([128, 512], BF16, "vb"); ab = sb([128, 16], BF16, "ab")
        alrT = sb([17, 128], BF16, "alrT")
        wa2f = Bbuf[0:17, 0, :]; wa2b = sb([17, 1024], BF16, "wa2b")
        TR.op("dve", lambda e: e.memset(alrT[:], 1.0), writes=["alrT"])
        ef = sb([128, 256], F32, "ef"); lgb = sb([128, 256], BF16, "lgb")
        E1 = sb([128, 256], F32, "E1"); E2 = sb([128, 256], F32, "E2"); E3 = sb([128, 256], F32, "E3")
        qe = sb([128, 2, 128], BF16, "qe"); ke = sb([128, 2, 128], BF16, "ke"); kd = sb([128, 256], BF16, "kd")
        attm = sb([128, 128], BF16, "attm")
        Sf = sb([128, 2, 512], F32, "Sf"); Sb = sb([128, 2, 512], BF16, "Sb")

        onf = sb([128, 512], F32, "onf")
        ggrow = sb([128, 512], F32, "ggrow")

        def gla_stage(l):
            TR.dma("sp", "ld0", Bbuf[0:16, 0, :], w_a2[l], writes=["B0"])
            TR.dma("sp", "ld1", Bbuf[16:17, 0, :], b_a[l:l + 1, :], writes=["B0"])
            TR.op("dve", lambda e: e.tensor_copy(out=wa2b[:], in_=wa2f), reads=["B0"], writes=["wa2b"])
            TR.dma("sp", "ld2", ggrow[:], ggla[l:l + 1, :].partition_broadcast(128), writes=["ggrow"])
            order = [(h_, t_) for h_ in range(4) for t_ in range(NT)]

            def gla_loads(i):
                h_, t_ = order[i]
                r_ = slice(t_ * 128, (t_ + 1) * 128)
                q_, a_, sfx = qkvs[i % 2], alrs[i % 2], str(i % 2)
                TR.dma("sp", "lq0" + sfx, q_[:, 0:256], p_scr[r_, OQ + h_ * 256:OQ + (h_ + 1) * 256], reads=[("p", t_)], writes=["qkv" + sfx])
                TR.dma("sp", "lq1" + sfx, q_[:, 256:512], p_scr[r_, OK_ + h_ * 256:OK_ + (h_ + 1) * 256], reads=[("p", t_)], writes=["qkv" + sfx])
                TR.dma("sp", "lq2" + sfx, q_[:, 512:1024], p_scr[r_, OV + h_ * 512:OV + (h_ + 1) * 512], reads=[("p", t_)], writes=["qkv" + sfx])
                TR.dma("sp", "lq3" + sfx, a_[:], p_scr[r_, OALR:OALR + 16], reads=[("p", t_)], writes=["alr_f" + sfx])

            gla_loads(0)
            for h in range(4):
                TR.op("dve", lambda e: e.memset(Sf[:], 0.0), writes=["Sf"])
                TR.op("dve", lambda e: e.memset(Sb[:], 0.0), writes=["Sb"])
                TR.op("dve", lambda e: e.memset(Gst[:], 1.0), writes=["Gst"])
                for t in range(NT):
                    ty = 1 if t == NT - 1 else 0
                    r = slice(t * 128, (t + 1) * 128)
                    gi_ = h * NT + t
                    if gi_ + 1 < len(order):
                        gla_loads(gi_ + 1)
                    qkv_, alr_, sfx = qkvs[gi_ % 2], alrs[gi_ % 2], str(gi_ % 2)
                    TR.op("dve", lambda e: e.tensor_scalar(out=qb[:], in0=qkv_[:, 0:256], scalar1=0.0625, scalar2=None, op0=ALU.mult),
                          reads=["qkv" + sfx], writes=["qb"])
                    TR.op("dve", lambda e: e.tensor_copy(out=kb[:], in_=qkv_[:, 256:512]), reads=["qkv" + sfx], writes=["kb"])
                    TR.op("dve", lambda e: e.tensor_copy(out=vb[:], in_=qkv_[:, 512:1024]), reads=["qkv" + sfx], writes=["vb"])
                    TR.op("dve", lambda e: e.tensor_copy(out=ab[:], in_=alr_[:]), reads=["alr_f" + sfx], writes=["ab"])
                    bk, bkey = bbank()
                    for ch in range(2):
                        TR.pe(lambda e, ch=ch: e.transpose(bk[:, ch * 128:(ch + 1) * 128], qb[:, ch * 128:(ch + 1) * 128], idb[:]),
                              reads=["qb", "idb"], writes=[bkey], mark=False)
                        TR.pe(lambda e, ch=ch: e.transpose(bk[:, 256 + ch * 128:256 + (ch + 1) * 128], kb[:, ch * 128:(ch + 1) * 128], idb[:]),
                              reads=["kb", "idb"], writes=[bkey], mark=False)
                    TR.pe(lambda e: e.transpose(bk[0:16, 512:640], ab[:, 0:16], idb[:]), reads=["ab", "idb"], writes=[bkey])
                    TR.op("act", lambda e: e.copy(out=alrT[0:16, :], in_=bk[0:16, 512:640]), reads=[bkey], writes=["alrT"])
                    fillf()
                    g, gkey = fbank()
                    TR.pe(lambda e: e.matmul(g[:, 0:256], lhsT=alrT[0:17, :], rhs=wa2b[0:17, h * 256:(h + 1) * 256], start=True, stop=True),
                          reads=["alrT", "wa2b"], writes=[gkey])
                    TR.op("act", lambda e: e.activation(out=ef[:], in_=g[:, 0:256], func=AF.Exp, scale=-1.0), reads=[gkey], writes=["ef"])
                    TR.op("act", lambda e: e.activation(out=ef[:], in_=ef[:], func=AF.Ln, bias=1.0), reads=["ef"], writes=["ef"])
                    TR.op("dve", lambda e: e.tensor_scalar(out=lgb[:], in0=ef[:], scalar1=-1.0 / 16, scalar2=None, op0=ALU.mult),
                          reads=["ef"], writes=["lgb"])
                    fillf()
                    B, Bkey = fbank()
                    for ch in range(2):
                        TR.pe(lambda e, ch=ch: e.matmul(B[:, ch * 128:(ch + 1) * 128], lhsT=lgb[:, ch * 128:(ch + 1) * 128],
                                                        rhs=cmat_b[:, MINCL[ty], :], start=True, stop=True),
                              reads=["lgb", "cmat"], writes=[Bkey], mark=False)
                    TR.pe(lambda e: e.matmul(B[:, 256:512], lhsT=cmat_b[:, MREV[ty], :], rhs=lgb[:], start=True, stop=True),
                          reads=["lgb", "cmat"], writes=[Bkey])
                    TR.op("act", lambda e: e.activation(out=E1[:], in_=B[:, 0:256], func=AF.Exp), reads=[Bkey], writes=["E1"])
                    TR.op("act", lambda e: e.activation(out=E2[:], in_=B[:, 0:256], func=AF.Exp, scale=-1.0), reads=[Bkey], writes=["E2"])
                    TR.op("act", lambda e: e.activation(out=E3[:], in_=B[:, 256:512], func=AF.Exp), reads=[Bkey], writes=["E3"])
                    TR.op("dve", lambda e: e.tensor_tensor(out=qe[:].rearrange("p c k -> p (c k)"), in0=bk[:, 0:256], in1=E1[:], op=ALU.mult),
                          reads=[bkey, "E1"], writes=["qe"])
                    TR.op("dve", lambda e: e.tensor_tensor(out=ke[:].rearrange("p c k -> p (c k)"), in0=bk[:, 256:512], in1=E2[:], op=ALU.mult),
                          reads=[bkey, "E2"], writes=["ke"])
                    TR.op("dve", lambda e: e.tensor_tensor(out=kd[:], in0=kb[:], in1=E3[:], op=ALU.mult), reads=["kb", "E3"], writes=["kd"])
                    if ty == 0:
                        qg, qgkey = stage("qg", [128, 2, 128], BF16, 2)
                        for ch in range(2):
                            TR.op("dve", lambda e, ch=ch: e.tensor_scalar(out=qg[:, ch, :], in0=qe[:, ch, :], scalar1=Gst[:, ch:ch + 1], scalar2=None, op0=ALU.mult),
                                  reads=["qe", "Gst"], writes=[qgkey])
                        TR.dma("sp", "s" + qgkey, qg_scr[h, t], qg[:].rearrange("p c k -> p (c k)"), reads=[qgkey], writes=[("qg", h, t)])
                    fillf()
                    A, Akey = fbank()
                    for ch in range(2):
                        TR.pe(lambda e, ch=ch: e.matmul(A[:, 0:128], lhsT=ke[:, ch, :], rhs=qe[:, ch, :], start=(ch == 0), stop=(ch == 1)),
                              reads=["ke", "qe"], writes=[Akey], mark=(ch == 1))
                    TR.op("dve", lambda e: e.tensor_tensor(out=attm[:], in0=A[:, 0:128], in1=cmat_b[:, MINCL[ty], :], op=ALU.mult),
                          reads=[Akey, "cmat"], writes=["attm"])
                    fillf()
                    O, Okey = fbank()
                    if ty == 0:
                        TR.pe(lambda e: e.matmul(O[:, :], lhsT=attm[:], rhs=vb[:], start=True, stop=False), reads=["attm", "vb"], writes=[Okey], mark=False)
                        for ch in range(2):
                            TR.pe(lambda e, ch=ch: e.matmul(O[:, :], lhsT=qe[:, ch, :], rhs=Sb[:, ch, :], start=False, stop=(ch == 1)),
                                  reads=["qe", "Sb"], writes=[Okey], mark=(ch == 1))
                        for ch in range(2):
                            U, Ukey = fbank()
                            TR.pe(lambda e, ch=ch: e.matmul(U[:, :], lhsT=kd[:, ch * 128:(ch + 1) * 128], rhs=vb[:], start=True, stop=True),
                                  reads=["kd", "vb"], writes=[Ukey])
                            TR.op("dve", lambda e, ch=ch: e.scalar_tensor_tensor(out=Sf[:, ch, :], in0=Sf[:, ch, :], scalar=E1[:, ch * 128 + 127:ch * 128 + 128],
                                                                                 in1=U[:, :], op0=ALU.mult, op1=ALU.add),
                                  reads=["Sf", "E1", Ukey], writes=["Sf"])
                        TR.op("act", lambda e: e.copy(out=Sb[:], in_=Sf[:]), reads=["Sf"], writes=["Sb"])
                        TR.op("dve", lambda e: e.tensor_tensor(out=Gst[:], in0=Gst[:], in1=E1[:].rearrange("p (c k) -> p c k", k=128)[:, :, 127], op=ALU.mult),
                              reads=["Gst", "E1"], writes=["Gst"])
                        if t == NPT - 1:
                            TR.dma("sp", "stS", exS(ex_in[l], h), Sf[:], reads=["Sf"], writes=[("exin", l)])
                            TR.op("dve", lambda e: e.tensor_copy(out=Gtot[:, h, :], in_=Gst[:]), reads=["Gst"], writes=["Gtot"])
                    else:
                        held.add(int(Okey[3:]))
                        TR.pe(lambda e: e.matmul(O[:, :], lhsT=attm[:], rhs=vb[:], start=True, stop=False), reads=["attm", "vb"], writes=[Okey], mark=False)
                        def s0_load(j_):
                            s0_, k_ = stage("s0f", [128, 2, 512], F32, 3)
                            for ch_ in range(2):
                                TR.dma("sp", f"l{k_}{ch_}", s0_[:, ch_, :], sgla[l, j_, h, ch_ * 128:(ch_ + 1) * 128, :], writes=[k_ + str(ch_)])
                            return s0_, k_

                        nxt_s0 = s0_load(0)
                        for j in range(SPC):
                            s0, s0key = nxt_s0
                            if j + 1 < SPC:
                                nxt_s0 = s0_load(j + 1)
                            s0b, s0bkey = stage("s0b", [128, 2, 512], BF16, 2)
                            qm, qmkey = stage("qm", [128, 2, 128], BF16, 2)
                            km, kmkey = stage("km", [128, 256], BF16, 2)
                            TR.op("act", lambda e: e.copy(out=s0b[:], in_=s0[:]), reads=[s0key + "0", s0key + "1"], writes=[s0bkey])
                            TR.op("dve", lambda e, j=j: e.tensor_tensor(out=qm[:], in0=qe[:], in1=cm_b[:, j, :].unsqueeze(1).to_broadcast([128, 2, 128]),
                                                                        op=ALU.mult), reads=["qe", "cm_b"], writes=[qmkey])
                            TR.op("dve", lambda e, j=j: e.tensor_scalar(out=km[:], in0=kd[:], scalar1=rm_b[:, j:j + 1], scalar2=None, op0=ALU.mult),
                                  reads=["kd", "rm_b"], writes=[kmkey])
                            for ch in range(2):
                                last = (j == SPC - 1 and ch == 1)
                                TR.pe(lambda e, ch=ch: e.matmul(O[:, :], lhsT=qm[:, ch, :], rhs=s0b[:, ch, :], start=False, stop=last),
                                      reads=[qmkey, s0bkey], writes=[Okey], mark=True)
                            for ch in range(2):
                                U, Ukey = fbank()
                                TR.pe(lambda e, ch=ch: e.matmul(U[:, :], lhsT=km[:, ch * 128:(ch + 1) * 128], rhs=vb[:], start=True, stop=True),
                                      reads=[kmkey, "vb"], writes=[Ukey])
                                cix = ch * 128 + 8 * j + 7
                                TR.op("dve", lambda e, ch=ch, cix=cix: e.scalar_tensor_tensor(out=s0[:, ch, :], in0=s0[:, ch, :], scalar=E1[:, cix:cix + 1],
                                                                                              in1=U[:, :], op0=ALU.mult, op1=ALU.add),
                                      reads=[s0key + str(ch), "E1", Ukey], writes=[s0key + str(ch)])
                            TR.dma("sp", "s" + s0key, gla_s[l, j, h].rearrange("(c p) e -> p c e", p=128), s0[:], reads=[s0key + "0", s0key + "1"],
                                   writes=[("glas", l, j, h)])
                    held.clear()
                    if ty == 0:
                        TR.op("act", lambda e: e.copy(out=onf[:], in_=O[:, :]), reads=[Okey], writes=["onf"])
                    else:
                        rstd_of(O[:, :], Okey, 512, 2)
                        TR.op("dve", lambda e: e.scalar_tensor_tensor(out=onf[:], in0=O[:, :], scalar=rs[:, 2:3], in1=ggrow[:], op0=ALU.mult, op1=ALU.mult),
                              reads=[Okey, "rs", "ggrow"], writes=["onf"])
                    TR.dma("sp", "sto_o", o_scr[r, h * 512:(h + 1) * 512], onf[:], reads=["onf"], writes=[("o", t)])

        def exS(buf, h, blk=0):
            return buf[blk * 512 + h * 128:blk * 512 + (h + 1) * 128, :].rearrange("r (two e) -> (r two) e", two=2).rearrange("(c p) e -> p c e", p=128)

        def exchange_stage(l):
            TR.dma("sp", "exu", exu_in[l][:, :], p_scr[SEQ - 128:SEQ, OU:OU + 1024], reads=[("p2", NPT - 1)], writes=[("exuin", l)])
            pairs = [[0, 1], [2, 3], [4, 5], [6, 7]]
            TR.collective(lambda g: g.collective_compute("AllGather", ALU.bypass, replica_groups=pairs,
                                                          ins=[exu_in_t[l].ap().opt()], outs=[exu_out_t[l].ap().opt()]),
                          reads=[("exuin", l)], writes=[("exuout", l)])
            TR.collective(lambda g: g.collective_compute("AllGather", ALU.bypass, replica_groups=pairs,
                                                          ins=[ex_in_t[l].ap().opt()], outs=[ex_out_t[l].ap().opt()]),
                          reads=[("exin", l)], writes=[("exout", l)])

        def correct_stage(l):
            for h in range(4):
                sp_, spkey = stage("s0f", [128, 2, 512], F32, 3)
                spb, spbkey = stage("s0b", [128, 2, 512], BF16, 2)
                sl, slkey = stage("s0f", [128, 2, 512], F32, 3)
                TR.dma("sp", "l" + spkey, sp_[:], exS(ex_out[l], h, 0), reads=[("exout", l)], writes=[spkey + "0", spkey + "1"])
                TR.op("dve", lambda e: e.tensor_scalar(out=sp_[:], in0=sp_[:], scalar1=flag[:, 0:1], scalar2=None, op0=ALU.mult),
                      reads=[spkey + "0", spkey + "1", "flag"], writes=[spkey + "0", spkey + "1"])
                TR.op("act", lambda e: e.copy(out=spb[:], in_=sp_[:]), reads=[spkey + "0", spkey + "1"], writes=[spbkey])
                TR.dma("sp", "l" + slkey, sl[:], exS(ex_in[l], h, 0), reads=[("exin", l)], writes=[slkey + "0", slkey + "1"])
                for ch in range(2):
                    TR.op("dve", lambda e, ch=ch: e.scalar_tensor_tensor(out=sl[:, ch, :], in0=sp_[:, ch, :], scalar=Gtot[:, h, ch:ch + 1], in1=sl[:, ch, :],
                                                                         op0=ALU.mult, op1=ALU.add), reads=[spkey + "0", spkey + "1", slkey + "0", slkey + "1", "Gtot"], writes=[slkey + "0", slkey + "1"])
                TR.dma("sp", "s" + slkey, gla_p[l, h].rearrange("(c p) e -> p c e", p=128), sl[:], reads=[slkey + "0", slkey + "1"], writes=[("glap", l, h)])
                crot = [0]

                def corr_loads(t_):
                    cq, cqk = stage("qm", [128, 2, 128], BF16, 2)
                    crot[0] += 1
                    co, cok = qkvs[crot[0] % 2][:, 0:512], "qkv" + str(crot[0] % 2)
                    TR.dma("sp", "l" + cqk, cq[:].rearrange("p c k -> p (c k)"), qg_scr[h, t_], reads=[("qg", h, t_)], writes=[cqk])
                    TR.dma("sp", "l" + cok, co, o_scr[t_ * 128:(t_ + 1) * 128, h * 512:(h + 1) * 512], reads=[("o", t_)], writes=[cok])
                    return cq, cqk, co, cok

                nxt_ld = corr_loads(0)
                for t in range(NPT):
                    r = slice(t * 128, (t + 1) * 128)
                    cq, cqk, co, cok = nxt_ld
                    if t + 1 < NPT:
                        nxt_ld = corr_loads(t + 1)
                    C, Ckey = fbank()
                    for ch in range(2):
                        TR.pe(lambda e, ch=ch: e.matmul(C[:, :], lhsT=cq[:, ch, :], rhs=spb[:, ch, :], start=(ch == 0), stop=(ch == 1)),
                              reads=[cqk, spbkey], writes=[Ckey], mark=(ch == 1))
                    TR.op("dve", lambda e: e.tensor_tensor(out=co, in0=C[:, :], in1=co, op=ALU.add),
                          reads=[Ckey, cok], writes=[cok])
                    rstd_of(co, cok, 512, 2)
                    TR.op("dve", lambda e: e.scalar_tensor_tensor(out=onf[:], in0=co, scalar=rs[:, 2:3], in1=ggrow[:], op0=ALU.mult, op1=ALU.mult),
                          reads=[cok, "rs", "ggrow"], writes=["onf"])
                    TR.dma("sp", "sto_o", o_scr[r, h * 512:(h + 1) * 512], onf[:], reads=["onf"], writes=[("o", t)])

        ogt = Bbuf[:, 0, :]; gat = Bbuf[:, 1, :]; gbt = Bbuf[:, 2, :]; ot = Bbuf[:, 3, :]; sgt = Bbuf[:, 4, :]; ut = Bbuf[:, 5, :]
        psrow = Bbuf[:, 6:8, :].rearrange("p a c -> p (a c)")
        ub = [sb([128, 1024], BF16, f"ub{i}") for i in range(2)]
        hist_b = [sb([120, 1024], BF16, f"hist_b{i}") for i in range(2)]
        dTb = sb([128, 8, 128], BF16, "dTb")
        wpool_b = sb([128, 8, 512], BF16, "wpool_b")

        def merge_stage(l):
            TR.dma("pool", "wp", wpool_b[:], w_pool[l].rearrange("(k p) o -> p k o", p=128), writes=["wpool"])
            TR.dma("sp", "ld0", psrow, pscale[l:l + 1, :].partition_broadcast(128), writes=["B6", "B7"])
            for hf in range(2):
                TR.dma("sp", "ld1", Bbuf[0:120, 5, :], spool[l, hf * 120:(hf + 1) * 120, :], writes=["B5"])
                TR.op("dve", lambda e, hf=hf: e.tensor_copy(out=hist_b[hf][:], in_=Bbuf[0:120, 5, :]), reads=["B5"], writes=[f"hist_b{hf}"])
            TR.dma("sp", "lm4", ut, exu_out[l][0:128, :], reads=[("exuout", l)], writes=["B5"])
            TR.op("dve", lambda e: e.tensor_copy(out=ub[1][:], in_=ut), reads=["B5"], writes=["ub1"])
            for t in range(NT):
                ty = 1 if t == NT - 1 else 0
                r = slice(t * 128, (t + 1) * 128)
                TR.dma("sp", "lm4", ut, p_scr[r, OU:OU + 1024], reads=[("p2", t)], writes=["B5"])
                cur, ckey = ub[t % 2], f"ub{t % 2}"
                prv, pvkey = ub[(t + 1) % 2], f"ub{(t + 1) % 2}"
                TR.op("dve", lambda e: e.tensor_copy(out=cur[:], in_=ut), reads=["B5"], writes=[ckey])
                dbanks = [fbank(), fbank()]
                for g in range(4):
                    Dk, Dkey = dbanks[g // 2]
                    base = 5 + g * 7
                    for ic in range(2):
                        col = ((g % 2) * 2 + ic) * 128
                        cs = slice(g * 256 + ic * 128, g * 256 + (ic + 1) * 128)
                        lastg = (g % 2 == 1 and ic == 1)
                        if ty == 0:
                            mc, mp = (base + 2, base + 6) if t == 0 else (base + 0, base + 1)
                            TR.pe(lambda e: e.matmul(Dk[:, col:col + 128], lhsT=cur[:, cs], rhs=cmat_b[:, mc, :], start=True, stop=False),
                                  reads=[ckey, "cmat"], writes=[Dkey], mark=False)
                            TR.pe(lambda e: e.matmul(Dk[:, col:col + 128], lhsT=prv[:, cs], rhs=cmat_b[:, mp, :], start=False, stop=True),
                                  reads=[pvkey, "cmat"], writes=[Dkey], mark=lastg)
                        else:
                            TR.pe(lambda e: e.matmul(Dk[:, col:col + 128], lhsT=cur[:, cs], rhs=cmat_b[:, base + 3, :], start=True, stop=False),
                                  reads=[ckey, "cmat"], writes=[Dkey], mark=False)
                            for hf in range(2):
                                TR.pe(lambda e, hf=hf: e.matmul(Dk[:, col:col + 128], lhsT=hist_b[hf][0:120, cs], rhs=cmat_b[0:120, base + 4 + hf, :],
                                                                start=False, stop=(hf == 1)),
                                      reads=[f"hist_b{hf}", "cmat"], writes=[Dkey], mark=(lastg and hf == 1))
                for i2 in range(2):
                    Dk, Dkey = dbanks[i2]
                    TR.op("act", lambda e, i2=i2, Dk=Dk: e.copy(out=dTb[:, i2 * 4:(i2 + 1) * 4, :], in_=Dk[:, :].rearrange("p (k c) -> p k c", c=128)),
                          reads=[Dkey], writes=["dTb"])
                for hfc in range(2):
                    bgf(4)
                    c0 = hfc * 1024
                    TR.dma("sp", "lm0", ogt, p_scr[r, OOG + c0:OOG + c0 + 1024], reads=[("p2", t)], writes=["B0"])
                    TR.dma("sp", "lm1", gat, p_scr[r, OGA + c0:OGA + c0 + 1024], reads=[("p2", t)], writes=["B1"])
                    TR.dma("sp", "lm2", gbt, p_scr[r, OGB + c0:OGB + c0 + 1024], reads=[("p2", t)], writes=["B2"])
                    TR.dma("sp", "lm3", ot, o_scr[r, c0:c0 + 1024], reads=[("o", t)], writes=["B3"])
                    TR.op("act", lambda e: e.activation(out=sgt, in_=ogt, func=AF.Sigmoid), reads=["B0"], writes=["B4"])
                    TR.op("dve", lambda e: e.tensor_tensor(out=ogt, in0=ogt, in1=sgt, op=ALU.mult), reads=["B0", "B4"], writes=["B0"])
                    TR.op("dve", lambda e: e.tensor_tensor(out=ogt, in0=ogt, in1=ot, op=ALU.mult), reads=["B0", "B3"], writes=["B0"])
                    TR.op("act", lambda e: e.activation(out=sgt, in_=gat, func=AF.Sigmoid), reads=["B1"], writes=["B4"])
                    TR.op("dve", lambda e: e.tensor_tensor(out=ogt, in0=ogt, in1=sgt, op=ALU.mult), reads=["B0", "B4"], writes=["B0"])
                    TR.op("act", lambda e: e.activation(out=sgt, in_=gbt, func=AF.Sigmoid), reads=["B2"], writes=["B4"])
                    TR.op("dve", lambda e: e.tensor_tensor(out=sgt, in0=sgt, in1=psrow[:, c0:c0 + 1024], op=ALU.mult), reads=["B4", "B6", "B7"], writes=["B4"])
                    for g2 in range(2):
                        g = hfc * 2 + g2
                        Y, Ykey = fbank()
                        for ic in range(2):
                            TR.pe(lambda e, ic=ic: e.matmul(Y[:, :], lhsT=dTb[:, g * 2 + ic, :], rhs=wpool_b[:, g * 2 + ic, :], start=(ic == 0), stop=(ic == 1)),
                                  reads=["dTb", "wpool"], writes=[Ykey], mark=(ic == 1))
                        gs = slice(g2 * 512, (g2 + 1) * 512)
                        TR.op("dve", lambda e, gs=gs: e.tensor_tensor(out=gbt[:, gs], in0=Y[:, :], in1=sgt[:, gs], op=ALU.mult),
                              reads=[Ykey, "B4", "B2"], writes=["B2"])
                    TR.op("dve", lambda e: e.tensor_tensor(out=mb[:, c0:c0 + 1024], in0=gbt, in1=ogt, op=ALU.add), reads=["B2", "B0"], writes=["hb"])
                to_feat(mb, "hb", 16, t)
            TR.dma("sp", "po0", pool_p[l], p_scr[SEQ - 15:SEQ, OU:OU + 1024], reads=[("p2", NPT - 1)], writes=[("poolp", l)])
            TR.dma("sp", "po1", pool_s[l, :, 0:7, :], spool[l].rearrange("(j r) c -> j r c", r=15)[:, 8:15, :], writes=[("pools0", l)])
            TR.dma("sp", "po2", pool_s[l, :, 7:15, :], p_scr[SEQ:T, OU:OU + 1024].rearrange("(j r) c -> j r c", r=8),
                   reads=[("p2", NT - 1)], writes=[("pools1", l)])

        g1row = sb([128, 2, 512], F32, "g1row")

        def resid_epi(l, gcol, xprev, pkey_prev, xnext, nkey):
            pend = {}

            def pre(t, c, pc):
                xs, xskey = stage("xs", [128, 512], F32, 2)
                TR.dma("sp", "l" + xskey, xs[:, 0:pc], xprev[t * 128:(t + 1) * 128, c:c + pc], reads=[(pkey_prev, t)], writes=[xskey])
                pend[(t, c)] = (xs, xskey)

            def epi(t, c, pc, ps, pkey):
                ty = 1 if t == NT - 1 else 0
                if t == 0:
                    for ty2 in range(2):
                        TR.dma("sp", "ldg", g1row[:, ty2, 0:pc], mod_scr[l, ty2 * 128:(ty2 + 1) * 128, gcol + c:gcol + c + pc],
                               reads=[("mod", l, gcol // (3 * D))], writes=["g1row"])
                xs, xskey = pend.pop((t, c))
                sg, skey = stage("sto", [128, 512], F32, 2)
                TR.op("dve", lambda e: e.tensor_tensor(out=sg[:, 0:pc], in0=ps[:, 0:pc], in1=g1row[:, ty, 0:pc], op=ALU.mult),
                      reads=[pkey, "g1row"], writes=[skey])
                TR.op("dve", lambda e: e.tensor_tensor(out=sg[:, 0:pc], in0=sg[:, 0:pc], in1=xs[:, 0:pc], op=ALU.add),
                      reads=[skey, xskey], writes=[skey])
                TR.dma("sp", "st" + skey, xnext[t * 128:(t + 1) * 128, c:c + pc], sg[:, 0:pc], reads=[skey], writes=[(nkey, t)])
            epi.pre = pre
            return epi

        xcur, xkey = xin, "xin"
        tiles = list(range(NT))
        for l in range(DEPTH):
            norm_stage(l, xcur, xkey, n1g, l, 0, D)
            linear(16, tiles, w_in[l], 0, OQ, OOG - OQ, store_epi(p_scr, "p"))
            linear(16, tiles, w_in[l], 0, OALR, 16, store_epi(p_scr, "p"))
            fill_state["gen"] = linear_gen(16, tiles, w_in[l], 0, OOG, OALR - OOG, store_epi(p_scr, "p2"))
            gla_stage(l)
            fillf(10 ** 6)
            if l == 0:
                bg_state["gen"] = chain_gens(ada_bg(0, 3 * D, 3 * D), ada_bg(1, 0, 6 * D))
            exchange_stage(l)
            correct_stage(l)
            merge_stage(l)
            bgf(10 ** 6)
            x1, x1key = xa, "xa"
            linear(16, tiles, w_o[l], 0, 0, D, resid_epi(l, 2 * D, xcur, xkey, x1, x1key))
            norm_stage(l, x1, x1key, n2g, l, 3 * D, 4 * D)
            linear(16, tiles, w_gu[l], 0, 0, 2 * DFF, store_epi(gu_scr, "gu"))
            prev, prevkey = x1, x1key
            steps = [(k0, kc, t, c0, min(1024, kc * 128 - c0)) for (k0, kc) in ((0, 16), (16, 16), (32, 12)) for t in tiles
                     for c0 in range(0, kc * 128, 1024)]
            sets = [(0, 1, 2), (3, 4, 5)]

            def ffn_load(si):
                k0_, kc_, t_, c0_, w_ = steps[si]
                a_, b_, _ = sets[si % 2]
                r_ = slice(t_ * 128, (t_ + 1) * 128)
                TR.dma("sp", f"lf0{si % 2}", Bbuf[:, a_, 0:w_], gu_scr[r_, k0_ * 128 + c0_:k0_ * 128 + c0_ + w_], reads=[("gu", t_)], writes=[f"B{a_}"])
                TR.dma("sp", f"lf1{si % 2}", Bbuf[:, b_, 0:w_], gu_scr[r_, DFF + k0_ * 128 + c0_:DFF + k0_ * 128 + c0_ + w_], reads=[("gu", t_)], writes=[f"B{b_}"])

            ffn_load(0)
            si = 0
            for gi, (k0, kc) in enumerate(((0, 16), (16, 16), (32, 12))):
                n = kc * 128
                for t in tiles:
                    for c0 in range(0, n, 1024):
                        w = min(1024, n - c0)
                        if si + 1 < len(steps):
                            ffn_load(si + 1)
                        a_, b_, c_ = sets[si % 2]
                        si += 1
                        gt_ = Bbuf[:, a_, 0:w]; upt = Bbuf[:, b_, 0:w]; sg2 = Bbuf[:, c_, 0:w]
                        TR.op("act", lambda e: e.activation(out=sg2, in_=gt_, func=AF.Sigmoid), reads=[f"B{a_}"], writes=[f"B{c_}"])
                        TR.op("dve", lambda e: e.tensor_tensor(out=gt_, in0=gt_, in1=sg2, op=ALU.mult), reads=[f"B{a_}", f"B{c_}"], writes=[f"B{a_}"])
                        TR.op("dve", lambda e: e.tensor_tensor(out=mb[:, c0:c0 + w], in0=gt_, in1=upt, op=ALU.mult), reads=[f"B{a_}", f"B{b_}"], writes=["hb"])
                    to_feat(mb, "hb", kc, t)
                nxt, nkey = ((xb, "xb"), (xc, "xc"), (xb, "xb"))[gi]
                linear(kc, tiles, w_down[l], k0 * 128, 0, D, resid_epi(l, 5 * D, prev, prevkey, nxt, nkey))
                if gi > 0:
                    pass
                prev, prevkey = nxt, nkey
            xcur, xkey = prev, prevkey

        TR.dma("sp", "ld0", rowA, fng[0:1, :].partition_broadcast(128), writes=RA)
        for t in tiles:
            TR.dma("sp", "ldx", xt, xcur[t * 128:(t + 1) * 128, :], reads=[(xkey, t)], writes=XT)
            rstd_of(xt, "B0", D, 0)
            TR.op("dve", lambda e: e.scalar_tensor_tensor(out=ht, in0=xt, scalar=rs[:, 0:1], in1=rowA, op0=ALU.mult, op1=ALU.mult),
                  reads=XT + ["rs"] + RA, writes=HT)
            TR.dma("sp", "sty", y[t * 128:(t + 1) * 128, :], ht, reads=HT, writes=[("y", t)])
        for d in TR.dsem.values():
            TR.wait("sp", (d[0], d[1]))
    return nc


def _consts(first_half):
    s = np.arange(128)[:, None]; c = np.arange(128)[None, :]
    m = np.zeros((NMAT, 128, 128), np.float32)
    m[0] = np.eye(128)
    m[1] = (s <= c)
    m[2] = (s > c)
    same = (s // 8) == (c // 8)
    m[3] = (s <= c) & same
    m[4] = (s > c) & same
    for g, w in enumerate(WINS):
        b = 5 + g * 7
        m[b + 0] = ((s <= c) & (s > c - w)) / w - (s == c)
        m[b + 1] = ((s - 128) > (c - w)) / w
        cntc = np.minimum(c + 1, w)
        if first_half:
            m[b + 2] = ((s <= c) & (s > c - w)) / cntc - (s == c)
            m[b + 6] = 0.0
        else:
            m[b + 2] = m[b + 0]
            m[b + 6] = m[b + 1]
        m[b + 3] = ((s <= c) & (s > c - w) & same) / w - (s == c)
        for hf in range(2):
            hr = np.arange(128)[:, None]
            jj = hr // 15 + hf * 8; rr = hr % 15
            pos_h = rr - 15
            ci = c % 8; cj = c // 8
            m[b + 4 + hf] = ((hr < 120) & (jj == cj) & (pos_h > ci - w)) / w
    cm = np.zeros((128, 16, 128), np.float32)
    for j in range(16):
        cm[:, j, 8 * j:8 * j + 8] = 1.0
    rm = np.zeros((128, 16), np.float32)
    for j in range(16):
        rm[8 * j:8 * j + 8, j] = 1.0
    return m, cm.reshape(128, 2048), rm


_NC = None


def kernel(x_prompt, x_sample, state_gla, state_pool, c_prompt, c_sample, w_ada, b_ada, norm1_g, w_in, w_a2, b_a,
           gla_norm_g, w_pool, pool_scale, w_o, norm2_g, w_gu, w_down, final_norm_g):
    global _NC
    f = lambda a: np.ascontiguousarray(np.asarray(a, dtype=np.float32))
    x_prompt, x_sample, state_gla, state_pool, c_prompt, c_sample = map(f, (x_prompt, x_sample, state_gla, state_pool, c_prompt, c_sample))
    shared = {
        "w_ada": f(w_ada), "b_ada": f(b_ada), "norm1_g": f(norm1_g), "w_in": f(w_in), "w_a2": f(w_a2), "b_a": f(b_a),
        "gla_norm_g": f(gla_norm_g), "w_pool": f(w_pool).reshape(DEPTH, 1024, 512), "pool_scale": f(pool_scale), "w_o": f(w_o),
        "norm2_g": f(norm2_g), "w_gu": f(w_gu), "w_down": f(w_down), "final_norm_g": f(final_norm_g).reshape(1, D),
    }
    cst = [_consts(True), _consts(False)]
    in_maps = []
    for c in range(NCORE):
        b, half = c // 2, c % 2
        sl = slice(SPC * c, SPC * (c + 1))
        m = dict(shared)
        m["cmats"], m["cmk"], m["rmk"] = cst[half][0], cst[half][1], cst[half][2]
        m["flag"] = np.full((128, 1), float(half), np.float32)
        m["xin"] = np.concatenate([x_prompt[b, half * SEQ:(half + 1) * SEQ], x_sample[sl].reshape(SPC * 8, D)], axis=0)
        m["crow"] = np.concatenate([np.repeat(c_prompt[b:b + 1], 128, axis=0), np.repeat(c_sample[sl], 8, axis=0)], axis=0)
        m["sgla"] = np.ascontiguousarray(state_gla[:, sl])
        m["spool"] = np.ascontiguousarray(state_pool[:, sl]).reshape(DEPTH, SPC * 15, 1024)
        in_maps.append(m)
    if _NC is None:
        _NC = build_program()
    res = run_bass_kernel_spmd(_NC, in_maps, core_ids=list(range(NCORE)))
    R = res.results
    y_prompt = np.stack([np.concatenate([R[2 * b]["y"][:SEQ], R[2 * b + 1]["y"][:SEQ]], axis=0) for b in range(4)])
    y_sample = np.concatenate([R[c]["y"][SEQ:].reshape(SPC, 8, D) for c in range(NCORE)], axis=0)
    gla_pp = np.stack([R[2 * b + 1]["gla_p"] for b in range(4)], axis=1)
    pool_pp = np.stack([R[2 * b + 1]["pool_p"] for b in range(4)], axis=1)
    gla_ss = np.concatenate([R[c]["gla_s"] for c in range(NCORE)], axis=1)
    pool_ss = np.concatenate([R[c]["pool_s"] for c in range(NCORE)], axis=1)
    return (y_prompt.astype(np.float32), y_sample.astype(np.float32), gla_pp.astype(np.float32), pool_pp.astype(np.float32),
            gla_ss.astype(np.float32), pool_ss.astype(np.float32))
```

```python
import contextlib
import numpy as np
import concourse.bass as bass
import concourse.mybir as mybir
from concourse.bass_utils import run_bass_kernel_spmd

F32 = mybir.dt.float32
BF16 = mybir.dt.bfloat16
ALU = mybir.AluOpType
AF = mybir.ActivationFunctionType

D = 2048
SEQ = 1024
NCORE = 8
SPC = 16
T = SEQ + SPC * 8
NT = T // 128
NPT = SEQ // 128
NMAT = 5 + 28
NIN = 11280
DFF = 5632
DEPTH = 2
OQ, OK_, OV, OOG, OU, OGA, OGB, OALR = 0, 1024, 2048, 4096, 6144, 7168, 9216, 11264
WINS = (2, 4, 8, 16)
EPS = 1e-6


class Tracker:
    SEM_CAP = 30000

    def __init__(self, nc, stack, n_sems):
        self.nc = nc
        self.engs = {"pe": nc.tensor, "act": nc.scalar, "dve": nc.vector, "pool": nc.gpsimd, "sp": nc.sync}
        self.free_sems = [stack.enter_context(nc.semaphore(f"s{i}")) for i in range(n_sems)]
        self.cur = {}
        self.seen = {e: {} for e in self.engs}
        self.lastw = {}
        self.readers = {}
        self.dsem = {}
        self._pend = []

    def wait(self, en, tok):
        if tok is None:
            return
        sem, val = tok
        sid = id(sem)
        if self.seen[en].get(sid, 0) >= val:
            return
        self.engs[en].wait_ge(sem, val)
        self.seen[en][sid] = val

    def _deps(self, en, reads, writes):
        for k in reads:
            self.wait(en, self.lastw.get(k))
        for k in writes:
            self.wait(en, self.lastw.get(k))
            for r in self.readers.get(k, ()):
                self.wait(en, r)

    def _commit(self, tok, reads, writes):
        for k in reads:
            self.readers.setdefault(k, []).append(tok)
        for k in writes:
            self.lastw[k] = tok
            self.readers[k] = []

    def op(self, en, fn, reads=(), writes=()):
        self._deps(en, reads, writes)
        ins = fn(self.engs[en])
        c = self.cur.get(en)
        if c is None or c[1] >= self.SEM_CAP:
            c = [self.free_sems.pop(), 0]
            self.cur[en] = c
        c[1] += 1
        ins.then_inc(c[0], 1)
        tok = (c[0], c[1])
        self._commit(tok, reads, writes)
        return tok

    def pe(self, fn, reads=(), writes=(), mark=True):
        if not mark:
            self._deps("pe", reads, writes)
            fn(self.engs["pe"])
            self._pend.append((tuple(reads), tuple(writes)))
            return None
        tok = self.op("pe", fn, reads, writes)
        for r, w in self._pend:
            self._commit(tok, r, w)
        self._pend = []
        return tok

    def collective(self, fn, reads=(), writes=()):
        self._deps("pool", reads, writes)
        ins = fn(self.engs["pool"])
        sem = self.free_sems.pop()
        ins.then_inc(sem)
        tok = (sem, 1)
        self._commit(tok, reads, writes)
        return tok

    def dma(self, q, semkey, out, in_, reads=(), writes=()):
        d = self.dsem.get(semkey)
        if d is None:
            d = [self.free_sems.pop(), 0]
            self.dsem[semkey] = d
        if d[1] > 0:
            self.wait(q, (d[0], d[1]))
        self._deps(q, reads, writes)
        ins = self.engs[q].dma_start(out=out, in_=in_)
        d[1] += 16
        ins.then_inc(d[0], 16)
        tok = (d[0], d[1])
        self._commit(tok, reads, writes)
        return tok


def build_program():
    nc = bass.Bass("TRN2", target_bir_lowering=False)
    din = lambda n, s: nc.dram_tensor(n, s, F32, kind="ExternalInput").ap()
    dout = lambda n, s: nc.dram_tensor(n, s, F32, kind="ExternalOutput").ap()
    xin = din("xin", [T, D]); crow = din("crow", [256, D])
    sgla = din("sgla", [DEPTH, SPC, 4, 256, 512]); spool = din("spool", [DEPTH, SPC * 15, 1024])
    w_ada = din("w_ada", [DEPTH, D, 6 * D]); b_ada = din("b_ada", [DEPTH, 6 * D]); n1g = din("norm1_g", [DEPTH, D])
    w_in = din("w_in", [DEPTH, D, NIN]); w_a2 = din("w_a2", [DEPTH, 16, 1024]); b_a = din("b_a", [DEPTH, 1024])
    ggla = din("gla_norm_g", [DEPTH, 512]); w_pool = din("w_pool", [DEPTH, 1024, 512]); pscale = din("pool_scale", [DEPTH, D])
    w_o = din("w_o", [DEPTH, D, D]); n2g = din("norm2_g", [DEPTH, D]); w_gu = din("w_gu", [DEPTH, D, 2 * DFF])
    w_down = din("w_down", [DEPTH, DFF, D]); fng = din("final_norm_g", [1, D])
    cmats = din("cmats", [NMAT, 128, 128]); flag_d = din("flag", [128, 1]); cmk = din("cmk", [128, 16 * 128]); rmk = din("rmk", [128, 16])
    y = dout("y", [T, D]); gla_p = dout("gla_p", [DEPTH, 4, 256, 512]); pool_p = dout("pool_p", [DEPTH, 15, 1024])
    gla_s = dout("gla_s", [DEPTH, SPC, 4, 256, 512]); pool_s = dout("pool_s", [DEPTH, SPC, 15, 1024])
    scr = lambda n, s: nc.dram_tensor(n, s, F32).ap()
    p_scr = scr("p_scr", [T, NIN]); gu_scr = scr("gu_scr", [T, 2 * DFF]); o_scr = scr("o_scr", [T, D])
    qg_scr = nc.dram_tensor("qg_scr", [4, NPT, 128, 256], BF16).ap()
    ex_in_t = [nc.dram_tensor(f"ex_in{l}", [512, 1024], F32) for l in range(DEPTH)]
    ex_out_t = [nc.dram_tensor(f"ex_out{l}", [1024, 1024], F32) for l in range(DEPTH)]
    exu_in_t = [nc.dram_tensor(f"exu_in{l}", [128, 1024], F32) for l in range(DEPTH)]
    exu_out_t = [nc.dram_tensor(f"exu_out{l}", [256, 1024], F32) for l in range(DEPTH)]
    ex_in = [t_.ap() for t_ in ex_in_t]; ex_out = [t_.ap() for t_ in ex_out_t]
    exu_in = [t_.ap() for t_ in exu_in_t]; exu_out = [t_.ap() for t_ in exu_out_t]
    xa = scr("xa", [T, D]); xb = scr("xb", [T, D]); xc = scr("xc", [T, D]); mod_scr = scr("mod_scr", [DEPTH, 256, 6 * D])

    with contextlib.ExitStack() as st:
        TR = Tracker(nc, st, 100)
        cnt = [0]

        def sb(shape, dt, name=None):
            cnt[0] += 1
            return st.enter_context(nc.sbuf_tensor(name or f"t{cnt[0]}", shape, dt))

        actT = sb([128, 16, T], BF16, "actT")
        NW = 2
        wbuf = [sb([128, 16, 512], BF16, f"wb{i}") for i in range(NW)]
        psf = [st.enter_context(nc.psum_tensor(f"psf{i}", [128, 512], F32)) for i in range(6)]
        psb = [st.enter_context(nc.psum_tensor(f"psb{i}", [128, 1024], BF16)) for i in range(2)]
        rot = {"f": 0, "b": 0, "w": 0}

        held = set()

        def fbank():
            while True:
                i = rot["f"] % 6; rot["f"] += 1
                if i not in held:
                    return psf[i], f"psf{i}"

        def bbank():
            i = rot["b"] % 2; rot["b"] += 1
            return psb[i], f"psb{i}"

        pools = {}

        def stage(tag, shape, dt, n=2):
            if tag not in pools:
                pools[tag] = [[sb(shape, dt, f"{tag}{i}") for i in range(n)], 0]
            p = pools[tag]
            i = p[1] % n; p[1] += 1
            return p[0][i], f"{tag}{i}"

        st.enter_context(nc.Block())

        Bbuf = sb([128, 8, 1024], F32, "Bbuf")
        BK = [f"B{i}" for i in range(8)]
        cm_f = Bbuf[:, 0:2, :].rearrange("p a c -> p (a c)")
        hb = sb([128, D], BF16, "hb")
        mb = hb
        idb = sb([128, 128], BF16, "idb")
        cmat_b = sb([128, NMAT, 128], BF16, "cmat_b")
        flag = sb([128, 1], F32, "flag_sb")
        Gst = sb([128, 2], F32, "Gst"); Gtot = sb([128, 4, 2], F32, "Gtot")
        cm_b = sb([128, 16, 128], BF16, "cm_b")
        rm_b = sb([128, 16], BF16, "rm_b")
        ones_b = sb([128, 128], BF16, "ones_b")
        rs = sb([128, 8], F32, "rs")
        junk = hb
        TR.dma("sp", "cl", flag[:], flag_d[:, :], writes=["flag"])
        cst_f = Bbuf[:, 0:5, :].rearrange("p a c -> p (a c)")[:, 0:NMAT * 128].rearrange("p (n c) -> p n c", c=128)
        TR.dma("sp", "cl", cst_f, cmats.rearrange("n p c -> p n c"), writes=["B0", "B1", "B2", "B3", "B4"])
        TR.op("dve", lambda e: e.tensor_copy(out=cmat_b[:], in_=cst_f), reads=["B0", "B1", "B2", "B3", "B4"], writes=["cmat"])
        TR.op("dve", lambda e: e.tensor_copy(out=idb[:], in_=cmat_b[:, 0, :]), reads=["cmat"], writes=["idb"])
        TR.dma("sp", "cl", cm_f, cmk[:, :], writes=["B0", "B1"])
        TR.op("dve", lambda e: e.tensor_copy(out=cm_b[:].rearrange("p j c -> p (j c)"), in_=cm_f), reads=["B0", "B1"], writes=["cm_b"])
        TR.dma("sp", "cl", cm_f[:, 0:16], rmk[:, :], reads=[], writes=["B0", "B1"])
        TR.op("dve", lambda e: e.tensor_copy(out=rm_b[:], in_=cm_f[:, 0:16]), reads=["B0", "B1"], writes=["rm_b"])
        TR.op("dve", lambda e: e.memset(ones_b[:], 1.0), writes=["ones_b"])
        MINCL = {0: 1, 1: 3}
        MREV = {0: 2, 1: 4}

        def to_feat(src_b, src_key, kc, t, dst=None, dst_key="actT"):
            dst = actT if dst is None else dst
            for k0 in range(0, kc, 8):
                kn = min(8, kc - k0)
                bk, bkey = bbank()
                for k in range(kn):
                    TR.pe(lambda e, k=k: e.transpose(bk[:, k * 128:(k + 1) * 128], src_b[:, (k0 + k) * 128:(k0 + k + 1) * 128], idb[:]),
                          reads=[src_key, "idb"], writes=[bkey], mark=(k == kn - 1))
                TR.op("act", lambda e: e.copy(out=dst[:, k0:k0 + kn, t * 128:(t + 1) * 128],
                                              in_=bk[:, 0:kn * 128].rearrange("p (k c) -> p k c", c=128)),
                      reads=[bkey], writes=[(dst_key, t)])

        def linear(*a, **k):
            for _ in linear_gen(*a, **k):
                pass

        fill_state = {"gen": None}

        def fillf(n=1):
            g = fill_state["gen"]
            for _ in range(n):
                if g is None:
                    return
                try:
                    next(g)
                except StopIteration:
                    fill_state["gen"] = None
                    return

        bg_state = {"gen": None, "on": False}

        def chain_gens(*gens):
            for g_ in gens:
                yield from g_

        def bgf(n=1):
            g = bg_state["gen"]
            for _ in range(n):
                if g is None:
                    return
                try:
                    next(g)
                except StopIteration:
                    bg_state["gen"] = None
                    return

        def linear_gen(kc, tiles, W, r0, c0, ncols, epi, pw=512, lhs=None, lhs_key="actT", own_w=None, bg=False):
            groups = [(t, c, min(pw, c0 + ncols - c)) for c in range(c0, c0 + ncols, pw) for t in tiles]
            pre = getattr(epi, "pre", None)
            if pre:
                pre(*groups[0])
            wb = wkey = None
            lhs = actT if lhs is None else lhs
            for gi, (t, c, pc) in enumerate(groups):
                if bg:
                    bgf(1)
                if t == tiles[0]:
                    if own_w is not None:
                        wb, wkey, wi = own_w, "wown", "own"
                    else:
                        wi = rot["w"] % NW; rot["w"] += 1
                        wb, wkey = wbuf[wi], f"wb{wi}"
                    TR.dma("pool", f"w{wi}", wb[:, 0:kc, 0:pc], W[r0:r0 + kc * 128, c:c + pc].rearrange("(k p) n -> p k n", p=128), writes=[wkey])
                ps, pkey = fbank()
                for k in range(kc):
                    TR.pe(lambda e, k=k: e.matmul(ps[:, 0:pc], lhsT=lhs[:, k, t * 128:(t + 1) * 128], rhs=wb[:, k, 0:pc],
                                                  start=(k == 0), stop=(k == kc - 1)),
                          reads=[(lhs_key, t), wkey], writes=[pkey], mark=(k == kc - 1))
                if pre and gi + 1 < len(groups):
                    pre(*groups[gi + 1])
                epi(t, c, pc, ps, pkey)
                yield

        def store_epi(dst, dkey):
            def epi(t, c, pc, ps, pkey):
                sg, skey = stage("sto", [128, 512], F32, 2)
                TR.op("act", lambda e: e.copy(out=sg[:, 0:pc], in_=ps[:, 0:pc]), reads=[pkey], writes=[skey])
                TR.dma("sp", "st" + skey, dst[t * 128:(t + 1) * 128, c:c + pc], sg[:, 0:pc], reads=[skey], writes=[(dkey, t)])
            return epi

        def rstd_of(src, skey, n, col):
            TR.op("dve", lambda e: e.memset(rs[:, col + 1:col + 2], 0.0), writes=["rs"])
            TR.op("act", lambda e: e.activation(out=junk[:, 0:n], in_=src, func=AF.Square, accum_out=rs[:, col + 1:col + 2]),
                  reads=[skey, "rs"], writes=["hb", "rs"])
            TR.op("act", lambda e: e.activation(out=rs[:, col + 1:col + 2], in_=rs[:, col + 1:col + 2], func=AF.Ln, scale=1.0 / n, bias=EPS),
                  reads=["rs"], writes=["rs"])
            TR.op("act", lambda e: e.activation(out=rs[:, col:col + 1], in_=rs[:, col + 1:col + 2], func=AF.Exp, scale=-0.5),
                  reads=["rs"], writes=["rs"])

        xt = Bbuf[:, 0:2, :].rearrange("p a c -> p (a c)")
        ht = Bbuf[:, 2:4, :].rearrange("p a c -> p (a c)")
        rowA = Bbuf[:, 4:6, :].rearrange("p a c -> p (a c)")
        rowB = Bbuf[:, 6:8, :].rearrange("p a c -> p (a c)")
        XT, HT, RA, RB = ["B0", "B1"], ["B2", "B3"], ["B4", "B5"], ["B6", "B7"]
        gT = sb([128, 16], F32, "gT"); scT = sb([128, 16], F32, "scT"); shT = sb([128, 16], F32, "shT")

        def norm_stage(l, xsrc, xkey, gain, gl, c_sh, c_sc):
            with nc.allow_non_contiguous_dma(reason="tiny per-feature vectors"):
                TR.dma("sp", "ld0", gT[:], gain[gl, :].rearrange("(k p) -> p k", p=128), writes=["gT"])
                TR.dma("sp", "ld1", scT[:], mod_scr[l, 0, c_sc:c_sc + D].rearrange("(k p) -> p k", p=128), reads=[("mod", l, c_sc // (3 * D))], writes=["scT"])
                TR.dma("sp", "ld2", shT[:], mod_scr[l, 0, c_sh:c_sh + D].rearrange("(k p) -> p k", p=128), reads=[("mod", l, c_sh // (3 * D))], writes=["shT"])
            TR.op("dve", lambda e: e.scalar_tensor_tensor(out=scT[:], in0=scT[:], scalar=1.0, in1=gT[:], op0=ALU.add, op1=ALU.mult),
                  reads=["scT", "gT"], writes=["scT"])
            xalt = Bbuf[:, 4:6, :].rearrange("p a c -> p (a c)")

            def xbuf(t_):
                return (xt, XT) if (t_ % 2 == 0 or t_ == NT - 1) else (xalt, RA)

            def xload(t_):
                xb_, xk_ = xbuf(t_)
                TR.dma("sp", "ldx" + str(t_ % 2), xb_, xsrc[t_ * 128:(t_ + 1) * 128, :], reads=[(xkey, t_)], writes=xk_)

            xload(0)
            for t in range(NT):
                if t + 1 < NT:
                    xload(t + 1)
                xt_, XT_ = xbuf(t)
                rstd_of(xt_, XT_[0], D, 0)
                if t < NT - 1:
                    TR.op("dve", lambda e: e.tensor_scalar(out=hb[:], in0=xt_, scalar1=rs[:, 0:1], scalar2=None, op0=ALU.mult),
                          reads=XT_ + ["rs"], writes=["hb"])
                    for k0 in range(0, 16, 8):
                        bk, bkey = bbank()
                        for k in range(8):
                            TR.pe(lambda e, k=k: e.transpose(bk[:, k * 128:(k + 1) * 128], hb[:, (k0 + k) * 128:(k0 + k + 1) * 128], idb[:]),
                                  reads=["hb", "idb"], writes=[bkey], mark=(k == 7))
                        for k in range(8):
                            TR.op("act", lambda e, k=k: e.activation(out=actT[:, k0 + k, t * 128:(t + 1) * 128], in_=bk[:, k * 128:(k + 1) * 128],
                                                                     func=AF.Identity, scale=scT[:, k0 + k:k0 + k + 1], bias=shT[:, k0 + k:k0 + k + 1]),
                                  reads=[bkey, "scT", "shT"], writes=[("actT", t)])
                else:
                    TR.dma("sp", "ld0", ht, gain[gl:gl + 1, :].partition_broadcast(128), writes=HT)
                    TR.dma("sp", "ld1", rowA, mod_scr[l, 128:256, c_sc:c_sc + D], reads=[("mod", l, c_sc // (3 * D))], writes=RA)
                    TR.dma("sp", "ld2", rowB, mod_scr[l, 128:256, c_sh:c_sh + D], reads=[("mod", l, c_sh // (3 * D))], writes=RB)
                    TR.op("dve", lambda e: e.scalar_tensor_tensor(out=rowA, in0=rowA, scalar=1.0, in1=ht, op0=ALU.add, op1=ALU.mult),
                          reads=RA + HT, writes=RA)
                    TR.op("dve", lambda e: e.scalar_tensor_tensor(out=ht, in0=xt, scalar=rs[:, 0:1], in1=rowA, op0=ALU.mult, op1=ALU.mult),
                          reads=XT + ["rs"] + RA, writes=HT)
                    TR.op("dve", lambda e: e.tensor_tensor(out=hb[:], in0=ht, in1=rowB, op=ALU.add), reads=HT + RB, writes=["hb"])
                    to_feat(hb, "hb", 16, t)

        brow = sb([128, 512], F32, "brow")
        csT = sb([128, 16, 256], BF16, "csT")
        for t in range(2):
            TR.dma("sp", "ldx", xt, crow[t * 128:(t + 1) * 128, :], writes=XT)
            TR.op("act", lambda e: e.activation(out=ht, in_=xt, func=AF.Sigmoid), reads=XT, writes=HT)
            TR.op("dve", lambda e: e.tensor_tensor(out=hb[:], in0=ht, in1=xt, op=ALU.mult), reads=HT + XT, writes=["hb"])
            to_feat(hb, "hb", 16, t, dst=csT, dst_key="csT")

        def epi_mod(l):
            def epi(t, c, pc, ps, pkey):
                if t == 0:
                    TR.dma("sp", "ldb", brow[:, 0:pc], b_ada[l:l + 1, c:c + pc].partition_broadcast(128), writes=["brow"])
                sg, skey = stage("sto", [128, 512], F32, 2)
                TR.op("dve", lambda e: e.tensor_tensor(out=sg[:, 0:pc], in0=ps[:, 0:pc], in1=brow[:, 0:pc], op=ALU.add),
                      reads=[pkey, "brow"], writes=[skey])
                TR.dma("sp", "st" + skey, mod_scr[l, t * 128:(t + 1) * 128, c:c + pc], sg[:, 0:pc], reads=[skey], writes=[("mod", l, c // (3 * D))])
            return epi

        linear(16, [0, 1], w_ada[0], 0, 0, 3 * D, epi_mod(0), lhs=csT, lhs_key="csT")

        def ada_bg(l, c0, ncols):
            return linear_gen(16, [0, 1], w_ada[l], 0, c0, ncols, epi_mod(l), lhs=csT, lhs_key="csT")

        qkvs = [sb([128, 1024], F32, f"qkv{i}") for i in range(2)]
        alrs = [sb([128, 16], F32, f"alr_f{i}") for i in range(2)]
        qkv = qkvs[0]
        qb = sb([128, 256], BF16, "qb"); kb = sb([128, 256], BF16, "kb"); vb = sb([128, 512], BF16, "vb"); ab = sb([128, 16], BF16, "ab")
        alrT = sb([17, 128], BF16, "alrT")
        wa2f = Bbuf[0:17, 0, :]; wa2b = sb([17, 1024], BF16, "wa2b")
        TR.op("dve", lambda e: e.memset(alrT[:], 1.0), writes=["alrT"])
        ef = sb([128, 256], F32, "ef"); lgb = sb([128, 256], BF16, "lgb")
        E1 = sb([128, 256], F32, "E1"); E2 = sb([128, 256], F32, "E2"); E3 = sb([128, 256], F32, "E3")
        qe = sb([128, 2, 128], BF16, "qe"); ke = sb([128, 2, 128], BF16, "ke"); kd = sb([128, 256], BF16, "kd")
        attm = sb([128, 128], BF16, "attm")
        Sf = sb([128, 2, 512], F32, "Sf"); Sb = sb([128, 2, 512], BF16, "Sb")

        onf = sb([128, 512], F32, "onf")
        ggrow = sb([128, 512], F32, "ggrow")

        def gla_stage(l):
            TR.dma("sp", "ld0", Bbuf[0:16, 0, :], w_a2[l], writes=["B0"])
            TR.dma("sp", "ld1", Bbuf[16:17, 0, :], b_a[l:l + 1, :], writes=["B0"])
            TR.op("dve", lambda e: e.tensor_copy(out=wa2b[:], in_=wa2f), reads=["B0"], writes=["wa2b"])
            TR.dma("sp", "ld2", ggrow[:], ggla[l:l + 1, :].partition_broadcast(128), writes=["ggrow"])
            order = [(h_, t_) for h_ in range(4) for t_ in range(NT)]

            def gla_loads(i):
                h_, t_ = order[i]
                r_ = slice(t_ * 128, (t_ + 1) * 128)
                q_, a_, sfx = qkvs[i % 2], alrs[i % 2], str(i % 2)
                TR.dma("sp", "lq0" + sfx, q_[:, 0:256], p_scr[r_, OQ + h_ * 256:OQ + (h_ + 1) * 256], reads=[("p", t_)], writes=["qkv" + sfx])
                TR.dma("sp", "lq1" + sfx, q_[:, 256:512], p_scr[r_, OK_ + h_ * 256:OK_ + (h_ + 1) * 256], reads=[("p", t_)], writes=["qkv" + sfx])
                TR.dma("sp", "lq2" + sfx, q_[:, 512:1024], p_scr[r_, OV + h_ * 512:OV + (h_ + 1) * 512], reads=[("p", t_)], writes=["qkv" + sfx])
                TR.dma("sp", "lq3" + sfx, a_[:], p_scr[r_, OALR:OALR + 16], reads=[("p", t_)], writes=["alr_f" + sfx])

            gla_loads(0)
            for h in range(4):
                TR.op("dve", lambda e: e.memset(Sf[:], 0.0), writes=["Sf"])
                TR.op("dve", lambda e: e.memset(Sb[:], 0.0), writes=["Sb"])
                TR.op("dve", lambda e: e.memset(Gst[:], 1.0), writes=["Gst"])
                for t in range(NT):
                    ty = 1 if t == NT - 1 else 0
                    r = slice(t * 128, (t + 1) * 128)
                    gi_ = h * NT + t
                    if gi_ + 1 < len(order):
                        gla_loads(gi_ + 1)
                    qkv_, alr_, sfx = qkvs[gi_ % 2], alrs[gi_ % 2], str(gi_ % 2)
                    TR.op("dve", lambda e: e.tensor_scalar(out=qb[:], in0=qkv_[:, 0:256], scalar1=0.0625, scalar2=None, op0=ALU.mult),
                          reads=["qkv" + sfx], writes=["qb"])
                    TR.op("dve", lambda e: e.tensor_copy(out=kb[:], in_=qkv_[:, 256:512]), reads=["qkv" + sfx], writes=["kb"])
                    TR.op("dve", lambda e: e.tensor_copy(out=vb[:], in_=qkv_[:, 512:1024]), reads=["qkv" + sfx], writes=["vb"])
                    TR.op("dve", lambda e: e.tensor_copy(out=ab[:], in_=alr_[:]), reads=["alr_f" + sfx], writes=["ab"])
                    bk, bkey = bbank()
                    for ch in range(2):
                        TR.pe(lambda e, ch=ch: e.transpose(bk[:, ch * 128:(ch + 1) * 128], qb[:, ch * 128:(ch + 1) * 128], idb[:]),
                              reads=["qb", "idb"], writes=[bkey], mark=False)
                        TR.pe(lambda e, ch=ch: e.transpose(bk[:, 256 + ch * 128:256 + (ch + 1) * 128], kb[:, ch * 128:(ch + 1) * 128], idb[:]),
                              reads=["kb", "idb"], writes=[bkey], mark=False)
                    TR.pe(lambda e: e.transpose(bk[0:16, 512:640], ab[:, 0:16], idb[:]), reads=["ab", "idb"], writes=[bkey])
                    TR.op("act", lambda e: e.copy(out=alrT[0:16, :], in_=bk[0:16, 512:640]), reads=[bkey], writes=["alrT"])
                    fillf()
                    g, gkey = fbank()
                    TR.pe(lambda e: e.matmul(g[:, 0:256], lhsT=alrT[0:17, :], rhs=wa2b[0:17, h * 256:(h + 1) * 256], start=True, stop=True),
                          reads=["alrT", "wa2b"], writes=[gkey])
                    TR.op("act", lambda e: e.activation(out=ef[:], in_=g[:, 0:256], func=AF.Exp, scale=-1.0), reads=[gkey], writes=["ef"])
                    TR.op("act", lambda e: e.activation(out=ef[:], in_=ef[:], func=AF.Ln, bias=1.0), reads=["ef"], writes=["ef"])
                    TR.op("dve", lambda e: e.tensor_scalar(out=lgb[:], in0=ef[:], scalar1=-1.0 / 16, scalar2=None, op0=ALU.mult),
                          reads=["ef"], writes=["lgb"])
                    fillf()
                    B, Bkey = fbank()
                    for ch in range(2):
                        TR.pe(lambda e, ch=ch: e.matmul(B[:, ch * 128:(ch + 1) * 128], lhsT=lgb[:, ch * 128:(ch + 1) * 128],
                                                        rhs=cmat_b[:, MINCL[ty], :], start=True, stop=True),
                              reads=["lgb", "cmat"], writes=[Bkey], mark=False)
                    TR.pe(lambda e: e.matmul(B[:, 256:512], lhsT=cmat_b[:, MREV[ty], :], rhs=lgb[:], start=True, stop=True),
                          reads=["lgb", "cmat"], writes=[Bkey])
                    TR.op("act", lambda e: e.activation(out=E1[:], in_=B[:, 0:256], func=AF.Exp), reads=[Bkey], writes=["E1"])
                    TR.op("act", lambda e: e.activation(out=E2[:], in_=B[:, 0:256], func=AF.Exp, scale=-1.0), reads=[Bkey], writes=["E2"])
                    TR.op("act", lambda e: e.activation(out=E3[:], in_=B[:, 256:512], func=AF.Exp), reads=[Bkey], writes=["E3"])
                    TR.op("dve", lambda e: e.tensor_tensor(out=qe[:].rearrange("p c k -> p (c k)"), in0=bk[:, 0:256], in1=E1[:], op=ALU.mult),
                          reads=[bkey, "E1"], writes=["qe"])
                    TR.op("dve", lambda e: e.tensor_tensor(out=ke[:].rearrange("p c k -> p (c k)"), in0=bk[:, 256:512], in1=E2[:], op=ALU.mult),
                          reads=[bkey, "E2"], writes=["ke"])
                    TR.op("dve", lambda e: e.tensor_tensor(out=kd[:], in0=kb[:], in1=E3[:], op=ALU.mult), reads=["kb", "E3"], writes=["kd"])
                    if ty == 0:
                        qg, qgkey = stage("qg", [128, 2, 128], BF16, 2)
                        for ch in range(2):
                            TR.op("dve", lambda e, ch=ch: e.tensor_scalar(out=qg[:, ch, :], in0=qe[:, ch, :], scalar1=Gst[:, ch:ch + 1], scalar2=None, op0=ALU.mult),
                                  reads=["qe", "Gst"], writes=[qgkey])
                        TR.dma("sp", "s" + qgkey, qg_scr[h, t], qg[:].rearrange("p c k -> p (c k)"), reads=[qgkey], writes=[("qg", h, t)])
                    fillf()
                    A, Akey = fbank()
                    for ch in range(2):
                        TR.pe(lambda e, ch=ch: e.matmul(A[:, 0:128], lhsT=ke[:, ch, :], rhs=qe[:, ch, :], start=(ch == 0), stop=(ch == 1)),
                              reads=["ke", "qe"], writes=[Akey], mark=(ch == 1))
                    TR.op("dve", lambda e: e.tensor_tensor(out=attm[:], in0=A[:, 0:128], in1=cmat_b[:, MINCL[ty], :], op=ALU.mult),
                          reads=[Akey, "cmat"], writes=["attm"])
                    fillf()
                    O, Okey = fbank()
                    if ty == 0:
                        TR.pe(lambda e: e.matmul(O[:, :], lhsT=attm[:], rhs=vb[:], start=True, stop=False), reads=["attm", "vb"], writes=[Okey], mark=False)
                        for ch in range(2):
                            TR.pe(lambda e, ch=ch: e.matmul(O[:, :], lhsT=qe[:, ch, :], rhs=Sb[:, ch, :], start=False, stop=(ch == 1)),
                                  reads=["qe", "Sb"], writes=[Okey], mark=(ch == 1))
                        for ch in range(2):
                            U, Ukey = fbank()
                            TR.pe(lambda e, ch=ch: e.matmul(U[:, :], lhsT=kd[:, ch * 128:(ch + 1) * 128], rhs=vb[:], start=True, stop=True),
                                  reads=["kd", "vb"], writes=[Ukey])
                            TR.op("dve", lambda e, ch=ch: e.scalar_tensor_tensor(out=Sf[:, ch, :], in0=Sf[:, ch, :], scalar=E1[:, ch * 128 + 127:ch * 128 + 128],
                                                                                 in1=U[:, :], op0=ALU.mult, op1=ALU.add),
                                  reads=["Sf", "E1", Ukey], writes=["Sf"])
                        TR.op("act", lambda e: e.copy(out=Sb[:], in_=Sf[:]), reads=["Sf"], writes=["Sb"])
                        TR.op("dve", lambda e: e.tensor_tensor(out=Gst[:], in0=Gst[:], in1=E1[:].rearrange("p (c k) -> p c k", k=128)[:, :, 127], op=ALU.mult),
                              reads=["Gst", "E1"], writes=["Gst"])
                        if t == NPT - 1:
                            TR.dma("sp", "stS", exS(ex_in[l], h), Sf[:], reads=["Sf"], writes=[("exin", l)])
                            TR.op("dve", lambda e: e.tensor_copy(out=Gtot[:, h, :], in_=Gst[:]), reads=["Gst"], writes=["Gtot"])
                    else:
                        held.add(int(Okey[3:]))
                        TR.pe(lambda e: e.matmul(O[:, :], lhsT=attm[:], rhs=vb[:], start=True, stop=False), reads=["attm", "vb"], writes=[Okey], mark=False)
                        def s0_load(j_):
                            s0_, k_ = stage("s0f", [128, 2, 512], F32, 3)
                            for ch_ in range(2):
                                TR.dma("sp", f"l{k_}{ch_}", s0_[:, ch_, :], sgla[l, j_, h, ch_ * 128:(ch_ + 1) * 128, :], writes=[k_ + str(ch_)])
                            return s0_, k_

                        nxt_s0 = s0_load(0)
                        for j in range(SPC):
                            s0, s0key = nxt_s0
                            if j + 1 < SPC:
                                nxt_s0 = s0_load(j + 1)
                            s0b, s0bkey = stage("s0b", [128, 2, 512], BF16, 2)
                            qm, qmkey = stage("qm", [128, 2, 128], BF16, 2)
                            km, kmkey = stage("km", [128, 256], BF16, 2)
                            TR.op("act", lambda e: e.copy(out=s0b[:], in_=s0[:]), reads=[s0key + "0", s0key + "1"], writes=[s0bkey])
                            TR.op("dve", lambda e, j=j: e.tensor_tensor(out=qm[:], in0=qe[:], in1=cm_b[:, j, :].unsqueeze(1).to_broadcast([128, 2, 128]),
                                                                        op=ALU.mult), reads=["qe", "cm_b"], writes=[qmkey])
                            TR.op("dve", lambda e, j=j: e.tensor_scalar(out=km[:], in0=kd[:], scalar1=rm_b[:, j:j + 1], scalar2=None, op0=ALU.mult),
                                  reads=["kd", "rm_b"], writes=[kmkey])
                            for ch in range(2):
                                last = (j == SPC - 1 and ch == 1)
                                TR.pe(lambda e, ch=ch: e.matmul(O[:, :], lhsT=qm[:, ch, :], rhs=s0b[:, ch, :], start=False, stop=last),
                                      reads=[qmkey, s0bkey], writes=[Okey], mark=True)
                            for ch in range(2):
                                U, Ukey = fbank()
                                TR.pe(lambda e, ch=ch: e.matmul(U[:, :], lhsT=km[:, ch * 128:(ch + 1) * 128], rhs=vb[:], start=True, stop=True),
                                      reads=[kmkey, "vb"], writes=[Ukey])
                                cix = ch * 128 + 8 * j + 7
                                TR.op("dve", lambda e, ch=ch, cix=cix: e.scalar_tensor_tensor(out=s0[:, ch, :], in0=s0[:, ch, :], scalar=E1[:, cix:cix + 1],
                                                                                              in1=U[:, :], op0=ALU.mult, op1=ALU.add),
                                      reads=[s0key + str(ch), "E1", Ukey], writes=[s0key + str(ch)])
                            TR.dma("sp", "s" + s0key, gla_s[l, j, h].rearrange("(c p) e -> p c e", p=128), s0[:], reads=[s0key + "0", s0key + "1"],
                                   writes=[("glas", l, j, h)])
                    held.clear()
                    if ty == 0:
                        TR.op("act", lambda e: e.copy(out=onf[:], in_=O[:, :]), reads=[Okey], writes=["onf"])
                    else:
                        rstd_of(O[:, :], Okey, 512, 2)
                        TR.op("dve", lambda e: e.scalar_tensor_tensor(out=onf[:], in0=O[:, :], scalar=rs[:, 2:3], in1=ggrow[:], op0=ALU.mult, op1=ALU.mult),
                              reads=[Okey, "rs", "ggrow"], writes=["onf"])
                    TR.dma("sp", "sto_o", o_scr[r, h * 512:(h + 1) * 512], onf[:], reads=["onf"], writes=[("o", t)])

        def exS(buf, h, blk=0):
            return buf[blk * 512 + h * 128:blk * 512 + (h + 1) * 128, :].rearrange("r (two e) -> (r two) e", two=2).rearrange("(c p) e -> p c e", p=128)

        def exchange_stage(l):
            TR.dma("sp", "exu", exu_in[l][:, :], p_scr[SEQ - 128:SEQ, OU:OU + 1024], reads=[("p2", NPT - 1)], writes=[("exuin", l)])
            pairs = [[0, 1], [2, 3], [4, 5], [6, 7]]
            TR.collective(lambda g: g.collective_compute("AllGather", ALU.bypass, replica_groups=pairs,
                                                          ins=[exu_in_t[l].ap().opt()], outs=[exu_out_t[l].ap().opt()]),
                          reads=[("exuin", l)], writes=[("exuout", l)])
            TR.collective(lambda g: g.collective_compute("AllGather", ALU.bypass, replica_groups=pairs,
                                                          ins=[ex_in_t[l].ap().opt()], outs=[ex_out_t[l].ap().opt()]),
                          reads=[("exin", l)], writes=[("exout", l)])

        def correct_stage(l):
            for h in range(4):
                sp_, spkey = stage("s0f", [128, 2, 512], F32, 3)
                spb, spbkey = stage("s0b", [128, 2, 512], BF16, 2)
                sl, slkey = stage("s0f", [128, 2, 512], F32, 3)
                TR.dma("sp", "l" + spkey, sp_[:], exS(ex_out[l], h, 0), reads=[("exout", l)], writes=[spkey + "0", spkey + "1"])
                TR.op("dve", lambda e: e.tensor_scalar(out=sp_[:], in0=sp_[:], scalar1=flag[:, 0:1], scalar2=None, op0=ALU.mult),
                      reads=[spkey + "0", spkey + "1", "flag"], writes=[spkey + "0", spkey + "1"])
                TR.op("act", lambda e: e.copy(out=spb[:], in_=sp_[:]), reads=[spkey + "0", spkey + "1"], writes=[spbkey])
                TR.dma("sp", "l" + slkey, sl[:], exS(ex_in[l], h, 0), reads=[("exin", l)], writes=[slkey + "0", slkey + "1"])
                for ch in range(2):
                    TR.op("dve", lambda e, ch=ch: e.scalar_tensor_tensor(out=sl[:, ch, :], in0=sp_[:, ch, :], scalar=Gtot[:, h, ch:ch + 1], in1=sl[:, ch, :],
                                                                         op0=ALU.mult, op1=ALU.add), reads=[spkey + "0", spkey + "1", slkey + "0", slkey + "1", "Gtot"], writes=[slkey + "0", slkey + "1"])
                TR.dma("sp", "s" + slkey, gla_p[l, h].rearrange("(c p) e -> p c e", p=128), sl[:], reads=[slkey + "0", slkey + "1"], writes=[("glap", l, h)])
                crot = [0]

                def corr_loads(t_):
                    cq, cqk = stage("qm", [128, 2, 128], BF16, 2)
                    crot[0] += 1
                    co, cok = qkvs[crot[0] % 2][:, 0:512], "qkv" + str(crot[0] % 2)
                    TR.dma("sp", "l" + cqk, cq[:].rearrange("p c k -> p (c k)"), qg_scr[h, t_], reads=[("qg", h, t_)], writes=[cqk])
                    TR.dma("sp", "l" + cok, co, o_scr[t_ * 128:(t_ + 1) * 128, h * 512:(h + 1) * 512], reads=[("o", t_)], writes=[cok])
                    return cq, cqk, co, cok

                nxt_ld = corr_loads(0)
                for t in range(NPT):
                    r = slice(t * 128, (t + 1) * 128)
                    cq, cqk, co, cok = nxt_ld
                    if t + 1 < NPT:
                        nxt_ld = corr_loads(t + 1)
                    C, Ckey = fbank()
                    for ch in range(2):
                        TR.pe(lambda e, ch=ch: e.matmul(C[:, :], lhsT=cq[:, ch, :], rhs=spb[:, ch, :], start=(ch == 0), stop=(ch == 1)),
                              reads=[cqk, spbkey], writes=[Ckey], mark=(ch == 1))
                    TR.op("dve", lambda e: e.tensor_tensor(out=co, in0=C[:, :], in1=co, op=ALU.add),
                          reads=[Ckey, cok], writes=[cok])
                    rstd_of(co, cok, 512, 2)
                    TR.op("dve", lambda e: e.scalar_tensor_tensor(out=onf[:], in0=co, scalar=rs[:, 2:3], in1=ggrow[:], op0=ALU.mult, op1=ALU.mult),
                          reads=[cok, "rs", "ggrow"], writes=["onf"])
                    TR.dma("sp", "sto_o", o_scr[r, h * 512:(h + 1) * 512], onf[:], reads=["onf"], writes=[("o", t)])

        ogt = Bbuf[:, 0, :]; gat = Bbuf[:, 1, :]; gbt = Bbuf[:, 2, :]; ot = Bbuf[:, 3, :]; sgt = Bbuf[:, 4, :]; ut = Bbuf[:, 5, :]
        psrow = Bbuf[:, 6:8, :].rearrange("p a c -> p (a c)")
        ub = [sb([128, 1024], BF16, f"ub{i}") for i in range(2)]
        hist_b = [sb([120, 1024], BF16, f"hist_b{i}") for i in range(2)]
        dTb = sb([128, 8, 128], BF16, "dTb")
        wpool_b = sb([128, 8, 512], BF16, "wpool_b")

        def merge_stage(l):
            TR.dma("pool", "wp", wpool_b[:], w_pool[l].rearrange("(k p) o -> p k o", p=128), writes=["wpool"])
            TR.dma("sp", "ld0", psrow, pscale[l:l + 1, :].partition_broadcast(128), writes=["B6", "B7"])
            for hf in range(2):
                TR.dma("sp", "ld1", Bbuf[0:120, 5, :], spool[l, hf * 120:(hf + 1) * 120, :], writes=["B5"])
                TR.op("dve", lambda e, hf=hf: e.tensor_copy(out=hist_b[hf][:], in_=Bbuf[0:120, 5, :]), reads=["B5"], writes=[f"hist_b{hf}"])
            TR.dma("sp", "lm4", ut, exu_out[l][0:128, :], reads=[("exuout", l)], writes=["B5"])
            TR.op("dve", lambda e: e.tensor_copy(out=ub[1][:], in_=ut), reads=["B5"], writes=["ub1"])
            halves = [(t_, h_) for t_ in range(NT) for h_ in range(2)]
            done = set()

            def mload(idx, which):
                if idx >= len(halves) or (idx, which) in done:
                    return
                done.add((idx, which))
                t_, h_ = halves[idx]
                r_ = slice(t_ * 128, (t_ + 1) * 128)
                c_ = h_ * 1024
                if which == "og":
                    TR.dma("sp", "lm0", ogt, p_scr[r_, OOG + c_:OOG + c_ + 1024], reads=[("p2", t_)], writes=["B0"])
                elif which == "ga":
                    TR.dma("sp", "lm1", gat, p_scr[r_, OGA + c_:OGA + c_ + 1024], reads=[("p2", t_)], writes=["B1"])
                elif which == "gb":
                    TR.dma("sp", "lm2", gbt, p_scr[r_, OGB + c_:OGB + c_ + 1024], reads=[("p2", t_)], writes=["B2"])
                elif which == "o":
                    TR.dma("sp", "lm3", ot, o_scr[r_, c_:c_ + 1024], reads=[("o", t_)], writes=["B3"])

            def uload(t_):
                if t_ < NT and ("u", t_) not in done:
                    done.add(("u", t_))
                    TR.dma("sp", "lm4", ut, p_scr[t_ * 128:(t_ + 1) * 128, OU:OU + 1024], reads=[("p2", t_)], writes=["B5"])

            for t in range(NT):
                ty = 1 if t == NT - 1 else 0
                r = slice(t * 128, (t + 1) * 128)
                uload(t)
                cur, ckey = ub[t % 2], f"ub{t % 2}"
                prv, pvkey = ub[(t + 1) % 2], f"ub{(t + 1) % 2}"
                TR.op("dve", lambda e: e.tensor_copy(out=cur[:], in_=ut), reads=["B5"], writes=[ckey])
                uload(t + 1)
                dbanks = [fbank(), fbank()]
                for g in range(4):
                    Dk, Dkey = dbanks[g // 2]
                    base = 5 + g * 7
                    for ic in range(2):
                        col = ((g % 2) * 2 + ic) * 128
                        cs = slice(g * 256 + ic * 128, g * 256 + (ic + 1) * 128)
                        lastg = (g % 2 == 1 and ic == 1)
                        if ty == 0:
                            mc, mp = (base + 2, base + 6) if t == 0 else (base + 0, base + 1)
                            TR.pe(lambda e: e.matmul(Dk[:, col:col + 128], lhsT=cur[:, cs], rhs=cmat_b[:, mc, :], start=True, stop=False),
                                  reads=[ckey, "cmat"], writes=[Dkey], mark=False)
                            TR.pe(lambda e: e.matmul(Dk[:, col:col + 128], lhsT=prv[:, cs], rhs=cmat_b[:, mp, :], start=False, stop=True),
                                  reads=[pvkey, "cmat"], writes=[Dkey], mark=lastg)
                        else:
                            TR.pe(lambda e: e.matmul(Dk[:, col:col + 128], lhsT=cur[:, cs], rhs=cmat_b[:, base + 3, :], start=True, stop=False),
                                  reads=[ckey, "cmat"], writes=[Dkey], mark=False)
                            for hf in range(2):
                                TR.pe(lambda e, hf=hf: e.matmul(Dk[:, col:col + 128], lhsT=hist_b[hf][0:120, cs], rhs=cmat_b[0:120, base + 4 + hf, :],
                                                                start=False, stop=(hf == 1)),
                                      reads=[f"hist_b{hf}", "cmat"], writes=[Dkey], mark=(lastg and hf == 1))
                for i2 in range(2):
                    Dk, Dkey = dbanks[i2]
                    TR.op("act", lambda e, i2=i2, Dk=Dk: e.copy(out=dTb[:, i2 * 4:(i2 + 1) * 4, :], in_=Dk[:, :].rearrange("p (k c) -> p k c", c=128)),
                          reads=[Dkey], writes=["dTb"])
                for hfc in range(2):
                    bgf(4)
                    c0 = hfc * 1024
                    hi = t * 2 + hfc
                    for wh in ("og", "ga", "gb", "o"):
                        mload(hi, wh)
                    TR.op("act", lambda e: e.activation(out=sgt, in_=ogt, func=AF.Sigmoid), reads=["B0"], writes=["B4"])
                    TR.op("dve", lambda e: e.tensor_tensor(out=ogt, in0=ogt, in1=sgt, op=ALU.mult), reads=["B0", "B4"], writes=["B0"])
                    TR.op("dve", lambda e: e.tensor_tensor(out=ogt, in0=ogt, in1=ot, op=ALU.mult), reads=["B0", "B3"], writes=["B0"])
                    mload(hi + 1, "o")
                    TR.op("act", lambda e: e.activation(out=sgt, in_=gat, func=AF.Sigmoid), reads=["B1"], writes=["B4"])
                    mload(hi + 1, "ga")
                    TR.op("dve", lambda e: e.tensor_tensor(out=ogt, in0=ogt, in1=sgt, op=ALU.mult), reads=["B0", "B4"], writes=["B0"])
                    TR.op("act", lambda e: e.activation(out=sgt, in_=gbt, func=AF.Sigmoid), reads=["B2"], writes=["B4"])
                    TR.op("dve", lambda e: e.tensor_tensor(out=sgt, in0=sgt, in1=psrow[:, c0:c0 + 1024], op=ALU.mult), reads=["B4", "B6", "B7"], writes=["B4"])
                    for g2 in range(2):
                        g = hfc * 2 + g2
                        Y, Ykey = fbank()
                        for ic in range(2):
                            TR.pe(lambda e, ic=ic: e.matmul(Y[:, :], lhsT=dTb[:, g * 2 + ic, :], rhs=wpool_b[:, g * 2 + ic, :], start=(ic == 0), stop=(ic == 1)),
                                  reads=["dTb", "wpool"], writes=[Ykey], mark=(ic == 1))
                        gs = slice(g2 * 512, (g2 + 1) * 512)
                        TR.op("dve", lambda e, gs=gs: e.tensor_tensor(out=gbt[:, gs], in0=Y[:, :], in1=sgt[:, gs], op=ALU.mult),
                              reads=[Ykey, "B4", "B2"], writes=["B2"])
                    TR.op("dve", lambda e: e.tensor_tensor(out=mb[:, c0:c0 + 1024], in0=gbt, in1=ogt, op=ALU.add), reads=["B2", "B0"], writes=["hb"])
                to_feat(mb, "hb", 16, t)
            TR.dma("sp", "po0", pool_p[l], p_scr[SEQ - 15:SEQ, OU:OU + 1024], reads=[("p2", NPT - 1)], writes=[("poolp", l)])
            TR.dma("sp", "po1", pool_s[l, :, 0:7, :], spool[l].rearrange("(j r) c -> j r c", r=15)[:, 8:15, :], writes=[("pools0", l)])
            TR.dma("sp", "po2", pool_s[l, :, 7:15, :], p_scr[SEQ:T, OU:OU + 1024].rearrange("(j r) c -> j r c", r=8),
                   reads=[("p2", NT - 1)], writes=[("pools1", l)])

        g1row = sb([128, 2, 512], F32, "g1row")

        def resid_epi(l, gcol, xprev, pkey_prev, xnext, nkey):
            pend = {}

            def pre(t, c, pc):
                xs, xskey = stage("xs", [128, 512], F32, 2)
                TR.dma("sp", "l" + xskey, xs[:, 0:pc], xprev[t * 128:(t + 1) * 128, c:c + pc], reads=[(pkey_prev, t)], writes=[xskey])
                pend[(t, c)] = (xs, xskey)

            def epi(t, c, pc, ps, pkey):
                ty = 1 if t == NT - 1 else 0
                if t == 0:
                    for ty2 in range(2):
                        TR.dma("sp", "ldg", g1row[:, ty2, 0:pc], mod_scr[l, ty2 * 128:(ty2 + 1) * 128, gcol + c:gcol + c + pc],
                               reads=[("mod", l, gcol // (3 * D))], writes=["g1row"])
                xs, xskey = pend.pop((t, c))
                sg, skey = stage("sto", [128, 512], F32, 2)
                TR.op("dve", lambda e: e.tensor_tensor(out=sg[:, 0:pc], in0=ps[:, 0:pc], in1=g1row[:, ty, 0:pc], op=ALU.mult),
                      reads=[pkey, "g1row"], writes=[skey])
                TR.op("dve", lambda e: e.tensor_tensor(out=sg[:, 0:pc], in0=sg[:, 0:pc], in1=xs[:, 0:pc], op=ALU.add),
                      reads=[skey, xskey], writes=[skey])
                TR.dma("sp", "st" + skey, xnext[t * 128:(t + 1) * 128, c:c + pc], sg[:, 0:pc], reads=[skey], writes=[(nkey, t)])
            epi.pre = pre
            return epi

        xcur, xkey = xin, "xin"
        tiles = list(range(NT))
        for l in range(DEPTH):
            norm_stage(l, xcur, xkey, n1g, l, 0, D)
            linear(16, tiles, w_in[l], 0, OQ, OOG - OQ, store_epi(p_scr, "p"))
            linear(16, tiles, w_in[l], 0, OALR, 16, store_epi(p_scr, "p"))
            fill_state["gen"] = linear_gen(16, tiles, w_in[l], 0, OOG, OALR - OOG, store_epi(p_scr, "p2"))
            gla_stage(l)
            fillf(10 ** 6)
            if l == 0:
                bg_state["gen"] = chain_gens(ada_bg(0, 3 * D, 3 * D), ada_bg(1, 0, 6 * D))
            exchange_stage(l)
            correct_stage(l)
            merge_stage(l)
            bgf(10 ** 6)
            x1, x1key = xa, "xa"
            linear(16, tiles, w_o[l], 0, 0, D, resid_epi(l, 2 * D, xcur, xkey, x1, x1key))
            norm_stage(l, x1, x1key, n2g, l, 3 * D, 4 * D)
            linear(16, tiles, w_gu[l], 0, 0, 2 * DFF, store_epi(gu_scr, "gu"))
            prev, prevkey = x1, x1key
            steps = [(k0, kc, t, c0, min(1024, kc * 128 - c0)) for (k0, kc) in ((0, 16), (16, 16), (32, 12)) for t in tiles
                     for c0 in range(0, kc * 128, 1024)]
            sets = [(0, 1, 2), (3, 4, 5)]

            def ffn_load(si):
                k0_, kc_, t_, c0_, w_ = steps[si]
                a_, b_, _ = sets[si % 2]
                r_ = slice(t_ * 128, (t_ + 1) * 128)
                TR.dma("sp", f"lf0{si % 2}", Bbuf[:, a_, 0:w_], gu_scr[r_, k0_ * 128 + c0_:k0_ * 128 + c0_ + w_], reads=[("gu", t_)], writes=[f"B{a_}"])
                TR.dma("sp", f"lf1{si % 2}", Bbuf[:, b_, 0:w_], gu_scr[r_, DFF + k0_ * 128 + c0_:DFF + k0_ * 128 + c0_ + w_], reads=[("gu", t_)], writes=[f"B{b_}"])

            ffn_load(0)
            si = 0
            for gi, (k0, kc) in enumerate(((0, 16), (16, 16), (32, 12))):
                n = kc * 128
                for t in tiles:
                    for c0 in range(0, n, 1024):
                        w = min(1024, n - c0)
                        if si + 1 < len(steps):
                            ffn_load(si + 1)
                        a_, b_, c_ = sets[si % 2]
                        si += 1
                        gt_ = Bbuf[:, a_, 0:w]; upt = Bbuf[:, b_, 0:w]; sg2 = Bbuf[:, c_, 0:w]
                        TR.op("act", lambda e: e.activation(out=sg2, in_=gt_, func=AF.Sigmoid), reads=[f"B{a_}"], writes=[f"B{c_}"])
                        TR.op("dve", lambda e: e.tensor_tensor(out=gt_, in0=gt_, in1=sg2, op=ALU.mult), reads=[f"B{a_}", f"B{c_}"], writes=[f"B{a_}"])
                        TR.op("dve", lambda e: e.tensor_tensor(out=mb[:, c0:c0 + w], in0=gt_, in1=upt, op=ALU.mult), reads=[f"B{a_}", f"B{b_}"], writes=["hb"])
                    to_feat(mb, "hb", kc, t)
                nxt, nkey = ((xb, "xb"), (xc, "xc"), (xb, "xb"))[gi]
                linear(kc, tiles, w_down[l], k0 * 128, 0, D, resid_epi(l, 5 * D, prev, prevkey, nxt, nkey))
                if gi > 0:
                    pass
                prev, prevkey = nxt, nkey
            xcur, xkey = prev, prevkey

        TR.dma("sp", "ld0", rowA, fng[0:1, :].partition_broadcast(128), writes=RA)
        for t in tiles:
            TR.dma("sp", "ldx", xt, xcur[t * 128:(t + 1) * 128, :], reads=[(xkey, t)], writes=XT)
            rstd_of(xt, "B0", D, 0)
            TR.op("dve", lambda e: e.scalar_tensor_tensor(out=ht, in0=xt, scalar=rs[:, 0:1], in1=rowA, op0=ALU.mult, op1=ALU.mult),
                  reads=XT + ["rs"] + RA, writes=HT)
            TR.dma("sp", "sty", y[t * 128:(t + 1) * 128, :], ht, reads=HT, writes=[("y", t)])
        for d in TR.dsem.values():
            TR.wait("sp", (d[0], d[1]))
    return nc


def _consts(first_half):
    s = np.arange(128)[:, None]; c = np.arange(128)[None, :]
    m = np.zeros((NMAT, 128, 128), np.float32)
    m[0] = np.eye(128)
    m[1] = (s <= c)
    m[2] = (s > c)
    same = (s // 8) == (c // 8)
    m[3] = (s <= c) & same
    m[4] = (s > c) & same
    for g, w in enumerate(WINS):
        b = 5 + g * 7
        m[b + 0] = ((s <= c) & (s > c - w)) / w - (s == c)
        m[b + 1] = ((s - 128) > (c - w)) / w
        cntc = np.minimum(c + 1, w)
        if first_half:
            m[b + 2] = ((s <= c) & (s > c - w)) / cntc - (s == c)
            m[b + 6] = 0.0
        else:
            m[b + 2] = m[b + 0]
            m[b + 6] = m[b + 1]
        m[b + 3] = ((s <= c) & (s > c - w) & same) / w - (s == c)
        for hf in range(2):
            hr = np.arange(128)[:, None]
            jj = hr // 15 + hf * 8; rr = hr % 15
            pos_h = rr - 15
            ci = c % 8; cj = c // 8
            m[b + 4 + hf] = ((hr < 120) & (jj == cj) & (pos_h > ci - w)) / w
    cm = np.zeros((128, 16, 128), np.float32)
    for j in range(16):
        cm[:, j, 8 * j:8 * j + 8] = 1.0
    rm = np.zeros((128, 16), np.float32)
    for j in range(16):
        rm[8 * j:8 * j + 8, j] = 1.0
    return m, cm.reshape(128, 2048), rm


_NC = None


def kernel(x_prompt, x_sample, state_gla, state_pool, c_prompt, c_sample, w_ada, b_ada, norm1_g, w_in, w_a2, b_a,
           gla_norm_g, w_pool, pool_scale, w_o, norm2_g, w_gu, w_down, final_norm_g):
    global _NC
    f = lambda a: np.ascontiguousarray(np.asarray(a, dtype=np.float32))
    x_prompt, x_sample, state_gla, state_pool, c_prompt, c_sample = map(f, (x_prompt, x_sample, state_gla, state_pool, c_prompt, c_sample))
    shared = {
        "w_ada": f(w_ada), "b_ada": f(b_ada), "norm1_g": f(norm1_g), "w_in": f(w_in), "w_a2": f(w_a2), "b_a": f(b_a),
        "gla_norm_g": f(gla_norm_g), "w_pool": f(w_pool).reshape(DEPTH, 1024, 512), "pool_scale": f(pool_scale), "w_o": f(w_o),
        "norm2_g": f(norm2_g), "w_gu": f(w_gu), "w_down": f(w_down), "final_norm_g": f(final_norm_g).reshape(1, D),
    }
    cst = [_consts(True), _consts(False)]
    in_maps = []
    for c in range(NCORE):
        b, half = c // 2, c % 2
        sl = slice(SPC * c, SPC * (c + 1))
        m = dict(shared)
        m["cmats"], m["cmk"], m["rmk"] = cst[half][0], cst[half][1], cst[half][2]
        m["flag"] = np.full((128, 1), float(half), np.float32)
        m["xin"] = np.concatenate([x_prompt[b, half * SEQ:(half + 1) * SEQ], x_sample[sl].reshape(SPC * 8, D)], axis=0)
        m["crow"] = np.concatenate([np.repeat(c_prompt[b:b + 1], 128, axis=0), np.repeat(c_sample[sl], 8, axis=0)], axis=0)
        m["sgla"] = np.ascontiguousarray(state_gla[:, sl])
        m["spool"] = np.ascontiguousarray(state_pool[:, sl]).reshape(DEPTH, SPC * 15, 1024)
        in_maps.append(m)
    if _NC is None:
        _NC = build_program()
    res = run_bass_kernel_spmd(_NC, in_maps, core_ids=list(range(NCORE)))
    R = res.results
    y_prompt = np.stack([np.concatenate([R[2 * b]["y"][:SEQ], R[2 * b + 1]["y"][:SEQ]], axis=0) for b in range(4)])
    y_sample = np.concatenate([R[c]["y"][SEQ:].reshape(SPC, 8, D) for c in range(NCORE)], axis=0)
    gla_pp = np.stack([R[2 * b + 1]["gla_p"] for b in range(4)], axis=1)
    pool_pp = np.stack([R[2 * b + 1]["pool_p"] for b in range(4)], axis=1)
    gla_ss = np.concatenate([R[c]["gla_s"] for c in range(NCORE)], axis=1)
    pool_ss = np.concatenate([R[c]["pool_s"] for c in range(NCORE)], axis=1)
    return (y_prompt.astype(np.float32), y_sample.astype(np.float32), gla_pp.astype(np.float32), pool_pp.astype(np.float32),
            gla_ss.astype(np.float32), pool_ss.astype(np.float32))
```

```python
import contextlib
import numpy as np
import concourse.bass as bass
import concourse.mybir as mybir
from concourse.bass_utils import run_bass_kernel_spmd

F32 = mybir.dt.float32
BF16 = mybir.dt.bfloat16
ALU = mybir.AluOpType
AF = mybir.ActivationFunctionType

D = 2048
SEQ = 1024
NCORE = 8
SPC = 16
T = SEQ + SPC * 8
NT = T // 128
NPT = SEQ // 128
NMAT = 5 + 28
NIN = 11280
DFF = 5632
DEPTH = 2
OQ, OK_, OV, OOG, OU, OGA, OGB, OALR = 0, 1024, 2048, 4096, 6144, 7168, 9216, 11264
WINS = (2, 4, 8, 16)
EPS = 1e-6


class Tracker:
    SEM_CAP = 30000

    def __init__(self, nc, stack, n_sems):
        self.nc = nc
        self.engs = {"pe": nc.tensor, "act": nc.scalar, "dve": nc.vector, "pool": nc.gpsimd, "sp": nc.sync}
        self.free_sems = [stack.enter_context(nc.semaphore(f"s{i}")) for i in range(n_sems)]
        self.cur = {}
        self.seen = {e: {} for e in self.engs}
        self.lastw = {}
        self.readers = {}
        self.dsem = {}
        self._pend = []

    def wait(self, en, tok):
        if tok is None:
            return
        sem, val = tok
        sid = id(sem)
        if self.seen[en].get(sid, 0) >= val:
            return
        self.engs[en].wait_ge(sem, val)
        self.seen[en][sid] = val

    def _deps(self, en, reads, writes):
        for k in reads:
            self.wait(en, self.lastw.get(k))
        for k in writes:
            self.wait(en, self.lastw.get(k))
            for r in self.readers.get(k, ()):
                self.wait(en, r)

    def _commit(self, tok, reads, writes):
        for k in reads:
            self.readers.setdefault(k, []).append(tok)
        for k in writes:
            self.lastw[k] = tok
            self.readers[k] = []

    def op(self, en, fn, reads=(), writes=()):
        self._deps(en, reads, writes)
        ins = fn(self.engs[en])
        c = self.cur.get(en)
        if c is None or c[1] >= self.SEM_CAP:
            c = [self.free_sems.pop(), 0]
            self.cur[en] = c
        c[1] += 1
        ins.then_inc(c[0], 1)
        tok = (c[0], c[1])
        self._commit(tok, reads, writes)
        return tok

    def pe(self, fn, reads=(), writes=(), mark=True):
        if not mark:
            self._deps("pe", reads, writes)
            fn(self.engs["pe"])
            self._pend.append((tuple(reads), tuple(writes)))
            return None
        tok = self.op("pe", fn, reads, writes)
        for r, w in self._pend:
            self._commit(tok, r, w)
        self._pend = []
        return tok

    def collective(self, fn, reads=(), writes=()):
        self._deps("pool", reads, writes)
        ins = fn(self.engs["pool"])
        sem = self.free_sems.pop()
        ins.then_inc(sem)
        tok = (sem, 1)
        self._commit(tok, reads, writes)
        return tok

    def dma(self, q, semkey, out, in_, reads=(), writes=()):
        d = self.dsem.get(semkey)
        if d is None:
            d = [self.free_sems.pop(), 0]
            self.dsem[semkey] = d
        if d[1] > 0:
            self.wait(q, (d[0], d[1]))
        self._deps(q, reads, writes)
        ins = self.engs[q].dma_start(out=out, in_=in_)
        d[1] += 16
        ins.then_inc(d[0], 16)
        tok = (d[0], d[1])
        self._commit(tok, reads, writes)
        return tok


def build_program():
    nc = bass.Bass("TRN2", target_bir_lowering=False)
    din = lambda n, s: nc.dram_tensor(n, s, F32, kind="ExternalInput").ap()
    dout = lambda n, s: nc.dram_tensor(n, s, F32, kind="ExternalOutput").ap()
    xin = din("xin", [T, D]); crow = din("crow", [256, D])
    sgla = din("sgla", [DEPTH, SPC, 4, 256, 512]); spool = din("spool", [DEPTH, SPC * 15, 1024])
    w_ada = din("w_ada", [DEPTH, D, 6 * D]); b_ada = din("b_ada", [DEPTH, 6 * D]); n1g = din("norm1_g", [DEPTH, D])
    w_in = din("w_in", [DEPTH, D, NIN]); w_a2 = din("w_a2", [DEPTH, 16, 1024]); b_a = din("b_a", [DEPTH, 1024])
    ggla = din("gla_norm_g", [DEPTH, 512]); w_pool = din("w_pool", [DEPTH, 1024, 512]); pscale = din("pool_scale", [DEPTH, D])
    w_o = din("w_o", [DEPTH, D, D]); n2g = din("norm2_g", [DEPTH, D]); w_gu = din("w_gu", [DEPTH, D, 2 * DFF])
    w_down = din("w_down", [DEPTH, DFF, D]); fng = din("final_norm_g", [1, D])
    cmats = din("cmats", [NMAT, 128, 128]); flag_d = din("flag", [128, 1]); cmk = din("cmk", [128, 16 * 128]); rmk = din("rmk", [128, 16])
    y = dout("y", [T, D]); gla_p = dout("gla_p", [DEPTH, 4, 256, 512]); pool_p = dout("pool_p", [DEPTH, 15, 1024])
    gla_s = dout("gla_s", [DEPTH, SPC, 4, 256, 512]); pool_s = dout("pool_s", [DEPTH, SPC, 15, 1024])
    scr = lambda n, s: nc.dram_tensor(n, s, F32).ap()
    p_scr = scr("p_scr", [T, NIN]); gu_scr = scr("gu_scr", [T, 2 * DFF]); o_scr = scr("o_scr", [T, D])
    qg_scr = nc.dram_tensor("qg_scr", [4, NPT, 128, 256], BF16).ap()
    ex_in_t = [nc.dram_tensor(f"ex_in{l}", [512, 1024], F32) for l in range(DEPTH)]
    ex_out_t = [nc.dram_tensor(f"ex_out{l}", [1024, 1024], F32) for l in range(DEPTH)]
    exu_in_t = [nc.dram_tensor(f"exu_in{l}", [128, 1024], F32) for l in range(DEPTH)]
    exu_out_t = [nc.dram_tensor(f"exu_out{l}", [256, 1024], F32) for l in range(DEPTH)]
    ex_in = [t_.ap() for t_ in ex_in_t]; ex_out = [t_.ap() for t_ in ex_out_t]
    exu_in = [t_.ap() for t_ in exu_in_t]; exu_out = [t_.ap() for t_ in exu_out_t]
    xa = scr("xa", [T, D]); xb = scr("xb", [T, D]); xc = scr("xc", [T, D]); mod_scr = scr("mod_scr", [DEPTH, 256, 6 * D])

    with contextlib.ExitStack() as st:
        TR = Tracker(nc, st, 100)
        cnt = [0]

        def sb(shape, dt, name=None):
            cnt[0] += 1
            return st.enter_context(nc.sbuf_tensor(name or f"t{cnt[0]}", shape, dt))

        actT = sb([128, 16, T], BF16, "actT")
        NW = 2
        wbuf = [sb([128, 16, 512], BF16, f"wb{i}") for i in range(NW)]
        psf = [st.enter_context(nc.psum_tensor(f"psf{i}", [128, 512], F32)) for i in range(6)]
        psb = [st.enter_context(nc.psum_tensor(f"psb{i}", [128, 1024], BF16)) for i in range(2)]
        rot = {"f": 0, "b": 0, "w": 0}

        held = set()

        def fbank():
            while True:
                i = rot["f"] % 6; rot["f"] += 1
                if i not in held:
                    return psf[i], f"psf{i}"

        def bbank():
            i = rot["b"] % 2; rot["b"] += 1
            return psb[i], f"psb{i}"

        pools = {}

        def stage(tag, shape, dt, n=2):
            if tag not in pools:
                pools[tag] = [[sb(shape, dt, f"{tag}{i}") for i in range(n)], 0]
            p = pools[tag]
            i = p[1] % n; p[1] += 1
            return p[0][i], f"{tag}{i}"

        st.enter_context(nc.Block())

        Bbuf = sb([128, 8, 1024], F32, "Bbuf")
        BK = [f"B{i}" for i in range(8)]
        cm_f = Bbuf[:, 0:2, :].rearrange("p a c -> p (a c)")
        hb = sb([128, D], BF16, "hb")
        mb = hb
        idb = sb([128, 128], BF16, "idb")
        cmat_b = sb([128, NMAT, 128], BF16, "cmat_b")
        flag = sb([128, 1], F32, "flag_sb")
        Gst = sb([128, 2], F32, "Gst"); Gtot = sb([128, 4, 2], F32, "Gtot")
        cm_b = sb([128, 16, 128], BF16, "cm_b")
        rm_b = sb([128, 16], BF16, "rm_b")
        ones_b = sb([128, 128], BF16, "ones_b")
        rs = sb([128, 8], F32, "rs")
        junk = hb
        TR.dma("sp", "cl", flag[:], flag_d[:, :], writes=["flag"])
        cst_f = Bbuf[:, 0:5, :].rearrange("p a c -> p (a c)")[:, 0:NMAT * 128].rearrange("p (n c) -> p n c", c=128)
        TR.dma("sp", "cl", cst_f, cmats.rearrange("n p c -> p n c"), writes=["B0", "B1", "B2", "B3", "B4"])
        TR.op("dve", lambda e: e.tensor_copy(out=cmat_b[:], in_=cst_f), reads=["B0", "B1", "B2", "B3", "B4"], writes=["cmat"])
        TR.op("dve", lambda e: e.tensor_copy(out=idb[:], in_=cmat_b[:, 0, :]), reads=["cmat"], writes=["idb"])
        TR.dma("sp", "cl", cm_f, cmk[:, :], writes=["B0", "B1"])
        TR.op("dve", lambda e: e.tensor_copy(out=cm_b[:].rearrange("p j c -> p (j c)"), in_=cm_f), reads=["B0", "B1"], writes=["cm_b"])
        TR.dma("sp", "cl", cm_f[:, 0:16], rmk[:, :], reads=[], writes=["B0", "B1"])
        TR.op("dve", lambda e: e.tensor_copy(out=rm_b[:], in_=cm_f[:, 0:16]), reads=["B0", "B1"], writes=["rm_b"])
        TR.op("dve", lambda e: e.memset(ones_b[:], 1.0), writes=["ones_b"])
        MINCL = {0: 1, 1: 3}
        MREV = {0: 2, 1: 4}

        def to_feat(src_b, src_key, kc, t, dst=None, dst_key="actT"):
            dst = actT if dst is None else dst
            for k0 in range(0, kc, 8):
                kn = min(8, kc - k0)
                bk, bkey = bbank()
                for k in range(kn):
                    TR.pe(lambda e, k=k: e.transpose(bk[:, k * 128:(k + 1) * 128], src_b[:, (k0 + k) * 128:(k0 + k + 1) * 128], idb[:]),
                          reads=[src_key, "idb"], writes=[bkey], mark=(k == kn - 1))
                TR.op("act", lambda e: e.copy(out=dst[:, k0:k0 + kn, t * 128:(t + 1) * 128],
                                              in_=bk[:, 0:kn * 128].rearrange("p (k c) -> p k c", c=128)),
                      reads=[bkey], writes=[(dst_key, t)])

        def linear(*a, **k):
            for _ in linear_gen(*a, **k):
                pass

        fill_state = {"gen": None}

        def fillf(n=1):
            g = fill_state["gen"]
            for _ in range(n):
                if g is None:
                    return
                try:
                    next(g)
                except StopIteration:
                    fill_state["gen"] = None
                    return

        bg_state = {"gen": None, "on": False}

        def chain_gens(*gens):
            for g_ in gens:
                yield from g_

        def bgf(n=1):
            g = bg_state["gen"]
            for _ in range(n):
                if g is None:
                    return
                try:
                    next(g)
                except StopIteration:
                    bg_state["gen"] = None
                    return

        def linear_gen(kc, tiles, W, r0, c0, ncols, epi, pw=512, lhs=None, lhs_key="actT", own_w=None, bg=False):
            groups = [(t, c, min(pw, c0 + ncols - c)) for c in range(c0, c0 + ncols, pw) for t in tiles]
            pre = getattr(epi, "pre", None)
            if pre:
                pre(*groups[0])
            wb = wkey = None
            lhs = actT if lhs is None else lhs
            for gi, (t, c, pc) in enumerate(groups):
                if bg:
                    bgf(1)
                if t == tiles[0]:
                    if own_w is not None:
                        wb, wkey, wi = own_w, "wown", "own"
                    else:
                        wi = rot["w"] % NW; rot["w"] += 1
                        wb, wkey = wbuf[wi], f"wb{wi}"
                    TR.dma("pool", f"w{wi}", wb[:, 0:kc, 0:pc], W[r0:r0 + kc * 128, c:c + pc].rearrange("(k p) n -> p k n", p=128), writes=[wkey])
                ps, pkey = fbank()
                for k in range(kc):
                    TR.pe(lambda e, k=k: e.matmul(ps[:, 0:pc], lhsT=lhs[:, k, t * 128:(t + 1) * 128], rhs=wb[:, k, 0:pc],
                                                  start=(k == 0), stop=(k == kc - 1)),
                          reads=[(lhs_key, t), wkey], writes=[pkey], mark=(k == kc - 1))
                if pre and gi + 1 < len(groups):
                    pre(*groups[gi + 1])
                epi(t, c, pc, ps, pkey)
                yield

        def store_epi(dst, dkey):
            def epi(t, c, pc, ps, pkey):
                sg, skey = stage("sto", [128, 512], F32, 2)
                TR.op("act", lambda e: e.copy(out=sg[:, 0:pc], in_=ps[:, 0:pc]), reads=[pkey], writes=[skey])
                TR.dma("sp", "st" + skey, dst[t * 128:(t + 1) * 128, c:c + pc], sg[:, 0:pc], reads=[skey], writes=[(dkey, t)])
            return epi

        def rstd_of(src, skey, n, col):
            TR.op("dve", lambda e: e.memset(rs[:, col + 1:col + 2], 0.0), writes=["rs"])
            TR.op("act", lambda e: e.activation(out=junk[:, 0:n], in_=src, func=AF.Square, accum_out=rs[:, col + 1:col + 2]),
                  reads=[skey, "rs"], writes=["hb", "rs"])
            TR.op("act", lambda e: e.activation(out=rs[:, col + 1:col + 2], in_=rs[:, col + 1:col + 2], func=AF.Ln, scale=1.0 / n, bias=EPS),
                  reads=["rs"], writes=["rs"])
            TR.op("act", lambda e: e.activation(out=rs[:, col:col + 1], in_=rs[:, col + 1:col + 2], func=AF.Exp, scale=-0.5),
                  reads=["rs"], writes=["rs"])

        xt = Bbuf[:, 0:2, :].rearrange("p a c -> p (a c)")
        ht = Bbuf[:, 2:4, :].rearrange("p a c -> p (a c)")
        rowA = Bbuf[:, 4:6, :].rearrange("p a c -> p (a c)")
        rowB = Bbuf[:, 6:8, :].rearrange("p a c -> p (a c)")
        XT, HT, RA, RB = ["B0", "B1"], ["B2", "B3"], ["B4", "B5"], ["B6", "B7"]
        gT = sb([128, 16], F32, "gT"); scT = sb([128, 16], F32, "scT"); shT = sb([128, 16], F32, "shT")

        def norm_stage(l, xsrc, xkey, gain, gl, c_sh, c_sc):
            with nc.allow_non_contiguous_dma(reason="tiny per-feature vectors"):
                TR.dma("sp", "ld0", gT[:], gain[gl, :].rearrange("(k p) -> p k", p=128), writes=["gT"])
                TR.dma("sp", "ld1", scT[:], mod_scr[l, 0, c_sc:c_sc + D].rearrange("(k p) -> p k", p=128), reads=[("mod", l, c_sc // (3 * D))], writes=["scT"])
                TR.dma("sp", "ld2", shT[:], mod_scr[l, 0, c_sh:c_sh + D].rearrange("(k p) -> p k", p=128), reads=[("mod", l, c_sh // (3 * D))], writes=["shT"])
            TR.op("dve", lambda e: e.scalar_tensor_tensor(out=scT[:], in0=scT[:], scalar=1.0, in1=gT[:], op0=ALU.add, op1=ALU.mult),
                  reads=["scT", "gT"], writes=["scT"])
            xalt = Bbuf[:, 4:6, :].rearrange("p a c -> p (a c)")

            def xbuf(t_):
                return (xt, XT) if (t_ % 2 == 0 or t_ == NT - 1) else (xalt, RA)

            def xload(t_):
                xb_, xk_ = xbuf(t_)
                TR.dma("sp", "ldx" + str(t_ % 2), xb_, xsrc[t_ * 128:(t_ + 1) * 128, :], reads=[(xkey, t_)], writes=xk_)

            xload(0)
            for t in range(NT):
                if t + 1 < NT:
                    xload(t + 1)
                xt_, XT_ = xbuf(t)
                rstd_of(xt_, XT_[0], D, 0)
                if t < NT - 1:
                    TR.op("dve", lambda e: e.tensor_scalar(out=hb[:], in0=xt_, scalar1=rs[:, 0:1], scalar2=None, op0=ALU.mult),
                          reads=XT_ + ["rs"], writes=["hb"])
                    for k0 in range(0, 16, 8):
                        bk, bkey = bbank()
                        for k in range(8):
                            TR.pe(lambda e, k=k: e.transpose(bk[:, k * 128:(k + 1) * 128], hb[:, (k0 + k) * 128:(k0 + k + 1) * 128], idb[:]),
                                  reads=["hb", "idb"], writes=[bkey], mark=(k == 7))
                        for k in range(8):
                            TR.op("act", lambda e, k=k: e.activation(out=actT[:, k0 + k, t * 128:(t + 1) * 128], in_=bk[:, k * 128:(k + 1) * 128],
                                                                     func=AF.Identity, scale=scT[:, k0 + k:k0 + k + 1], bias=shT[:, k0 + k:k0 + k + 1]),
                                  reads=[bkey, "scT", "shT"], writes=[("actT", t)])
                else:
                    TR.dma("sp", "ld0", ht, gain[gl:gl + 1, :].partition_broadcast(128), writes=HT)
                    TR.dma("sp", "ld1", rowA, mod_scr[l, 128:256, c_sc:c_sc + D], reads=[("mod", l, c_sc // (3 * D))], writes=RA)
                    TR.dma("sp", "ld2", rowB, mod_scr[l, 128:256, c_sh:c_sh + D], reads=[("mod", l, c_sh // (3 * D))], writes=RB)
                    TR.op("dve", lambda e: e.scalar_tensor_tensor(out=rowA, in0=rowA, scalar=1.0, in1=ht, op0=ALU.add, op1=ALU.mult),
                          reads=RA + HT, writes=RA)
                    TR.op("dve", lambda e: e.scalar_tensor_tensor(out=ht, in0=xt, scalar=rs[:, 0:1], in1=rowA, op0=ALU.mult, op1=ALU.mult),
                          reads=XT + ["rs"] + RA, writes=HT)
                    TR.op("dve", lambda e: e.tensor_tensor(out=hb[:], in0=ht, in1=rowB, op=ALU.add), reads=HT + RB, writes=["hb"])
                    to_feat(hb, "hb", 16, t)

        brow = sb([128, 512], F32, "brow")
        csT = sb([128, 16, 256], BF16, "csT")
        for t in range(2):
            TR.dma("sp", "ldx", xt, crow[t * 128:(t + 1) * 128, :], writes=XT)
            TR.op("act", lambda e: e.activation(out=ht, in_=xt, func=AF.Sigmoid), reads=XT, writes=HT)
            TR.op("dve", lambda e: e.tensor_tensor(out=hb[:], in0=ht, in1=xt, op=ALU.mult), reads=HT + XT, writes=["hb"])
            to_feat(hb, "hb", 16, t, dst=csT, dst_key="csT")

        def epi_mod(l):
            def epi(t, c, pc, ps, pkey):
                if t == 0:
                    TR.dma("sp", "ldb", brow[:, 0:pc], b_ada[l:l + 1, c:c + pc].partition_broadcast(128), writes=["brow"])
                sg, skey = stage("sto", [128, 512], F32, 2)
                TR.op("dve", lambda e: e.tensor_tensor(out=sg[:, 0:pc], in0=ps[:, 0:pc], in1=brow[:, 0:pc], op=ALU.add),
                      reads=[pkey, "brow"], writes=[skey])
                TR.dma("sp", "st" + skey, mod_scr[l, t * 128:(t + 1) * 128, c:c + pc], sg[:, 0:pc], reads=[skey], writes=[("mod", l, c // (3 * D))])
            return epi

        linear(16, [0, 1], w_ada[0], 0, 0, 2 * D, epi_mod(0), lhs=csT, lhs_key="csT")

        def ada_bg(l, c0, ncols):
            return linear_gen(16, [0, 1], w_ada[l], 0, c0, ncols, epi_mod(l), lhs=csT, lhs_key="csT")

        qkvs = [sb([128, 1024], F32, f"qkv{i}") for i in range(2)]
        alrs = [sb([128, 16], F32, f"alr_f{i}") for i in range(2)]
        qkv = qkvs[0]
        qb = sb([128, 256], BF16, "qb"); kb = sb([128, 256], BF16, "kb"); vb = sb([128, 512], BF16, "vb"); ab = sb([128, 16], BF16, "ab")
        alrT = sb([17, 128], BF16, "alrT")
        wa2f = Bbuf[0:17, 0, :]; wa2b = sb([17, 1024], BF16, "wa2b")
        TR.op("dve", lambda e: e.memset(alrT[:], 1.0), writes=["alrT"])
        ef = sb([128, 256], F32, "ef"); lgb = sb([128, 256], BF16, "lgb")
        E1 = sb([128, 256], F32, "E1"); E2 = sb([128, 256], F32, "E2"); E3 = sb([128, 256], F32, "E3")
        qe = sb([128, 2, 128], BF16, "qe"); ke = sb([128, 2, 128], BF16, "ke"); kd = sb([128, 256], BF16, "kd")
        attm = sb([128, 128], BF16, "attm")
        Sf = sb([128, 2, 512], F32, "Sf"); Sb = sb([128, 2, 512], BF16, "Sb")

        onf = sb([128, 512], F32, "onf")
        ggrow = sb([128, 512], F32, "ggrow")

        def gla_stage(l):
            TR.dma("sp", "ld0", Bbuf[0:16, 0, :], w_a2[l], writes=["B0"])
            TR.dma("sp", "ld1", Bbuf[16:17, 0, :], b_a[l:l + 1, :], writes=["B0"])
            TR.op("dve", lambda e: e.tensor_copy(out=wa2b[:], in_=wa2f), reads=["B0"], writes=["wa2b"])
            TR.dma("sp", "ld2", ggrow[:], ggla[l:l + 1, :].partition_broadcast(128), writes=["ggrow"])
            order = [(h_, t_) for h_ in range(4) for t_ in range(NT)]

            def gla_loads(i):
                h_, t_ = order[i]
                r_ = slice(t_ * 128, (t_ + 1) * 128)
                q_, a_, sfx = qkvs[i % 2], alrs[i % 2], str(i % 2)
                TR.dma("sp", "lq0" + sfx, q_[:, 0:256], p_scr[r_, OQ + h_ * 256:OQ + (h_ + 1) * 256], reads=[("p", t_)], writes=["qkv" + sfx])
                TR.dma("sp", "lq1" + sfx, q_[:, 256:512], p_scr[r_, OK_ + h_ * 256:OK_ + (h_ + 1) * 256], reads=[("p", t_)], writes=["qkv" + sfx])
                TR.dma("sp", "lq2" + sfx, q_[:, 512:1024], p_scr[r_, OV + h_ * 512:OV + (h_ + 1) * 512], reads=[("p", t_)], writes=["qkv" + sfx])
                TR.dma("sp", "lq3" + sfx, a_[:], p_scr[r_, OALR:OALR + 16], reads=[("p", t_)], writes=["alr_f" + sfx])

            gla_loads(0)
            for h in range(4):
                TR.op("dve", lambda e: e.memset(Sf[:], 0.0), writes=["Sf"])
                TR.op("dve", lambda e: e.memset(Sb[:], 0.0), writes=["Sb"])
                TR.op("dve", lambda e: e.memset(Gst[:], 1.0), writes=["Gst"])
                for t in range(NT):
                    ty = 1 if t == NT - 1 else 0
                    r = slice(t * 128, (t + 1) * 128)
                    gi_ = h * NT + t
                    if gi_ + 1 < len(order):
                        gla_loads(gi_ + 1)
                    qkv_, alr_, sfx = qkvs[gi_ % 2], alrs[gi_ % 2], str(gi_ % 2)
                    TR.op("dve", lambda e: e.tensor_scalar(out=qb[:], in0=qkv_[:, 0:256], scalar1=0.0625, scalar2=None, op0=ALU.mult),
                          reads=["qkv" + sfx], writes=["qb"])
                    TR.op("dve", lambda e: e.tensor_copy(out=kb[:], in_=qkv_[:, 256:512]), reads=["qkv" + sfx], writes=["kb"])
                    TR.op("dve", lambda e: e.tensor_copy(out=vb[:], in_=qkv_[:, 512:1024]), reads=["qkv" + sfx], writes=["vb"])
                    TR.op("dve", lambda e: e.tensor_copy(out=ab[:], in_=alr_[:]), reads=["alr_f" + sfx], writes=["ab"])
                    bk, bkey = bbank()
                    for ch in range(2):
                        TR.pe(lambda e, ch=ch: e.transpose(bk[:, ch * 128:(ch + 1) * 128], qb[:, ch * 128:(ch + 1) * 128], idb[:]),
                              reads=["qb", "idb"], writes=[bkey], mark=False)
                        TR.pe(lambda e, ch=ch: e.transpose(bk[:, 256 + ch * 128:256 + (ch + 1) * 128], kb[:, ch * 128:(ch + 1) * 128], idb[:]),
                              reads=["kb", "idb"], writes=[bkey], mark=False)
                    TR.pe(lambda e: e.transpose(bk[0:16, 512:640], ab[:, 0:16], idb[:]), reads=["ab", "idb"], writes=[bkey])
                    TR.op("act", lambda e: e.copy(out=alrT[0:16, :], in_=bk[0:16, 512:640]), reads=[bkey], writes=["alrT"])
                    fillf()
                    g, gkey = fbank()
                    TR.pe(lambda e: e.matmul(g[:, 0:256], lhsT=alrT[0:17, :], rhs=wa2b[0:17, h * 256:(h + 1) * 256], start=True, stop=True),
                          reads=["alrT", "wa2b"], writes=[gkey])
                    TR.op("act", lambda e: e.activation(out=ef[:], in_=g[:, 0:256], func=AF.Exp, scale=-1.0), reads=[gkey], writes=["ef"])
                    TR.op("act", lambda e: e.activation(out=ef[:], in_=ef[:], func=AF.Ln, bias=1.0), reads=["ef"], writes=["ef"])
                    TR.op("dve", lambda e: e.tensor_scalar(out=lgb[:], in0=ef[:], scalar1=-1.0 / 16, scalar2=None, op0=ALU.mult),
                          reads=["ef"], writes=["lgb"])
                    fillf()
                    B, Bkey = fbank()
                    for ch in range(2):
                        TR.pe(lambda e, ch=ch: e.matmul(B[:, ch * 128:(ch + 1) * 128], lhsT=lgb[:, ch * 128:(ch + 1) * 128],
                                                        rhs=cmat_b[:, MINCL[ty], :], start=True, stop=True),
                              reads=["lgb", "cmat"], writes=[Bkey], mark=False)
                    TR.pe(lambda e: e.matmul(B[:, 256:512], lhsT=cmat_b[:, MREV[ty], :], rhs=lgb[:], start=True, stop=True),
                          reads=["lgb", "cmat"], writes=[Bkey])
                    TR.op("act", lambda e: e.activation(out=E1[:], in_=B[:, 0:256], func=AF.Exp), reads=[Bkey], writes=["E1"])
                    TR.op("act", lambda e: e.activation(out=E2[:], in_=B[:, 0:256], func=AF.Exp, scale=-1.0), reads=[Bkey], writes=["E2"])
                    TR.op("act", lambda e: e.activation(out=E3[:], in_=B[:, 256:512], func=AF.Exp), reads=[Bkey], writes=["E3"])
                    TR.op("dve", lambda e: e.tensor_tensor(out=qe[:].rearrange("p c k -> p (c k)"), in0=bk[:, 0:256], in1=E1[:], op=ALU.mult),
                          reads=[bkey, "E1"], writes=["qe"])
                    TR.op("dve", lambda e: e.tensor_tensor(out=ke[:].rearrange("p c k -> p (c k)"), in0=bk[:, 256:512], in1=E2[:], op=ALU.mult),
                          reads=[bkey, "E2"], writes=["ke"])
                    TR.op("dve", lambda e: e.tensor_tensor(out=kd[:], in0=kb[:], in1=E3[:], op=ALU.mult), reads=["kb", "E3"], writes=["kd"])
                    if ty == 0:
                        qg, qgkey = stage("qg", [128, 2, 128], BF16, 2)
                        for ch in range(2):
                            TR.op("dve", lambda e, ch=ch: e.tensor_scalar(out=qg[:, ch, :], in0=qe[:, ch, :], scalar1=Gst[:, ch:ch + 1], scalar2=None, op0=ALU.mult),
                                  reads=["qe", "Gst"], writes=[qgkey])
                        TR.dma("sp", "s" + qgkey, qg_scr[h, t], qg[:].rearrange("p c k -> p (c k)"), reads=[qgkey], writes=[("qg", h, t)])
                    fillf()
                    A, Akey = fbank()
                    for ch in range(2):
                        TR.pe(lambda e, ch=ch: e.matmul(A[:, 0:128], lhsT=ke[:, ch, :], rhs=qe[:, ch, :], start=(ch == 0), stop=(ch == 1)),
                              reads=["ke", "qe"], writes=[Akey], mark=(ch == 1))
                    TR.op("dve", lambda e: e.tensor_tensor(out=attm[:], in0=A[:, 0:128], in1=cmat_b[:, MINCL[ty], :], op=ALU.mult),
                          reads=[Akey, "cmat"], writes=["attm"])
                    fillf()
                    O, Okey = fbank()
                    if ty == 0:
                        TR.pe(lambda e: e.matmul(O[:, :], lhsT=attm[:], rhs=vb[:], start=True, stop=False), reads=["attm", "vb"], writes=[Okey], mark=False)
                        for ch in range(2):
                            TR.pe(lambda e, ch=ch: e.matmul(O[:, :], lhsT=qe[:, ch, :], rhs=Sb[:, ch, :], start=False, stop=(ch == 1)),
                                  reads=["qe", "Sb"], writes=[Okey], mark=(ch == 1))
                        for ch in range(2):
                            U, Ukey = fbank()
                            TR.pe(lambda e, ch=ch: e.matmul(U[:, :], lhsT=kd[:, ch * 128:(ch + 1) * 128], rhs=vb[:], start=True, stop=True),
                                  reads=["kd", "vb"], writes=[Ukey])
                            TR.op("dve", lambda e, ch=ch: e.scalar_tensor_tensor(out=Sf[:, ch, :], in0=Sf[:, ch, :], scalar=E1[:, ch * 128 + 127:ch * 128 + 128],
                                                                                 in1=U[:, :], op0=ALU.mult, op1=ALU.add),
                                  reads=["Sf", "E1", Ukey], writes=["Sf"])
                        TR.op("act", lambda e: e.copy(out=Sb[:], in_=Sf[:]), reads=["Sf"], writes=["Sb"])
                        TR.op("dve", lambda e: e.tensor_tensor(out=Gst[:], in0=Gst[:], in1=E1[:].rearrange("p (c k) -> p c k", k=128)[:, :, 127], op=ALU.mult),
                              reads=["Gst", "E1"], writes=["Gst"])
                        if t == NPT - 1:
                            TR.dma("sp", "stS", exS(ex_in[l], h), Sf[:], reads=["Sf"], writes=[("exin", l)])
                            TR.op("dve", lambda e: e.tensor_copy(out=Gtot[:, h, :], in_=Gst[:]), reads=["Gst"], writes=["Gtot"])
                    else:
                        held.add(int(Okey[3:]))
                        TR.pe(lambda e: e.matmul(O[:, :], lhsT=attm[:], rhs=vb[:], start=True, stop=False), reads=["attm", "vb"], writes=[Okey], mark=False)
                        def s0_load(j_):
                            s0_, k_ = stage("s0f", [128, 2, 512], F32, 3)
                            for ch_ in range(2):
                                TR.dma("sp", f"l{k_}{ch_}", s0_[:, ch_, :], sgla[l, j_, h, ch_ * 128:(ch_ + 1) * 128, :], writes=[k_ + str(ch_)])
                            return s0_, k_

                        nxt_s0 = s0_load(0)
                        for j in range(SPC):
                            s0, s0key = nxt_s0
                            if j + 1 < SPC:
                                nxt_s0 = s0_load(j + 1)
                            s0b, s0bkey = stage("s0b", [128, 2, 512], BF16, 2)
                            qm, qmkey = stage("qm", [128, 2, 128], BF16, 2)
                            km, kmkey = stage("km", [128, 256], BF16, 2)
                            TR.op("act", lambda e: e.copy(out=s0b[:], in_=s0[:]), reads=[s0key + "0", s0key + "1"], writes=[s0bkey])
                            TR.op("dve", lambda e, j=j: e.tensor_tensor(out=qm[:], in0=qe[:], in1=cm_b[:, j, :].unsqueeze(1).to_broadcast([128, 2, 128]),
                                                                        op=ALU.mult), reads=["qe", "cm_b"], writes=[qmkey])
                            TR.op("dve", lambda e, j=j: e.tensor_scalar(out=km[:], in0=kd[:], scalar1=rm_b[:, j:j + 1], scalar2=None, op0=ALU.mult),
                                  reads=["kd", "rm_b"], writes=[kmkey])
                            for ch in range(2):
                                last = (j == SPC - 1 and ch == 1)
                                TR.pe(lambda e, ch=ch: e.matmul(O[:, :], lhsT=qm[:, ch, :], rhs=s0b[:, ch, :], start=False, stop=last),
                                      reads=[qmkey, s0bkey], writes=[Okey], mark=True)
                            for ch in range(2):
                                U, Ukey = fbank()
                                TR.pe(lambda e, ch=ch: e.matmul(U[:, :], lhsT=km[:, ch * 128:(ch + 1) * 128], rhs=vb[:], start=True, stop=True),
                                      reads=[kmkey, "vb"], writes=[Ukey])
                                cix = ch * 128 + 8 * j + 7
                                TR.op("dve", lambda e, ch=ch, cix=cix: e.scalar_tensor_tensor(out=s0[:, ch, :], in0=s0[:, ch, :], scalar=E1[:, cix:cix + 1],
                                                                                              in1=U[:, :], op0=ALU.mult, op1=ALU.add),
                                      reads=[s0key + str(ch), "E1", Ukey], writes=[s0key + str(ch)])
                            TR.dma("sp", "s" + s0key, gla_s[l, j, h].rearrange("(c p) e -> p c e", p=128), s0[:], reads=[s0key + "0", s0key + "1"],
                                   writes=[("glas", l, j, h)])
                    held.clear()
                    if ty == 0:
                        TR.op("act", lambda e: e.copy(out=onf[:], in_=O[:, :]), reads=[Okey], writes=["onf"])
                    else:
                        rstd_of(O[:, :], Okey, 512, 2)
                        TR.op("dve", lambda e: e.scalar_tensor_tensor(out=onf[:], in0=O[:, :], scalar=rs[:, 2:3], in1=ggrow[:], op0=ALU.mult, op1=ALU.mult),
                              reads=[Okey, "rs", "ggrow"], writes=["onf"])
                    TR.dma("sp", "sto_o", o_scr[r, h * 512:(h + 1) * 512], onf[:], reads=["onf"], writes=[("o", t)])

        def exS(buf, h, blk=0):
            return buf[blk * 512 + h * 128:blk * 512 + (h + 1) * 128, :].rearrange("r (two e) -> (r two) e", two=2).rearrange("(c p) e -> p c e", p=128)

        def exchange_stage(l):
            TR.dma("sp", "exu", exu_in[l][:, :], p_scr[SEQ - 128:SEQ, OU:OU + 1024], reads=[("p2", NPT - 1)], writes=[("exuin", l)])
            pairs = [[0, 1], [2, 3], [4, 5], [6, 7]]
            TR.collective(lambda g: g.collective_compute("AllGather", ALU.bypass, replica_groups=pairs,
                                                          ins=[exu_in_t[l].ap().opt()], outs=[exu_out_t[l].ap().opt()]),
                          reads=[("exuin", l)], writes=[("exuout", l)])
            TR.collective(lambda g: g.collective_compute("AllGather", ALU.bypass, replica_groups=pairs,
                                                          ins=[ex_in_t[l].ap().opt()], outs=[ex_out_t[l].ap().opt()]),
                          reads=[("exin", l)], writes=[("exout", l)])

        def correct_stage(l):
            for h in range(4):
                sp_, spkey = stage("s0f", [128, 2, 512], F32, 3)
                spb, spbkey = stage("s0b", [128, 2, 512], BF16, 2)
                sl, slkey = stage("s0f", [128, 2, 512], F32, 3)
                TR.dma("sp", "l" + spkey, sp_[:], exS(ex_out[l], h, 0), reads=[("exout", l)], writes=[spkey + "0", spkey + "1"])
                TR.op("dve", lambda e: e.tensor_scalar(out=sp_[:], in0=sp_[:], scalar1=flag[:, 0:1], scalar2=None, op0=ALU.mult),
                      reads=[spkey + "0", spkey + "1", "flag"], writes=[spkey + "0", spkey + "1"])
                TR.op("act", lambda e: e.copy(out=spb[:], in_=sp_[:]), reads=[spkey + "0", spkey + "1"], writes=[spbkey])
                TR.dma("sp", "l" + slkey, sl[:], exS(ex_in[l], h, 0), reads=[("exin", l)], writes=[slkey + "0", slkey + "1"])
                for ch in range(2):
                    TR.op("dve", lambda e, ch=ch: e.scalar_tensor_tensor(out=sl[:, ch, :], in0=sp_[:, ch, :], scalar=Gtot[:, h, ch:ch + 1], in1=sl[:, ch, :],
                                                                         op0=ALU.mult, op1=ALU.add), reads=[spkey + "0", spkey + "1", slkey + "0", slkey + "1", "Gtot"], writes=[slkey + "0", slkey + "1"])
                TR.dma("sp", "s" + slkey, gla_p[l, h].rearrange("(c p) e -> p c e", p=128), sl[:], reads=[slkey + "0", slkey + "1"], writes=[("glap", l, h)])
                crot = [0]

                def corr_loads(t_):
                    cq, cqk = stage("qm", [128, 2, 128], BF16, 2)
                    crot[0] += 1
                    co, cok = qkvs[crot[0] % 2][:, 0:512], "qkv" + str(crot[0] % 2)
                    TR.dma("sp", "l" + cqk, cq[:].rearrange("p c k -> p (c k)"), qg_scr[h, t_], reads=[("qg", h, t_)], writes=[cqk])
                    TR.dma("sp", "l" + cok, co, o_scr[t_ * 128:(t_ + 1) * 128, h * 512:(h + 1) * 512], reads=[("o", t_)], writes=[cok])
                    return cq, cqk, co, cok

                nxt_ld = corr_loads(0)
                for t in range(NPT):
                    r = slice(t * 128, (t + 1) * 128)
                    cq, cqk, co, cok = nxt_ld
                    if t + 1 < NPT:
                        nxt_ld = corr_loads(t + 1)
                    C, Ckey = fbank()
                    for ch in range(2):
                        TR.pe(lambda e, ch=ch: e.matmul(C[:, :], lhsT=cq[:, ch, :], rhs=spb[:, ch, :], start=(ch == 0), stop=(ch == 1)),
                              reads=[cqk, spbkey], writes=[Ckey], mark=(ch == 1))
                    TR.op("dve", lambda e: e.tensor_tensor(out=co, in0=C[:, :], in1=co, op=ALU.add),
                          reads=[Ckey, cok], writes=[cok])
                    rstd_of(co, cok, 512, 2)
                    TR.op("dve", lambda e: e.scalar_tensor_tensor(out=onf[:], in0=co, scalar=rs[:, 2:3], in1=ggrow[:], op0=ALU.mult, op1=ALU.mult),
                          reads=[cok, "rs", "ggrow"], writes=["onf"])
                    TR.dma("sp", "sto_o", o_scr[r, h * 512:(h + 1) * 512], onf[:], reads=["onf"], writes=[("o", t)])

        ogt = Bbuf[:, 0, :]; gat = Bbuf[:, 1, :]; gbt = Bbuf[:, 2, :]; ot = Bbuf[:, 3, :]; sgt = Bbuf[:, 4, :]; ut = Bbuf[:, 5, :]
        psrow = Bbuf[:, 6:8, :].rearrange("p a c -> p (a c)")
        ub = [sb([128, 1024], BF16, f"ub{i}") for i in range(2)]
        hist_b = [sb([120, 1024], BF16, f"hist_b{i}") for i in range(2)]
        dTb = sb([128, 8, 128], BF16, "dTb")
        wpool_b = sb([128, 8, 512], BF16, "wpool_b")

        def merge_stage(l):
            TR.dma("pool", "wp", wpool_b[:], w_pool[l].rearrange("(k p) o -> p k o", p=128), writes=["wpool"])
            TR.dma("sp", "ld0", psrow, pscale[l:l + 1, :].partition_broadcast(128), writes=["B6", "B7"])
            for hf in range(2):
                TR.dma("sp", "ld1", Bbuf[0:120, 5, :], spool[l, hf * 120:(hf + 1) * 120, :], writes=["B5"])
                TR.op("dve", lambda e, hf=hf: e.tensor_copy(out=hist_b[hf][:], in_=Bbuf[0:120, 5, :]), reads=["B5"], writes=[f"hist_b{hf}"])
            TR.dma("sp", "lm4", ut, exu_out[l][0:128, :], reads=[("exuout", l)], writes=["B5"])
            TR.op("dve", lambda e: e.tensor_copy(out=ub[1][:], in_=ut), reads=["B5"], writes=["ub1"])
            halves = [(t_, h_) for t_ in range(NT) for h_ in range(2)]
            done = set()

            def mload(idx, which):
                if idx >= len(halves) or (idx, which) in done:
                    return
                done.add((idx, which))
                t_, h_ = halves[idx]
                r_ = slice(t_ * 128, (t_ + 1) * 128)
                c_ = h_ * 1024
                if which == "og":
                    TR.dma("sp", "lm0", ogt, p_scr[r_, OOG + c_:OOG + c_ + 1024], reads=[("p2", t_)], writes=["B0"])
                elif which == "ga":
                    TR.dma("sp", "lm1", gat, p_scr[r_, OGA + c_:OGA + c_ + 1024], reads=[("p2", t_)], writes=["B1"])
                elif which == "gb":
                    TR.dma("sp", "lm2", gbt, p_scr[r_, OGB + c_:OGB + c_ + 1024], reads=[("p2", t_)], writes=["B2"])
                elif which == "o":
                    TR.dma("sp", "lm3", ot, o_scr[r_, c_:c_ + 1024], reads=[("o", t_)], writes=["B3"])

            def uload(t_):
                if t_ < NT and ("u", t_) not in done:
                    done.add(("u", t_))
                    TR.dma("sp", "lm4", ut, p_scr[t_ * 128:(t_ + 1) * 128, OU:OU + 1024], reads=[("p2", t_)], writes=["B5"])

            for t in range(NT):
                ty = 1 if t == NT - 1 else 0
                r = slice(t * 128, (t + 1) * 128)
                uload(t)
                cur, ckey = ub[t % 2], f"ub{t % 2}"
                prv, pvkey = ub[(t + 1) % 2], f"ub{(t + 1) % 2}"
                TR.op("dve", lambda e: e.tensor_copy(out=cur[:], in_=ut), reads=["B5"], writes=[ckey])
                uload(t + 1)
                dbanks = [fbank(), fbank()]
                for g in range(4):
                    Dk, Dkey = dbanks[g // 2]
                    base = 5 + g * 7
                    for ic in range(2):
                        col = ((g % 2) * 2 + ic) * 128
                        cs = slice(g * 256 + ic * 128, g * 256 + (ic + 1) * 128)
                        lastg = (g % 2 == 1 and ic == 1)
                        if ty == 0:
                            mc, mp = (base + 2, base + 6) if t == 0 else (base + 0, base + 1)
                            TR.pe(lambda e: e.matmul(Dk[:, col:col + 128], lhsT=cur[:, cs], rhs=cmat_b[:, mc, :], start=True, stop=False),
                                  reads=[ckey, "cmat"], writes=[Dkey], mark=False)
                            TR.pe(lambda e: e.matmul(Dk[:, col:col + 128], lhsT=prv[:, cs], rhs=cmat_b[:, mp, :], start=False, stop=True),
                                  reads=[pvkey, "cmat"], writes=[Dkey], mark=lastg)
                        else:
                            TR.pe(lambda e: e.matmul(Dk[:, col:col + 128], lhsT=cur[:, cs], rhs=cmat_b[:, base + 3, :], start=True, stop=False),
                                  reads=[ckey, "cmat"], writes=[Dkey], mark=False)
                            for hf in range(2):
                                TR.pe(lambda e, hf=hf: e.matmul(Dk[:, col:col + 128], lhsT=hist_b[hf][0:120, cs], rhs=cmat_b[0:120, base + 4 + hf, :],
                                                                start=False, stop=(hf == 1)),
                                      reads=[f"hist_b{hf}", "cmat"], writes=[Dkey], mark=(lastg and hf == 1))
                for i2 in range(2):
                    Dk, Dkey = dbanks[i2]
                    TR.op("act", lambda e, i2=i2, Dk=Dk: e.copy(out=dTb[:, i2 * 4:(i2 + 1) * 4, :], in_=Dk[:, :].rearrange("p (k c) -> p k c", c=128)),
                          reads=[Dkey], writes=["dTb"])
                for hfc in range(2):
                    bgf(5)
                    c0 = hfc * 1024
                    hi = t * 2 + hfc
                    for wh in ("og", "ga", "gb", "o"):
                        mload(hi, wh)
                    TR.op("act", lambda e: e.activation(out=sgt, in_=ogt, func=AF.Sigmoid), reads=["B0"], writes=["B4"])
                    TR.op("dve", lambda e: e.tensor_tensor(out=ogt, in0=ogt, in1=sgt, op=ALU.mult), reads=["B0", "B4"], writes=["B0"])
                    TR.op("dve", lambda e: e.tensor_tensor(out=ogt, in0=ogt, in1=ot, op=ALU.mult), reads=["B0", "B3"], writes=["B0"])
                    mload(hi + 1, "o")
                    TR.op("act", lambda e: e.activation(out=sgt, in_=gat, func=AF.Sigmoid), reads=["B1"], writes=["B4"])
                    mload(hi + 1, "ga")
                    TR.op("dve", lambda e: e.tensor_tensor(out=ogt, in0=ogt, in1=sgt, op=ALU.mult), reads=["B0", "B4"], writes=["B0"])
                    TR.op("act", lambda e: e.activation(out=sgt, in_=gbt, func=AF.Sigmoid), reads=["B2"], writes=["B4"])
                    TR.op("dve", lambda e: e.tensor_tensor(out=sgt, in0=sgt, in1=psrow[:, c0:c0 + 1024], op=ALU.mult), reads=["B4", "B6", "B7"], writes=["B4"])
                    for g2 in range(2):
                        g = hfc * 2 + g2
                        Y, Ykey = fbank()
                        for ic in range(2):
                            TR.pe(lambda e, ic=ic: e.matmul(Y[:, :], lhsT=dTb[:, g * 2 + ic, :], rhs=wpool_b[:, g * 2 + ic, :], start=(ic == 0), stop=(ic == 1)),
                                  reads=["dTb", "wpool"], writes=[Ykey], mark=(ic == 1))
                        gs = slice(g2 * 512, (g2 + 1) * 512)
                        TR.op("dve", lambda e, gs=gs: e.tensor_tensor(out=gbt[:, gs], in0=Y[:, :], in1=sgt[:, gs], op=ALU.mult),
                              reads=[Ykey, "B4", "B2"], writes=["B2"])
                    TR.op("dve", lambda e: e.tensor_tensor(out=mb[:, c0:c0 + 1024], in0=gbt, in1=ogt, op=ALU.add), reads=["B2", "B0"], writes=["hb"])
                to_feat(mb, "hb", 16, t)
            TR.dma("sp", "po0", pool_p[l], p_scr[SEQ - 15:SEQ, OU:OU + 1024], reads=[("p2", NPT - 1)], writes=[("poolp", l)])
            TR.dma("sp", "po1", pool_s[l, :, 0:7, :], spool[l].rearrange("(j r) c -> j r c", r=15)[:, 8:15, :], writes=[("pools0", l)])
            TR.dma("sp", "po2", pool_s[l, :, 7:15, :], p_scr[SEQ:T, OU:OU + 1024].rearrange("(j r) c -> j r c", r=8),
                   reads=[("p2", NT - 1)], writes=[("pools1", l)])

        g1row = sb([128, 2, 512], F32, "g1row")

        def resid_epi(l, gcol, xprev, pkey_prev, xnext, nkey):
            pend = {}

            def pre(t, c, pc):
                xs, xskey = stage("xs", [128, 512], F32, 2)
                TR.dma("sp", "l" + xskey, xs[:, 0:pc], xprev[t * 128:(t + 1) * 128, c:c + pc], reads=[(pkey_prev, t)], writes=[xskey])
                pend[(t, c)] = (xs, xskey)

            def epi(t, c, pc, ps, pkey):
                ty = 1 if t == NT - 1 else 0
                if t == 0:
                    for ty2 in range(2):
                        TR.dma("sp", "ldg", g1row[:, ty2, 0:pc], mod_scr[l, ty2 * 128:(ty2 + 1) * 128, gcol + c:gcol + c + pc],
                               reads=[("mod", l, gcol // (3 * D))], writes=["g1row"])
                xs, xskey = pend.pop((t, c))
                sg, skey = stage("sto", [128, 512], F32, 2)
                TR.op("dve", lambda e: e.tensor_tensor(out=sg[:, 0:pc], in0=ps[:, 0:pc], in1=g1row[:, ty, 0:pc], op=ALU.mult),
                      reads=[pkey, "g1row"], writes=[skey])
                TR.op("dve", lambda e: e.tensor_tensor(out=sg[:, 0:pc], in0=sg[:, 0:pc], in1=xs[:, 0:pc], op=ALU.add),
                      reads=[skey, xskey], writes=[skey])
                TR.dma("sp", "st" + skey, xnext[t * 128:(t + 1) * 128, c:c + pc], sg[:, 0:pc], reads=[skey], writes=[(nkey, t)])
            epi.pre = pre
            return epi

        xcur, xkey = xin, "xin"
        tiles = list(range(NT))
        for l in range(DEPTH):
            norm_stage(l, xcur, xkey, n1g, l, 0, D)
            linear(16, tiles, w_in[l], 0, OQ, OOG - OQ, store_epi(p_scr, "p"))
            linear(16, tiles, w_in[l], 0, OALR, 16, store_epi(p_scr, "p"))
            fill_state["gen"] = linear_gen(16, tiles, w_in[l], 0, OOG, OALR - OOG, store_epi(p_scr, "p2"))
            gla_stage(l)
            fillf(10 ** 6)
            if l == 0:
                bg_state["gen"] = chain_gens(ada_bg(0, 2 * D, 4 * D), ada_bg(1, 0, 6 * D))
            exchange_stage(l)
            correct_stage(l)
            merge_stage(l)
            bgf(10 ** 6)
            x1, x1key = xa, "xa"
            linear(16, tiles, w_o[l], 0, 0, D, resid_epi(l, 2 * D, xcur, xkey, x1, x1key))
            norm_stage(l, x1, x1key, n2g, l, 3 * D, 4 * D)
            linear(16, tiles, w_gu[l], 0, 0, 2 * DFF, store_epi(gu_scr, "gu"))
            prev, prevkey = x1, x1key
            steps = [(k0, kc, t, c0, min(1024, kc * 128 - c0)) for (k0, kc) in ((0, 16), (16, 16), (32, 12)) for t in tiles
                     for c0 in range(0, kc * 128, 1024)]
            sets = [(0, 1, 2), (3, 4, 5)]

            def ffn_load(si):
                k0_, kc_, t_, c0_, w_ = steps[si]
                a_, b_, _ = sets[si % 2]
                r_ = slice(t_ * 128, (t_ + 1) * 128)
                TR.dma("sp", f"lf0{si % 2}", Bbuf[:, a_, 0:w_], gu_scr[r_, k0_ * 128 + c0_:k0_ * 128 + c0_ + w_], reads=[("gu", t_)], writes=[f"B{a_}"])
                TR.dma("sp", f"lf1{si % 2}", Bbuf[:, b_, 0:w_], gu_scr[r_, DFF + k0_ * 128 + c0_:DFF + k0_ * 128 + c0_ + w_], reads=[("gu", t_)], writes=[f"B{b_}"])

            ffn_load(0)
            si = 0
            for gi, (k0, kc) in enumerate(((0, 16), (16, 16), (32, 12))):
                n = kc * 128
                for t in tiles:
                    for c0 in range(0, n, 1024):
                        w = min(1024, n - c0)
                        if si + 1 < len(steps):
                            ffn_load(si + 1)
                        a_, b_, c_ = sets[si % 2]
                        si += 1
                        gt_ = Bbuf[:, a_, 0:w]; upt = Bbuf[:, b_, 0:w]; sg2 = Bbuf[:, c_, 0:w]
                        TR.op("act", lambda e: e.activation(out=sg2, in_=gt_, func=AF.Sigmoid), reads=[f"B{a_}"], writes=[f"B{c_}"])
                        TR.op("dve", lambda e: e.tensor_tensor(out=gt_, in0=gt_, in1=sg2, op=ALU.mult), reads=[f"B{a_}", f"B{c_}"], writes=[f"B{a_}"])
                        TR.op("dve", lambda e: e.tensor_tensor(out=mb[:, c0:c0 + w], in0=gt_, in1=upt, op=ALU.mult), reads=[f"B{a_}", f"B{b_}"], writes=["hb"])
                    to_feat(mb, "hb", kc, t)
                nxt, nkey = ((xb, "xb"), (xc, "xc"), (xb, "xb"))[gi]
                linear(kc, tiles, w_down[l], k0 * 128, 0, D, resid_epi(l, 5 * D, prev, prevkey, nxt, nkey))
                if gi > 0:
                    pass
                prev, prevkey = nxt, nkey
            xcur, xkey = prev, prevkey

        TR.dma("sp", "ld0", rowA, fng[0:1, :].partition_broadcast(128), writes=RA)
        for t in tiles:
            TR.dma("sp", "ldx", xt, xcur[t * 128:(t + 1) * 128, :], reads=[(xkey, t)], writes=XT)
            rstd_of(xt, "B0", D, 0)
            TR.op("dve", lambda e: e.scalar_tensor_tensor(out=ht, in0=xt, scalar=rs[:, 0:1], in1=rowA, op0=ALU.mult, op1=ALU.mult),
                  reads=XT + ["rs"] + RA, writes=HT)
            TR.dma("sp", "sty", y[t * 128:(t + 1) * 128, :], ht, reads=HT, writes=[("y", t)])
        for d in TR.dsem.values():
            TR.wait("sp", (d[0], d[1]))
    return nc


def _consts(first_half):
    s = np.arange(128)[:, None]; c = np.arange(128)[None, :]
    m = np.zeros((NMAT, 128, 128), np.float32)
    m[0] = np.eye(128)
    m[1] = (s <= c)
    m[2] = (s > c)
    same = (s // 8) == (c // 8)
    m[3] = (s <= c) & same
    m[4] = (s > c) & same
    for g, w in enumerate(WINS):
        b = 5 + g * 7
        m[b + 0] = ((s <= c) & (s > c - w)) / w - (s == c)
        m[b + 1] = ((s - 128) > (c - w)) / w
        cntc = np.minimum(c + 1, w)
        if first_half:
            m[b + 2] = ((s <= c) & (s > c - w)) / cntc - (s == c)
            m[b + 6] = 0.0
        else:
            m[b + 2] = m[b + 0]
            m[b + 6] = m[b + 1]
        m[b + 3] = ((s <= c) & (s > c - w) & same) / w - (s == c)
        for hf in range(2):
            hr = np.arange(128)[:, None]
            jj = hr // 15 + hf * 8; rr = hr % 15
            pos_h = rr - 15
            ci = c % 8; cj = c // 8
            m[b + 4 + hf] = ((hr < 120) & (jj == cj) & (pos_h > ci - w)) / w
    cm = np.zeros((128, 16, 128), np.float32)
    for j in range(16):
        cm[:, j, 8 * j:8 * j + 8] = 1.0
    rm = np.zeros((128, 16), np.float32)
    for j in range(16):
        rm[8 * j:8 * j + 8, j] = 1.0
    return m, cm.reshape(128, 2048), rm


_NC = None


def kernel(x_prompt, x_sample, state_gla, state_pool, c_prompt, c_sample, w_ada, b_ada, norm1_g, w_in, w_a2, b_a,
           gla_norm_g, w_pool, pool_scale, w_o, norm2_g, w_gu, w_down, final_norm_g):
    global _NC
    f = lambda a: np.ascontiguousarray(np.asarray(a, dtype=np.float32))
    x_prompt, x_sample, state_gla, state_pool, c_prompt, c_sample = map(f, (x_prompt, x_sample, state_gla, state_pool, c_prompt, c_sample))
    shared = {
        "w_ada": f(w_ada), "b_ada": f(b_ada), "norm1_g": f(norm1_g), "w_in": f(w_in), "w_a2": f(w_a2), "b_a": f(b_a),
        "gla_norm_g": f(gla_norm_g), "w_pool": f(w_pool).reshape(DEPTH, 1024, 512), "pool_scale": f(pool_scale), "w_o": f(w_o),
        "norm2_g": f(norm2_g), "w_gu": f(w_gu), "w_down": f(w_down), "final_norm_g": f(final_norm_g).reshape(1, D),
    }
    cst = [_consts(True), _consts(False)]
    in_maps = []
    for c in range(NCORE):
        b, half = c // 2, c % 2
        sl = slice(SPC * c, SPC * (c + 1))
        m = dict(shared)
        m["cmats"], m["cmk"], m["rmk"] = cst[half][0], cst[half][1], cst[half][2]
        m["flag"] = np.full((128, 1), float(half), np.float32)
        m["xin"] = np.concatenate([x_prompt[b, half * SEQ:(half + 1) * SEQ], x_sample[sl].reshape(SPC * 8, D)], axis=0)
        m["crow"] = np.concatenate([np.repeat(c_prompt[b:b + 1], 128, axis=0), np.repeat(c_sample[sl], 8, axis=0)], axis=0)
        m["sgla"] = np.ascontiguousarray(state_gla[:, sl])
        m["spool"] = np.ascontiguousarray(state_pool[:, sl]).reshape(DEPTH, SPC * 15, 1024)
        in_maps.append(m)
    if _NC is None:
        _NC = build_program()
    res = run_bass_kernel_spmd(_NC, in_maps, core_ids=list(range(NCORE)))
    R = res.results
    y_prompt = np.stack([np.concatenate([R[2 * b]["y"][:SEQ], R[2 * b + 1]["y"][:SEQ]], axis=0) for b in range(4)])
    y_sample = np.concatenate([R[c]["y"][SEQ:].reshape(SPC, 8, D) for c in range(NCORE)], axis=0)
    gla_pp = np.stack([R[2 * b + 1]["gla_p"] for b in range(4)], axis=1)
    pool_pp = np.stack([R[2 * b + 1]["pool_p"] for b in range(4)], axis=1)
    gla_ss = np.concatenate([R[c]["gla_s"] for c in range(NCORE)], axis=1)
    pool_ss = np.concatenate([R[c]["pool_s"] for c in range(NCORE)], axis=1)
    return (y_prompt.astype(np.float32), y_sample.astype(np.float32), gla_pp.astype(np.float32), pool_pp.astype(np.float32),
            gla_ss.astype(np.float32), pool_ss.astype(np.float32))
```
